# Optimizing a Trainium2 kernel written in Bass

```python
import math
import jax, jax.numpy as jnp
from jax import lax
import numpy as np

D_MODEL = 1024
BATCH = 2
SEQ = 8192
DEPTH = 1
DEC_BATCH = 16
DEC_SEQ = 16
PAST_LEN = 2048

CHUNK = 64
HA = 8
DH = 64
H_IDX = 8
D_IDX = 64
TOPK_MAX = 256
Q_BLOCK = 128
NUM_BUCKETS = 32
MAX_DISTANCE = 128
HB = 4
DK = 128
DV = 128
CONV_B = 4
D_FF = 2816
CONV_F = 3
D_PLE = 256
EPS = 1e-6
NEG = -1e30

WA = HA * DH
WB_K = HB * DK
WB_V = HB * DV
C_CONV_B = 2 * WB_K + WB_V
SPLITS = [WA, WA, WA, H_IDX * D_IDX, D_IDX, H_IDX, WB_K, WB_K, WB_V, WB_V, HB, HB, D_MODEL, D_MODEL]
D_IN = sum(SPLITS)

kernel_name = 'hybrid_dsa_gdn_stream'


def _split_offsets():
    offs, acc = [], 0
    for s in SPLITS[:-1]:
        acc += s
        offs.append(acc)
    return offs


def _rmsnorm(x, g):
    xf = x.astype(jnp.float32)
    y = xf * lax.rsqrt(jnp.mean(xf * xf, axis=-1, keepdims=True) + EPS)
    return (y * g.astype(jnp.float32)).astype(x.dtype)


def _l2norm(x):
    return x * lax.rsqrt(jnp.sum(x * x, axis=-1, keepdims=True) + EPS)


def _causal_dwconv(ext, w):
    width = w.shape[0]
    T = ext.shape[1] - width + 1
    out = ext[:, 0:T] * w[0]
    for j in range(1, width):
        out = out + ext[:, j:j + T] * w[j]
    return out


def _t5_bucket(rel):
    half = NUM_BUCKETS // 2
    max_exact = half // 2
    ret = jnp.where(rel > 0, half, 0)
    n = jnp.abs(rel)
    nf = jnp.maximum(n, 1).astype(jnp.float32)
    large = max_exact + (jnp.log(nf / max_exact) / math.log(MAX_DISTANCE / max_exact)
                         * (half - max_exact)).astype(jnp.int32)
    large = jnp.minimum(large, half - 1)
    return ret + jnp.where(n < max_exact, n, large)


def _dsa_block(q, qi, wi, q_pos, k, v, ki, topk, rel_bias):
    f32 = jnp.float32
    L = k.shape[1]
    k_pos = jnp.arange(L, dtype=jnp.int32)
    s = jnp.einsum('bqhd,bsd->bqhs', qi.astype(f32), ki.astype(f32)) * (D_IDX ** -0.5)
    score = jnp.einsum('bqhs,bqh->bqs', jax.nn.relu(s), wi.astype(f32))
    adm = (k_pos[None, :] // CHUNK) <= (q_pos[:, None] // CHUNK)
    score = jnp.where(adm[None], score, NEG)
    _, idx = lax.top_k(score, topk)
    k_sel = jax.vmap(lambda kb, ib: kb[ib])(k, idx)
    v_sel = jax.vmap(lambda vb, ib: vb[ib])(v, idx)
    valid = (idx // CHUNK) <= (q_pos[None, :, None] // CHUNK)
    bias = jnp.moveaxis(rel_bias[_t5_bucket(idx - q_pos[None, :, None])].astype(f32), -1, 2)
    logits = jnp.einsum('bqhd,bqkhd->bqhk', q.astype(f32), k_sel.astype(f32)) * (DH ** -0.5) + bias
    logits = jnp.where(valid[:, :, None, :], logits, NEG)
    prob = jax.nn.softmax(logits, axis=-1)
    out = jnp.einsum('bqhk,bqkhd->bqhd', prob, v_sel.astype(f32))
    return out.astype(q.dtype)


def _gated_delta_chunked(q, k, v, g, beta, s0):
    B, T = q.shape[0], q.shape[1]
    n = -(-T // CHUNK)
    pad = n * CHUNK - T

    def prep(a):
        a = jnp.pad(a, [(0, 0), (0, pad)] + [(0, 0)] * (a.ndim - 2))
        a = a.reshape((B, n, CHUNK) + a.shape[2:])
        return jnp.moveaxis(a, (1, 3), (0, 2))

    qc, kc, vc, gc, bc = prep(q), prep(k), prep(v), prep(g), prep(beta)
    gcum = jnp.cumsum(gc, axis=-1)
    tri = jnp.tril(jnp.ones((CHUNK, CHUNK), dtype=bool))
    strict = jnp.tril(jnp.ones((CHUNK, CHUNK), dtype=bool), -1)
    diff = gcum[..., :, None] - gcum[..., None, :]
    decay = jnp.where(tri, jnp.exp(jnp.where(tri, diff, 0.0)), 0.0)
    kb = kc * bc[..., None]
    a_mat = jnp.where(strict, jnp.einsum('...id,...jd->...ij', kb, kc) * decay, 0.0) + jnp.eye(CHUNK, dtype=jnp.float32)
    rhs = jnp.concatenate([vc * bc[..., None], kb * jnp.exp(gcum)[..., None]], axis=-1)
    sol = lax.linalg.triangular_solve(a_mat, rhs, left_side=True, lower=True, unit_diagonal=True)
    u, w = sol[..., :DV], sol[..., DV:]
    attn_intra = jnp.where(tri, jnp.einsum('...id,...jd->...ij', qc, kc) * decay, 0.0)

    def step(S, xs):
        qi, ki, ui, wi, gi, ai = xs
        v_new = ui - jnp.einsum('bhcd,bhde->bhce', wi, S)
        o = jnp.einsum('bhcd,bhde->bhce', qi * jnp.exp(gi)[..., None], S) + jnp.einsum('bhij,bhje->bhie', ai, v_new)
        g_last = gi[..., -1:]
        S = S * jnp.exp(g_last)[..., None] + jnp.einsum('bhcd,bhce->bhde', ki * jnp.exp(g_last - gi)[..., None], v_new)
        return S, o

    s_fin, o = lax.scan(step, s0, (qc, kc, u, w, gcum, attn_intra))
    o = jnp.moveaxis(o, (0, 2), (1, 3)).reshape(B, n * CHUNK, HB, DV)[:, :T]
    return o, s_fin


def _layer(x, p, past_k, past_v, past_kidx, s_gdn, conv_b_buf, ffn_buf,
           norm_mix, w_in, conv_b, a_log, dt_bias, norm_gdn, w_proj_a, w_proj_b, w_out,
           norm_ffn, w_up, conv_ffn, w_down, norm_ple, w_ple, w_ple_gate, rel_bias):
    f32 = jnp.float32
    B, T, _ = x.shape
    past = past_k.shape[1]
    L = past + T
    h = _rmsnorm(x, norm_mix)
    qa, ka, va, qi, ki, wi, qb, kb, vb, gb, bb, ab, ga, gbr = jnp.split(h @ w_in, _split_offsets(), axis=-1)

    qa = qa.reshape(B, T, HA, DH)
    ka = ka.reshape(B, T, HA, DH)
    va = va.reshape(B, T, HA, DH)
    qi = qi.reshape(B, T, H_IDX, D_IDX)
    wi = wi * (H_IDX ** -0.5)
    k_all = jnp.concatenate([past_k.astype(ka.dtype), ka], axis=1)
    v_all = jnp.concatenate([past_v.astype(va.dtype), va], axis=1)
    ki_all = jnp.concatenate([past_kidx.astype(ki.dtype), ki], axis=1)
    q_pos = past + jnp.arange(T, dtype=jnp.int32)
    topk = min(TOPK_MAX, L // 4)
    if T > Q_BLOCK and T % Q_BLOCK == 0:
        nb = T // Q_BLOCK

        def to_blocks(a):
            return jnp.moveaxis(a.reshape((B, nb, Q_BLOCK) + a.shape[2:]), 1, 0)

        def blk(args):
            qb_, qib_, wib_, posb_ = args
            return _dsa_block(qb_, qib_, wib_, posb_, k_all, v_all, ki_all, topk, rel_bias)

        oa = lax.map(blk, (to_blocks(qa), to_blocks(qi), to_blocks(wi), q_pos.reshape(nb, Q_BLOCK)))
        oa = jnp.moveaxis(oa, 0, 1).reshape(B, T, HA, DH)
    else:
        oa = _dsa_block(qa, qi, wi, q_pos, k_all, v_all, ki_all, topk, rel_bias)
    ya = oa.reshape(B, T, WA) @ w_proj_a

    conv_in = jnp.concatenate([qb, kb, vb], axis=-1)
    ext = jnp.concatenate([conv_b_buf.astype(conv_in.dtype), conv_in], axis=1)
    new_conv_b = ext[:, T:]
    cb = jax.nn.silu(_causal_dwconv(ext, conv_b))
    qb_c, kb_c, vb_c = jnp.split(cb, [WB_K, 2 * WB_K], axis=-1)
    qh = _l2norm(qb_c.reshape(B, T, HB, DK).astype(f32)) * (DK ** -0.5)
    kh = _l2norm(kb_c.reshape(B, T, HB, DK).astype(f32))
    vh = vb_c.reshape(B, T, HB, DV).astype(f32)
    beta = jax.nn.sigmoid(bb.astype(f32))
    g = -jnp.exp(a_log.astype(f32)) * jax.nn.softplus(ab.astype(f32) + dt_bias.astype(f32))
    ob, s_new = _gated_delta_chunked(qh, kh, vh, g, beta, s_gdn.astype(f32))
    ob = _rmsnorm(ob, norm_gdn) * jax.nn.silu(gb.reshape(B, T, HB, DV).astype(f32))
    yb = ob.reshape(B, T, WB_V).astype(x.dtype) @ w_proj_b

    mix = jax.nn.sigmoid(ga) * ya + jax.nn.sigmoid(gbr) * yb
    x = x + mix @ w_out

    h2 = _rmsnorm(x, norm_ffn)
    u_gate, u_val = jnp.split(h2 @ w_up, [D_FF], axis=-1)
    ext_f = jnp.concatenate([ffn_buf.astype(u_gate.dtype), u_gate], axis=1)
    new_ffn = ext_f[:, T:]
    act = jax.nn.gelu(_causal_dwconv(ext_f, conv_ffn), approximate=True)
    x = x + (act * u_val) @ w_down

    gate_p = jax.nn.sigmoid(_rmsnorm(x, norm_ple) @ w_ple_gate)
    x = x + gate_p * (p @ w_ple)
    return x, ka, va, ki, s_new.astype(x.dtype), new_conv_b, new_ffn


def setup_inputs(seed: int = 0) -> dict:
    key = jax.random.key(seed)
    ks = jax.random.split(key, 32)
    nrm = jax.random.normal
    f32 = jnp.float32
    inp = {}
    inp['x_prompt'] = nrm(ks[0], (BATCH, SEQ, D_MODEL), f32)
    inp['x_sample'] = nrm(ks[1], (DEC_BATCH, DEC_SEQ, D_MODEL), f32)
    inp['p_prompt'] = nrm(ks[2], (DEPTH, BATCH, SEQ, D_PLE), f32)
    inp['p_sample'] = nrm(ks[3], (DEPTH, DEC_BATCH, DEC_SEQ, D_PLE), f32)
    inp['cache_k'] = nrm(ks[4], (DEPTH, DEC_BATCH, PAST_LEN, HA, DH), f32)
    inp['cache_v'] = nrm(ks[5], (DEPTH, DEC_BATCH, PAST_LEN, HA, DH), f32)
    inp['cache_kidx'] = nrm(ks[6], (DEPTH, DEC_BATCH, PAST_LEN, D_IDX), f32)
    inp['state_gdn'] = 0.1 * nrm(ks[7], (DEPTH, DEC_BATCH, HB, DK, DV), f32)
    inp['state_gdn_conv'] = nrm(ks[8], (DEPTH, DEC_BATCH, CONV_B - 1, C_CONV_B), f32)
    inp['state_ffn_conv'] = nrm(ks[9], (DEPTH, DEC_BATCH, CONV_F - 1, D_FF), f32)
    inp['norm_mix'] = 1.0 + 0.05 * nrm(ks[10], (DEPTH, D_MODEL), f32)
    inp['w_in'] = nrm(ks[11], (DEPTH, D_MODEL, D_IN), f32) * D_MODEL ** -0.5
    inp['conv_b'] = nrm(ks[12], (DEPTH, CONV_B, C_CONV_B), f32) * CONV_B ** -0.5
    inp['a_log'] = jnp.log(jax.random.uniform(ks[13], (DEPTH, HB), f32, 1.0, 16.0))
    dt = jax.random.uniform(ks[14], (DEPTH, HB), f32, 0.001, 0.1)
    inp['dt_bias'] = jnp.log(jnp.expm1(dt))
    inp['norm_gdn'] = 1.0 + 0.05 * nrm(ks[15], (DEPTH, DV), f32)
    inp['w_proj_a'] = nrm(ks[16], (DEPTH, WA, D_MODEL), f32) * WA ** -0.5
    inp['w_proj_b'] = nrm(ks[17], (DEPTH, WB_V, D_MODEL), f32) * WB_V ** -0.5
    inp['w_out'] = nrm(ks[18], (DEPTH, D_MODEL, D_MODEL), f32) * D_MODEL ** -0.5
    inp['norm_ffn'] = 1.0 + 0.05 * nrm(ks[19], (DEPTH, D_MODEL), f32)
    inp['w_up'] = nrm(ks[20], (DEPTH, D_MODEL, 2 * D_FF), f32) * D_MODEL ** -0.5
    inp['conv_ffn'] = nrm(ks[21], (DEPTH, CONV_F, D_FF), f32) * CONV_F ** -0.5
    inp['w_down'] = nrm(ks[22], (DEPTH, D_FF, D_MODEL), f32) * D_FF ** -0.5
    inp['norm_ple'] = 1.0 + 0.05 * nrm(ks[23], (DEPTH, D_MODEL), f32)
    inp['w_ple'] = nrm(ks[24], (DEPTH, D_PLE, D_MODEL), f32) * D_PLE ** -0.5
    inp['w_ple_gate'] = nrm(ks[25], (DEPTH, D_MODEL, D_MODEL), f32) * D_MODEL ** -0.5
    inp['rel_bias'] = 0.5 * nrm(ks[26], (NUM_BUCKETS, HA), f32)
    inp['norm_final'] = 1.0 + 0.05 * nrm(ks[27], (D_MODEL,), f32)
    return inp


def reference(x_prompt, x_sample, p_prompt, p_sample, cache_k, cache_v, cache_kidx, state_gdn,
              state_gdn_conv, state_ffn_conv, norm_mix, w_in, conv_b, a_log, dt_bias, norm_gdn,
              w_proj_a, w_proj_b, w_out, norm_ffn, w_up, conv_ffn, w_down, norm_ple, w_ple,
              w_ple_gate, rel_bias, norm_final):
    xp, xs = x_prompt, x_sample
    bp = xp.shape[0]
    dt = xp.dtype
    outs_p = [[] for _ in range(6)]
    outs_s = [[] for _ in range(6)]
    for i in range(DEPTH):
        lw = (norm_mix[i], w_in[i], conv_b[i], a_log[i], dt_bias[i], norm_gdn[i], w_proj_a[i], w_proj_b[i],
              w_out[i], norm_ffn[i], w_up[i], conv_ffn[i], w_down[i], norm_ple[i], w_ple[i], w_ple_gate[i], rel_bias)
        xp, *st_p = _layer(xp, p_prompt[i],
                           jnp.zeros((bp, 0, HA, DH), dt), jnp.zeros((bp, 0, HA, DH), dt),
                           jnp.zeros((bp, 0, D_IDX), dt), jnp.zeros((bp, HB, DK, DV), dt),
                           jnp.zeros((bp, CONV_B - 1, C_CONV_B), dt), jnp.zeros((bp, CONV_F - 1, D_FF), dt),
                           *lw)
        xs, *st_s = _layer(xs, p_sample[i], cache_k[i], cache_v[i], cache_kidx[i], state_gdn[i],
                           state_gdn_conv[i], state_ffn_conv[i], *lw)
        for lst, a in zip(outs_p, st_p):
            lst.append(a)
        for lst, a in zip(outs_s, st_s):
            lst.append(a)
    y_prompt = _rmsnorm(xp, norm_final)
    y_sample = _rmsnorm(xs, norm_final)
    k_p, v_p, kidx_p, gdn_p, gconv_p, fconv_p = [jnp.stack(a, axis=0) for a in outs_p]
    k_s, v_s, kidx_s, gdn_s, gconv_s, fconv_s = [jnp.stack(a, axis=0) for a in outs_s]
    return (y_prompt, y_sample, k_p, v_p, kidx_p, gdn_p, gconv_p, fconv_p,
            k_s, v_s, kidx_s, gdn_s, gconv_s, fconv_s)
```

```python
import os
import numpy as np
from contextlib import ExitStack
import concourse.bass as bass
import concourse.mybir as mybir
from concourse.bass_utils import run_bass_kernel_spmd

F32 = mybir.dt.float32
BF16 = mybir.dt.bfloat16
AF = mybir.ActivationFunctionType
ALU = mybir.AluOpType
AX = mybir.AxisListType

EPOCH = 4096
KDBG = int(os.environ.get('KDBG', '9'))
KJOB = os.environ.get('KJOB', 'ps')
KSUB = int(os.environ.get('KSUB', '99'))
KJOB2 = os.environ.get('KJOB2', 'PS')
KP2 = int(os.environ.get('KP2', '99'))
KP3 = int(os.environ.get('KP3', '99'))
EPS = 1e-6
NEG = -1e30

D = 1024
LKP = 8192
NTP = 256
PAST = 2048
LKS = PAST + 128
DFF = 2816

C_QA, C_KA, C_VA, C_QI, C_KI, C_WI, C_QB, C_GB, C_BB, C_AB, C_GA, C_GBR = 0, 512, 1024, 1536, 2048, 2112, 2120, 3656, 4168, 4172, 4176, 5200
DIN = 6224


class Tl:
    __slots__ = ("t", "name", "lw", "rd", "excl")

    def __init__(self, t, name, excl=False):
        self.t = t
        self.name = name
        self.lw = None
        self.rd = []
        self.excl = excl

    def __getitem__(self, idx):
        return self.t[idx]


class Sched:
    ENGS = ("pe", "act", "dve", "pool", "sp")

    def __init__(self, nc, es):
        self.nc = nc
        self.es = es
        self.q = {e: [] for e in self.ENGS}
        self.cnt = {e: 0 for e in self.ENGS}
        self.sems = {e: [] for e in self.ENGS}
        self.seen = {e: {} for e in self.ENGS}
        self.dma_sems = {}
        self.keep = []
        self.nsem = 0

    def _newsem(self, name):
        self.nsem += 1
        return self.es.enter_context(self.nc.semaphore(name))

    def _eng_token(self, e):
        c = self.cnt[e]
        ep = c // EPOCH
        while len(self.sems[e]) <= ep:
            self.sems[e].append(self._newsem(f"s_{e}_{len(self.sems[e])}"))
        self.cnt[e] = c + 1
        return (("e", e, ep), self.sems[e][ep], (c % EPOCH) + 1)

    def _waits(self, e, reads, writes):
        toks = []
        for t in reads:
            if t.lw is not None:
                toks.append(t.lw)
            if t.excl:
                toks.extend(t.rd)
        for t in writes:
            if t.lw is not None:
                toks.append(t.lw)
            toks.extend(t.rd)
        best = {}
        for (key, sem, val) in toks:
            if key[0] == "e" and key[1] == "pe" and e == "pe":
                continue
            if best.get(key, (None, 0))[1] < val:
                best[key] = (sem, val)
        out = []
        seen = self.seen[e]
        for key, (sem, val) in best.items():
            if seen.get(key, 0) >= val:
                continue
            seen[key] = val
            out.append((sem, val))
        return out

    def op(self, e, fn, reads=(), writes=()):
        w = self._waits(e, reads, writes)
        tok = self._eng_token(e)
        self.q[e].append((w, fn, tok[1], 1))
        for t in writes:
            t.lw = tok
            t.rd = []
        for t in reads:
            if t.lw is not tok:
                t.rd.append(tok)
        return tok

    def dma(self, e, fn, reads=(), writes=(), key=None):
        w = self._waits(e, reads, writes)
        k = key if key is not None else (writes[0] if writes else reads[0])
        kid = id(k)
        ent = self.dma_sems.get(kid)
        if ent is None or ent[1] >= 32000:
            if ent is None:
                self.keep.append(k)
            if getattr(self, "pool", None):
                ent = self.pool.pop()
            else:
                self.semid = getattr(self, "semid", 0) + 1
                ent = [self._newsem(f"d_{self.nsem}"), 0, self.semid]
            self.dma_sems[kid] = ent
        ent[1] += 16
        tok = (("d", ent[2], 0), ent[0], ent[1])
        self.q[e].append((w, fn, ent[0], 16))
        for t in writes:
            t.lw = tok
            t.rd = []
        for t in reads:
            if t.lw is not tok:
                t.rd.append(tok)
        return tok

    def barrier(self):
        toks = []
        for e in self.ENGS:
            c = self.cnt[e]
            if c > 0:
                ep = (c - 1) // EPOCH
                toks.append((("e", e, ep), self.sems[e][ep], ((c - 1) % EPOCH) + 1))
        for kid, ent in self.dma_sems.items():
            if ent[1] > 0:
                toks.append((("d", ent[2], 0), ent[0], ent[1]))
        for e in self.ENGS:
            w = []
            seen = self.seen[e]
            for (key, sem, val) in toks:
                if key[0] == "e" and key[1] == e:
                    continue
                if seen.get(key, 0) >= val:
                    continue
                seen[key] = val
                w.append((sem, val))
            if w:
                self.q[e].append((w, None, None, 0))
        if not hasattr(self, "pool"):
            self.pool = []
        for kid, ent in self.dma_sems.items():
            if ent[1] < 30000:
                self.pool.append(ent)
        self.dma_sems = {}

    def final_wait(self, e, tiles):
        w = self._waits(e, tiles, tiles)
        self.q[e].append((w, None, None, 0))

    def emit(self):
        nc = self.nc
        with nc.Block() as block:
            def run(eng, name):
                for (w, fn, sem, inc) in self.q[name]:
                    for (s, v) in w:
                        eng.wait_ge(s, v)
                    if fn is not None:
                        fn(eng).then_inc(sem, inc)

            @block.tensor
            def _(eng):
                run(eng, "pe")

            @block.scalar
            def _(eng):
                run(eng, "act")

            @block.vector
            def _(eng):
                run(eng, "dve")

            @block.gpsimd
            def _(eng):
                run(eng, "pool")

            @block.sync
            def _(eng):
                run(eng, "sp")


class K:
    def __init__(self, nc, S):
        self.nc = nc
        self.S = S
        self.rr = 0

    def mm(self, out, lhsT, rhs, rd, wr, start=True, stop=True):
        self.S.op("pe", lambda e, o=out, l=lhsT, r=rhs, a=start, b=stop: e.matmul(o, lhsT=l, rhs=r, start=a, stop=b), reads=rd, writes=wr)

    def tr(self, out, in_, ident, rd, wr):
        self.S.op("pe", lambda e, o=out, i=in_, d=ident: e.matmul(o, lhsT=i, rhs=d, start=True, stop=True), reads=rd, writes=wr)

    def act(self, out, in_, func, rd, wr, scale=None, bias=None, accum=None):
        kw = {}
        if scale is not None:
            kw["scale"] = scale
        if bias is not None:
            kw["bias"] = bias
        if accum is not None:
            kw["accum_out"] = accum
        self.S.op("act", lambda e, o=out, i=in_, f=func, k=kw: e.activation(out=o, in_=i, func=f, **k), reads=rd, writes=wr)

    def ts(self, eng, out, in0, s1, op0, rd, wr, s2=None, op1=None, accum=None):
        kw = {}
        if op1 is not None:
            kw["op1"] = op1
        if accum is not None:
            kw["accum_out"] = accum
        self.S.op(eng, lambda e, o=out, i=in0, a=s1, b=s2, p=op0, k=kw: e.tensor_scalar(out=o, in0=i, scalar1=a, scalar2=b, op0=p, **k), reads=rd, writes=wr)

    def tt(self, eng, out, in0, in1, op, rd, wr):
        self.S.op(eng, lambda e, o=out, i=in0, j=in1, p=op: e.tensor_tensor(out=o, in0=i, in1=j, op=p), reads=rd, writes=wr)

    def stt(self, out, in0, sc, in1, op0, op1, rd, wr):
        self.S.op("dve", lambda e, o=out, i=in0, s=sc, j=in1, p=op0, q=op1: e.scalar_tensor_tensor(out=o, in0=i, scalar=s, in1=j, op0=p, op1=q), reads=rd, writes=wr)

    def cp(self, eng, out, in_, rd, wr):
        if eng == "act":
            self.S.op("act", lambda e, o=out, i=in_: e.copy(out=o, in_=i), reads=rd, writes=wr)
        else:
            self.S.op(eng, lambda e, o=out, i=in_: e.tensor_copy(out=o, in_=i), reads=rd, writes=wr)

    def memset(self, eng, ap, val, wr):
        self.S.op(eng, lambda e, a=ap, v=val: e.memset(a, v), writes=wr)

    def recip(self, out, in_, rd, wr):
        self.S.op("dve", lambda e, o=out, i=in_: e.reciprocal(out=o, in_=i), reads=rd, writes=wr)

    def dma(self, out, in_, rd, wr, q="sp", key=None, slow=False):
        if slow:
            self.S.dma(q, lambda e, o=out, i=in_: e.dma_start(out=o, in_=i, allow_slow_non_contiguous=True), reads=rd, writes=wr, key=key)
        else:
            self.S.dma(q, lambda e, o=out, i=in_: e.dma_start(out=o, in_=i), reads=rd, writes=wr, key=key)


def build_program():
    nc = bass.Bass("TRN2", target_bir_lowering=False)

    def din(name, shape, dt=F32):
        return nc.dram_tensor(name, list(shape), dt, kind="ExternalInput").ap()

    def dout(name, shape, dt=F32):
        return nc.dram_tensor(name, list(shape), dt, kind="ExternalOutput").ap()

    def dscr(name, shape, dt):
        return nc.dram_tensor(name, list(shape), dt, kind="Internal").ap()

    I = {}
    I["xa"] = din("xa", [D, LKP])
    I["xs"] = din("xs", [2, D, 128])
    I["cst"] = din("cst", [128, 7 * 128])
    I["gains"] = din("gains", [128, 32])
    I["convb"] = din("convb", [128, 12, 4])
    I["alog"] = din("alog", [128, 4])
    I["dtb"] = din("dtb", [128, 4])
    I["ngdn"] = din("ngdn", [128, 128])
    I["gval"] = din("gval", [128, 2])
    I["w_in"] = din("w_in", [D, DIN])
    I["cache_kT"] = din("cache_kT", [2, 512, PAST])
    I["cache_v"] = din("cache_v", [2, PAST, 512])
    I["cache_kiT"] = din("cache_kiT", [2, 64, PAST])
    I["state_gdn"] = din("state_gdn", [2, 4, 128, 128])
    I["gconvT"] = din("gconvT", [2, 128, 12, 3])
    I["pT"] = din("pT", [256, LKP])
    I["psT"] = din("psT", [2, 256, 128])
    I["w_pa"] = din("w_pa", [512, D])
    I["w_pb"] = din("w_pb", [512, D])
    I["w_out"] = din("w_out", [D, D])
    I["w_up"] = din("w_up", [D, 2 * DFF])
    I["w_down"] = din("w_down", [DFF, D])
    I["w_ple"] = din("w_ple", [256, D])
    I["w_pg"] = din("w_pg", [D, D])
    I["convf"] = din("convf", [128, 22, 3])
    I["relb"] = din("relb", [32, 8])
    I["relb15"] = din("relb15", [8, 1])
    I["ohrev"] = din("ohrev", [32, 384])
    I["iota"] = din("iota", [128, 512])
    I["lims"] = din("lims", [128, 24])
    I["fconvT"] = din("fconvT", [2, 128, 22, 2])
    O = {}
    O["y_own"] = dout("y_own", [2048, D])
    O["fconv_p"] = dout("fconv_p", [128, 22, 2])
    O["y_s"] = dout("y_s", [2, 16, D])
    O["fconv_s"] = dout("fconv_s", [2, 128, 22, 2])
    O["k_all"] = dout("k_all", [LKP, 512])
    O["v_all"] = dout("v_all", [LKP, 512])
    O["ki_all"] = dout("ki_all", [LKP, 64])
    O["gdn_p"] = dout("gdn_p", [4, 128, 128])
    O["gconv_p"] = dout("gconv_p", [128, 12, 3])
    O["k_s"] = dout("k_s", [2, 16, 512])
    O["v_s"] = dout("v_s", [2, 16, 512])
    O["ki_s"] = dout("ki_s", [2, 16, 64])
    O["gdn_s"] = dout("gdn_s", [2, 4, 128, 128])
    O["gconv_s"] = dout("gconv_s", [2, 128, 12, 3])
    wscr_in = dscr("wscr_in", [128, 8, DIN], BF16)
    ws_pa = dscr("ws_pa", [128, 4, D], BF16)
    ws_pb = dscr("ws_pb", [128, 4, D], BF16)
    ws_out = dscr("ws_out", [128, 8, D], BF16)
    ws_up = dscr("ws_up", [128, 8, 2 * DFF], BF16)
    ws_down = dscr("ws_down", [128, 22, D], BF16)
    ws_ple = dscr("ws_ple", [128, 2, D], BF16)
    ws_pg = dscr("ws_pg", [128, 8, D], BF16)
    tab_scr = dscr("tab_scr", [8, 384], F32)
    T_tab = Tl(None, "tab")
    kT_p = dscr("kT_p", [128, 4, LKP], BF16)
    v_p = dscr("v_p", [LKP, 528], BF16)
    kiT_p = dscr("kiT_p", [64, LKP], BF16)
    obT_p = dscr("obT_p", [128, 4, LKP], BF16)
    kT_s = dscr("kT_s", [2, 128, 4, LKS], BF16)
    v_s = dscr("v_s_scr", [2, LKS, 528], BF16)
    kiT_s = dscr("kiT_s", [2, 64, LKS], BF16)
    obT_s = dscr("obT_s", [2, 128, 4, 128], BF16)

    outT = Tl(None, "outputs")
    T_wscr_in = Tl(None, "wscr_in")
    T_kT = Tl(None, "kT")
    T_v = Tl(None, "v")
    T_ki = Tl(None, "kiT")
    T_ob = Tl(None, "obT")

    with ExitStack() as es0:
        S = Sched(nc, es0)
        k = K(nc, S)

        uid = [0]

        def sbt(es, name, shape, dt):
            uid[0] += 1
            nm = f"s{uid[0]}_{name}"
            return Tl(es.enter_context(nc.sbuf_tensor(nm, list(shape), dt)), nm)

        banks = [es0.enter_context(nc.psum_tensor(f"pb{i}", [128, 512], F32)) for i in range(8)]
        PB = [Tl(banks[b], f"pb{b}", excl=True) for b in range(8)]
        PQ = [[PB[b]] * 4 for b in range(8)]
        banks_bf = None

        def pq(b, q, rows=slice(0, 128), w=128):
            return banks[b][rows, q * 128:q * 128 + w]

        cst = sbt(es0, "cst", [128, 7 * 128], F32)
        k.dma(cst[:], I["cst"], [], [cst])
        ident = cst[:, 0:128]
        Ublk = cst[:, 128:256]
        Lsblk = cst[:, 256:384]
        NEGML = cst[:, 384:512]
        POSMU = cst[:, 512:640]
        half0 = cst[:, 640:768]
        half1 = cst[:, 768:896]
        gains = sbt(es0, "gains", [128, 32], F32)
        k.dma(gains[:], I["gains"], [], [gains])
        convb = sbt(es0, "convb", [128, 12, 4], F32)
        k.dma(convb[:], I["convb"], [], [convb])
        cA = sbt(es0, "cA", [128, 4], F32)
        k.dma(cA[:], I["alog"], [], [cA])
        dtb = sbt(es0, "dtb", [128, 4], F32)
        k.dma(dtb[:], I["dtb"], [], [dtb])
        ngdn = sbt(es0, "ngdn", [128, 128], F32)
        k.dma(ngdn[:], I["ngdn"], [], [ngdn])
        gval = sbt(es0, "gval", [128, 2], F32)
        k.dma(gval[:], I["gval"], [], [gval])
        epsT = sbt(es0, "epsT", [128, 1], F32)
        k.memset("pool", epsT[:], EPS, [epsT])
        ident_bf = sbt(es0, "ident_bf", [128, 128], BF16)
        k.cp("dve", ident_bf[:], ident, [cst], [ident_bf])
        ones_bf = sbt(es0, "ones_bf", [128, 128], BF16)
        k.memset("pool", ones_bf[:], 1.0, [ones_bf])
        k.act(cA[:], cA[:], AF.Exp, [cA], [cA])
        k.ts("dve", cA[:], cA[:], -1.0, ALU.mult, [cA], [cA])

        with ExitStack() as es:
            if KDBG < -1:
                raise_skip = True
            wf = [sbt(es, f"wf{i}", [128, 1556], F32) for i in range(2)]
            wb = [sbt(es, f"wb{i}", [128, 1556], BF16) for i in range(2)]
            cnt_ = [0]

            pcs = []

            def conv_w(src, dst, KC, C, gbase):
                v = src.rearrange("(kc p) c -> p kc c", p=128)
                npc = 1 if C <= 1556 else 4
                pw = C // npc
                for kc in range(KC):
                    for pc in range(npc):
                        pcs.append((v[:, kc, pc * pw:(pc + 1) * pw], dst[:, kc, pc * pw:(pc + 1) * pw], pw, None if gbase is None else gbase + kc))

            def conv_emit():
                def load(n):
                    k.dma(wf[n % 2][:, 0:pcs[n][2]], pcs[n][0], [], [wf[n % 2]], q="sp")
                if pcs:
                    load(0)
                for n, (src_, dst_, pw, gcol) in enumerate(pcs):
                    a, b_ = wf[n % 2], wb[n % 2]
                    if n + 1 < len(pcs):
                        load(n + 1)
                    eng = "dve" if n % 2 == 0 else "pool"
                    if gcol is None:
                        k.cp(eng, b_[:, 0:pw], a[:, 0:pw], [a], [b_])
                    else:
                        k.ts(eng, b_[:, 0:pw], a[:, 0:pw], gains[:, gcol:gcol + 1], ALU.mult, [a, gains], [b_])
                    k.dma(dst_, b_[:, 0:pw], [b_], [T_wscr_in], q="sp", key=b_)

            if KDBG >= -1:
                conv_w(I["w_in"], wscr_in, 8, DIN, 0)
                conv_w(I["w_pa"], ws_pa, 4, D, None)
                conv_w(I["w_pb"], ws_pb, 4, D, None)
                conv_w(I["w_out"], ws_out, 8, D, None)
                conv_w(I["w_up"], ws_up, 8, 2 * DFF, 8)
                conv_w(I["w_down"], ws_down, 22, D, None)
                conv_w(I["w_ple"], ws_ple, 2, D, None)
                conv_w(I["w_pg"], ws_pg, 8, D, 16)
                conv_emit()
            S.barrier()

        def all_pass(es, job):
            NT = job["NT"]
            ntt = NT // 128
            nsteps = min(job["L"] // NT, int(os.environ.get('KSTEPS', '999')))
            xsrc = job["x"]
            W1 = sbt(es, "W1", [128, 8, 3144], BF16)
            k.dma(W1[:, :, 0:1024], wscr_in[:, :, C_KA:C_KA + 1024], [T_wscr_in], [W1])
            k.dma(W1[:, :, 1024:1088], wscr_in[:, :, C_KI:C_KI + 64], [T_wscr_in], [W1])
            k.dma(W1[:, :, 1088:3144], wscr_in[:, :, C_QB:C_QB + 2056], [T_wscr_in], [W1])
            xa_t = [sbt(es, f"xa_t{i}", [128, 8, NT], F32) for i in range(2)]
            sq = sbt(es, "sq", [128, 8, NT], BF16)
            hT = sbt(es, "hT", [128, 8, NT], BF16)
            rs = sbt(es, "rs", [128, NT], F32)
            cin = sbt(es, "cin", [128, 12, NT + 3], F32)
            cv = sbt(es, "cv", [128, 12, NT], F32)
            cvb = sbt(es, "cvb", [128, 8, NT], BF16)
            sq2 = sbt(es, "sq2", [128, NT], BF16)
            rs2 = sbt(es, "rs2", [128, NT], F32)
            gbs = sbt(es, "gbs", [128, ntt, 512], F32)
            bbab = sbt(es, "bbab", [128, ntt, 8], F32)
            ktok = [sbt(es, f"ktok{i}", [128, 512], F32) for i in range(2)]
            vtok = [sbt(es, f"vtok{i}", [128, 512], F32) for i in range(2)]
            vbf = [sbt(es, f"vbf{i}", [128, 8, 66], BF16) for i in range(2)]
            for i in range(2):
                k.memset("pool", vbf[i][:, :, 64:66], 1.0, [vbf[i]])
            kitok = [sbt(es, f"kitok{i}", [128, 64], F32) for i in range(2)]
            kf_t = sbt(es, "kf_t", [128, 4, NT], BF16)
            kif_t = sbt(es, "kif_t", [64, NT], BF16)
            obT_t = sbt(es, "obT_t", [128, 4, NT], BF16)
            Sst = [sbt(es, f"Sst{h}", [128, 128], F32) for h in range(4)]
            bet = sbt(es, "bet", [128, ntt, 4], F32)
            nbet = sbt(es, "nbet", [128, ntt, 4], F32)
            gg = sbt(es, "gg", [128, ntt, 4], F32)
            ngg = sbt(es, "ngg", [128, ntt, 4], F32)
            egc = sbt(es, "egc", [128, ntt, 4], F32)
            ekd = sbt(es, "ekd", [128, ntt, 4], F32)
            egl = sbt(es, "egl", [128, ntt, 8], F32)
            bege = sbt(es, "bege", [128, ntt, 4], F32)
            HT = []
            for h in range(4):
                d = {}
                for nm in ("NGb", "E1m", "E2m", "P", "attnT", "kbg", "kd", "vb", "u", "wT", "vnew", "o1", "o", "tmp"):
                    d[nm] = sbt(es, f"{nm}{h}", [128, 128], F32)
                for nm in ("N", "Nt", "M0", "M1", "Mt0", "Mt1", "Pb"):
                    d[nm] = sbt(es, f"{nm}{h}", [128, 128], BF16)
                d["obn"] = sbt(es, f"obn{h}", [128, 128], BF16)
                d["ms"] = sbt(es, f"ms{h}", [128, 1], F32)
                HT.append(d)

            if job["S0"] is None:
                for h in range(4):
                    k.memset("pool", Sst[h][:], 0.0, [Sst[h]])
                k.memset("pool", cin[:, :, 0:3], 0.0, [cin])
            else:
                for h in range(4):
                    k.dma(Sst[h][:], job["S0"][h], [], [Sst[h]])
                k.dma(cin[:, :, 0:3], job["conv0"], [], [cin], slow=True)

            xv = xsrc.rearrange("(kc p) t -> p kc t", p=128)
            k.dma(xa_t[0][:], xv[:, :, 0:NT], [], [xa_t[0]])
            for st in range(nsteps):
                t0 = st * NT
                xt = xa_t[st % 2]
                if st + 1 < nsteps:
                    xn = xa_t[(st + 1) % 2]
                    k.dma(xn[:], xv[:, :, t0 + NT:t0 + 2 * NT], [], [xn])
                k.act(sq[:], xt[:], AF.Square, [xt], [sq])
                A = PQ[0]
                for kc in range(8):
                    k.mm(banks[0][:, 0:NT], ones_bf[:], sq[:, kc, :], [ones_bf, sq], A, start=(kc == 0), stop=(kc == 7))
                k.act(rs[:], banks[0][:, 0:NT], AF.Sqrt, A + [epsT], [rs], scale=1.0 / D, bias=epsT[:, 0:1])
                k.recip(rs[:], rs[:], [rs], [rs])
                for kc in range(8):
                    k.tt("dve" if kc % 2 == 0 else "pool", hT[:, kc, :], xt[:, kc, :], rs[:], ALU.mult, [xt, rs], [hT])
                if KDBG < 1:
                    continue
                for tt in range(ntt):
                    ts_ = slice(tt * 128, (tt + 1) * 128)
                    r0 = t0 + tt * 128
                    kt, vt, vb_, kit = ktok[tt % 2], vtok[tt % 2], vbf[tt % 2], kitok[tt % 2]
                    for (bk, c0, cw) in ((1, 0, 512), (2, 512, 512)):
                        for kc in range(8):
                            k.mm(banks[bk][:, 0:cw], hT[:, kc, ts_], W1[:, kc, c0:c0 + cw], [hT, W1], PQ[bk], start=(kc == 0), stop=(kc == 7))
                    k.cp("act", kt[:], banks[1][:, :], PQ[1], [kt])
                    k.cp("dve", vt[:], banks[2][:, :], PQ[2], [vt])
                    k.cp("pool", vb_[:, :, 0:64], vt[:].rearrange("p (h d) -> p h d", d=64), [vt], [vb_])
                    nv = job["nvalid"]
                    if nv >= 128:
                        k.dma(job["k_out"][r0:r0 + 128, :], kt[:], [kt], [outT], key=kt)
                        k.dma(job["v_out"][r0:r0 + 128, :], vt[:], [vt], [outT], key=vt)
                    else:
                        k.dma(job["k_out"][0:nv, :], kt[0:nv, :], [kt], [outT], key=kt)
                        k.dma(job["v_out"][0:nv, :], vt[0:nv, :], [vt], [outT], key=vt)
                    k.dma(job["v_scr"][job["koff"] + r0:job["koff"] + r0 + 128, :], vb_[:].rearrange("p h d -> p (h d)"), [vb_], [T_v], key=vb_)
                    for kc in range(8):
                        k.mm(banks[3][:, 0:64], hT[:, kc, ts_], W1[:, kc, 1024:1088], [hT, W1], PQ[3], start=(kc == 0), stop=(kc == 7))
                    k.cp("act", kit[:], banks[3][:, 0:64], PQ[3], [kit])
                    if nv >= 128:
                        k.dma(job["ki_out"][r0:r0 + 128, :], kit[:], [kit], [outT], key=kit)
                    else:
                        k.dma(job["ki_out"][0:nv, :], kit[0:nv, :], [kit], [outT], key=kit)
                    for kc in range(8):
                        k.mm(banks[4][:, :], hT[:, kc, ts_], W1[:, kc, 2624:3136], [hT, W1], PQ[4], start=(kc == 0), stop=(kc == 7))
                    k.act(gbs[:, tt, :], banks[4][:, :], AF.Silu, PQ[4], [gbs])
                    for kc in range(8):
                        k.mm(banks[3][:, 128:136], hT[:, kc, ts_], W1[:, kc, 3136:3144], [hT, W1], PQ[3], start=(kc == 0), stop=(kc == 7))
                    k.cp("dve", bbab[:, tt, :], banks[3][:, 128:136], PQ[3], [bbab])
                for p in range(4):
                    bk = 5 + (p % 2)
                    for kc in range(8):
                        k.mm(banks[bk][:, 0:NT], W1[:, kc, p * 128:(p + 1) * 128], hT[:, kc, :], [W1, hT], PQ[bk], start=(kc == 0), stop=(kc == 7))
                    k.cp("act" if p % 2 == 0 else "dve", kf_t[:, p, :], banks[bk][:, 0:NT], PQ[bk], [kf_t])
                k.dma(job["kT_scr"][:, :, job["koff"] + t0:job["koff"] + t0 + NT], kf_t[:], [kf_t], [T_kT], key=kf_t)
                for kc in range(8):
                    k.mm(banks[7][0:64, 0:NT], W1[:, kc, 1024:1088], hT[:, kc, :], [W1, hT], PQ[7], start=(kc == 0), stop=(kc == 7))
                k.cp("act", kif_t[:], banks[7][0:64, 0:NT], PQ[7], [kif_t])
                k.dma(job["kiT_scr"][:, job["koff"] + t0:job["koff"] + t0 + NT], kif_t[:], [kif_t], [T_ki], key=kif_t)
                if KDBG < 2:
                    continue
                for c in range(12):
                    bk = 5 + (c % 3)
                    for kc in range(8):
                        k.mm(banks[bk][:, 0:NT], W1[:, kc, 1088 + c * 128:1088 + (c + 1) * 128], hT[:, kc, :], [W1, hT], PQ[bk], start=(kc == 0), stop=(kc == 7))
                    k.cp("act" if c % 2 == 0 else "dve", cin[:, c, 3:3 + NT], banks[bk][:, 0:NT], PQ[bk], [cin])
                for c in range(12):
                    k.ts("dve", cv[:, c, :], cin[:, c, 0:NT], convb[:, c, 0:1], ALU.mult, [cin, convb], [cv])
                    for j in range(1, 4):
                        k.stt(cv[:, c, :], cin[:, c, j:j + NT], convb[:, c, j:j + 1], cv[:, c, :], ALU.mult, ALU.add, [cin, convb, cv], [cv])
                if st == nsteps - 1:
                    k.dma(job["conv_out"], cin[:, :, job["nvalid_last"]:job["nvalid_last"] + 3], [cin], [outT], key=cin, slow=True)
                k.cp("pool", cin[:, :, 0:3], cin[:, :, NT:NT + 3], [cin], [cin])
                k.act(cv[:], cv[:], AF.Silu, [cv], [cv])
                for c in range(8):
                    k.act(sq2[:], cv[:, c, :], AF.Square, [cv], [sq2])
                    k.mm(banks[0][:, 0:NT], ones_bf[:], sq2[:], [ones_bf, sq2], PQ[0])
                    k.act(rs2[:], banks[0][:, 0:NT], AF.Sqrt, PQ[0] + [epsT], [rs2], bias=epsT[:, 0:1])
                    k.recip(rs2[:], rs2[:], [rs2], [rs2])
                    if c < 4:
                        k.stt(cv[:, c, :], cv[:, c, :], 128.0 ** -0.5, rs2[:], ALU.mult, ALU.mult, [cv, rs2], [cv])
                    else:
                        k.tt("dve", cv[:, c, :], cv[:, c, :], rs2[:], ALU.mult, [cv, rs2], [cv])
                    k.cp("pool", cvb[:, c, :], cv[:, c, :], [cv], [cvb])
                k.act(bet[:], bbab[:, :, 0:4], AF.Sigmoid, [bbab], [bet])
                if job["gval"] is not None:
                    for tt in range(ntt):
                        k.ts("dve", bet[:, tt, :], bet[:, tt, :], gval[:, job["gval"]:job["gval"] + 1], ALU.mult, [bet, gval], [bet])
                k.ts("dve", nbet[:], bet[:], -1.0, ALU.mult, [bet], [nbet])
                for tt in range(ntt):
                    k.tt("dve", gg[:, tt, :], bbab[:, tt, 4:8], dtb[:], ALU.add, [bbab, dtb], [gg])
                k.act(gg[:], gg[:], AF.Exp, [gg], [gg])
                k.act(gg[:], gg[:], AF.Ln, [gg], [gg], bias=1.0)
                for tt in range(ntt):
                    k.tt("dve", gg[:, tt, :], gg[:, tt, :], cA[:], ALU.mult, [gg, cA], [gg])
                    if job["gval"] is not None:
                        k.ts("dve", gg[:, tt, :], gg[:, tt, :], gval[:, job["gval"]:job["gval"] + 1], ALU.mult, [gg, gval], [gg])
                k.ts("dve", ngg[:], gg[:], -1.0, ALU.mult, [gg], [ngg])
                if KDBG < 3:
                    continue
                for tt in range(ntt):
                    ts_ = slice(tt * 128, (tt + 1) * 128)
                    G = PQ[0]
                    k.mm(banks[0][:, 0:4], Ublk, gg[:, tt, :], [cst, gg], G)
                    k.mm(banks[0][:, 4:8], Lsblk, gg[:, tt, :], [cst, gg], G)
                    k.mm(banks[0][:, 8:12], half0, gg[:, tt, :], [cst, gg], G)
                    k.mm(banks[0][:, 12:16], half1, gg[:, tt, :], [cst, gg], G)
                    k.act(egc[:, tt, :], banks[0][:, 0:4], AF.Exp, G, [egc])
                    k.act(ekd[:, tt, :], banks[0][:, 4:8], AF.Exp, G, [ekd])
                    k.act(egl[:, tt, :], banks[0][:, 8:16], AF.Exp, G, [egl])
                    k.tt("dve", bege[:, tt, :], bet[:, tt, :], egc[:, tt, :], ALU.mult, [bet, egc], [bege])
                    BK = ((1, 2), (3, 4), (5, 6), (7, 0))
                    H4 = range(4)
                    for h in H4:
                        T_ = HT[h]
                        k.cp("pool", T_["NGb"][:], ngg[:, tt, h:h + 1].to_broadcast([128, 128]), [ngg], [T_["NGb"]])
                    for h in H4:
                        T_ = HT[h]
                        b0, b1 = BK[h]
                        k.mm(pq(b0, 0), Ublk, gg[:, tt, h:h + 1].to_broadcast([128, 128]), [cst, gg], [PB[b0]], start=True, stop=False)
                        k.mm(pq(b0, 0), T_["NGb"][:], Ublk, [cst, T_["NGb"]], [PB[b0]], start=False, stop=True)
                        k.mm(pq(b1, 1), cvb[:, 4 + h, ts_], cvb[:, 4 + h, ts_], [cvb], [PB[b1]])
                    for h in H4:
                        T_ = HT[h]
                        b0, b1 = BK[h]
                        k.stt(T_["E1m"][:], pq(b0, 0), 0.0, NEGML, ALU.min, ALU.add, [PB[b0], cst], [T_["E1m"]])
                        k.stt(T_["E2m"][:], pq(b0, 0), 0.0, POSMU, ALU.max, ALU.add, [PB[b0], cst], [T_["E2m"]])
                    for h in H4:
                        T_ = HT[h]
                        k.act(T_["E1m"][:], T_["E1m"][:], AF.Exp, [T_["E1m"]], [T_["E1m"]])
                        k.act(T_["E2m"][:], T_["E2m"][:], AF.Exp, [T_["E2m"]], [T_["E2m"]], scale=-1.0)
                    for h in H4:
                        T_ = HT[h]
                        b0, b1 = BK[h]
                        k.mm(pq(b0, 2), cvb[:, 4 + h, ts_], cvb[:, h, ts_], [cvb], [PB[b0]])
                    for h in H4:
                        T_ = HT[h]
                        b0, b1 = BK[h]
                        k.stt(T_["N"][:], pq(b1, 1), nbet[:, tt, h:h + 1], T_["E1m"][:], ALU.mult, ALU.mult, [PB[b1], nbet, T_["E1m"]], [T_["N"]])
                    for h in H4:
                        T_ = HT[h]
                        b0, b1 = BK[h]
                        k.tt("dve", T_["attnT"][:], pq(b0, 2), T_["E2m"][:], ALU.mult, [PB[b0], T_["E2m"]], [T_["attnT"]])
                    for h in H4:
                        T_ = HT[h]
                        b0, b1 = BK[h]
                        k.tr(pq(b1, 3), cv[:, 4 + h, ts_], ident, [cv, cst], [PB[b1]])
                        k.tr(pq(b0, 0), cv[:, 8 + h, ts_], ident, [cv, cst], [PB[b0]])
                    for h in H4:
                        T_ = HT[h]
                        b0, b1 = BK[h]
                        k.ts("dve", T_["kbg"][:], pq(b1, 3), bege[:, tt, h:h + 1], ALU.mult, [PB[b1], bege], [T_["kbg"]])
                        k.ts("dve", T_["kd"][:], pq(b1, 3), ekd[:, tt, h:h + 1], ALU.mult, [PB[b1], ekd], [T_["kd"]])
                    for h in H4:
                        T_ = HT[h]
                        b0, b1 = BK[h]
                        k.ts("dve", T_["vb"][:], pq(b0, 0), bet[:, tt, h:h + 1], ALU.mult, [PB[b0], bet], [T_["vb"]])
                    for h in H4:
                        T_ = HT[h]
                        b0, b1 = BK[h]
                        k.mm(pq(b1, 1), T_["N"][:], ident_bf[:], [T_["N"], ident_bf], [PB[b1]])
                    for h in H4:
                        T_ = HT[h]
                        b0, b1 = BK[h]
                        k.cp("act", T_["Nt"][:], pq(b1, 1), [PB[b1]], [T_["Nt"]])
                    for h in H4:
                        T_ = HT[h]
                        b0, b1 = BK[h]
                        k.tt("dve", T_["P"][:], pq(b1, 1), ident, ALU.add, [PB[b1], cst], [T_["P"]])
                    for h in H4:
                        T_ = HT[h]
                        k.cp("pool", T_["Pb"][:], T_["P"][:], [T_["P"]], [T_["Pb"]])
                    if KDBG < 4:
                        continue
                    for lv in range(1, 6):
                        for h in range(4):
                            T_ = HT[h]
                            b0, b1 = ((1, 2), (3, 4), (5, 6), (7, 0))[h]
                            Mp = T_["N"] if lv == 1 else T_[f"M{(lv - 1) % 2}"]
                            Mtp = T_["Nt"] if lv == 1 else T_[f"Mt{(lv - 1) % 2}"]
                            Mn = T_[f"M{lv % 2}"]
                            Mtn = T_[f"Mt{lv % 2}"]
                            k.mm(pq(b1, 2), Mtp[:], Mp[:], [Mtp, Mp], [PQ[b1][2]])
                            k.cp("act", Mn[:], pq(b1, 2), [PQ[b1][2]], [Mn])
                            if lv < 5:
                                k.mm(pq(b1, 3), Mp[:], Mtp[:], [Mtp, Mp], [PQ[b1][3]])
                                k.cp("pool" if False else "dve", Mtn[:], pq(b1, 3), [PQ[b1][3]], [Mtn])
                            k.mm(pq(b0, 0), Mn[:], T_["Pb"][:], [Mn, T_["Pb"]], [PQ[b0][0]])
                            k.tt("dve", T_["P"][:], T_["P"][:], pq(b0, 0), ALU.add, [T_["P"], PQ[b0][0]], [T_["P"]])
                            if lv < 5:
                                k.cp("pool", T_["Pb"][:], T_["P"][:], [T_["P"]], [T_["Pb"]])
                    if KDBG < 5:
                        continue
                    BK = ((1, 2), (3, 4), (5, 6), (7, 0))
                    for h in range(4):
                        T_ = HT[h]
                        b0, b1 = BK[h]
                        k.mm(pq(b0, 1), T_["P"][:], T_["vb"][:], [T_["P"], T_["vb"]], [PQ[b0][1]])
                        k.cp("act", T_["u"][:], pq(b0, 1), [PQ[b0][1]], [T_["u"]])
                    for h in range(4):
                        T_ = HT[h]
                        b0, b1 = BK[h]
                        k.mm(pq(b1, 2), T_["kbg"][:], T_["P"][:], [T_["P"], T_["kbg"]], [PQ[b1][2]])
                        k.cp("dve", T_["wT"][:], pq(b1, 2), [PQ[b1][2]], [T_["wT"]])
                    for c in range(2):
                        r = slice(64 * c, 64 * c + 64)
                        for h in range(4):
                            T_ = HT[h]
                            b0, b1 = BK[h]
                            k.mm(banks[b0][r, 384:512], T_["wT"][:, r], Sst[h][:], [T_["wT"], Sst[h]], [PQ[b0][3]])
                            k.mm(banks[b0][r, 0:128], cv[:, h, tt * 128 + 64 * c:tt * 128 + 64 * c + 64], Sst[h][:], [cv, Sst[h]], [PQ[b0][0]])
                        for h in range(4):
                            T_ = HT[h]
                            b0, b1 = BK[h]
                            k.tt("dve", T_["vnew"][r, :], T_["u"][r, :], banks[b0][r, 384:512], ALU.subtract, [T_["u"], PQ[b0][3]], [T_["vnew"]])
                        for h in range(4):
                            T_ = HT[h]
                            b0, b1 = BK[h]
                            k.mm(banks[b1][r, 128:256], T_["attnT"][r, r], T_["vnew"][r, :], [T_["attnT"], T_["vnew"]], [PQ[b1][1]])
                            k.mm(pq(b1, 2), T_["kd"][r, :], T_["vnew"][r, :], [T_["kd"], T_["vnew"]], [PQ[b1][2]])
                        for h in range(4):
                            T_ = HT[h]
                            b0, b1 = BK[h]
                            k.stt(Sst[h][:], Sst[h][:], egl[:, tt, 4 * c + h:4 * c + h + 1], pq(b1, 2), ALU.mult, ALU.add, [Sst[h], egl, PQ[b1][2]], [Sst[h]])
                        for h in range(4):
                            T_ = HT[h]
                            b0, b1 = BK[h]
                            k.cp("act", T_["o1"][r, :], banks[b1][r, 128:256], [PQ[b1][1]], [T_["o1"]])
                        for h in range(4):
                            T_ = HT[h]
                            b0, b1 = BK[h]
                            k.stt(T_["o"][r, :], banks[b0][r, 0:128], egc[r, tt, h:h + 1], T_["o1"][r, :], ALU.mult, ALU.add, [PQ[b0][0], egc, T_["o1"]], [T_["o"]])
                    for h in range(4):
                        T_ = HT[h]
                        k.act(T_["tmp"][:], T_["o"][:], AF.Square, [T_["o"]], [T_["tmp"], T_["ms"]], accum=T_["ms"][:, 0:1])
                    for h in range(4):
                        T_ = HT[h]
                        k.act(T_["ms"][:], T_["ms"][:], AF.Sqrt, [T_["ms"], epsT], [T_["ms"]], scale=1.0 / 128, bias=epsT[:, 0:1])
                    for h in range(4):
                        T_ = HT[h]
                        k.recip(T_["ms"][:], T_["ms"][:], [T_["ms"]], [T_["ms"]])
                    for h in range(4):
                        T_ = HT[h]
                        k.stt(T_["tmp"][:], T_["o"][:], T_["ms"][:, 0:1], ngdn[:], ALU.mult, ALU.mult, [T_["o"], T_["ms"], ngdn], [T_["tmp"]])
                    for h in range(4):
                        T_ = HT[h]
                        k.tt("pool", T_["obn"][:], T_["tmp"][:], gbs[:, tt, h * 128:(h + 1) * 128], ALU.mult, [T_["tmp"], gbs], [T_["obn"]])
                    for h in range(4):
                        T_ = HT[h]
                        b0, b1 = BK[h]
                        k.tr(banks[b1][:, 384:512], T_["obn"][:], ident_bf[:], [T_["obn"], ident_bf], [PQ[b1][3]])
                    for h in range(4):
                        b0, b1 = BK[h]
                        k.cp("act", obT_t[:, h, ts_], banks[b1][:, 384:512], [PQ[b1][3]], [obT_t])
                if KDBG >= 5:
                    k.dma(job["obT_scr"][:, :, t0:t0 + NT], obT_t[:], [obT_t], [T_ob], key=obT_t)
            for h in range(4):
                k.dma(job["S_out"][h], Sst[h][:], [Sst[h]], [outT], key=Sst[h])
            S.barrier()

        with ExitStack() as es:
          if KDBG >= 0 and 'p' in KJOB:
            all_pass(es, dict(NT=NTP, L=LKP, x=I["xa"], S0=None, conv0=None, nvalid=128, nvalid_last=NTP,
                              k_out=O["k_all"], v_out=O["v_all"], ki_out=O["ki_all"], kT_scr=kT_p, v_scr=v_p, kiT_scr=kiT_p,
                              koff=0, obT_scr=obT_p, S_out=O["gdn_p"], conv_out=O["gconv_p"], gval=None))
        for sb_ in range(2 if (KDBG >= 0 and 's' in KJOB) else 0):
            with ExitStack() as es:
                all_pass(es, dict(NT=128, L=128, x=I["xs"][sb_], S0=I["state_gdn"][sb_], conv0=I["gconvT"][sb_], nvalid=16, nvalid_last=16,
                                  k_out=O["k_s"][sb_], v_out=O["v_s"][sb_], ki_out=O["ki_s"][sb_], kT_scr=kT_s[sb_], v_scr=v_s[sb_],
                                  kiT_scr=kiT_s[sb_], koff=PAST, obT_scr=obT_s[sb_], S_out=O["gdn_s"][sb_], conv_out=O["gconv_s"][sb_], gval=0))

        convf = sbt(es0, "convf", [128, 22, 3], F32)
        k.dma(convf[:], I["convf"], [], [convf])
        iota = sbt(es0, "iota", [128, 512], F32)
        k.dma(iota[:], I["iota"], [], [iota])
        lims = sbt(es0, "lims", [128, 24], F32)
        k.dma(lims[:], I["lims"], [], [lims])
        BT = sbt(es0, "BT", [128, 2, 8, 128], BF16)
        with ExitStack() as es:
            rb = sbt(es, "rb", [32, 8], F32)
            oh = sbt(es, "oh", [32, 384], F32)
            rb15 = sbt(es, "rb15", [8, 1], F32)
            tabs = sbt(es, "tabs", [8, 384], F32)
            BTf = sbt(es, "BTf", [128, 2, 8, 128], F32)
            k.dma(rb[:], I["relb"], [], [rb])
            k.dma(oh[:], I["ohrev"], [], [oh])
            k.dma(rb15[:], I["relb15"], [], [rb15])
            k.mm(banks[0][0:8, 0:384], rb[:], oh[:], [rb, oh], PQ[0])
            k.ts("dve", tabs[:], banks[0][0:8, 0:384], rb15[:, 0:1], ALU.subtract, PQ[0] + [rb15], [tabs])
            k.dma(tab_scr, tabs[:], [tabs], [T_tab], key=tabs)
            for kp in range(128):
                for dd in range(2):
                    base = 127 + 128 * dd - kp
                    k.dma(BTf[kp:kp + 1, dd, :, :], tab_scr[:, base:base + 128].unsqueeze(0), [T_tab], [BTf])
            k.cp("dve", BT[:], BTf[:], [BTf], [BT])
            ckf = sbt(es, "ckf", [128, 4, 512], F32)
            ckb = sbt(es, "ckb", [128, 4, 512], BF16)
            cvf = sbt(es, "cvf", [128, 512], F32)
            cvb = sbt(es, "cvb", [128, 512], BF16)
            cvb2 = sbt(es, "cvb2", [128, 8, 66], BF16)
            k.memset("pool", cvb2[:, :, 64:66], 1.0, [cvb2])
            for sb_ in range(2):
                ckv = I["cache_kT"][sb_].rearrange("(p r) t -> r p t", r=128)
                for pc in range(4):
                    k.dma(ckf[:], ckv[:, :, pc * 512:(pc + 1) * 512], [], [ckf])
                    k.cp("dve", ckb[:], ckf[:], [ckf], [ckb])
                    k.dma(kT_s[sb_][:, :, pc * 512:(pc + 1) * 512], ckb[:], [ckb], [T_kT], key=ckb)
                    k.dma(cvf[0:64, :], I["cache_kiT"][sb_][:, pc * 512:(pc + 1) * 512], [], [cvf])
                    k.cp("pool", cvb[0:64, :], cvf[0:64, :], [cvf], [cvb])
                    k.dma(kiT_s[sb_][:, pc * 512:(pc + 1) * 512], cvb[0:64, :], [cvb], [T_ki], key=cvb)
                for rb_ in range(16):
                    k.dma(cvf[:], I["cache_v"][sb_][rb_ * 128:(rb_ + 1) * 128, :], [], [cvf])
                    k.cp("pool", cvb2[:, :, 0:64], cvf[:].rearrange("p (h d) -> p h d", d=64), [cvf], [cvb2])
                    k.dma(v_s[sb_][rb_ * 128:(rb_ + 1) * 128, :], cvb2[:].rearrange("p h d -> p (h d)"), [cvb2], [T_v], key=cvb2)
            S.barrier()

        def pieces(n):
            out, c = [], 0
            while c < n:
                w = min(512, n - c)
                out.append((c, w))
                c += w
            return out

        def rms(src, dst, c0, n, sqb, rsb, bank=0):
            k.act(sqb[:, :, 0:n], src[:, :, c0:c0 + n], AF.Square, [src], [sqb])
            for kc in range(8):
                k.mm(banks[bank][:, 0:n], ones_bf[:], sqb[:, kc, 0:n], [ones_bf, sqb], PQ[bank], start=(kc == 0), stop=(kc == 7))
            k.act(rsb[:, 0:n], banks[bank][:, 0:n], AF.Sqrt, PQ[bank] + [epsT], [rsb], scale=1.0 / D, bias=epsT[:, 0:1])
            k.recip(rsb[:, 0:n], rsb[:, 0:n], [rsb], [rsb])
            if dst is not None:
                for kc in range(8):
                    k.tt("dve" if kc % 2 == 0 else "pool", dst[:, kc, c0:c0 + n], src[:, kc, c0:c0 + n], rsb[:, 0:n], ALU.mult, [src, rsb], [dst])

        NIT = 26

        def own_block(jb):
            NQ = jb["NQ"]
            NTOK = 128 * NQ
            with ExitStack() as esA:
                xo = sbt(esA, "xo", [128, 8, NTOK], F32)
                hT = sbt(esA, "hTo", [128, 8, NTOK], BF16)
                oaT = sbt(esA, "oaT", [128, 4, NTOK], BF16)
                sqb = sbt(esA, "sqb", [128, 8, 512], BF16)
                rsb = sbt(esA, "rsb", [128, 512], F32)
                k.dma(xo[:], jb["xsrc"], [], [xo])
                for (c0, n) in pieces(NTOK):
                    rms(xo, hT, c0, n, sqb, rsb)
                if KP2 < 1:
                    return
                with ExitStack() as es:
                    W2 = sbt(es, "W2", [128, 8, 1032], BF16)
                    k.dma(W2[:, :, 0:512], wscr_in[:, :, C_QA:C_QA + 512], [T_wscr_in], [W2])
                    k.dma(W2[:, :, 512:1024], wscr_in[:, :, C_QI:C_QI + 512], [T_wscr_in], [W2])
                    k.dma(W2[:, :, 1024:1032], wscr_in[:, :, C_WI:C_WI + 8], [T_wscr_in], [W2])
                    qaT = sbt(es, "qaT", [128, 4, NTOK], BF16)
                    qiT = sbt(es, "qiT", [128, 4, NTOK], BF16)
                    wiT = sbt(es, "wiT", [128, NQ, 8], F32)
                    n_ = 0
                    for (dstq, cb) in ((qaT, 0), (qiT, 512)):
                        for p in range(4):
                            for (c0, n) in pieces(NTOK):
                                bk = n_ % 2
                                n_ += 1
                                for kc in range(8):
                                    k.mm(banks[bk][:, 0:n], W2[:, kc, cb + p * 128:cb + (p + 1) * 128], hT[:, kc, c0:c0 + n], [W2, hT], PQ[bk], start=(kc == 0), stop=(kc == 7))
                                k.act(dstq[:, p, c0:c0 + n], banks[bk][:, 0:n], AF.Copy, PQ[bk], [dstq], scale=0.125)
                    for qb in range(NQ):
                        for kc in range(8):
                            k.mm(banks[2][:, 0:8], hT[:, kc, qb * 128:(qb + 1) * 128], W2[:, kc, 1024:1032], [W2, hT], PQ[2], start=(kc == 0), stop=(kc == 7))
                        k.ts("dve", wiT[:, qb, :], banks[2][:, 0:8], 8.0 ** -0.5, ALU.mult, PQ[2], [wiT])
                    if KP2 < 2:
                        return
                    LMAX = max(jb["L"])
                    kiT2 = sbt(es, "kiT2", [128, LMAX], BF16)
                    sc = sbt(es, "sc", [128, LMAX], F32)
                    Mb = sbt(es, "Mb", [128, LMAX], BF16)
                    MT = sbt(es, "MT", [128, LMAX // 128, 128], BF16)
                    tmpf = [sbt(es, f"tmpf{i}", [128, 512], F32) for i in range(2)]
                    pen = sbt(es, "pen", [128, 512], F32)
                    Kc = [sbt(es, f"Kc{i}", [128, 4, 512], BF16) for i in range(2)]
                    Vc = [sbt(es, f"Vc{i}", [128, 4, 8, 66], BF16) for i in range(2)]
                    PT = [sbt(es, f"PT{i}", [128, 4, 128], BF16) for i in range(4)]
                    oa = sbt(es, "oa", [128, 8, 64], BF16)
                    den = sbt(es, "den", [128, 8], F32)
                    sm = {nm: sbt(es, nm, [128, 1], F32) for nm in ("mx", "lo", "hi", "mid", "cnt", "ge", "d1", "d2", "off")}
                    for i in range(2):
                        k.memset("pool", Vc[i][:, :, :, 64:65], 1.0, [Vc[i]])
                    lgs = sbt(es, "lgs", [128, 512], F32)

                    def gen_S(qb):
                        L = jb["L"][qb]
                        nkb = L // 128
                        q0 = qb * 128
                        lcol = jb["limcol"] + qb
                        k.dma(kiT2[0:64, 0:L], jb["kiT"][:, 0:L], [T_ki], [kiT2])
                        k.dma(kiT2[64:128, 0:L], jb["kiT"][:, 0:L], [T_ki], [kiT2])
                        tiles = pieces(L)
                        n_ = 0
                        for (c0, w) in tiles:
                            for h in range(8):
                                p, r0 = h // 2, 64 * (h % 2)
                                bk = n_ % 2
                                tf = tmpf[n_ % 2]
                                n_ += 1
                                k.mm(banks[bk][:, 0:w], qiT[r0:r0 + 64, p, q0:q0 + 128], kiT2[r0:r0 + 64, c0:c0 + w], [qiT, kiT2], PQ[bk])
                                k.act(tf[:, 0:w], banks[bk][:, 0:w], AF.Relu, PQ[bk], [tf])
                                if h == 0:
                                    k.ts("dve", sc[:, c0:c0 + w], tf[:, 0:w], wiT[:, qb, 0:1], ALU.mult, [tf, wiT], [sc])
                                else:
                                    k.stt(sc[:, c0:c0 + w], tf[:, 0:w], wiT[:, qb, h:h + 1], sc[:, c0:c0 + w], ALU.mult, ALU.add, [tf, wiT, sc], [sc])
                            yield
                        k.S.op("dve", lambda e, o=sm["mx"][:], i=sc[:, 0:L]: e.tensor_reduce(out=o, in_=i, axis=AX.X, op=ALU.max, apply_absolute_value=True), reads=[sc], writes=[sm["mx"]])
                        k.ts("dve", sm["hi"][:], sm["mx"][:], 1.0, ALU.add, [sm["mx"]], [sm["hi"]])
                        k.ts("dve", sm["lo"][:], sm["mx"][:], -1.0, ALU.mult, [sm["mx"]], [sm["lo"]], s2=-1.0, op1=ALU.add)
                        (c0, w) = tiles[-1]
                        k.ts("dve", sm["off"][:], lims[:, lcol:lcol + 1], float(-c0), ALU.add, [lims], [sm["off"]])
                        k.ts("dve", pen[:, 0:w], iota[:, 0:w], sm["off"][:, 0:1], ALU.is_ge, [iota, sm["off"]], [pen], s2=NEG, op1=ALU.mult)
                        k.tt("dve", sc[:, c0:c0 + w], sc[:, c0:c0 + w], pen[:, 0:w], ALU.add, [sc, pen], [sc])
                        if jb["lolim"] is not None:
                            for ti in range(min(3, len(tiles))):
                                (c0, w) = tiles[ti]
                                k.ts("dve", sm["off"][:], lims[:, jb["lolim"]:jb["lolim"] + 1], float(-c0), ALU.add, [lims], [sm["off"]])
                                k.ts("dve", pen[:, 0:w], iota[:, 0:w], sm["off"][:, 0:1], ALU.is_lt, [iota, sm["off"]], [pen], s2=NEG, op1=ALU.mult)
                                k.tt("dve", sc[:, c0:c0 + w], sc[:, c0:c0 + w], pen[:, 0:w], ALU.add, [sc, pen], [sc])

                    def do_B(qb):
                        L = jb["L"][qb]
                        nkb = L // 128
                        q0 = qb * 128
                        lcol = jb["limcol"] + qb
                        k.tt("dve", sm["d1"][:], sm["hi"][:], sm["lo"][:], ALU.subtract, [sm["hi"], sm["lo"]], [sm["d1"]])
                        for it in range(NIT):
                            k.ts("dve", sm["d1"][:], sm["d1"][:], 0.5, ALU.mult, [sm["d1"]], [sm["d1"]])
                            k.tt("dve", sm["mid"][:], sm["lo"][:], sm["d1"][:], ALU.add, [sm["lo"], sm["d1"]], [sm["mid"]])
                            k.ts("dve", Mb[:, 0:L], sc[:, 0:L], sm["mid"][:, 0:1], ALU.is_gt, [sc, sm["mid"]], [Mb, sm["cnt"]], s2=0.0, op1=ALU.add, accum=sm["cnt"][:, 0:1])
                            k.stt(sm["ge"][:], sm["cnt"][:], 255.5, sm["d1"][:], ALU.is_ge, ALU.mult, [sm["cnt"], sm["d1"]], [sm["ge"]])
                            k.tt("dve", sm["lo"][:], sm["lo"][:], sm["ge"][:], ALU.add, [sm["lo"], sm["ge"]], [sm["lo"]])
                        k.ts("dve", Mb[:, 0:L], sc[:, 0:L], sm["lo"][:, 0:1], ALU.is_gt, [sc, sm["lo"]], [Mb])

                    def do_T(qb):
                        L = jb["L"][qb]
                        nkb = L // 128
                        q0 = qb * 128
                        lcol = jb["limcol"] + qb
                        for kb0 in range(0, nkb, 4):
                            nb = min(4, nkb - kb0)
                            bk = 2 + (kb0 // 4) % 2
                            for i in range(nb):
                                k.mm(banks[bk][:, i * 128:(i + 1) * 128], Mb[:, (kb0 + i) * 128:(kb0 + i + 1) * 128], ident_bf[:], [Mb, ident_bf], PQ[bk])
                            k.cp("act", MT[:, kb0:kb0 + nb, :].rearrange("p a b -> p (a b)"), banks[bk][:, 0:nb * 128], PQ[bk], [MT])

                    def gen_A(qb):
                        L = jb["L"][qb]
                        nkb = L // 128
                        q0 = qb * 128
                        lcol = jb["limcol"] + qb
                        nch = (L + 511) // 512
                        n_ = 0
                        for ch in range(nch):
                            wch = min(512, L - 512 * ch)
                            Kt, Vt = Kc[ch % 2], Vc[ch % 2]
                            k.dma(Kt[:, :, 0:wch], jb["kT"][:, :, 512 * ch:512 * ch + wch], [T_kT], [Kt])
                            k.dma(Vt[:, 0:wch // 128, :, :], jb["v"][512 * ch:512 * ch + wch, :].rearrange("(i p) (h d) -> p i h d", p=128, d=66), [T_v], [Vt])
                            yield
                            for i in range(wch // 128 if KP3 >= 1 else 0):
                                kbg = 4 * ch + i
                                for g in range(2):
                                    bk = 2 + n_ % 4
                                    pt = PT[n_ % 4]
                                    n_ += 1
                                    diag = kbg >= nkb - 2
                                    dd = 0 if kbg == nkb - 1 else 1
                                    r0 = 64 * g
                                    for e4 in range(4):
                                        k.mm(banks[bk][:, e4 * 128:(e4 + 1) * 128], Kt[r0:r0 + 64, e4, i * 128:(i + 1) * 128], qaT[r0:r0 + 64, e4, q0:q0 + 128], [Kt, qaT], PQ[bk], start=True, stop=True)
                                    if diag:
                                        k.tt("dve", lgs[:, :].rearrange("p (h q) -> p h q", q=128), banks[bk][:, :].rearrange("p (h q) -> p h q", q=128), BT[:, dd, g:8:2, :], ALU.add, PQ[bk] + [BT], [lgs])
                                        k.act(pt[:].rearrange("p h q -> p (h q)"), lgs[:, :], AF.Exp, [lgs], [pt])
                                    else:
                                        k.act(pt[:].rearrange("p h q -> p (h q)"), banks[bk][:, :], AF.Exp, PQ[bk], [pt])
                                    k.tt("dve", pt[:], pt[:], MT[:, kbg:kbg + 1, :].to_broadcast([128, 4, 128]), ALU.mult, [pt, MT], [pt])
                                    for e4 in range(4 if KP3 >= 4 else 0):
                                        h = 2 * e4 + g
                                        k.mm(banks[6 + g][:, e4 * 65:(e4 + 1) * 65], pt[:, e4, :], Vt[:, i, h, 0:65], [pt, Vt], PQ[6 + g], start=(kbg == 0 and e4 == 0), stop=(kbg == nkb - 1 and e4 == 3))

                    def do_N(qb):
                        L = jb["L"][qb]
                        nkb = L // 128
                        q0 = qb * 128
                        lcol = jb["limcol"] + qb
                        for g in range(2):
                            ov = banks[6 + g][:, 0:260].rearrange("p (h d) -> p h d", d=65)
                            k.ts("dve", den[:, 4 * g:4 * g + 4], ov[:, :, 64], 1e-30, ALU.add, PQ[6 + g], [den])
                            k.recip(den[:, 4 * g:4 * g + 4], den[:, 4 * g:4 * g + 4], [den], [den])
                            k.tt("dve", oa[:, g:8:2, :], ov[:, :, 0:64], den[:, 4 * g:4 * g + 4].unsqueeze(2).to_broadcast([128, 4, 64]), ALU.mult, PQ[6 + g] + [den], [oa])
                        oaf = oa[:].rearrange("p h d -> p (h d)")
                        for c in range(4):
                            k.mm(banks[2][:, c * 128:(c + 1) * 128], oaf[:, c * 128:(c + 1) * 128], ident_bf[:], [oa, ident_bf], PQ[2])
                        k.cp("act", oaT[:, :, q0:q0 + 128], banks[2][:, :].rearrange("p (c t) -> p c t", t=128), PQ[2], [oaT])

                    def drain(g):
                        for _ in g:
                            pass

                    def interleave(ga, gb):
                        alive = [ga, gb]
                        while alive:
                            for g_ in list(alive):
                                try:
                                    next(g_)
                                except StopIteration:
                                    alive.remove(g_)

                    drain(gen_S(0))
                    for qb in range(NQ):
                        do_B(qb)
                        do_T(qb)
                        if qb + 1 < NQ:
                            interleave(gen_A(qb), gen_S(qb + 1))
                        else:
                            drain(gen_A(qb))
                        do_N(qb)
                    S.barrier()
                if KP2 < 8:
                    return
                with ExitStack() as es:
                    Wpa = sbt(es, "Wpa", [128, 4, D], BF16)
                    Wpb = sbt(es, "Wpb", [128, 4, D], BF16)
                    Wg = sbt(es, "Wg", [128, 8, 2048], BF16)
                    Wo = sbt(es, "Wo", [128, 8, D], BF16)
                    Wpl = sbt(es, "Wpl", [128, 2, D], BF16)
                    k.dma(Wpa[:], ws_pa, [T_wscr_in], [Wpa])
                    k.dma(Wpb[:], ws_pb, [T_wscr_in], [Wpb])
                    k.dma(Wg[:], wscr_in[:, :, C_GA:C_GA + 2048], [T_wscr_in], [Wg])
                    k.dma(Wpl[:], ws_ple, [T_wscr_in], [Wpl])
                    obT = sbt(es, "obT", [128, 4, NTOK], BF16)
                    k.dma(obT[:], jb["obT"], [T_ob], [obT])
                    pf = sbt(es, "pf", [128, 2, NTOK], F32)
                    pb = sbt(es, "pb", [128, 2, NTOK], BF16)
                    k.dma(pf[:], jb["psrc"], [], [pf])
                    k.cp("pool", pb[:], pf[:], [pf], [pb])
                    mixT = sbt(es, "mixT", [128, 8, 512], BF16)
                    h2T = sbt(es, "h2T", [128, 8, 512], BF16)
                    actT = sbt(es, "actT", [128, 22, 512], BF16)
                    ughalo = sbt(es, "ughalo", [128, 22, 2], F32)
                    fco = sbt(es, "fco", [128, 22, 2], F32)
                    ugc = [sbt(es, f"ugc{i}", [128, 514], F32) for i in range(2)]
                    cva = [sbt(es, f"cva{i}", [128, 512], F32) for i in range(2)]
                    sga = sbt(es, "sga", [128, 512], F32)
                    sgb = sbt(es, "sgb", [128, 512], F32)
                    t1 = sbt(es, "t1", [128, 512], F32)
                    wug = [sbt(es, f"wug{i}", [128, 8, 128], BF16) for i in range(2)]
                    wuv = [sbt(es, f"wuv{i}", [128, 8, 128], BF16) for i in range(2)]
                    wdn = [sbt(es, "wdn0", [128, 22, 128], BF16)] * 2
                    ytok = sbt(es, "ytok", [128, D], F32)
                    if jb["fhalo"] is not None:
                        k.dma(ughalo[:], jb["fhalo"], [], [ughalo])
                    for (c0, n, halo) in jb["segs"]:
                        sg = slice(c0, c0 + n)
                        k.dma(Wo[:], ws_out, [T_wscr_in], [Wo])
                        for c in range(8):
                            cs = slice(c * 128, (c + 1) * 128)
                            for kc in range(4):
                                k.mm(banks[0][:, 0:n], Wpa[:, kc, cs], oaT[:, kc, sg], [Wpa, oaT], PQ[0], start=(kc == 0), stop=(kc == 3))
                            for kc in range(8):
                                k.mm(banks[1][:, 0:n], Wg[:, kc, cs], hT[:, kc, sg], [Wg, hT], PQ[1], start=(kc == 0), stop=(kc == 7))
                            k.act(sga[:, 0:n], banks[1][:, 0:n], AF.Sigmoid, PQ[1], [sga])
                            k.tt("dve", t1[:, 0:n], banks[0][:, 0:n], sga[:, 0:n], ALU.mult, PQ[0] + [sga], [t1])
                            for kc in range(4):
                                k.mm(banks[2][:, 0:n], Wpb[:, kc, cs], obT[:, kc, sg], [Wpb, obT], PQ[2], start=(kc == 0), stop=(kc == 3))
                            for kc in range(8):
                                k.mm(banks[3][:, 0:n], Wg[:, kc, 1024 + c * 128:1024 + (c + 1) * 128], hT[:, kc, sg], [Wg, hT], PQ[3], start=(kc == 0), stop=(kc == 7))
                            k.act(sgb[:, 0:n], banks[3][:, 0:n], AF.Sigmoid, PQ[3], [sgb])
                            k.tt("dve", sgb[:, 0:n], banks[2][:, 0:n], sgb[:, 0:n], ALU.mult, PQ[2] + [sgb], [sgb])
                            k.tt("pool", mixT[:, c, 0:n], t1[:, 0:n], sgb[:, 0:n], ALU.add, [t1, sgb], [mixT])
                        for c in range(8):
                            cs = slice(c * 128, (c + 1) * 128)
                            bk = 4 + c % 2
                            for kc in range(8):
                                k.mm(banks[bk][:, 0:n], Wo[:, kc, cs], mixT[:, kc, 0:n], [Wo, mixT], PQ[bk], start=(kc == 0), stop=(kc == 7))
                            k.tt("dve", xo[:, c, sg], xo[:, c, sg], banks[bk][:, 0:n], ALU.add, [xo] + PQ[bk], [xo])
                        k.act(sqb[:, :, 0:n], xo[:, :, sg], AF.Square, [xo], [sqb])
                        for kc in range(8):
                            k.mm(banks[0][:, 0:n], ones_bf[:], sqb[:, kc, 0:n], [ones_bf, sqb], PQ[0], start=(kc == 0), stop=(kc == 7))
                        k.act(rsb[:, 0:n], banks[0][:, 0:n], AF.Sqrt, PQ[0] + [epsT], [rsb], scale=1.0 / D, bias=epsT[:, 0:1])
                        k.recip(rsb[:, 0:n], rsb[:, 0:n], [rsb], [rsb])
                        for kc in range(8):
                            k.tt("dve" if kc % 2 == 0 else "pool", h2T[:, kc, 0:n], xo[:, kc, sg], rsb[:, 0:n], ALU.mult, [xo, rsb], [h2T])
                        for cc in range(22):
                            wg_, wv_ = wug[cc % 2], wuv[cc % 2]
                            k.dma(wg_[:], ws_up[:, :, cc * 128:(cc + 1) * 128], [T_wscr_in], [wg_])
                            bk = 1 + cc % 2
                            for kc in range(8):
                                k.mm(banks[bk][:, 0:n], wg_[:, kc, :], h2T[:, kc, 0:n], [wg_, h2T], PQ[bk], start=(kc == 0), stop=(kc == 7))
                            if halo:
                                k.cp("act", ughalo[:, cc, 0:n], banks[bk][:, 0:n], PQ[bk], [ughalo])
                                continue
                            k.dma(wv_[:], ws_up[:, :, DFF + cc * 128:DFF + (cc + 1) * 128], [T_wscr_in], [wv_])
                            ug, ca = ugc[cc % 2], cva[cc % 2]
                            k.cp("act", ug[:, 2:2 + n], banks[bk][:, 0:n], PQ[bk], [ug])
                            k.cp("pool", ug[:, 0:2], ughalo[:, cc, :], [ughalo], [ug])
                            k.ts("dve", ca[:, 0:n], ug[:, 0:n], convf[:, cc, 0:1], ALU.mult, [ug, convf], [ca])
                            k.stt(ca[:, 0:n], ug[:, 1:1 + n], convf[:, cc, 1:2], ca[:, 0:n], ALU.mult, ALU.add, [ug, convf, ca], [ca])
                            k.stt(ca[:, 0:n], ug[:, 2:2 + n], convf[:, cc, 2:3], ca[:, 0:n], ALU.mult, ALU.add, [ug, convf, ca], [ca])
                            k.act(ca[:, 0:n], ca[:, 0:n], AF.Gelu_apprx_tanh, [ca], [ca])
                            nv = jb["nvalid"]
                            k.cp("pool", fco[:, cc, :], ug[:, nv:nv + 2], [ug], [fco])
                            bk2 = 3 + cc % 2
                            for kc in range(8):
                                k.mm(banks[bk2][:, 0:n], wv_[:, kc, :], h2T[:, kc, 0:n], [wv_, h2T], PQ[bk2], start=(kc == 0), stop=(kc == 7))
                            k.tt("dve", actT[:, cc, 0:n], ca[:, 0:n], banks[bk2][:, 0:n], ALU.mult, [ca] + PQ[bk2], [actT])
                        if halo:
                            continue
                        for c in range(8):
                            wd_ = wdn[c % 2]
                            k.dma(wd_[:], ws_down[:, :, c * 128:(c + 1) * 128], [T_wscr_in], [wd_])
                            bk = 5 + c % 2
                            for cc in range(22):
                                k.mm(banks[bk][:, 0:n], wd_[:, cc, :], actT[:, cc, 0:n], [wd_, actT], PQ[bk], start=(cc == 0), stop=(cc == 21))
                            k.tt("dve", xo[:, c, sg], xo[:, c, sg], banks[bk][:, 0:n], ALU.add, [xo] + PQ[bk], [xo])
                        k.dma(Wo[:], ws_pg, [T_wscr_in], [Wo])
                        Wpg = Wo
                        k.act(sqb[:, :, 0:n], xo[:, :, sg], AF.Square, [xo], [sqb])
                        for kc in range(8):
                            k.mm(banks[0][:, 0:n], ones_bf[:], sqb[:, kc, 0:n], [ones_bf, sqb], PQ[0], start=(kc == 0), stop=(kc == 7))
                        k.act(rsb[:, 0:n], banks[0][:, 0:n], AF.Sqrt, PQ[0] + [epsT], [rsb], scale=1.0 / D, bias=epsT[:, 0:1])
                        k.recip(rsb[:, 0:n], rsb[:, 0:n], [rsb], [rsb])
                        for kc in range(8):
                            k.tt("dve" if kc % 2 == 0 else "pool", h2T[:, kc, 0:n], xo[:, kc, sg], rsb[:, 0:n], ALU.mult, [xo, rsb], [h2T])
                        for c in range(8):
                            cs = slice(c * 128, (c + 1) * 128)
                            for kc in range(8):
                                k.mm(banks[1][:, 0:n], Wpg[:, kc, cs], h2T[:, kc, 0:n], [Wpg, h2T], PQ[1], start=(kc == 0), stop=(kc == 7))
                            k.act(sga[:, 0:n], banks[1][:, 0:n], AF.Sigmoid, PQ[1], [sga])
                            for kc in range(2):
                                k.mm(banks[2][:, 0:n], Wpl[:, kc, cs], pb[:, kc, sg], [Wpl, pb], PQ[2], start=(kc == 0), stop=(kc == 1))
                            k.tt("dve", t1[:, 0:n], banks[2][:, 0:n], sga[:, 0:n], ALU.mult, PQ[2] + [sga], [t1])
                            k.tt("pool", xo[:, c, sg], xo[:, c, sg], t1[:, 0:n], ALU.add, [xo, t1], [xo])
                        k.act(sqb[:, :, 0:n], xo[:, :, sg], AF.Square, [xo], [sqb])
                        for kc in range(8):
                            k.mm(banks[0][:, 0:n], ones_bf[:], sqb[:, kc, 0:n], [ones_bf, sqb], PQ[0], start=(kc == 0), stop=(kc == 7))
                        k.act(rsb[:, 0:n], banks[0][:, 0:n], AF.Sqrt, PQ[0] + [epsT], [rsb], scale=1.0 / D, bias=epsT[:, 0:1])
                        k.recip(rsb[:, 0:n], rsb[:, 0:n], [rsb], [rsb])
                        for kc in range(8):
                            k.stt(xo[:, kc, sg], xo[:, kc, sg], gains[:, 24 + kc:25 + kc], rsb[:, 0:n], ALU.mult, ALU.mult, [xo, gains, rsb], [xo])
                        for tt in range(n // 128):
                            for cg in range(2):
                                bk = 3 + cg
                                for c4 in range(4):
                                    k.mm(banks[bk][:, c4 * 128:(c4 + 1) * 128], xo[:, 4 * cg + c4, c0 + tt * 128:c0 + (tt + 1) * 128], ident, [xo, cst], PQ[bk])
                                k.cp("act" if cg == 0 else "dve", ytok[:, cg * 512:(cg + 1) * 512], banks[bk][:, :], PQ[bk], [ytok])
                            nv = min(128, jb["nvalid"])
                            k.dma(jb["y_out"][tt * 128:tt * 128 + nv, :], ytok[0:nv, :], [ytok], [outT], key=ytok)
                        k.dma(jb["fconv_out"], fco[:], [fco], [outT], key=fco)
                    S.barrier()

        xav = I["xa"].rearrange("(kc p) t -> p kc t", p=128)
        pav = I["pT"].rearrange("(kc p) t -> p kc t", p=128)
        if "P" in KJOB2:
            for m in range(4):
                ps_ = 4 * m + 3
                t0 = 512 * ps_ - 128
                own_block(dict(NQ=5, xsrc=xav[:, :, t0:t0 + 640], psrc=pav[:, :, t0:t0 + 640], L=[128 * (4 * ps_ + r) for r in range(5)],
                               limcol=5 * m, lolim=22, kiT=kiT_p, kT=kT_p, v=v_p, obT=obT_p[:, :, t0:t0 + 640], fhalo=None,
                               segs=[(126, 2, True), (128, 512, False)], nvalid=512, y_out=O["y_own"][512 * m:512 * (m + 1), :], fconv_out=O["fconv_p"]))
        if "S" in KJOB2:
            for sb_ in range(2):
                own_block(dict(NQ=1, xsrc=I["xs"][sb_].rearrange("(kc p) t -> p kc t", p=128), psrc=I["psT"][sb_].rearrange("(kc p) t -> p kc t", p=128),
                               L=[LKS], limcol=20 + sb_, lolim=None, kiT=kiT_s[sb_], kT=kT_s[sb_], v=v_s[sb_], obT=obT_s[sb_], fhalo=I["fconvT"][sb_],
                               segs=[(0, 128, False)], nvalid=16, y_out=O["y_s"][sb_], fconv_out=O["fconv_s"][sb_]))

        print('nsem', S.nsem, {e: len(q) for e, q in S.q.items()})
        S.final_wait("sp", [outT])
        S.emit()
    return nc


def _consts():
    c = np.zeros((128, 7 * 128), np.float32)
    i = np.arange(128)
    same = (i[:, None] // 64) == (i[None, :] // 64)
    c[:, 0:128] = np.eye(128)
    c[:, 128:256] = ((i[:, None] <= i[None, :]) & same)
    c[:, 256:384] = ((i[:, None] > i[None, :]) & same)
    c[:, 384:512] = np.where((i[:, None] > i[None, :]) & same, 0.0, NEG)
    c[:, 512:640] = np.where((i[:, None] <= i[None, :]) & same, 0.0, -NEG)
    c[:, 640:768] = (i[:, None] < 64)
    c[:, 768:896] = (i[:, None] >= 64)
    return c


_NC = None


def kernel(**inp):
    global _NC
    f32 = np.float32
    xp = np.asarray(inp["x_prompt"], f32)
    xs = np.asarray(inp["x_sample"], f32)
    cst = _consts()
    gains = np.zeros((128, 32), f32)
    gains[:, 0:8] = np.asarray(inp["norm_mix"], f32)[0].reshape(8, 128).T
    gains[:, 8:16] = np.asarray(inp["norm_ffn"], f32)[0].reshape(8, 128).T
    gains[:, 16:24] = np.asarray(inp["norm_ple"], f32)[0].reshape(8, 128).T
    gains[:, 24:32] = np.asarray(inp["norm_final"], f32).reshape(8, 128).T
    convb = np.ascontiguousarray(np.asarray(inp["conv_b"], f32)[0].T.reshape(12, 128, 4).transpose(1, 0, 2))
    alog = np.ascontiguousarray(np.broadcast_to(np.asarray(inp["a_log"], f32)[0][None, :], (128, 4)))
    dtb = np.ascontiguousarray(np.broadcast_to(np.asarray(inp["dt_bias"], f32)[0][None, :], (128, 4)))
    ngdn = np.ascontiguousarray(np.broadcast_to(np.asarray(inp["norm_gdn"], f32)[0][None, :], (128, 128)))
    gval = np.zeros((128, 2), f32)
    gval[0:16, 0] = 1.0
    gval[:, 1] = 1.0
    w_in = np.ascontiguousarray(np.asarray(inp["w_in"], f32)[0])
    ck = np.asarray(inp["cache_k"], f32)[0]
    cvv = np.asarray(inp["cache_v"], f32)[0]
    cki = np.asarray(inp["cache_kidx"], f32)[0]
    sg = np.asarray(inp["state_gdn"], f32)[0]
    sgc = np.asarray(inp["state_gdn_conv"], f32)[0]
    def bucket_np(rel):
        half, max_exact = 16, 8
        ret = np.where(rel > 0, half, 0)
        n = np.abs(rel)
        nf = np.maximum(n, 1).astype(np.float32)
        large = max_exact + (np.log(nf / np.float32(max_exact)) / np.float32(np.log(128 / 8)) * np.float32(half - max_exact)).astype(np.int32)
        large = np.minimum(large, half - 1)
        return ret + np.where(n < max_exact, n, large)
    ohrev = np.zeros((32, 384), f32)
    sp_ = np.arange(383)
    ohrev[bucket_np(127 - sp_), sp_] = 1.0
    iota = np.ascontiguousarray(np.broadcast_to(np.arange(512, dtype=f32)[None, :], (128, 512)))
    relb = np.ascontiguousarray(np.asarray(inp["rel_bias"], f32))
    relb15 = np.ascontiguousarray(relb[15][:, None])
    convf = np.ascontiguousarray(np.asarray(inp["conv_ffn"], f32)[0].T.reshape(22, 128, 3).transpose(1, 0, 2))
    pp = np.asarray(inp["p_prompt"], f32)[0]
    psm = np.asarray(inp["p_sample"], f32)[0]
    sfc = np.asarray(inp["state_ffn_conv"], f32)[0]
    wts = dict(w_pa=np.ascontiguousarray(np.asarray(inp["w_proj_a"], f32)[0]), w_pb=np.ascontiguousarray(np.asarray(inp["w_proj_b"], f32)[0]),
               w_out=np.ascontiguousarray(np.asarray(inp["w_out"], f32)[0]), w_up=np.ascontiguousarray(np.asarray(inp["w_up"], f32)[0]),
               w_down=np.ascontiguousarray(np.asarray(inp["w_down"], f32)[0]), w_ple=np.ascontiguousarray(np.asarray(inp["w_ple"], f32)[0]),
               w_pg=np.ascontiguousarray(np.asarray(inp["w_ple_gate"], f32)[0]))
    in_maps = []
    for c in range(8):
        b, j = c // 4, c % 4
        pad = 512 * (3 - j)
        xa = np.zeros((D, LKP), f32)
        xa[:, pad:] = xp[b].T[:, :LKP - pad]
        sl = slice(2 * c, 2 * c + 2)
        xs_c = np.zeros((2, D, 128), f32)
        xs_c[:, :, 0:16] = xs[sl].transpose(0, 2, 1)
        m = dict(xa=xa, xs=xs_c, cst=cst, gains=gains, convb=convb, alog=alog, dtb=dtb, ngdn=ngdn, gval=gval, w_in=w_in,
                 cache_kT=np.ascontiguousarray(ck[sl].reshape(2, PAST, 512).transpose(0, 2, 1)),
                 cache_v=np.ascontiguousarray(cvv[sl].reshape(2, PAST, 512)),
                 cache_kiT=np.ascontiguousarray(cki[sl].transpose(0, 2, 1)),
                 state_gdn=np.ascontiguousarray(sg[sl]),
                 gconvT=np.ascontiguousarray(sgc[sl].transpose(0, 2, 1).reshape(2, 12, 128, 3).transpose(0, 2, 1, 3)))
        pT = np.zeros((256, LKP), f32)
        pT[:, pad:] = pp[b].T[:, :LKP - pad]
        psT = np.zeros((2, 256, 128), f32)
        psT[:, :, 0:16] = psm[sl].transpose(0, 2, 1)
        lims = np.zeros((128, 24), f32)
        ii = np.arange(128)
        for m_ in range(4):
            for qb in range(5):
                pt = 512 * (4 * m_ + 3) - 128 + 128 * qb + ii
                lims[:, 5 * m_ + qb] = (pt // 64 + 1) * 64
        lims[:, 20] = PAST + 16
        lims[:, 21] = PAST + 16
        lims[:, 22] = pad
        m.update(pT=pT, psT=psT, convf=convf, relb=relb, relb15=relb15, ohrev=ohrev, iota=iota, lims=lims,
                 fconvT=np.ascontiguousarray(sfc[sl].transpose(0, 2, 1).reshape(2, 22, 128, 2).transpose(0, 2, 1, 3)), **wts)
        in_maps.append(m)
    if _NC is None:
        _NC = build_program()
    res = run_bass_kernel_spmd(_NC, in_maps, core_ids=list(range(8)))
    R = res.results
    y_p = np.zeros((2, 8192, D), f32)
    for c in range(8):
        b, j = c // 4, c % 4
        for m_ in range(4):
            sg_ = 4 * m_ + j
            y_p[b, 512 * sg_:512 * (sg_ + 1)] = R[c]["y_own"][512 * m_:512 * (m_ + 1)]
    y_s = np.concatenate([R[c]["y_s"] for c in range(8)]).reshape(16, 16, D)
    k_p = np.stack([R[4 * b + 3]["k_all"] for b in range(2)]).reshape(1, 2, 8192, 8, 64)
    v_p = np.stack([R[4 * b + 3]["v_all"] for b in range(2)]).reshape(1, 2, 8192, 8, 64)
    ki_p = np.stack([R[4 * b + 3]["ki_all"] for b in range(2)]).reshape(1, 2, 8192, 64)
    gdn_p = np.stack([R[4 * b + 3]["gdn_p"] for b in range(2)]).reshape(1, 2, 4, 128, 128)
    gconv_p = np.stack([R[4 * b + 3]["gconv_p"].transpose(1, 0, 2).reshape(1536, 3).T for b in range(2)]).reshape(1, 2, 3, 1536)
    fconv_p = np.stack([R[4 * b + 3]["fconv_p"].transpose(1, 0, 2).reshape(DFF, 2).T for b in range(2)]).reshape(1, 2, 2, DFF)
    k_s = np.concatenate([R[c]["k_s"] for c in range(8)]).reshape(1, 16, 16, 8, 64)
    v_s = np.concatenate([R[c]["v_s"] for c in range(8)]).reshape(1, 16, 16, 8, 64)
    ki_s = np.concatenate([R[c]["ki_s"] for c in range(8)]).reshape(1, 16, 16, 64)
    gdn_s = np.concatenate([R[c]["gdn_s"] for c in range(8)]).reshape(1, 16, 4, 128, 128)
    gconv_s = np.concatenate([R[c]["gconv_s"] for c in range(8)])
    gconv_s = np.ascontiguousarray(gconv_s.transpose(0, 2, 1, 3).reshape(16, 1536, 3).transpose(0, 2, 1)).reshape(1, 16, 3, 1536)
    fconv_s = np.concatenate([R[c]["fconv_s"] for c in range(8)])
    fconv_s = np.ascontiguousarray(fconv_s.transpose(0, 2, 1, 3).reshape(16, DFF, 2).transpose(0, 2, 1)).reshape(1, 16, 2, DFF)
    return (y_p, y_s, k_p, v_p, ki_p, gdn_p, gconv_p, fconv_p, k_s, v_s, ki_s, gdn_s, gconv_s, fconv_s)
```

```python
import os
import numpy as np
from contextlib import ExitStack
import concourse.bass as bass
import concourse.mybir as mybir
from concourse.bass_utils import run_bass_kernel_spmd

F32 = mybir.dt.float32
BF16 = mybir.dt.bfloat16
AF = mybir.ActivationFunctionType
ALU = mybir.AluOpType
AX = mybir.AxisListType

EPOCH = 4096
KDBG = int(os.environ.get('KDBG', '9'))
KJOB = os.environ.get('KJOB', 'ps')
KSUB = int(os.environ.get('KSUB', '99'))
KJOB2 = os.environ.get('KJOB2', 'PS')
KP2 = int(os.environ.get('KP2', '99'))
KP3 = int(os.environ.get('KP3', '99'))
EPS = 1e-6
NEG = -1e30

D = 1024
LKP = 8192
NTP = 256
PAST = 2048
LKS = PAST + 128
DFF = 2816

C_QA, C_KA, C_VA, C_QI, C_KI, C_WI, C_QB, C_GB, C_BB, C_AB, C_GA, C_GBR = 0, 512, 1024, 1536, 2048, 2112, 2120, 3656, 4168, 4172, 4176, 5200
DIN = 6224


class Tl:
    __slots__ = ("t", "name", "lw", "rd", "excl")

    def __init__(self, t, name, excl=False):
        self.t = t
        self.name = name
        self.lw = None
        self.rd = []
        self.excl = excl

    def __getitem__(self, idx):
        return self.t[idx]


class Sched:
    ENGS = ("pe", "act", "dve", "pool", "sp")

    def __init__(self, nc, es):
        self.nc = nc
        self.es = es
        self.q = {e: [] for e in self.ENGS}
        self.cnt = {e: 0 for e in self.ENGS}
        self.sems = {e: [] for e in self.ENGS}
        self.seen = {e: {} for e in self.ENGS}
        self.dma_sems = {}
        self.keep = []
        self.nsem = 0

    def _newsem(self, name):
        self.nsem += 1
        return self.es.enter_context(self.nc.semaphore(name))

    def _eng_token(self, e):
        c = self.cnt[e]
        ep = c // EPOCH
        while len(self.sems[e]) <= ep:
            self.sems[e].append(self._newsem(f"s_{e}_{len(self.sems[e])}"))
        self.cnt[e] = c + 1
        return (("e", e, ep), self.sems[e][ep], (c % EPOCH) + 1)

    def _waits(self, e, reads, writes):
        toks = []
        for t in reads:
            if t.lw is not None:
                toks.append(t.lw)
            if t.excl:
                toks.extend(t.rd)
        for t in writes:
            if t.lw is not None:
                toks.append(t.lw)
            toks.extend(t.rd)
        best = {}
        for (key, sem, val) in toks:
            if key[0] == "e" and key[1] == "pe" and e == "pe":
                continue
            if best.get(key, (None, 0))[1] < val:
                best[key] = (sem, val)
        out = []
        seen = self.seen[e]
        for key, (sem, val) in best.items():
            if seen.get(key, 0) >= val:
                continue
            seen[key] = val
            out.append((sem, val))
        return out

    def op(self, e, fn, reads=(), writes=()):
        w = self._waits(e, reads, writes)
        tok = self._eng_token(e)
        self.q[e].append((w, fn, tok[1], 1))
        for t in writes:
            t.lw = tok
            t.rd = []
        for t in reads:
            if t.lw is not tok:
                t.rd.append(tok)
        return tok

    def dma(self, e, fn, reads=(), writes=(), key=None):
        w = self._waits(e, reads, writes)
        k = key if key is not None else (writes[0] if writes else reads[0])
        kid = id(k)
        ent = self.dma_sems.get(kid)
        if ent is None or ent[1] >= 32000:
            if ent is None:
                self.keep.append(k)
            if getattr(self, "pool", None):
                ent = self.pool.pop()
            else:
                self.semid = getattr(self, "semid", 0) + 1
                ent = [self._newsem(f"d_{self.nsem}"), 0, self.semid]
            self.dma_sems[kid] = ent
        ent[1] += 16
        tok = (("d", ent[2], 0), ent[0], ent[1])
        self.q[e].append((w, fn, ent[0], 16))
        for t in writes:
            t.lw = tok
            t.rd = []
        for t in reads:
            if t.lw is not tok:
                t.rd.append(tok)
        return tok

    def barrier(self):
        toks = []
        for e in self.ENGS:
            c = self.cnt[e]
            if c > 0:
                ep = (c - 1) // EPOCH
                toks.append((("e", e, ep), self.sems[e][ep], ((c - 1) % EPOCH) + 1))
        for kid, ent in self.dma_sems.items():
            if ent[1] > 0:
                toks.append((("d", ent[2], 0), ent[0], ent[1]))
        for e in self.ENGS:
            w = []
            seen = self.seen[e]
            for (key, sem, val) in toks:
                if key[0] == "e" and key[1] == e:
                    continue
                if seen.get(key, 0) >= val:
                    continue
                seen[key] = val
                w.append((sem, val))
            if w:
                self.q[e].append((w, None, None, 0))
        if not hasattr(self, "pool"):
            self.pool = []
        for kid, ent in self.dma_sems.items():
            if ent[1] < 30000:
                self.pool.append(ent)
        self.dma_sems = {}

    def final_wait(self, e, tiles):
        w = self._waits(e, tiles, tiles)
        self.q[e].append((w, None, None, 0))

    def emit(self):
        nc = self.nc
        with nc.Block() as block:
            def run(eng, name):
                for (w, fn, sem, inc) in self.q[name]:
                    for (s, v) in w:
                        eng.wait_ge(s, v)
                    if fn is not None:
                        fn(eng).then_inc(sem, inc)

            @block.tensor
            def _(eng):
                run(eng, "pe")

            @block.scalar
            def _(eng):
                run(eng, "act")

            @block.vector
            def _(eng):
                run(eng, "dve")

            @block.gpsimd
            def _(eng):
                run(eng, "pool")

            @block.sync
            def _(eng):
                run(eng, "sp")


class K:
    def __init__(self, nc, S):
        self.nc = nc
        self.S = S
        self.rr = 0

    def mm(self, out, lhsT, rhs, rd, wr, start=True, stop=True):
        self.S.op("pe", lambda e, o=out, l=lhsT, r=rhs, a=start, b=stop: e.matmul(o, lhsT=l, rhs=r, start=a, stop=b), reads=rd, writes=wr)

    def tr(self, out, in_, ident, rd, wr):
        self.S.op("pe", lambda e, o=out, i=in_, d=ident: e.matmul(o, lhsT=i, rhs=d, start=True, stop=True), reads=rd, writes=wr)

    def act(self, out, in_, func, rd, wr, scale=None, bias=None, accum=None):
        kw = {}
        if scale is not None:
            kw["scale"] = scale
        if bias is not None:
            kw["bias"] = bias
        if accum is not None:
            kw["accum_out"] = accum
        self.S.op("act", lambda e, o=out, i=in_, f=func, k=kw: e.activation(out=o, in_=i, func=f, **k), reads=rd, writes=wr)

    def ts(self, eng, out, in0, s1, op0, rd, wr, s2=None, op1=None, accum=None):
        kw = {}
        if op1 is not None:
            kw["op1"] = op1
        if accum is not None:
            kw["accum_out"] = accum
        self.S.op(eng, lambda e, o=out, i=in0, a=s1, b=s2, p=op0, k=kw: e.tensor_scalar(out=o, in0=i, scalar1=a, scalar2=b, op0=p, **k), reads=rd, writes=wr)

    def tt(self, eng, out, in0, in1, op, rd, wr):
        self.S.op(eng, lambda e, o=out, i=in0, j=in1, p=op: e.tensor_tensor(out=o, in0=i, in1=j, op=p), reads=rd, writes=wr)

    def stt(self, out, in0, sc, in1, op0, op1, rd, wr):
        self.S.op("dve", lambda e, o=out, i=in0, s=sc, j=in1, p=op0, q=op1: e.scalar_tensor_tensor(out=o, in0=i, scalar=s, in1=j, op0=p, op1=q), reads=rd, writes=wr)

    def cp(self, eng, out, in_, rd, wr):
        if eng == "act":
            self.S.op("act", lambda e, o=out, i=in_: e.copy(out=o, in_=i), reads=rd, writes=wr)
        else:
            self.S.op(eng, lambda e, o=out, i=in_: e.tensor_copy(out=o, in_=i), reads=rd, writes=wr)

    def memset(self, eng, ap, val, wr):
        self.S.op(eng, lambda e, a=ap, v=val: e.memset(a, v), writes=wr)

    def recip(self, out, in_, rd, wr):
        self.S.op("dve", lambda e, o=out, i=in_: e.reciprocal(out=o, in_=i), reads=rd, writes=wr)

    def dma(self, out, in_, rd, wr, q="sp", key=None, slow=False):
        if slow:
            self.S.dma(q, lambda e, o=out, i=in_: e.dma_start(out=o, in_=i, allow_slow_non_contiguous=True), reads=rd, writes=wr, key=key)
        else:
            self.S.dma(q, lambda e, o=out, i=in_: e.dma_start(out=o, in_=i), reads=rd, writes=wr, key=key)


def build_program():
    nc = bass.Bass("TRN2", target_bir_lowering=False)

    def din(name, shape, dt=F32):
        return nc.dram_tensor(name, list(shape), dt, kind="ExternalInput").ap()

    def dout(name, shape, dt=F32):
        return nc.dram_tensor(name, list(shape), dt, kind="ExternalOutput").ap()

    def dscr(name, shape, dt):
        return nc.dram_tensor(name, list(shape), dt, kind="Internal").ap()

    I = {}
    I["xa"] = din("xa", [D, LKP])
    I["xs"] = din("xs", [2, D, 128])
    I["cst"] = din("cst", [128, 7 * 128])
    I["gains"] = din("gains", [128, 32])
    I["convb"] = din("convb", [128, 12, 4])
    I["alog"] = din("alog", [128, 4])
    I["dtb"] = din("dtb", [128, 4])
    I["ngdn"] = din("ngdn", [128, 128])
    I["gval"] = din("gval", [128, 2])
    I["w_in"] = din("w_in", [D, DIN])
    I["cache_kT"] = din("cache_kT", [2, 512, PAST])
    I["cache_v"] = din("cache_v", [2, PAST, 512])
    I["cache_kiT"] = din("cache_kiT", [2, 64, PAST])
    I["state_gdn"] = din("state_gdn", [2, 4, 128, 128])
    I["gconvT"] = din("gconvT", [2, 128, 12, 3])
    I["pT"] = din("pT", [256, LKP])
    I["psT"] = din("psT", [2, 256, 128])
    I["w_pa"] = din("w_pa", [512, D])
    I["w_pb"] = din("w_pb", [512, D])
    I["w_out"] = din("w_out", [D, D])
    I["w_up"] = din("w_up", [D, 2 * DFF])
    I["w_down"] = din("w_down", [DFF, D])
    I["w_ple"] = din("w_ple", [256, D])
    I["w_pg"] = din("w_pg", [D, D])
    I["convf"] = din("convf", [128, 22, 3])
    I["relb"] = din("relb", [32, 8])
    I["relb15"] = din("relb15", [8, 1])
    I["ohrev"] = din("ohrev", [32, 384])
    I["iota"] = din("iota", [128, 512])
    I["lims"] = din("lims", [128, 24])
    I["fconvT"] = din("fconvT", [2, 128, 22, 2])
    O = {}
    O["y_own"] = dout("y_own", [2048, D])
    O["fconv_p"] = dout("fconv_p", [128, 22, 2])
    O["y_s"] = dout("y_s", [2, 16, D])
    O["fconv_s"] = dout("fconv_s", [2, 128, 22, 2])
    O["k_all"] = dout("k_all", [LKP, 512])
    O["v_all"] = dout("v_all", [LKP, 512])
    O["ki_all"] = dout("ki_all", [LKP, 64])
    O["gdn_p"] = dout("gdn_p", [4, 128, 128])
    O["gconv_p"] = dout("gconv_p", [128, 12, 3])
    O["k_s"] = dout("k_s", [2, 16, 512])
    O["v_s"] = dout("v_s", [2, 16, 512])
    O["ki_s"] = dout("ki_s", [2, 16, 64])
    O["gdn_s"] = dout("gdn_s", [2, 4, 128, 128])
    O["gconv_s"] = dout("gconv_s", [2, 128, 12, 3])
    wscr_in = dscr("wscr_in", [128, 8, DIN], BF16)
    ws_pa = dscr("ws_pa", [128, 4, D], BF16)
    ws_pb = dscr("ws_pb", [128, 4, D], BF16)
    ws_out = dscr("ws_out", [128, 8, D], BF16)
    ws_up = dscr("ws_up", [128, 8, 2 * DFF], BF16)
    ws_down = dscr("ws_down", [128, 22, D], BF16)
    ws_ple = dscr("ws_ple", [128, 2, D], BF16)
    ws_pg = dscr("ws_pg", [128, 8, D], BF16)
    tab_scr = dscr("tab_scr", [8, 384], F32)
    T_tab = Tl(None, "tab")
    kT_p = dscr("kT_p", [128, 4, LKP], BF16)
    v_p = dscr("v_p", [LKP, 528], BF16)
    kiT_p = dscr("kiT_p", [64, LKP], BF16)
    obT_p = dscr("obT_p", [128, 4, LKP], BF16)
    kT_s = dscr("kT_s", [2, 128, 4, LKS], BF16)
    v_s = dscr("v_s_scr", [2, LKS, 528], BF16)
    kiT_s = dscr("kiT_s", [2, 64, LKS], BF16)
    obT_s = dscr("obT_s", [2, 128, 4, 128], BF16)

    outT = Tl(None, "outputs")
    T_wscr_in = Tl(None, "wscr_in")
    T_kT = Tl(None, "kT")
    T_v = Tl(None, "v")
    T_ki = Tl(None, "kiT")
    T_ob = Tl(None, "obT")

    with ExitStack() as es0:
        S = Sched(nc, es0)
        k = K(nc, S)

        uid = [0]

        def sbt(es, name, shape, dt):
            uid[0] += 1
            nm = f"s{uid[0]}_{name}"
            return Tl(es.enter_context(nc.sbuf_tensor(nm, list(shape), dt)), nm)

        banks = [es0.enter_context(nc.psum_tensor(f"pb{i}", [128, 512], F32)) for i in range(8)]
        PB = [Tl(banks[b], f"pb{b}", excl=True) for b in range(8)]
        PQ = [[PB[b]] * 4 for b in range(8)]
        banks_bf = None

        def pq(b, q, rows=slice(0, 128), w=128):
            return banks[b][rows, q * 128:q * 128 + w]

        cst = sbt(es0, "cst", [128, 7 * 128], F32)
        k.dma(cst[:], I["cst"], [], [cst])
        ident = cst[:, 0:128]
        Ublk = cst[:, 128:256]
        Lsblk = cst[:, 256:384]
        NEGML = cst[:, 384:512]
        POSMU = cst[:, 512:640]
        half0 = cst[:, 640:768]
        half1 = cst[:, 768:896]
        gains = sbt(es0, "gains", [128, 32], F32)
        k.dma(gains[:], I["gains"], [], [gains])
        convb = sbt(es0, "convb", [128, 12, 4], F32)
        k.dma(convb[:], I["convb"], [], [convb])
        cA = sbt(es0, "cA", [128, 4], F32)
        k.dma(cA[:], I["alog"], [], [cA])
        dtb = sbt(es0, "dtb", [128, 4], F32)
        k.dma(dtb[:], I["dtb"], [], [dtb])
        ngdn = sbt(es0, "ngdn", [128, 128], F32)
        k.dma(ngdn[:], I["ngdn"], [], [ngdn])
        gval = sbt(es0, "gval", [128, 2], F32)
        k.dma(gval[:], I["gval"], [], [gval])
        epsT = sbt(es0, "epsT", [128, 1], F32)
        k.memset("pool", epsT[:], EPS, [epsT])
        ident_bf = sbt(es0, "ident_bf", [128, 128], BF16)
        k.cp("dve", ident_bf[:], ident, [cst], [ident_bf])
        ones_bf = sbt(es0, "ones_bf", [128, 128], BF16)
        k.memset("pool", ones_bf[:], 1.0, [ones_bf])
        k.act(cA[:], cA[:], AF.Exp, [cA], [cA])
        k.ts("dve", cA[:], cA[:], -1.0, ALU.mult, [cA], [cA])

        with ExitStack() as es:
            if KDBG < -1:
                raise_skip = True
            wf = [sbt(es, f"wf{i}", [128, 1556], F32) for i in range(2)]
            wb = [sbt(es, f"wb{i}", [128, 1556], BF16) for i in range(2)]
            cnt_ = [0]

            pcs = []

            def conv_w(src, dst, KC, C, gbase):
                v = src.rearrange("(kc p) c -> p kc c", p=128)
                npc = 1 if C <= 1556 else 4
                pw = C // npc
                for kc in range(KC):
                    for pc in range(npc):
                        pcs.append((v[:, kc, pc * pw:(pc + 1) * pw], dst[:, kc, pc * pw:(pc + 1) * pw], pw, None if gbase is None else gbase + kc))

            def conv_emit():
                def load(n):
                    k.dma(wf[n % 2][:, 0:pcs[n][2]], pcs[n][0], [], [wf[n % 2]], q="sp")
                if pcs:
                    load(0)
                for n, (src_, dst_, pw, gcol) in enumerate(pcs):
                    a, b_ = wf[n % 2], wb[n % 2]
                    if n + 1 < len(pcs):
                        load(n + 1)
                    eng = "dve" if n % 2 == 0 else "pool"
                    if gcol is None:
                        k.cp(eng, b_[:, 0:pw], a[:, 0:pw], [a], [b_])
                    else:
                        k.ts(eng, b_[:, 0:pw], a[:, 0:pw], gains[:, gcol:gcol + 1], ALU.mult, [a, gains], [b_])
                    k.dma(dst_, b_[:, 0:pw], [b_], [T_wscr_in], q="sp", key=b_)

            if KDBG >= -1:
                conv_w(I["w_in"], wscr_in, 8, DIN, 0)
                conv_w(I["w_pa"], ws_pa, 4, D, None)
                conv_w(I["w_pb"], ws_pb, 4, D, None)
                conv_w(I["w_out"], ws_out, 8, D, None)
                conv_w(I["w_up"], ws_up, 8, 2 * DFF, 8)
                conv_w(I["w_down"], ws_down, 22, D, None)
                conv_w(I["w_ple"], ws_ple, 2, D, None)
                conv_w(I["w_pg"], ws_pg, 8, D, 16)
                conv_emit()
            S.barrier()

        def all_pass(es, job):
            NT = job["NT"]
            ntt = NT // 128
            nsteps = min(job["L"] // NT, int(os.environ.get('KSTEPS', '999')))
            xsrc = job["x"]
            W1 = sbt(es, "W1", [128, 8, 3144], BF16)
            k.dma(W1[:, :, 0:1024], wscr_in[:, :, C_KA:C_KA + 1024], [T_wscr_in], [W1])
            k.dma(W1[:, :, 1024:1088], wscr_in[:, :, C_KI:C_KI + 64], [T_wscr_in], [W1])
            k.dma(W1[:, :, 1088:3144], wscr_in[:, :, C_QB:C_QB + 2056], [T_wscr_in], [W1])
            xa_t = [sbt(es, f"xa_t{i}", [128, 8, NT], F32) for i in range(2)]
            sq = sbt(es, "sq", [128, 8, NT], BF16)
            hT = sbt(es, "hT", [128, 8, NT], BF16)
            rs = sbt(es, "rs", [128, NT], F32)
            cin = sbt(es, "cin", [128, 12, NT + 3], F32)
            cv = sbt(es, "cv", [128, 12, NT], F32)
            cvb = sbt(es, "cvb", [128, 8, NT], BF16)
            sq2 = sbt(es, "sq2", [128, NT], BF16)
            rs2 = sbt(es, "rs2", [128, NT], F32)
            gbs = sbt(es, "gbs", [128, ntt, 512], F32)
            bbab = sbt(es, "bbab", [128, ntt, 8], F32)
            ktok = [sbt(es, f"ktok{i}", [128, 512], F32) for i in range(2)]
            vtok = [sbt(es, f"vtok{i}", [128, 512], F32) for i in range(2)]
            vbf = [sbt(es, f"vbf{i}", [128, 8, 66], BF16) for i in range(2)]
            for i in range(2):
                k.memset("pool", vbf[i][:, :, 64:66], 1.0, [vbf[i]])
            kitok = [sbt(es, f"kitok{i}", [128, 64], F32) for i in range(2)]
            kf_t = sbt(es, "kf_t", [128, 4, NT], BF16)
            kif_t = sbt(es, "kif_t", [64, NT], BF16)
            obT_t = sbt(es, "obT_t", [128, 4, NT], BF16)
            Sst = [sbt(es, f"Sst{h}", [128, 128], F32) for h in range(4)]
            bet = sbt(es, "bet", [128, ntt, 4], F32)
            nbet = sbt(es, "nbet", [128, ntt, 4], F32)
            gg = sbt(es, "gg", [128, ntt, 4], F32)
            ngg = sbt(es, "ngg", [128, ntt, 4], F32)
            egc = sbt(es, "egc", [128, ntt, 4], F32)
            ekd = sbt(es, "ekd", [128, ntt, 4], F32)
            egl = sbt(es, "egl", [128, ntt, 8], F32)
            bege = sbt(es, "bege", [128, ntt, 4], F32)
            HT = []
            for h in range(4):
                d = {}
                for nm in ("NGb", "E1m", "E2m", "P", "attnT", "kbg", "kd", "vb", "u", "wT", "vnew", "o1", "o", "tmp"):
                    d[nm] = sbt(es, f"{nm}{h}", [128, 128], F32)
                for nm in ("N", "Nt", "M0", "M1", "Mt0", "Mt1", "Pb"):
                    d[nm] = sbt(es, f"{nm}{h}", [128, 128], BF16)
                d["obn"] = sbt(es, f"obn{h}", [128, 128], BF16)
                d["ms"] = sbt(es, f"ms{h}", [128, 1], F32)
                HT.append(d)

            if job["S0"] is None:
                for h in range(4):
                    k.memset("pool", Sst[h][:], 0.0, [Sst[h]])
                k.memset("pool", cin[:, :, 0:3], 0.0, [cin])
            else:
                for h in range(4):
                    k.dma(Sst[h][:], job["S0"][h], [], [Sst[h]])
                k.dma(cin[:, :, 0:3], job["conv0"], [], [cin], slow=True)

            xv = xsrc.rearrange("(kc p) t -> p kc t", p=128)
            k.dma(xa_t[0][:], xv[:, :, 0:NT], [], [xa_t[0]])
            for st in range(nsteps):
                t0 = st * NT
                xt = xa_t[st % 2]
                if st + 1 < nsteps:
                    xn = xa_t[(st + 1) % 2]
                    k.dma(xn[:], xv[:, :, t0 + NT:t0 + 2 * NT], [], [xn])
                k.act(sq[:], xt[:], AF.Square, [xt], [sq])
                A = PQ[0]
                for kc in range(8):
                    k.mm(banks[0][:, 0:NT], ones_bf[:], sq[:, kc, :], [ones_bf, sq], A, start=(kc == 0), stop=(kc == 7))
                k.act(rs[:], banks[0][:, 0:NT], AF.Sqrt, A + [epsT], [rs], scale=1.0 / D, bias=epsT[:, 0:1])
                k.recip(rs[:], rs[:], [rs], [rs])
                for kc in range(8):
                    k.tt("dve" if kc % 2 == 0 else "pool", hT[:, kc, :], xt[:, kc, :], rs[:], ALU.mult, [xt, rs], [hT])
                if KDBG < 1:
                    continue
                for tt in range(ntt):
                    ts_ = slice(tt * 128, (tt + 1) * 128)
                    r0 = t0 + tt * 128
                    kt, vt, vb_, kit = ktok[tt % 2], vtok[tt % 2], vbf[tt % 2], kitok[tt % 2]
                    for (bk, c0, cw) in ((1, 0, 512), (2, 512, 512)):
                        for kc in range(8):
                            k.mm(banks[bk][:, 0:cw], hT[:, kc, ts_], W1[:, kc, c0:c0 + cw], [hT, W1], PQ[bk], start=(kc == 0), stop=(kc == 7))
                    k.cp("act", kt[:], banks[1][:, :], PQ[1], [kt])
                    k.cp("dve", vt[:], banks[2][:, :], PQ[2], [vt])
                    k.cp("pool", vb_[:, :, 0:64], vt[:].rearrange("p (h d) -> p h d", d=64), [vt], [vb_])
                    nv = job["nvalid"]
                    if nv >= 128:
                        k.dma(job["k_out"][r0:r0 + 128, :], kt[:], [kt], [outT], key=kt)
                        k.dma(job["v_out"][r0:r0 + 128, :], vt[:], [vt], [outT], key=vt)
                    else:
                        k.dma(job["k_out"][0:nv, :], kt[0:nv, :], [kt], [outT], key=kt)
                        k.dma(job["v_out"][0:nv, :], vt[0:nv, :], [vt], [outT], key=vt)
                    k.dma(job["v_scr"][job["koff"] + r0:job["koff"] + r0 + 128, :], vb_[:].rearrange("p h d -> p (h d)"), [vb_], [T_v], key=vb_)
                    for kc in range(8):
                        k.mm(banks[3][:, 0:64], hT[:, kc, ts_], W1[:, kc, 1024:1088], [hT, W1], PQ[3], start=(kc == 0), stop=(kc == 7))
                    k.cp("act", kit[:], banks[3][:, 0:64], PQ[3], [kit])
                    if nv >= 128:
                        k.dma(job["ki_out"][r0:r0 + 128, :], kit[:], [kit], [outT], key=kit)
                    else:
                        k.dma(job["ki_out"][0:nv, :], kit[0:nv, :], [kit], [outT], key=kit)
                    for kc in range(8):
                        k.mm(banks[4][:, :], hT[:, kc, ts_], W1[:, kc, 2624:3136], [hT, W1], PQ[4], start=(kc == 0), stop=(kc == 7))
                    k.act(gbs[:, tt, :], banks[4][:, :], AF.Silu, PQ[4], [gbs])
                    for kc in range(8):
                        k.mm(banks[3][:, 128:136], hT[:, kc, ts_], W1[:, kc, 3136:3144], [hT, W1], PQ[3], start=(kc == 0), stop=(kc == 7))
                    k.cp("dve", bbab[:, tt, :], banks[3][:, 128:136], PQ[3], [bbab])
                for p in range(4):
                    bk = 5 + (p % 2)
                    for kc in range(8):
                        k.mm(banks[bk][:, 0:NT], W1[:, kc, p * 128:(p + 1) * 128], hT[:, kc, :], [W1, hT], PQ[bk], start=(kc == 0), stop=(kc == 7))
                    k.cp("act" if p % 2 == 0 else "dve", kf_t[:, p, :], banks[bk][:, 0:NT], PQ[bk], [kf_t])
                k.dma(job["kT_scr"][:, :, job["koff"] + t0:job["koff"] + t0 + NT], kf_t[:], [kf_t], [T_kT], key=kf_t)
                for kc in range(8):
                    k.mm(banks[7][0:64, 0:NT], W1[:, kc, 1024:1088], hT[:, kc, :], [W1, hT], PQ[7], start=(kc == 0), stop=(kc == 7))
                k.cp("act", kif_t[:], banks[7][0:64, 0:NT], PQ[7], [kif_t])
                k.dma(job["kiT_scr"][:, job["koff"] + t0:job["koff"] + t0 + NT], kif_t[:], [kif_t], [T_ki], key=kif_t)
                if KDBG < 2:
                    continue
                for c in range(12):
                    bk = 5 + (c % 3)
                    for kc in range(8):
                        k.mm(banks[bk][:, 0:NT], W1[:, kc, 1088 + c * 128:1088 + (c + 1) * 128], hT[:, kc, :], [W1, hT], PQ[bk], start=(kc == 0), stop=(kc == 7))
                    k.cp("act" if c % 2 == 0 else "dve", cin[:, c, 3:3 + NT], banks[bk][:, 0:NT], PQ[bk], [cin])
                for c in range(12):
                    k.ts("dve", cv[:, c, :], cin[:, c, 0:NT], convb[:, c, 0:1], ALU.mult, [cin, convb], [cv])
                    for j in range(1, 4):
                        k.stt(cv[:, c, :], cin[:, c, j:j + NT], convb[:, c, j:j + 1], cv[:, c, :], ALU.mult, ALU.add, [cin, convb, cv], [cv])
                if st == nsteps - 1:
                    k.dma(job["conv_out"], cin[:, :, job["nvalid_last"]:job["nvalid_last"] + 3], [cin], [outT], key=cin, slow=True)
                k.cp("pool", cin[:, :, 0:3], cin[:, :, NT:NT + 3], [cin], [cin])
                k.act(cv[:], cv[:], AF.Silu, [cv], [cv])
                for c in range(8):
                    k.act(sq2[:], cv[:, c, :], AF.Square, [cv], [sq2])
                    k.mm(banks[0][:, 0:NT], ones_bf[:], sq2[:], [ones_bf, sq2], PQ[0])
                    k.act(rs2[:], banks[0][:, 0:NT], AF.Sqrt, PQ[0] + [epsT], [rs2], bias=epsT[:, 0:1])
                    k.recip(rs2[:], rs2[:], [rs2], [rs2])
                    if c < 4:
                        k.stt(cv[:, c, :], cv[:, c, :], 128.0 ** -0.5, rs2[:], ALU.mult, ALU.mult, [cv, rs2], [cv])
                    else:
                        k.tt("dve", cv[:, c, :], cv[:, c, :], rs2[:], ALU.mult, [cv, rs2], [cv])
                    k.cp("pool", cvb[:, c, :], cv[:, c, :], [cv], [cvb])
                k.act(bet[:], bbab[:, :, 0:4], AF.Sigmoid, [bbab], [bet])
                if job["gval"] is not None:
                    for tt in range(ntt):
                        k.ts("dve", bet[:, tt, :], bet[:, tt, :], gval[:, job["gval"]:job["gval"] + 1], ALU.mult, [bet, gval], [bet])
                k.ts("dve", nbet[:], bet[:], -1.0, ALU.mult, [bet], [nbet])
                for tt in range(ntt):
                    k.tt("dve", gg[:, tt, :], bbab[:, tt, 4:8], dtb[:], ALU.add, [bbab, dtb], [gg])
                k.act(gg[:], gg[:], AF.Exp, [gg], [gg])
                k.act(gg[:], gg[:], AF.Ln, [gg], [gg], bias=1.0)
                for tt in range(ntt):
                    k.tt("dve", gg[:, tt, :], gg[:, tt, :], cA[:], ALU.mult, [gg, cA], [gg])
                    if job["gval"] is not None:
                        k.ts("dve", gg[:, tt, :], gg[:, tt, :], gval[:, job["gval"]:job["gval"] + 1], ALU.mult, [gg, gval], [gg])
                k.ts("dve", ngg[:], gg[:], -1.0, ALU.mult, [gg], [ngg])
                if KDBG < 3:
                    continue
                for tt in range(ntt):
                    ts_ = slice(tt * 128, (tt + 1) * 128)
                    G = PQ[0]
                    k.mm(banks[0][:, 0:4], Ublk, gg[:, tt, :], [cst, gg], G)
                    k.mm(banks[0][:, 4:8], Lsblk, gg[:, tt, :], [cst, gg], G)
                    k.mm(banks[0][:, 8:12], half0, gg[:, tt, :], [cst, gg], G)
                    k.mm(banks[0][:, 12:16], half1, gg[:, tt, :], [cst, gg], G)
                    k.act(egc[:, tt, :], banks[0][:, 0:4], AF.Exp, G, [egc])
                    k.act(ekd[:, tt, :], banks[0][:, 4:8], AF.Exp, G, [ekd])
                    k.act(egl[:, tt, :], banks[0][:, 8:16], AF.Exp, G, [egl])
                    k.tt("dve", bege[:, tt, :], bet[:, tt, :], egc[:, tt, :], ALU.mult, [bet, egc], [bege])
                    BK = ((1, 2), (3, 4), (5, 6), (7, 0))
                    H4 = range(4)
                    for h in H4:
                        T_ = HT[h]
                        k.cp("pool", T_["NGb"][:], ngg[:, tt, h:h + 1].to_broadcast([128, 128]), [ngg], [T_["NGb"]])
                    for h in H4:
                        T_ = HT[h]
                        b0, b1 = BK[h]
                        k.mm(pq(b0, 0), Ublk, gg[:, tt, h:h + 1].to_broadcast([128, 128]), [cst, gg], [PB[b0]], start=True, stop=False)
                        k.mm(pq(b0, 0), T_["NGb"][:], Ublk, [cst, T_["NGb"]], [PB[b0]], start=False, stop=True)
                        k.mm(pq(b1, 1), cvb[:, 4 + h, ts_], cvb[:, 4 + h, ts_], [cvb], [PB[b1]])
                    for h in H4:
                        T_ = HT[h]
                        b0, b1 = BK[h]
                        k.stt(T_["E1m"][:], pq(b0, 0), 0.0, NEGML, ALU.min, ALU.add, [PB[b0], cst], [T_["E1m"]])
                        k.stt(T_["E2m"][:], pq(b0, 0), 0.0, POSMU, ALU.max, ALU.add, [PB[b0], cst], [T_["E2m"]])
                    for h in H4:
                        T_ = HT[h]
                        k.act(T_["E1m"][:], T_["E1m"][:], AF.Exp, [T_["E1m"]], [T_["E1m"]])
                        k.act(T_["E2m"][:], T_["E2m"][:], AF.Exp, [T_["E2m"]], [T_["E2m"]], scale=-1.0)
                    for h in H4:
                        T_ = HT[h]
                        b0, b1 = BK[h]
                        k.mm(pq(b0, 2), cvb[:, 4 + h, ts_], cvb[:, h, ts_], [cvb], [PB[b0]])
                    for h in H4:
                        T_ = HT[h]
                        b0, b1 = BK[h]
                        k.stt(T_["N"][:], pq(b1, 1), nbet[:, tt, h:h + 1], T_["E1m"][:], ALU.mult, ALU.mult, [PB[b1], nbet, T_["E1m"]], [T_["N"]])
                    for h in H4:
                        T_ = HT[h]
                        b0, b1 = BK[h]
                        k.tt("dve", T_["attnT"][:], pq(b0, 2), T_["E2m"][:], ALU.mult, [PB[b0], T_["E2m"]], [T_["attnT"]])
                    for h in H4:
                        T_ = HT[h]
                        b0, b1 = BK[h]
                        k.tr(pq(b1, 3), cv[:, 4 + h, ts_], ident, [cv, cst], [PB[b1]])
                        k.tr(pq(b0, 0), cv[:, 8 + h, ts_], ident, [cv, cst], [PB[b0]])
                    for h in H4:
                        T_ = HT[h]
                        b0, b1 = BK[h]
                        k.ts("dve", T_["kbg"][:], pq(b1, 3), bege[:, tt, h:h + 1], ALU.mult, [PB[b1], bege], [T_["kbg"]])
                        k.ts("dve", T_["kd"][:], pq(b1, 3), ekd[:, tt, h:h + 1], ALU.mult, [PB[b1], ekd], [T_["kd"]])
                    for h in H4:
                        T_ = HT[h]
                        b0, b1 = BK[h]
                        k.ts("dve", T_["vb"][:], pq(b0, 0), bet[:, tt, h:h + 1], ALU.mult, [PB[b0], bet], [T_["vb"]])
                    for h in H4:
                        T_ = HT[h]
                        b0, b1 = BK[h]
                        k.mm(pq(b1, 1), T_["N"][:], ident_bf[:], [T_["N"], ident_bf], [PB[b1]])
                    for h in H4:
                        T_ = HT[h]
                        b0, b1 = BK[h]
                        k.cp("act", T_["Nt"][:], pq(b1, 1), [PB[b1]], [T_["Nt"]])
                    for h in H4:
                        T_ = HT[h]
                        b0, b1 = BK[h]
                        k.tt("dve", T_["P"][:], pq(b1, 1), ident, ALU.add, [PB[b1], cst], [T_["P"]])
                    for h in H4:
                        T_ = HT[h]
                        k.cp("pool", T_["Pb"][:], T_["P"][:], [T_["P"]], [T_["Pb"]])
                    if KDBG < 4:
                        continue
                    for lv in range(1, 6):
                        for h in range(4):
                            T_ = HT[h]
                            b0, b1 = ((1, 2), (3, 4), (5, 6), (7, 0))[h]
                            Mp = T_["N"] if lv == 1 else T_[f"M{(lv - 1) % 2}"]
                            Mtp = T_["Nt"] if lv == 1 else T_[f"Mt{(lv - 1) % 2}"]
                            Mn = T_[f"M{lv % 2}"]
                            Mtn = T_[f"Mt{lv % 2}"]
                            k.mm(pq(b1, 2), Mtp[:], Mp[:], [Mtp, Mp], [PQ[b1][2]])
                            k.cp("act", Mn[:], pq(b1, 2), [PQ[b1][2]], [Mn])
                            if lv < 5:
                                k.mm(pq(b1, 3), Mp[:], Mtp[:], [Mtp, Mp], [PQ[b1][3]])
                                k.cp("pool" if False else "dve", Mtn[:], pq(b1, 3), [PQ[b1][3]], [Mtn])
                            k.mm(pq(b0, 0), Mn[:], T_["Pb"][:], [Mn, T_["Pb"]], [PQ[b0][0]])
                            k.tt("dve", T_["P"][:], T_["P"][:], pq(b0, 0), ALU.add, [T_["P"], PQ[b0][0]], [T_["P"]])
                            if lv < 5:
                                k.cp("pool", T_["Pb"][:], T_["P"][:], [T_["P"]], [T_["Pb"]])
                    if KDBG < 5:
                        continue
                    BK = ((1, 2), (3, 4), (5, 6), (7, 0))
                    for h in range(4):
                        T_ = HT[h]
                        b0, b1 = BK[h]
                        k.mm(pq(b0, 1), T_["P"][:], T_["vb"][:], [T_["P"], T_["vb"]], [PQ[b0][1]])
                        k.cp("act", T_["u"][:], pq(b0, 1), [PQ[b0][1]], [T_["u"]])
                    for h in range(4):
                        T_ = HT[h]
                        b0, b1 = BK[h]
                        k.mm(pq(b1, 2), T_["kbg"][:], T_["P"][:], [T_["P"], T_["kbg"]], [PQ[b1][2]])
                        k.cp("dve", T_["wT"][:], pq(b1, 2), [PQ[b1][2]], [T_["wT"]])
                    for c in range(2):
                        r = slice(64 * c, 64 * c + 64)
                        for h in range(4):
                            T_ = HT[h]
                            b0, b1 = BK[h]
                            k.mm(banks[b0][r, 384:512], T_["wT"][:, r], Sst[h][:], [T_["wT"], Sst[h]], [PQ[b0][3]])
                            k.mm(banks[b0][r, 0:128], cv[:, h, tt * 128 + 64 * c:tt * 128 + 64 * c + 64], Sst[h][:], [cv, Sst[h]], [PQ[b0][0]])
                        for h in range(4):
                            T_ = HT[h]
                            b0, b1 = BK[h]
                            k.tt("dve", T_["vnew"][r, :], T_["u"][r, :], banks[b0][r, 384:512], ALU.subtract, [T_["u"], PQ[b0][3]], [T_["vnew"]])
                        for h in range(4):
                            T_ = HT[h]
                            b0, b1 = BK[h]
                            k.mm(banks[b1][r, 128:256], T_["attnT"][r, r], T_["vnew"][r, :], [T_["attnT"], T_["vnew"]], [PQ[b1][1]])
                            k.mm(pq(b1, 2), T_["kd"][r, :], T_["vnew"][r, :], [T_["kd"], T_["vnew"]], [PQ[b1][2]])
                        for h in range(4):
                            T_ = HT[h]
                            b0, b1 = BK[h]
                            k.stt(Sst[h][:], Sst[h][:], egl[:, tt, 4 * c + h:4 * c + h + 1], pq(b1, 2), ALU.mult, ALU.add, [Sst[h], egl, PQ[b1][2]], [Sst[h]])
                        for h in range(4):
                            T_ = HT[h]
                            b0, b1 = BK[h]
                            k.cp("act", T_["o1"][r, :], banks[b1][r, 128:256], [PQ[b1][1]], [T_["o1"]])
                        for h in range(4):
                            T_ = HT[h]
                            b0, b1 = BK[h]
                            k.stt(T_["o"][r, :], banks[b0][r, 0:128], egc[r, tt, h:h + 1], T_["o1"][r, :], ALU.mult, ALU.add, [PQ[b0][0], egc, T_["o1"]], [T_["o"]])
                    for h in range(4):
                        T_ = HT[h]
                        k.act(T_["tmp"][:], T_["o"][:], AF.Square, [T_["o"]], [T_["tmp"], T_["ms"]], accum=T_["ms"][:, 0:1])
                    for h in range(4):
                        T_ = HT[h]
                        k.act(T_["ms"][:], T_["ms"][:], AF.Sqrt, [T_["ms"], epsT], [T_["ms"]], scale=1.0 / 128, bias=epsT[:, 0:1])
                    for h in range(4):
                        T_ = HT[h]
                        k.recip(T_["ms"][:], T_["ms"][:], [T_["ms"]], [T_["ms"]])
                    for h in range(4):
                        T_ = HT[h]
                        k.stt(T_["tmp"][:], T_["o"][:], T_["ms"][:, 0:1], ngdn[:], ALU.mult, ALU.mult, [T_["o"], T_["ms"], ngdn], [T_["tmp"]])
                    for h in range(4):
                        T_ = HT[h]
                        k.tt("pool", T_["obn"][:], T_["tmp"][:], gbs[:, tt, h * 128:(h + 1) * 128], ALU.mult, [T_["tmp"], gbs], [T_["obn"]])
                    for h in range(4):
                        T_ = HT[h]
                        b0, b1 = BK[h]
                        k.tr(banks[b1][:, 384:512], T_["obn"][:], ident_bf[:], [T_["obn"], ident_bf], [PQ[b1][3]])
                    for h in range(4):
                        b0, b1 = BK[h]
                        k.cp("act", obT_t[:, h, ts_], banks[b1][:, 384:512], [PQ[b1][3]], [obT_t])
                if KDBG >= 5:
                    k.dma(job["obT_scr"][:, :, t0:t0 + NT], obT_t[:], [obT_t], [T_ob], key=obT_t)
            for h in range(4):
                k.dma(job["S_out"][h], Sst[h][:], [Sst[h]], [outT], key=Sst[h])
            S.barrier()

        with ExitStack() as es:
          if KDBG >= 0 and 'p' in KJOB:
            all_pass(es, dict(NT=NTP, L=LKP, x=I["xa"], S0=None, conv0=None, nvalid=128, nvalid_last=NTP,
                              k_out=O["k_all"], v_out=O["v_all"], ki_out=O["ki_all"], kT_scr=kT_p, v_scr=v_p, kiT_scr=kiT_p,
                              koff=0, obT_scr=obT_p, S_out=O["gdn_p"], conv_out=O["gconv_p"], gval=None))
        for sb_ in range(2 if (KDBG >= 0 and 's' in KJOB) else 0):
            with ExitStack() as es:
                all_pass(es, dict(NT=128, L=128, x=I["xs"][sb_], S0=I["state_gdn"][sb_], conv0=I["gconvT"][sb_], nvalid=16, nvalid_last=16,
                                  k_out=O["k_s"][sb_], v_out=O["v_s"][sb_], ki_out=O["ki_s"][sb_], kT_scr=kT_s[sb_], v_scr=v_s[sb_],
                                  kiT_scr=kiT_s[sb_], koff=PAST, obT_scr=obT_s[sb_], S_out=O["gdn_s"][sb_], conv_out=O["gconv_s"][sb_], gval=0))

        convf = sbt(es0, "convf", [128, 22, 3], F32)
        k.dma(convf[:], I["convf"], [], [convf])
        iota = sbt(es0, "iota", [128, 512], F32)
        k.dma(iota[:], I["iota"], [], [iota])
        lims = sbt(es0, "lims", [128, 24], F32)
        k.dma(lims[:], I["lims"], [], [lims])
        BT = sbt(es0, "BT", [128, 2, 8, 128], BF16)
        with ExitStack() as es:
            rb = sbt(es, "rb", [32, 8], F32)
            oh = sbt(es, "oh", [32, 384], F32)
            rb15 = sbt(es, "rb15", [8, 1], F32)
            tabs = sbt(es, "tabs", [8, 384], F32)
            BTf = sbt(es, "BTf", [128, 2, 8, 128], F32)
            k.dma(rb[:], I["relb"], [], [rb])
            k.dma(oh[:], I["ohrev"], [], [oh])
            k.dma(rb15[:], I["relb15"], [], [rb15])
            k.mm(banks[0][0:8, 0:384], rb[:], oh[:], [rb, oh], PQ[0])
            k.ts("dve", tabs[:], banks[0][0:8, 0:384], rb15[:, 0:1], ALU.subtract, PQ[0] + [rb15], [tabs])
            k.dma(tab_scr, tabs[:], [tabs], [T_tab], key=tabs)
            for kp in range(128):
                for dd in range(2):
                    base = 127 + 128 * dd - kp
                    k.dma(BTf[kp:kp + 1, dd, :, :], tab_scr[:, base:base + 128].unsqueeze(0), [T_tab], [BTf])
            k.cp("dve", BT[:], BTf[:], [BTf], [BT])
            ckf = sbt(es, "ckf", [128, 4, 512], F32)
            ckb = sbt(es, "ckb", [128, 4, 512], BF16)
            cvf = sbt(es, "cvf", [128, 512], F32)
            cvb = sbt(es, "cvb", [128, 512], BF16)
            cvb2 = sbt(es, "cvb2", [128, 8, 66], BF16)
            k.memset("pool", cvb2[:, :, 64:66], 1.0, [cvb2])
            for sb_ in range(2):
                ckv = I["cache_kT"][sb_].rearrange("(p r) t -> r p t", r=128)
                for pc in range(4):
                    k.dma(ckf[:], ckv[:, :, pc * 512:(pc + 1) * 512], [], [ckf])
                    k.cp("dve", ckb[:], ckf[:], [ckf], [ckb])
                    k.dma(kT_s[sb_][:, :, pc * 512:(pc + 1) * 512], ckb[:], [ckb], [T_kT], key=ckb)
                    k.dma(cvf[0:64, :], I["cache_kiT"][sb_][:, pc * 512:(pc + 1) * 512], [], [cvf])
                    k.cp("pool", cvb[0:64, :], cvf[0:64, :], [cvf], [cvb])
                    k.dma(kiT_s[sb_][:, pc * 512:(pc + 1) * 512], cvb[0:64, :], [cvb], [T_ki], key=cvb)
                for rb_ in range(16):
                    k.dma(cvf[:], I["cache_v"][sb_][rb_ * 128:(rb_ + 1) * 128, :], [], [cvf])
                    k.cp("pool", cvb2[:, :, 0:64], cvf[:].rearrange("p (h d) -> p h d", d=64), [cvf], [cvb2])
                    k.dma(v_s[sb_][rb_ * 128:(rb_ + 1) * 128, :], cvb2[:].rearrange("p h d -> p (h d)"), [cvb2], [T_v], key=cvb2)
            S.barrier()

        def pieces(n):
            out, c = [], 0
            while c < n:
                w = min(512, n - c)
                out.append((c, w))
                c += w
            return out

        def rms(src, dst, c0, n, sqb, rsb, bank=0):
            k.act(sqb[:, :, 0:n], src[:, :, c0:c0 + n], AF.Square, [src], [sqb])
            for kc in range(8):
                k.mm(banks[bank][:, 0:n], ones_bf[:], sqb[:, kc, 0:n], [ones_bf, sqb], PQ[bank], start=(kc == 0), stop=(kc == 7))
            k.act(rsb[:, 0:n], banks[bank][:, 0:n], AF.Sqrt, PQ[bank] + [epsT], [rsb], scale=1.0 / D, bias=epsT[:, 0:1])
            k.recip(rsb[:, 0:n], rsb[:, 0:n], [rsb], [rsb])
            if dst is not None:
                for kc in range(8):
                    k.tt("dve" if kc % 2 == 0 else "pool", dst[:, kc, c0:c0 + n], src[:, kc, c0:c0 + n], rsb[:, 0:n], ALU.mult, [src, rsb], [dst])

        NIT = 26

        def own_block(jb):
            NQ = jb["NQ"]
            NTOK = 128 * NQ
            with ExitStack() as esA:
                xo = sbt(esA, "xo", [128, 8, NTOK], F32)
                hT = sbt(esA, "hTo", [128, 8, NTOK], BF16)
                oaT = sbt(esA, "oaT", [128, 4, NTOK], BF16)
                sqb = sbt(esA, "sqb", [128, 8, 512], BF16)
                rsb = sbt(esA, "rsb", [128, 512], F32)
                k.dma(xo[:], jb["xsrc"], [], [xo])
                for (c0, n) in pieces(NTOK):
                    rms(xo, hT, c0, n, sqb, rsb)
                if KP2 < 1:
                    return
                with ExitStack() as es:
                    W2 = sbt(es, "W2", [128, 8, 1032], BF16)
                    k.dma(W2[:, :, 0:512], wscr_in[:, :, C_QA:C_QA + 512], [T_wscr_in], [W2])
                    k.dma(W2[:, :, 512:1024], wscr_in[:, :, C_QI:C_QI + 512], [T_wscr_in], [W2])
                    k.dma(W2[:, :, 1024:1032], wscr_in[:, :, C_WI:C_WI + 8], [T_wscr_in], [W2])
                    qaT = sbt(es, "qaT", [128, 4, NTOK], BF16)
                    qiT = sbt(es, "qiT", [128, 4, NTOK], BF16)
                    wiT = sbt(es, "wiT", [128, NQ, 8], F32)
                    n_ = 0
                    for (dstq, cb) in ((qaT, 0), (qiT, 512)):
                        for p in range(4):
                            for (c0, n) in pieces(NTOK):
                                bk = n_ % 2
                                n_ += 1
                                for kc in range(8):
                                    k.mm(banks[bk][:, 0:n], W2[:, kc, cb + p * 128:cb + (p + 1) * 128], hT[:, kc, c0:c0 + n], [W2, hT], PQ[bk], start=(kc == 0), stop=(kc == 7))
                                k.act(dstq[:, p, c0:c0 + n], banks[bk][:, 0:n], AF.Copy, PQ[bk], [dstq], scale=0.125)
                    for qb in range(NQ):
                        for kc in range(8):
                            k.mm(banks[2][:, 0:8], hT[:, kc, qb * 128:(qb + 1) * 128], W2[:, kc, 1024:1032], [W2, hT], PQ[2], start=(kc == 0), stop=(kc == 7))
                        k.ts("dve", wiT[:, qb, :], banks[2][:, 0:8], 8.0 ** -0.5, ALU.mult, PQ[2], [wiT])
                    if KP2 < 2:
                        return
                    LMAX = max(jb["L"])
                    kiT2 = sbt(es, "kiT2", [128, LMAX], BF16)
                    sc = sbt(es, "sc", [128, LMAX], F32)
                    Mb = sbt(es, "Mb", [128, LMAX], BF16)
                    MT = sbt(es, "MT", [128, LMAX // 128, 128], BF16)
                    tmpf = [sbt(es, f"tmpf{i}", [128, 512], F32) for i in range(2)]
                    pen = sbt(es, "pen", [128, 512], F32)
                    Kc = [sbt(es, f"Kc{i}", [128, 4, 512], BF16) for i in range(2)]
                    Vc = [sbt(es, f"Vc{i}", [128, 4, 8, 66], BF16) for i in range(2)]
                    PT = [sbt(es, f"PT{i}", [128, 4, 128], BF16) for i in range(4)]
                    oa = sbt(es, "oa", [128, 8, 64], BF16)
                    den = sbt(es, "den", [128, 8], F32)
                    sm = {nm: sbt(es, nm, [128, 1], F32) for nm in ("mx", "lo", "hi", "mid", "cnt", "ge", "d1", "d2", "off")}
                    for i in range(2):
                        k.memset("pool", Vc[i][:, :, :, 64:65], 1.0, [Vc[i]])
                    lgs = sbt(es, "lgs", [128, 512], F32)

                    def gen_S(qb):
                        L = jb["L"][qb]
                        nkb = L // 128
                        q0 = qb * 128
                        lcol = jb["limcol"] + qb
                        k.dma(kiT2[0:64, 0:L], jb["kiT"][:, 0:L], [T_ki], [kiT2])
                        k.dma(kiT2[64:128, 0:L], jb["kiT"][:, 0:L], [T_ki], [kiT2])
                        tiles = pieces(L)
                        n_ = 0
                        for (c0, w) in tiles:
                            for h in range(8):
                                p, r0 = h // 2, 64 * (h % 2)
                                bk = n_ % 2
                                tf = tmpf[n_ % 2]
                                n_ += 1
                                k.mm(banks[bk][:, 0:w], qiT[r0:r0 + 64, p, q0:q0 + 128], kiT2[r0:r0 + 64, c0:c0 + w], [qiT, kiT2], PQ[bk])
                                k.act(tf[:, 0:w], banks[bk][:, 0:w], AF.Relu, PQ[bk], [tf])
                                if h == 0:
                                    k.ts("dve", sc[:, c0:c0 + w], tf[:, 0:w], wiT[:, qb, 0:1], ALU.mult, [tf, wiT], [sc])
                                else:
                                    k.stt(sc[:, c0:c0 + w], tf[:, 0:w], wiT[:, qb, h:h + 1], sc[:, c0:c0 + w], ALU.mult, ALU.add, [tf, wiT, sc], [sc])
                            yield
                        k.S.op("dve", lambda e, o=sm["mx"][:], i=sc[:, 0:L]: e.tensor_reduce(out=o, in_=i, axis=AX.X, op=ALU.max, apply_absolute_value=True), reads=[sc], writes=[sm["mx"]])
                        k.ts("dve", sm["hi"][:], sm["mx"][:], 1.0, ALU.add, [sm["mx"]], [sm["hi"]])
                        k.ts("dve", sm["lo"][:], sm["mx"][:], -1.0, ALU.mult, [sm["mx"]], [sm["lo"]], s2=-1.0, op1=ALU.add)
                        (c0, w) = tiles[-1]
                        k.ts("dve", sm["off"][:], lims[:, lcol:lcol + 1], float(-c0), ALU.add, [lims], [sm["off"]])
                        k.ts("dve", pen[:, 0:w], iota[:, 0:w], sm["off"][:, 0:1], ALU.is_ge, [iota, sm["off"]], [pen], s2=NEG, op1=ALU.mult)
                        k.tt("dve", sc[:, c0:c0 + w], sc[:, c0:c0 + w], pen[:, 0:w], ALU.add, [sc, pen], [sc])
                        if jb["lolim"] is not None:
                            for ti in range(min(3, len(tiles))):
                                (c0, w) = tiles[ti]
                                k.ts("dve", sm["off"][:], lims[:, jb["lolim"]:jb["lolim"] + 1], float(-c0), ALU.add, [lims], [sm["off"]])
                                k.ts("dve", pen[:, 0:w], iota[:, 0:w], sm["off"][:, 0:1], ALU.is_lt, [iota, sm["off"]], [pen], s2=NEG, op1=ALU.mult)
                                k.tt("dve", sc[:, c0:c0 + w], sc[:, c0:c0 + w], pen[:, 0:w], ALU.add, [sc, pen], [sc])

                    def do_B(qb):
                        L = jb["L"][qb]
                        nkb = L // 128
                        q0 = qb * 128
                        lcol = jb["limcol"] + qb
                        k.tt("dve", sm["d1"][:], sm["hi"][:], sm["lo"][:], ALU.subtract, [sm["hi"], sm["lo"]], [sm["d1"]])
                        for it in range(NIT):
                            k.ts("dve", sm["d1"][:], sm["d1"][:], 0.5, ALU.mult, [sm["d1"]], [sm["d1"]])
                            k.tt("dve", sm["mid"][:], sm["lo"][:], sm["d1"][:], ALU.add, [sm["lo"], sm["d1"]], [sm["mid"]])
                            k.ts("dve", Mb[:, 0:L], sc[:, 0:L], sm["mid"][:, 0:1], ALU.is_gt, [sc, sm["mid"]], [Mb, sm["cnt"]], s2=0.0, op1=ALU.add, accum=sm["cnt"][:, 0:1])
                            k.stt(sm["ge"][:], sm["cnt"][:], 255.5, sm["d1"][:], ALU.is_ge, ALU.mult, [sm["cnt"], sm["d1"]], [sm["ge"]])
                            k.tt("dve", sm["lo"][:], sm["lo"][:], sm["ge"][:], ALU.add, [sm["lo"], sm["ge"]], [sm["lo"]])
                        k.ts("dve", Mb[:, 0:L], sc[:, 0:L], sm["lo"][:, 0:1], ALU.is_gt, [sc, sm["lo"]], [Mb])

                    def do_T(qb):
                        L = jb["L"][qb]
                        nkb = L // 128
                        q0 = qb * 128
                        lcol = jb["limcol"] + qb
                        for kb0 in range(0, nkb, 4):
                            nb = min(4, nkb - kb0)
                            bk = 2 + (kb0 // 4) % 2
                            for i in range(nb):
                                k.mm(banks[bk][:, i * 128:(i + 1) * 128], Mb[:, (kb0 + i) * 128:(kb0 + i + 1) * 128], ident_bf[:], [Mb, ident_bf], PQ[bk])
                            k.cp("act", MT[:, kb0:kb0 + nb, :].rearrange("p a b -> p (a b)"), banks[bk][:, 0:nb * 128], PQ[bk], [MT])

                    def gen_A(qb):
                        L = jb["L"][qb]
                        nkb = L // 128
                        q0 = qb * 128
                        nch = (L + 511) // 512
                        DEPTH = 2

                        def loads(ch):
                            wch = min(512, L - 512 * ch)
                            Kt, Vt = Kc[ch % 2], Vc[ch % 2]
                            k.dma(Kt[:, :, 0:wch], jb["kT"][:, :, 512 * ch:512 * ch + wch], [T_kT], [Kt])
                            k.dma(Vt[:, 0:wch // 128, :, :], jb["v"][512 * ch:512 * ch + wch, :].rearrange("(i p) (h d) -> p i h d", p=128, d=66), [T_v], [Vt])

                        units = []
                        for ch in range(nch):
                            wch = min(512, L - 512 * ch)
                            for i in range(wch // 128):
                                for g in range(2):
                                    units.append((ch, i, g))
                        last_of_chunk = {}
                        for idx, (ch, i, g) in enumerate(units):
                            last_of_chunk[ch] = idx

                        def logits(idx):
                            ch, i, g = units[idx]
                            bk = 2 + idx % 4
                            Kt = Kc[ch % 2]
                            r0 = 64 * g
                            for e4 in range(4):
                                k.mm(banks[bk][:, e4 * 128:(e4 + 1) * 128], Kt[r0:r0 + 64, e4, i * 128:(i + 1) * 128], qaT[r0:r0 + 64, e4, q0:q0 + 128], [Kt, qaT], PQ[bk], start=True, stop=True)

                        def softmax_pv(idx):
                            ch, i, g = units[idx]
                            kbg = 4 * ch + i
                            bk = 2 + idx % 4
                            pt = PT[idx % 4]
                            Vt = Vc[ch % 2]
                            diag = kbg >= nkb - 2
                            dd = 0 if kbg == nkb - 1 else 1
                            if diag:
                                k.tt("dve", lgs[:, :].rearrange("p (h q) -> p h q", q=128), banks[bk][:, :].rearrange("p (h q) -> p h q", q=128), BT[:, dd, g:8:2, :], ALU.add, PQ[bk] + [BT], [lgs])
                                k.act(pt[:].rearrange("p h q -> p (h q)"), lgs[:, :], AF.Exp, [lgs], [pt])
                            else:
                                k.act(pt[:].rearrange("p h q -> p (h q)"), banks[bk][:, :], AF.Exp, PQ[bk], [pt])
                            k.tt("dve", pt[:], pt[:], MT[:, kbg:kbg + 1, :].to_broadcast([128, 4, 128]), ALU.mult, [pt, MT], [pt])
                            for e4 in range(4):
                                h = 2 * e4 + g
                                k.mm(banks[6 + g][:, e4 * 65:(e4 + 1) * 65], pt[:, e4, :], Vt[:, i, h, 0:65], [pt, Vt], PQ[6 + g], start=(kbg == 0 and e4 == 0), stop=(kbg == nkb - 1 and e4 == 3))
                            if last_of_chunk[ch] == idx and ch + 2 < nch:
                                loads(ch + 2)

                        loads(0)
                        if nch > 1:
                            loads(1)
                        nu = len(units)
                        for idx in range(nu + DEPTH):
                            if idx < nu:
                                logits(idx)
                            if idx - DEPTH >= 0:
                                softmax_pv(idx - DEPTH)
                            if idx % 8 == 7:
                                yield

                    def do_N(qb):
                        L = jb["L"][qb]
                        nkb = L // 128
                        q0 = qb * 128
                        lcol = jb["limcol"] + qb
                        for g in range(2):
                            ov = banks[6 + g][:, 0:260].rearrange("p (h d) -> p h d", d=65)
                            k.ts("dve", den[:, 4 * g:4 * g + 4], ov[:, :, 64], 1e-30, ALU.add, PQ[6 + g], [den])
                            k.recip(den[:, 4 * g:4 * g + 4], den[:, 4 * g:4 * g + 4], [den], [den])
                            k.tt("dve", oa[:, g:8:2, :], ov[:, :, 0:64], den[:, 4 * g:4 * g + 4].unsqueeze(2).to_broadcast([128, 4, 64]), ALU.mult, PQ[6 + g] + [den], [oa])
                        oaf = oa[:].rearrange("p h d -> p (h d)")
                        for c in range(4):
                            k.mm(banks[2][:, c * 128:(c + 1) * 128], oaf[:, c * 128:(c + 1) * 128], ident_bf[:], [oa, ident_bf], PQ[2])
                        k.cp("act", oaT[:, :, q0:q0 + 128], banks[2][:, :].rearrange("p (c t) -> p c t", t=128), PQ[2], [oaT])

                    def drain(g):
                        for _ in g:
                            pass

                    def interleave(ga, gb):
                        alive = [ga, gb]
                        while alive:
                            for g_ in list(alive):
                                try:
                                    next(g_)
                                except StopIteration:
                                    alive.remove(g_)

                    drain(gen_S(0))
                    for qb in range(NQ):
                        do_B(qb)
                        do_T(qb)
                        if qb + 1 < NQ:
                            interleave(gen_A(qb), gen_S(qb + 1))
                        else:
                            drain(gen_A(qb))
                        do_N(qb)
                    S.barrier()
                if KP2 < 8:
                    return
                with ExitStack() as es:
                    Wpa = sbt(es, "Wpa", [128, 4, D], BF16)
                    Wpb = sbt(es, "Wpb", [128, 4, D], BF16)
                    Wg = sbt(es, "Wg", [128, 8, 2048], BF16)
                    Wo = sbt(es, "Wo", [128, 8, D], BF16)
                    Wpl = sbt(es, "Wpl", [128, 2, D], BF16)
                    k.dma(Wpa[:], ws_pa, [T_wscr_in], [Wpa])
                    k.dma(Wpb[:], ws_pb, [T_wscr_in], [Wpb])
                    k.dma(Wg[:], wscr_in[:, :, C_GA:C_GA + 2048], [T_wscr_in], [Wg])
                    k.dma(Wpl[:], ws_ple, [T_wscr_in], [Wpl])
                    obT = sbt(es, "obT", [128, 4, NTOK], BF16)
                    k.dma(obT[:], jb["obT"], [T_ob], [obT])
                    pf = sbt(es, "pf", [128, 2, NTOK], F32)
                    pb = sbt(es, "pb", [128, 2, NTOK], BF16)
                    k.dma(pf[:], jb["psrc"], [], [pf])
                    k.cp("pool", pb[:], pf[:], [pf], [pb])
                    mixT = sbt(es, "mixT", [128, 8, 512], BF16)
                    h2T = sbt(es, "h2T", [128, 8, 512], BF16)
                    actT = sbt(es, "actT", [128, 22, 512], BF16)
                    ughalo = sbt(es, "ughalo", [128, 22, 2], F32)
                    fco = sbt(es, "fco", [128, 22, 2], F32)
                    ugc = [sbt(es, f"ugc{i}", [128, 514], F32) for i in range(2)]
                    cva = [sbt(es, f"cva{i}", [128, 512], F32) for i in range(2)]
                    sga = sbt(es, "sga", [128, 512], F32)
                    sgb = sbt(es, "sgb", [128, 512], F32)
                    t1 = sbt(es, "t1", [128, 512], F32)
                    wug = [sbt(es, f"wug{i}", [128, 8, 128], BF16) for i in range(2)]
                    wuv = [sbt(es, f"wuv{i}", [128, 8, 128], BF16) for i in range(2)]
                    wdn = [sbt(es, "wdn0", [128, 22, 128], BF16)] * 2
                    ytok = sbt(es, "ytok", [128, D], F32)
                    if jb["fhalo"] is not None:
                        k.dma(ughalo[:], jb["fhalo"], [], [ughalo])
                    for (c0, n, halo) in jb["segs"]:
                        sg = slice(c0, c0 + n)
                        k.dma(Wo[:], ws_out, [T_wscr_in], [Wo])
                        for c in range(8):
                            cs = slice(c * 128, (c + 1) * 128)
                            for kc in range(4):
                                k.mm(banks[0][:, 0:n], Wpa[:, kc, cs], oaT[:, kc, sg], [Wpa, oaT], PQ[0], start=(kc == 0), stop=(kc == 3))
                            for kc in range(8):
                                k.mm(banks[1][:, 0:n], Wg[:, kc, cs], hT[:, kc, sg], [Wg, hT], PQ[1], start=(kc == 0), stop=(kc == 7))
                            k.act(sga[:, 0:n], banks[1][:, 0:n], AF.Sigmoid, PQ[1], [sga])
                            k.tt("dve", t1[:, 0:n], banks[0][:, 0:n], sga[:, 0:n], ALU.mult, PQ[0] + [sga], [t1])
                            for kc in range(4):
                                k.mm(banks[2][:, 0:n], Wpb[:, kc, cs], obT[:, kc, sg], [Wpb, obT], PQ[2], start=(kc == 0), stop=(kc == 3))
                            for kc in range(8):
                                k.mm(banks[3][:, 0:n], Wg[:, kc, 1024 + c * 128:1024 + (c + 1) * 128], hT[:, kc, sg], [Wg, hT], PQ[3], start=(kc == 0), stop=(kc == 7))
                            k.act(sgb[:, 0:n], banks[3][:, 0:n], AF.Sigmoid, PQ[3], [sgb])
                            k.tt("dve", sgb[:, 0:n], banks[2][:, 0:n], sgb[:, 0:n], ALU.mult, PQ[2] + [sgb], [sgb])
                            k.tt("pool", mixT[:, c, 0:n], t1[:, 0:n], sgb[:, 0:n], ALU.add, [t1, sgb], [mixT])
                        for c in range(8):
                            cs = slice(c * 128, (c + 1) * 128)
                            bk = 4 + c % 2
                            for kc in range(8):
                                k.mm(banks[bk][:, 0:n], Wo[:, kc, cs], mixT[:, kc, 0:n], [Wo, mixT], PQ[bk], start=(kc == 0), stop=(kc == 7))
                            k.tt("dve", xo[:, c, sg], xo[:, c, sg], banks[bk][:, 0:n], ALU.add, [xo] + PQ[bk], [xo])
                        k.act(sqb[:, :, 0:n], xo[:, :, sg], AF.Square, [xo], [sqb])
                        for kc in range(8):
                            k.mm(banks[0][:, 0:n], ones_bf[:], sqb[:, kc, 0:n], [ones_bf, sqb], PQ[0], start=(kc == 0), stop=(kc == 7))
                        k.act(rsb[:, 0:n], banks[0][:, 0:n], AF.Sqrt, PQ[0] + [epsT], [rsb], scale=1.0 / D, bias=epsT[:, 0:1])
                        k.recip(rsb[:, 0:n], rsb[:, 0:n], [rsb], [rsb])
                        for kc in range(8):
                            k.tt("dve" if kc % 2 == 0 else "pool", h2T[:, kc, 0:n], xo[:, kc, sg], rsb[:, 0:n], ALU.mult, [xo, rsb], [h2T])
                        for cc in range(22):
                            wg_, wv_ = wug[cc % 2], wuv[cc % 2]
                            k.dma(wg_[:], ws_up[:, :, cc * 128:(cc + 1) * 128], [T_wscr_in], [wg_])
                            bk = 1 + cc % 2
                            for kc in range(8):
                                k.mm(banks[bk][:, 0:n], wg_[:, kc, :], h2T[:, kc, 0:n], [wg_, h2T], PQ[bk], start=(kc == 0), stop=(kc == 7))
                            if halo:
                                k.cp("act", ughalo[:, cc, 0:n], banks[bk][:, 0:n], PQ[bk], [ughalo])
                                continue
                            k.dma(wv_[:], ws_up[:, :, DFF + cc * 128:DFF + (cc + 1) * 128], [T_wscr_in], [wv_])
                            ug, ca = ugc[cc % 2], cva[cc % 2]
                            k.cp("act", ug[:, 2:2 + n], banks[bk][:, 0:n], PQ[bk], [ug])
                            k.cp("pool", ug[:, 0:2], ughalo[:, cc, :], [ughalo], [ug])
                            k.ts("dve", ca[:, 0:n], ug[:, 0:n], convf[:, cc, 0:1], ALU.mult, [ug, convf], [ca])
                            k.stt(ca[:, 0:n], ug[:, 1:1 + n], convf[:, cc, 1:2], ca[:, 0:n], ALU.mult, ALU.add, [ug, convf, ca], [ca])
                            k.stt(ca[:, 0:n], ug[:, 2:2 + n], convf[:, cc, 2:3], ca[:, 0:n], ALU.mult, ALU.add, [ug, convf, ca], [ca])
                            k.act(ca[:, 0:n], ca[:, 0:n], AF.Gelu_apprx_tanh, [ca], [ca])
                            nv = jb["nvalid"]
                            k.cp("pool", fco[:, cc, :], ug[:, nv:nv + 2], [ug], [fco])
                            bk2 = 3 + cc % 2
                            for kc in range(8):
                                k.mm(banks[bk2][:, 0:n], wv_[:, kc, :], h2T[:, kc, 0:n], [wv_, h2T], PQ[bk2], start=(kc == 0), stop=(kc == 7))
                            k.tt("dve", actT[:, cc, 0:n], ca[:, 0:n], banks[bk2][:, 0:n], ALU.mult, [ca] + PQ[bk2], [actT])
                        if halo:
                            continue
                        for c in range(8):
                            wd_ = wdn[c % 2]
                            k.dma(wd_[:], ws_down[:, :, c * 128:(c + 1) * 128], [T_wscr_in], [wd_])
                            bk = 5 + c % 2
                            for cc in range(22):
                                k.mm(banks[bk][:, 0:n], wd_[:, cc, :], actT[:, cc, 0:n], [wd_, actT], PQ[bk], start=(cc == 0), stop=(cc == 21))
                            k.tt("dve", xo[:, c, sg], xo[:, c, sg], banks[bk][:, 0:n], ALU.add, [xo] + PQ[bk], [xo])
                        k.dma(Wo[:], ws_pg, [T_wscr_in], [Wo])
                        Wpg = Wo
                        k.act(sqb[:, :, 0:n], xo[:, :, sg], AF.Square, [xo], [sqb])
                        for kc in range(8):
                            k.mm(banks[0][:, 0:n], ones_bf[:], sqb[:, kc, 0:n], [ones_bf, sqb], PQ[0], start=(kc == 0), stop=(kc == 7))
                        k.act(rsb[:, 0:n], banks[0][:, 0:n], AF.Sqrt, PQ[0] + [epsT], [rsb], scale=1.0 / D, bias=epsT[:, 0:1])
                        k.recip(rsb[:, 0:n], rsb[:, 0:n], [rsb], [rsb])
                        for kc in range(8):
                            k.tt("dve" if kc % 2 == 0 else "pool", h2T[:, kc, 0:n], xo[:, kc, sg], rsb[:, 0:n], ALU.mult, [xo, rsb], [h2T])
                        for c in range(8):
                            cs = slice(c * 128, (c + 1) * 128)
                            for kc in range(8):
                                k.mm(banks[1][:, 0:n], Wpg[:, kc, cs], h2T[:, kc, 0:n], [Wpg, h2T], PQ[1], start=(kc == 0), stop=(kc == 7))
                            k.act(sga[:, 0:n], banks[1][:, 0:n], AF.Sigmoid, PQ[1], [sga])
                            for kc in range(2):
                                k.mm(banks[2][:, 0:n], Wpl[:, kc, cs], pb[:, kc, sg], [Wpl, pb], PQ[2], start=(kc == 0), stop=(kc == 1))
                            k.tt("dve", t1[:, 0:n], banks[2][:, 0:n], sga[:, 0:n], ALU.mult, PQ[2] + [sga], [t1])
                            k.tt("pool", xo[:, c, sg], xo[:, c, sg], t1[:, 0:n], ALU.add, [xo, t1], [xo])
                        k.act(sqb[:, :, 0:n], xo[:, :, sg], AF.Square, [xo], [sqb])
                        for kc in range(8):
                            k.mm(banks[0][:, 0:n], ones_bf[:], sqb[:, kc, 0:n], [ones_bf, sqb], PQ[0], start=(kc == 0), stop=(kc == 7))
                        k.act(rsb[:, 0:n], banks[0][:, 0:n], AF.Sqrt, PQ[0] + [epsT], [rsb], scale=1.0 / D, bias=epsT[:, 0:1])
                        k.recip(rsb[:, 0:n], rsb[:, 0:n], [rsb], [rsb])
                        for kc in range(8):
                            k.stt(xo[:, kc, sg], xo[:, kc, sg], gains[:, 24 + kc:25 + kc], rsb[:, 0:n], ALU.mult, ALU.mult, [xo, gains, rsb], [xo])
                        for tt in range(n // 128):
                            for cg in range(2):
                                bk = 3 + cg
                                for c4 in range(4):
                                    k.mm(banks[bk][:, c4 * 128:(c4 + 1) * 128], xo[:, 4 * cg + c4, c0 + tt * 128:c0 + (tt + 1) * 128], ident, [xo, cst], PQ[bk])
                                k.cp("act" if cg == 0 else "dve", ytok[:, cg * 512:(cg + 1) * 512], banks[bk][:, :], PQ[bk], [ytok])
                            nv = min(128, jb["nvalid"])
                            k.dma(jb["y_out"][tt * 128:tt * 128 + nv, :], ytok[0:nv, :], [ytok], [outT], key=ytok)
                        k.dma(jb["fconv_out"], fco[:], [fco], [outT], key=fco)
                    S.barrier()

        xav = I["xa"].rearrange("(kc p) t -> p kc t", p=128)
        pav = I["pT"].rearrange("(kc p) t -> p kc t", p=128)
        if "P" in KJOB2:
            for m in range(4):
                ps_ = 4 * m + 3
                t0 = 512 * ps_ - 128
                own_block(dict(NQ=5, xsrc=xav[:, :, t0:t0 + 640], psrc=pav[:, :, t0:t0 + 640], L=[128 * (4 * ps_ + r) for r in range(5)],
                               limcol=5 * m, lolim=22, kiT=kiT_p, kT=kT_p, v=v_p, obT=obT_p[:, :, t0:t0 + 640], fhalo=None,
                               segs=[(126, 2, True), (128, 512, False)], nvalid=512, y_out=O["y_own"][512 * m:512 * (m + 1), :], fconv_out=O["fconv_p"]))
        if "S" in KJOB2:
            for sb_ in range(2):
                own_block(dict(NQ=1, xsrc=I["xs"][sb_].rearrange("(kc p) t -> p kc t", p=128), psrc=I["psT"][sb_].rearrange("(kc p) t -> p kc t", p=128),
                               L=[LKS], limcol=20 + sb_, lolim=None, kiT=kiT_s[sb_], kT=kT_s[sb_], v=v_s[sb_], obT=obT_s[sb_], fhalo=I["fconvT"][sb_],
                               segs=[(0, 128, False)], nvalid=16, y_out=O["y_s"][sb_], fconv_out=O["fconv_s"][sb_]))

        print('nsem', S.nsem, {e: len(q) for e, q in S.q.items()})
        S.final_wait("sp", [outT])
        S.emit()
    return nc


def _consts():
    c = np.zeros((128, 7 * 128), np.float32)
    i = np.arange(128)
    same = (i[:, None] // 64) == (i[None, :] // 64)
    c[:, 0:128] = np.eye(128)
    c[:, 128:256] = ((i[:, None] <= i[None, :]) & same)
    c[:, 256:384] = ((i[:, None] > i[None, :]) & same)
    c[:, 384:512] = np.where((i[:, None] > i[None, :]) & same, 0.0, NEG)
    c[:, 512:640] = np.where((i[:, None] <= i[None, :]) & same, 0.0, -NEG)
    c[:, 640:768] = (i[:, None] < 64)
    c[:, 768:896] = (i[:, None] >= 64)
    return c


_NC = None


def kernel(**inp):
    global _NC
    f32 = np.float32
    xp = np.asarray(inp["x_prompt"], f32)
    xs = np.asarray(inp["x_sample"], f32)
    cst = _consts()
    gains = np.zeros((128, 32), f32)
    gains[:, 0:8] = np.asarray(inp["norm_mix"], f32)[0].reshape(8, 128).T
    gains[:, 8:16] = np.asarray(inp["norm_ffn"], f32)[0].reshape(8, 128).T
    gains[:, 16:24] = np.asarray(inp["norm_ple"], f32)[0].reshape(8, 128).T
    gains[:, 24:32] = np.asarray(inp["norm_final"], f32).reshape(8, 128).T
    convb = np.ascontiguousarray(np.asarray(inp["conv_b"], f32)[0].T.reshape(12, 128, 4).transpose(1, 0, 2))
    alog = np.ascontiguousarray(np.broadcast_to(np.asarray(inp["a_log"], f32)[0][None, :], (128, 4)))
    dtb = np.ascontiguousarray(np.broadcast_to(np.asarray(inp["dt_bias"], f32)[0][None, :], (128, 4)))
    ngdn = np.ascontiguousarray(np.broadcast_to(np.asarray(inp["norm_gdn"], f32)[0][None, :], (128, 128)))
    gval = np.zeros((128, 2), f32)
    gval[0:16, 0] = 1.0
    gval[:, 1] = 1.0
    w_in = np.ascontiguousarray(np.asarray(inp["w_in"], f32)[0])
    ck = np.asarray(inp["cache_k"], f32)[0]
    cvv = np.asarray(inp["cache_v"], f32)[0]
    cki = np.asarray(inp["cache_kidx"], f32)[0]
    sg = np.asarray(inp["state_gdn"], f32)[0]
    sgc = np.asarray(inp["state_gdn_conv"], f32)[0]
    def bucket_np(rel):
        half, max_exact = 16, 8
        ret = np.where(rel > 0, half, 0)
        n = np.abs(rel)
        nf = np.maximum(n, 1).astype(np.float32)
        large = max_exact + (np.log(nf / np.float32(max_exact)) / np.float32(np.log(128 / 8)) * np.float32(half - max_exact)).astype(np.int32)
        large = np.minimum(large, half - 1)
        return ret + np.where(n < max_exact, n, large)
    ohrev = np.zeros((32, 384), f32)
    sp_ = np.arange(383)
    ohrev[bucket_np(127 - sp_), sp_] = 1.0
    iota = np.ascontiguousarray(np.broadcast_to(np.arange(512, dtype=f32)[None, :], (128, 512)))
    relb = np.ascontiguousarray(np.asarray(inp["rel_bias"], f32))
    relb15 = np.ascontiguousarray(relb[15][:, None])
    convf = np.ascontiguousarray(np.asarray(inp["conv_ffn"], f32)[0].T.reshape(22, 128, 3).transpose(1, 0, 2))
    pp = np.asarray(inp["p_prompt"], f32)[0]
    psm = np.asarray(inp["p_sample"], f32)[0]
    sfc = np.asarray(inp["state_ffn_conv"], f32)[0]
    wts = dict(w_pa=np.ascontiguousarray(np.asarray(inp["w_proj_a"], f32)[0]), w_pb=np.ascontiguousarray(np.asarray(inp["w_proj_b"], f32)[0]),
               w_out=np.ascontiguousarray(np.asarray(inp["w_out"], f32)[0]), w_up=np.ascontiguousarray(np.asarray(inp["w_up"], f32)[0]),
               w_down=np.ascontiguousarray(np.asarray(inp["w_down"], f32)[0]), w_ple=np.ascontiguousarray(np.asarray(inp["w_ple"], f32)[0]),
               w_pg=np.ascontiguousarray(np.asarray(inp["w_ple_gate"], f32)[0]))
    in_maps = []
    for c in range(8):
        b, j = c // 4, c % 4
        pad = 512 * (3 - j)
        xa = np.zeros((D, LKP), f32)
        xa[:, pad:] = xp[b].T[:, :LKP - pad]
        sl = slice(2 * c, 2 * c + 2)
        xs_c = np.zeros((2, D, 128), f32)
        xs_c[:, :, 0:16] = xs[sl].transpose(0, 2, 1)
        m = dict(xa=xa, xs=xs_c, cst=cst, gains=gains, convb=convb, alog=alog, dtb=dtb, ngdn=ngdn, gval=gval, w_in=w_in,
                 cache_kT=np.ascontiguousarray(ck[sl].reshape(2, PAST, 512).transpose(0, 2, 1)),
                 cache_v=np.ascontiguousarray(cvv[sl].reshape(2, PAST, 512)),
                 cache_kiT=np.ascontiguousarray(cki[sl].transpose(0, 2, 1)),
                 state_gdn=np.ascontiguousarray(sg[sl]),
                 gconvT=np.ascontiguousarray(sgc[sl].transpose(0, 2, 1).reshape(2, 12, 128, 3).transpose(0, 2, 1, 3)))
        pT = np.zeros((256, LKP), f32)
        pT[:, pad:] = pp[b].T[:, :LKP - pad]
        psT = np.zeros((2, 256, 128), f32)
        psT[:, :, 0:16] = psm[sl].transpose(0, 2, 1)
        lims = np.zeros((128, 24), f32)
        ii = np.arange(128)
        for m_ in range(4):
            for qb in range(5):
                pt = 512 * (4 * m_ + 3) - 128 + 128 * qb + ii
                lims[:, 5 * m_ + qb] = (pt // 64 + 1) * 64
        lims[:, 20] = PAST + 16
        lims[:, 21] = PAST + 16
        lims[:, 22] = pad
        m.update(pT=pT, psT=psT, convf=convf, relb=relb, relb15=relb15, ohrev=ohrev, iota=iota, lims=lims,
                 fconvT=np.ascontiguousarray(sfc[sl].transpose(0, 2, 1).reshape(2, 22, 128, 2).transpose(0, 2, 1, 3)), **wts)
        in_maps.append(m)
    if _NC is None:
        _NC = build_program()
    res = run_bass_kernel_spmd(_NC, in_maps, core_ids=list(range(8)))
    R = res.results
    y_p = np.zeros((2, 8192, D), f32)
    for c in range(8):
        b, j = c // 4, c % 4
        for m_ in range(4):
            sg_ = 4 * m_ + j
            y_p[b, 512 * sg_:512 * (sg_ + 1)] = R[c]["y_own"][512 * m_:512 * (m_ + 1)]
    y_s = np.concatenate([R[c]["y_s"] for c in range(8)]).reshape(16, 16, D)
    k_p = np.stack([R[4 * b + 3]["k_all"] for b in range(2)]).reshape(1, 2, 8192, 8, 64)
    v_p = np.stack([R[4 * b + 3]["v_all"] for b in range(2)]).reshape(1, 2, 8192, 8, 64)
    ki_p = np.stack([R[4 * b + 3]["ki_all"] for b in range(2)]).reshape(1, 2, 8192, 64)
    gdn_p = np.stack([R[4 * b + 3]["gdn_p"] for b in range(2)]).reshape(1, 2, 4, 128, 128)
    gconv_p = np.stack([R[4 * b + 3]["gconv_p"].transpose(1, 0, 2).reshape(1536, 3).T for b in range(2)]).reshape(1, 2, 3, 1536)
    fconv_p = np.stack([R[4 * b + 3]["fconv_p"].transpose(1, 0, 2).reshape(DFF, 2).T for b in range(2)]).reshape(1, 2, 2, DFF)
    k_s = np.concatenate([R[c]["k_s"] for c in range(8)]).reshape(1, 16, 16, 8, 64)
    v_s = np.concatenate([R[c]["v_s"] for c in range(8)]).reshape(1, 16, 16, 8, 64)
    ki_s = np.concatenate([R[c]["ki_s"] for c in range(8)]).reshape(1, 16, 16, 64)
    gdn_s = np.concatenate([R[c]["gdn_s"] for c in range(8)]).reshape(1, 16, 4, 128, 128)
    gconv_s = np.concatenate([R[c]["gconv_s"] for c in range(8)])
    gconv_s = np.ascontiguousarray(gconv_s.transpose(0, 2, 1, 3).reshape(16, 1536, 3).transpose(0, 2, 1)).reshape(1, 16, 3, 1536)
    fconv_s = np.concatenate([R[c]["fconv_s"] for c in range(8)])
    fconv_s = np.ascontiguousarray(fconv_s.transpose(0, 2, 1, 3).reshape(16, DFF, 2).transpose(0, 2, 1)).reshape(1, 16, 2, DFF)
    return (y_p, y_s, k_p, v_p, ki_p, gdn_p, gconv_p, fconv_p, k_s, v_s, ki_s, gdn_s, gconv_s, fconv_s)
```

```python
import os
import numpy as np
from contextlib import ExitStack
import concourse.bass as bass
import concourse.mybir as mybir
from concourse.bass_utils import run_bass_kernel_spmd

F32 = mybir.dt.float32
BF16 = mybir.dt.bfloat16
AF = mybir.ActivationFunctionType
ALU = mybir.AluOpType
AX = mybir.AxisListType

EPOCH = 4096
KDBG = int(os.environ.get('KDBG', '9'))
KJOB = os.environ.get('KJOB', 'ps')
KSUB = int(os.environ.get('KSUB', '99'))
KJOB2 = os.environ.get('KJOB2', 'PS')
KP2 = int(os.environ.get('KP2', '99'))
KP3 = int(os.environ.get('KP3', '99'))
EPS = 1e-6
NEG = -1e30

D = 1024
LKP = 8192
NTP = 256
PAST = 2048
LKS = PAST + 128
DFF = 2816

C_QA, C_KA, C_VA, C_QI, C_KI, C_WI, C_QB, C_GB, C_BB, C_AB, C_GA, C_GBR = 0, 512, 1024, 1536, 2048, 2112, 2120, 3656, 4168, 4172, 4176, 5200
DIN = 6224


class Tl:
    __slots__ = ("t", "name", "lw", "rd", "excl")

    def __init__(self, t, name, excl=False):
        self.t = t
        self.name = name
        self.lw = None
        self.rd = []
        self.excl = excl

    def __getitem__(self, idx):
        return self.t[idx]


class Sched:
    ENGS = ("pe", "act", "dve", "pool", "sp")

    def __init__(self, nc, es):
        self.nc = nc
        self.es = es
        self.q = {e: [] for e in self.ENGS}
        self.cnt = {e: 0 for e in self.ENGS}
        self.sems = {e: [] for e in self.ENGS}
        self.seen = {e: {} for e in self.ENGS}
        self.dma_sems = {}
        self.keep = []
        self.nsem = 0

    def _newsem(self, name):
        self.nsem += 1
        return self.es.enter_context(self.nc.semaphore(name))

    def _eng_token(self, e):
        c = self.cnt[e]
        ep = c // EPOCH
        while len(self.sems[e]) <= ep:
            self.sems[e].append(self._newsem(f"s_{e}_{len(self.sems[e])}"))
        self.cnt[e] = c + 1
        return (("e", e, ep), self.sems[e][ep], (c % EPOCH) + 1)

    def _waits(self, e, reads, writes):
        toks = []
        for t in reads:
            if t.lw is not None:
                toks.append(t.lw)
            if t.excl:
                toks.extend(t.rd)
        for t in writes:
            if t.lw is not None:
                toks.append(t.lw)
            toks.extend(t.rd)
        best = {}
        for (key, sem, val) in toks:
            if key[0] == "e" and key[1] == "pe" and e == "pe":
                continue
            if best.get(key, (None, 0))[1] < val:
                best[key] = (sem, val)
        out = []
        seen = self.seen[e]
        for key, (sem, val) in best.items():
            if seen.get(key, 0) >= val:
                continue
            seen[key] = val
            out.append((sem, val))
        return out

    def op(self, e, fn, reads=(), writes=()):
        w = self._waits(e, reads, writes)
        tok = self._eng_token(e)
        self.q[e].append((w, fn, tok[1], 1))
        for t in writes:
            t.lw = tok
            t.rd = []
        for t in reads:
            if t.lw is not tok:
                t.rd.append(tok)
        return tok

    def dma(self, e, fn, reads=(), writes=(), key=None):
        w = self._waits(e, reads, writes)
        k = key if key is not None else (writes[0] if writes else reads[0])
        kid = id(k)
        ent = self.dma_sems.get(kid)
        if ent is None or ent[1] >= 32000:
            if ent is None:
                self.keep.append(k)
            if getattr(self, "pool", None):
                ent = self.pool.pop()
            else:
                self.semid = getattr(self, "semid", 0) + 1
                ent = [self._newsem(f"d_{self.nsem}"), 0, self.semid]
            self.dma_sems[kid] = ent
        ent[1] += 16
        tok = (("d", ent[2], 0), ent[0], ent[1])
        self.q[e].append((w, fn, ent[0], 16))
        for t in writes:
            t.lw = tok
            t.rd = []
        for t in reads:
            if t.lw is not tok:
                t.rd.append(tok)
        return tok

    def barrier(self):
        toks = []
        for e in self.ENGS:
            c = self.cnt[e]
            if c > 0:
                ep = (c - 1) // EPOCH
                toks.append((("e", e, ep), self.sems[e][ep], ((c - 1) % EPOCH) + 1))
        for kid, ent in self.dma_sems.items():
            if ent[1] > 0:
                toks.append((("d", ent[2], 0), ent[0], ent[1]))
        for e in self.ENGS:
            w = []
            seen = self.seen[e]
            for (key, sem, val) in toks:
                if key[0] == "e" and key[1] == e:
                    continue
                if seen.get(key, 0) >= val:
                    continue
                seen[key] = val
                w.append((sem, val))
            if w:
                self.q[e].append((w, None, None, 0))
        if not hasattr(self, "pool"):
            self.pool = []
        for kid, ent in self.dma_sems.items():
            if ent[1] < 30000:
                self.pool.append(ent)
        self.dma_sems = {}

    def final_wait(self, e, tiles):
        w = self._waits(e, tiles, tiles)
        self.q[e].append((w, None, None, 0))

    def emit(self):
        nc = self.nc
        with nc.Block() as block:
            def run(eng, name):
                for (w, fn, sem, inc) in self.q[name]:
                    for (s, v) in w:
                        eng.wait_ge(s, v)
                    if fn is not None:
                        fn(eng).then_inc(sem, inc)

            @block.tensor
            def _(eng):
                run(eng, "pe")

            @block.scalar
            def _(eng):
                run(eng, "act")

            @block.vector
            def _(eng):
                run(eng, "dve")

            @block.gpsimd
            def _(eng):
                run(eng, "pool")

            @block.sync
            def _(eng):
                run(eng, "sp")


class K:
    def __init__(self, nc, S):
        self.nc = nc
        self.S = S
        self.rr = 0

    def mm(self, out, lhsT, rhs, rd, wr, start=True, stop=True):
        self.S.op("pe", lambda e, o=out, l=lhsT, r=rhs, a=start, b=stop: e.matmul(o, lhsT=l, rhs=r, start=a, stop=b), reads=rd, writes=wr)

    def tr(self, out, in_, ident, rd, wr):
        self.S.op("pe", lambda e, o=out, i=in_, d=ident: e.matmul(o, lhsT=i, rhs=d, start=True, stop=True), reads=rd, writes=wr)

    def act(self, out, in_, func, rd, wr, scale=None, bias=None, accum=None):
        kw = {}
        if scale is not None:
            kw["scale"] = scale
        if bias is not None:
            kw["bias"] = bias
        if accum is not None:
            kw["accum_out"] = accum
        self.S.op("act", lambda e, o=out, i=in_, f=func, k=kw: e.activation(out=o, in_=i, func=f, **k), reads=rd, writes=wr)

    def ts(self, eng, out, in0, s1, op0, rd, wr, s2=None, op1=None, accum=None):
        kw = {}
        if op1 is not None:
            kw["op1"] = op1
        if accum is not None:
            kw["accum_out"] = accum
        self.S.op(eng, lambda e, o=out, i=in0, a=s1, b=s2, p=op0, k=kw: e.tensor_scalar(out=o, in0=i, scalar1=a, scalar2=b, op0=p, **k), reads=rd, writes=wr)

    def tt(self, eng, out, in0, in1, op, rd, wr):
        self.S.op(eng, lambda e, o=out, i=in0, j=in1, p=op: e.tensor_tensor(out=o, in0=i, in1=j, op=p), reads=rd, writes=wr)

    def stt(self, out, in0, sc, in1, op0, op1, rd, wr):
        self.S.op("dve", lambda e, o=out, i=in0, s=sc, j=in1, p=op0, q=op1: e.scalar_tensor_tensor(out=o, in0=i, scalar=s, in1=j, op0=p, op1=q), reads=rd, writes=wr)

    def cp(self, eng, out, in_, rd, wr):
        if eng == "act":
            self.S.op("act", lambda e, o=out, i=in_: e.copy(out=o, in_=i), reads=rd, writes=wr)
        else:
            self.S.op(eng, lambda e, o=out, i=in_: e.tensor_copy(out=o, in_=i), reads=rd, writes=wr)

    def memset(self, eng, ap, val, wr):
        self.S.op(eng, lambda e, a=ap, v=val: e.memset(a, v), writes=wr)

    def recip(self, out, in_, rd, wr):
        self.S.op("dve", lambda e, o=out, i=in_: e.reciprocal(out=o, in_=i), reads=rd, writes=wr)

    def dma(self, out, in_, rd, wr, q="sp", key=None, slow=False):
        if slow:
            self.S.dma(q, lambda e, o=out, i=in_: e.dma_start(out=o, in_=i, allow_slow_non_contiguous=True), reads=rd, writes=wr, key=key)
        else:
            self.S.dma(q, lambda e, o=out, i=in_: e.dma_start(out=o, in_=i), reads=rd, writes=wr, key=key)


def build_program():
    nc = bass.Bass("TRN2", target_bir_lowering=False)

    def din(name, shape, dt=F32):
        return nc.dram_tensor(name, list(shape), dt, kind="ExternalInput").ap()

    def dout(name, shape, dt=F32):
        return nc.dram_tensor(name, list(shape), dt, kind="ExternalOutput").ap()

    def dscr(name, shape, dt):
        return nc.dram_tensor(name, list(shape), dt, kind="Internal").ap()

    I = {}
    I["xa"] = din("xa", [D, LKP])
    I["xs"] = din("xs", [2, D, 128])
    I["cst"] = din("cst", [128, 7 * 128])
    I["gains"] = din("gains", [128, 32])
    I["convb"] = din("convb", [128, 12, 4])
    I["alog"] = din("alog", [128, 4])
    I["dtb"] = din("dtb", [128, 4])
    I["ngdn"] = din("ngdn", [128, 128])
    I["gval"] = din("gval", [128, 2])
    I["w_in"] = din("w_in", [D, DIN])
    I["cache_kT"] = din("cache_kT", [2, 512, PAST])
    I["cache_v"] = din("cache_v", [2, PAST, 512])
    I["cache_kiT"] = din("cache_kiT", [2, 64, PAST])
    I["state_gdn"] = din("state_gdn", [2, 4, 128, 128])
    I["gconvT"] = din("gconvT", [2, 128, 12, 3])
    I["pT"] = din("pT", [256, LKP])
    I["psT"] = din("psT", [2, 256, 128])
    I["w_pa"] = din("w_pa", [512, D])
    I["w_pb"] = din("w_pb", [512, D])
    I["w_out"] = din("w_out", [D, D])
    I["w_up"] = din("w_up", [D, 2 * DFF])
    I["w_down"] = din("w_down", [DFF, D])
    I["w_ple"] = din("w_ple", [256, D])
    I["w_pg"] = din("w_pg", [D, D])
    I["convf"] = din("convf", [128, 22, 3])
    I["relb"] = din("relb", [32, 8])
    I["relb15"] = din("relb15", [8, 1])
    I["ohrev"] = din("ohrev", [32, 384])
    I["iota"] = din("iota", [128, 512])
    I["lims"] = din("lims", [128, 24])
    I["fconvT"] = din("fconvT", [2, 128, 22, 2])
    O = {}
    O["y_own"] = dout("y_own", [2048, D])
    O["fconv_p"] = dout("fconv_p", [128, 22, 2])
    O["y_s"] = dout("y_s", [2, 16, D])
    O["fconv_s"] = dout("fconv_s", [2, 128, 22, 2])
    O["k_all"] = dout("k_all", [LKP, 512])
    O["v_all"] = dout("v_all", [LKP, 512])
    O["ki_all"] = dout("ki_all", [LKP, 64])
    O["gdn_p"] = dout("gdn_p", [4, 128, 128])
    O["gconv_p"] = dout("gconv_p", [128, 12, 3])
    O["k_s"] = dout("k_s", [2, 16, 512])
    O["v_s"] = dout("v_s", [2, 16, 512])
    O["ki_s"] = dout("ki_s", [2, 16, 64])
    O["gdn_s"] = dout("gdn_s", [2, 4, 128, 128])
    O["gconv_s"] = dout("gconv_s", [2, 128, 12, 3])
    wscr_in = dscr("wscr_in", [128, 8, DIN], BF16)
    ws_pa = dscr("ws_pa", [128, 4, D], BF16)
    ws_pb = dscr("ws_pb", [128, 4, D], BF16)
    ws_out = dscr("ws_out", [128, 8, D], BF16)
    ws_up = dscr("ws_up", [128, 8, 2 * DFF], BF16)
    ws_down = dscr("ws_down", [128, 22, D], BF16)
    ws_ple = dscr("ws_ple", [128, 2, D], BF16)
    ws_pg = dscr("ws_pg", [128, 8, D], BF16)
    tab_scr = dscr("tab_scr", [8, 384], F32)
    T_tab = Tl(None, "tab")
    kT_p = dscr("kT_p", [128, 4, LKP], BF16)
    v_p = dscr("v_p", [LKP, 528], BF16)
    kiT_p = dscr("kiT_p", [64, LKP], BF16)
    obT_p = dscr("obT_p", [128, 4, LKP], BF16)
    kT_s = dscr("kT_s", [2, 128, 4, LKS], BF16)
    v_s = dscr("v_s_scr", [2, LKS, 528], BF16)
    kiT_s = dscr("kiT_s", [2, 64, LKS], BF16)
    obT_s = dscr("obT_s", [2, 128, 4, 128], BF16)

    outT = Tl(None, "outputs")
    T_wscr_in = Tl(None, "wscr_in")
    T_kT = Tl(None, "kT")
    T_v = Tl(None, "v")
    T_ki = Tl(None, "kiT")
    T_ob = Tl(None, "obT")

    with ExitStack() as es0:
        S = Sched(nc, es0)
        k = K(nc, S)

        uid = [0]

        def sbt(es, name, shape, dt):
            uid[0] += 1
            nm = f"s{uid[0]}_{name}"
            return Tl(es.enter_context(nc.sbuf_tensor(nm, list(shape), dt)), nm)

        banks = [es0.enter_context(nc.psum_tensor(f"pb{i}", [128, 512], F32)) for i in range(8)]
        PB = [Tl(banks[b], f"pb{b}", excl=True) for b in range(8)]
        PQ = [[PB[b]] * 4 for b in range(8)]
        banks_bf = None

        def pq(b, q, rows=slice(0, 128), w=128):
            return banks[b][rows, q * 128:q * 128 + w]

        cst = sbt(es0, "cst", [128, 7 * 128], F32)
        k.dma(cst[:], I["cst"], [], [cst])
        ident = cst[:, 0:128]
        Ublk = cst[:, 128:256]
        Lsblk = cst[:, 256:384]
        NEGML = cst[:, 384:512]
        POSMU = cst[:, 512:640]
        half0 = cst[:, 640:768]
        half1 = cst[:, 768:896]
        gains = sbt(es0, "gains", [128, 32], F32)
        k.dma(gains[:], I["gains"], [], [gains])
        convb = sbt(es0, "convb", [128, 12, 4], F32)
        k.dma(convb[:], I["convb"], [], [convb])
        cA = sbt(es0, "cA", [128, 4], F32)
        k.dma(cA[:], I["alog"], [], [cA])
        dtb = sbt(es0, "dtb", [128, 4], F32)
        k.dma(dtb[:], I["dtb"], [], [dtb])
        ngdn = sbt(es0, "ngdn", [128, 128], F32)
        k.dma(ngdn[:], I["ngdn"], [], [ngdn])
        gval = sbt(es0, "gval", [128, 2], F32)
        k.dma(gval[:], I["gval"], [], [gval])
        epsT = sbt(es0, "epsT", [128, 1], F32)
        k.memset("pool", epsT[:], EPS, [epsT])
        ident_bf = sbt(es0, "ident_bf", [128, 128], BF16)
        k.cp("dve", ident_bf[:], ident, [cst], [ident_bf])
        ones_bf = sbt(es0, "ones_bf", [128, 128], BF16)
        k.memset("pool", ones_bf[:], 1.0, [ones_bf])
        k.act(cA[:], cA[:], AF.Exp, [cA], [cA])
        k.ts("dve", cA[:], cA[:], -1.0, ALU.mult, [cA], [cA])

        with ExitStack() as es:
            if KDBG < -1:
                raise_skip = True
            wf = [sbt(es, f"wf{i}", [128, 1556], F32) for i in range(2)]
            wb = [sbt(es, f"wb{i}", [128, 1556], BF16) for i in range(2)]
            cnt_ = [0]

            pcs = []

            def conv_w(src, dst, KC, C, gbase):
                v = src.rearrange("(kc p) c -> p kc c", p=128)
                npc = 1 if C <= 1556 else 4
                pw = C // npc
                for kc in range(KC):
                    for pc in range(npc):
                        pcs.append((v[:, kc, pc * pw:(pc + 1) * pw], dst[:, kc, pc * pw:(pc + 1) * pw], pw, None if gbase is None else gbase + kc))

            def conv_emit():
                def load(n):
                    k.dma(wf[n % 2][:, 0:pcs[n][2]], pcs[n][0], [], [wf[n % 2]], q="sp")
                if pcs:
                    load(0)
                for n, (src_, dst_, pw, gcol) in enumerate(pcs):
                    a, b_ = wf[n % 2], wb[n % 2]
                    if n + 1 < len(pcs):
                        load(n + 1)
                    eng = "dve" if n % 2 == 0 else "pool"
                    if gcol is None:
                        k.cp(eng, b_[:, 0:pw], a[:, 0:pw], [a], [b_])
                    else:
                        k.ts(eng, b_[:, 0:pw], a[:, 0:pw], gains[:, gcol:gcol + 1], ALU.mult, [a, gains], [b_])
                    k.dma(dst_, b_[:, 0:pw], [b_], [T_wscr_in], q="sp", key=b_)

            if KDBG >= -1:
                conv_w(I["w_in"], wscr_in, 8, DIN, 0)
                conv_w(I["w_pa"], ws_pa, 4, D, None)
                conv_w(I["w_pb"], ws_pb, 4, D, None)
                conv_w(I["w_out"], ws_out, 8, D, None)
                conv_w(I["w_up"], ws_up, 8, 2 * DFF, 8)
                conv_w(I["w_down"], ws_down, 22, D, None)
                conv_w(I["w_ple"], ws_ple, 2, D, None)
                conv_w(I["w_pg"], ws_pg, 8, D, 16)
                conv_emit()
            S.barrier()

        def all_pass(es, job):
            NT = job["NT"]
            ntt = NT // 128
            nsteps = min(job["L"] // NT, int(os.environ.get('KSTEPS', '999')))
            xsrc = job["x"]
            W1 = sbt(es, "W1", [128, 8, 3144], BF16)
            k.dma(W1[:, :, 0:1024], wscr_in[:, :, C_KA:C_KA + 1024], [T_wscr_in], [W1])
            k.dma(W1[:, :, 1024:1088], wscr_in[:, :, C_KI:C_KI + 64], [T_wscr_in], [W1])
            k.dma(W1[:, :, 1088:3144], wscr_in[:, :, C_QB:C_QB + 2056], [T_wscr_in], [W1])
            xa_t = [sbt(es, f"xa_t{i}", [128, 8, NT], F32) for i in range(2)]
            sq = sbt(es, "sq", [128, 8, NT], BF16)
            hT = sbt(es, "hT", [128, 8, NT], BF16)
            rs = sbt(es, "rs", [128, NT], F32)
            cin = sbt(es, "cin", [128, 12, NT + 3], F32)
            cv = sbt(es, "cv", [128, 12, NT], F32)
            cvb = sbt(es, "cvb", [128, 8, NT], BF16)
            sq2 = sbt(es, "sq2", [128, NT], BF16)
            rs2 = sbt(es, "rs2", [128, NT], F32)
            gbs = sbt(es, "gbs", [128, ntt, 512], F32)
            bbab = sbt(es, "bbab", [128, ntt, 8], F32)
            ktok = [sbt(es, f"ktok{i}", [128, 512], F32) for i in range(2)]
            vtok = [sbt(es, f"vtok{i}", [128, 512], F32) for i in range(2)]
            vbf = [sbt(es, f"vbf{i}", [128, 8, 66], BF16) for i in range(2)]
            for i in range(2):
                k.memset("pool", vbf[i][:, :, 64:66], 1.0, [vbf[i]])
            kitok = [sbt(es, f"kitok{i}", [128, 64], F32) for i in range(2)]
            kf_t = sbt(es, "kf_t", [128, 4, NT], BF16)
            kif_t = sbt(es, "kif_t", [64, NT], BF16)
            obT_t = sbt(es, "obT_t", [128, 4, NT], BF16)
            Sst = [sbt(es, f"Sst{h}", [128, 128], F32) for h in range(4)]
            bet = sbt(es, "bet", [128, ntt, 4], F32)
            nbet = sbt(es, "nbet", [128, ntt, 4], F32)
            gg = sbt(es, "gg", [128, ntt, 4], F32)
            ngg = sbt(es, "ngg", [128, ntt, 4], F32)
            egc = sbt(es, "egc", [128, ntt, 4], F32)
            ekd = sbt(es, "ekd", [128, ntt, 4], F32)
            egl = sbt(es, "egl", [128, ntt, 8], F32)
            bege = sbt(es, "bege", [128, ntt, 4], F32)
            HT = []
            for h in range(4):
                d = {}
                for nm in ("NGb", "E1m", "E2m", "P", "attnT", "kbg", "kd", "vb", "u", "wT", "vnew", "o1", "o", "tmp"):
                    d[nm] = sbt(es, f"{nm}{h}", [128, 128], F32)
                for nm in ("N", "Nt", "M0", "M1", "Mt0", "Mt1", "Pb"):
                    d[nm] = sbt(es, f"{nm}{h}", [128, 128], BF16)
                d["obn"] = sbt(es, f"obn{h}", [128, 128], BF16)
                d["ms"] = sbt(es, f"ms{h}", [128, 1], F32)
                HT.append(d)

            if job["S0"] is None:
                for h in range(4):
                    k.memset("pool", Sst[h][:], 0.0, [Sst[h]])
                k.memset("pool", cin[:, :, 0:3], 0.0, [cin])
            else:
                for h in range(4):
                    k.dma(Sst[h][:], job["S0"][h], [], [Sst[h]])
                k.dma(cin[:, :, 0:3], job["conv0"], [], [cin], slow=True)

            xv = xsrc.rearrange("(kc p) t -> p kc t", p=128)
            k.dma(xa_t[0][:], xv[:, :, 0:NT], [], [xa_t[0]])
            for st in range(nsteps):
                t0 = st * NT
                xt = xa_t[st % 2]
                if st + 1 < nsteps:
                    xn = xa_t[(st + 1) % 2]
                    k.dma(xn[:], xv[:, :, t0 + NT:t0 + 2 * NT], [], [xn])
                k.act(sq[:], xt[:], AF.Square, [xt], [sq])
                A = PQ[0]
                for kc in range(8):
                    k.mm(banks[0][:, 0:NT], ones_bf[:], sq[:, kc, :], [ones_bf, sq], A, start=(kc == 0), stop=(kc == 7))
                k.act(rs[:], banks[0][:, 0:NT], AF.Sqrt, A + [epsT], [rs], scale=1.0 / D, bias=epsT[:, 0:1])
                k.recip(rs[:], rs[:], [rs], [rs])
                for kc in range(8):
                    k.tt("dve" if kc % 2 == 0 else "pool", hT[:, kc, :], xt[:, kc, :], rs[:], ALU.mult, [xt, rs], [hT])
                if KDBG < 1:
                    continue
                for tt in range(ntt):
                    ts_ = slice(tt * 128, (tt + 1) * 128)
                    r0 = t0 + tt * 128
                    kt, vt, vb_, kit = ktok[tt % 2], vtok[tt % 2], vbf[tt % 2], kitok[tt % 2]
                    for (bk, c0, cw) in ((1, 0, 512), (2, 512, 512)):
                        for kc in range(8):
                            k.mm(banks[bk][:, 0:cw], hT[:, kc, ts_], W1[:, kc, c0:c0 + cw], [hT, W1], PQ[bk], start=(kc == 0), stop=(kc == 7))
                    k.cp("act", kt[:], banks[1][:, :], PQ[1], [kt])
                    k.cp("dve", vt[:], banks[2][:, :], PQ[2], [vt])
                    k.cp("pool", vb_[:, :, 0:64], vt[:].rearrange("p (h d) -> p h d", d=64), [vt], [vb_])
                    nv = job["nvalid"]
                    if nv >= 128:
                        k.dma(job["k_out"][r0:r0 + 128, :], kt[:], [kt], [outT], key=kt)
                        k.dma(job["v_out"][r0:r0 + 128, :], vt[:], [vt], [outT], key=vt)
                    else:
                        k.dma(job["k_out"][0:nv, :], kt[0:nv, :], [kt], [outT], key=kt)
                        k.dma(job["v_out"][0:nv, :], vt[0:nv, :], [vt], [outT], key=vt)
                    k.dma(job["v_scr"][job["koff"] + r0:job["koff"] + r0 + 128, :], vb_[:].rearrange("p h d -> p (h d)"), [vb_], [T_v], key=vb_)
                    for kc in range(8):
                        k.mm(banks[3][:, 0:64], hT[:, kc, ts_], W1[:, kc, 1024:1088], [hT, W1], PQ[3], start=(kc == 0), stop=(kc == 7))
                    k.cp("act", kit[:], banks[3][:, 0:64], PQ[3], [kit])
                    if nv >= 128:
                        k.dma(job["ki_out"][r0:r0 + 128, :], kit[:], [kit], [outT], key=kit)
                    else:
                        k.dma(job["ki_out"][0:nv, :], kit[0:nv, :], [kit], [outT], key=kit)
                    for kc in range(8):
                        k.mm(banks[4][:, :], hT[:, kc, ts_], W1[:, kc, 2624:3136], [hT, W1], PQ[4], start=(kc == 0), stop=(kc == 7))
                    k.act(gbs[:, tt, :], banks[4][:, :], AF.Silu, PQ[4], [gbs])
                    for kc in range(8):
                        k.mm(banks[3][:, 128:136], hT[:, kc, ts_], W1[:, kc, 3136:3144], [hT, W1], PQ[3], start=(kc == 0), stop=(kc == 7))
                    k.cp("dve", bbab[:, tt, :], banks[3][:, 128:136], PQ[3], [bbab])
                for p in range(4):
                    bk = 5 + (p % 2)
                    for kc in range(8):
                        k.mm(banks[bk][:, 0:NT], W1[:, kc, p * 128:(p + 1) * 128], hT[:, kc, :], [W1, hT], PQ[bk], start=(kc == 0), stop=(kc == 7))
                    k.cp("act" if p % 2 == 0 else "dve", kf_t[:, p, :], banks[bk][:, 0:NT], PQ[bk], [kf_t])
                k.dma(job["kT_scr"][:, :, job["koff"] + t0:job["koff"] + t0 + NT], kf_t[:], [kf_t], [T_kT], key=kf_t)
                for kc in range(8):
                    k.mm(banks[7][0:64, 0:NT], W1[:, kc, 1024:1088], hT[:, kc, :], [W1, hT], PQ[7], start=(kc == 0), stop=(kc == 7))
                k.cp("act", kif_t[:], banks[7][0:64, 0:NT], PQ[7], [kif_t])
                k.dma(job["kiT_scr"][:, job["koff"] + t0:job["koff"] + t0 + NT], kif_t[:], [kif_t], [T_ki], key=kif_t)
                if KDBG < 2:
                    continue
                for c in range(12):
                    bk = 5 + (c % 3)
                    for kc in range(8):
                        k.mm(banks[bk][:, 0:NT], W1[:, kc, 1088 + c * 128:1088 + (c + 1) * 128], hT[:, kc, :], [W1, hT], PQ[bk], start=(kc == 0), stop=(kc == 7))
                    k.cp("act" if c % 2 == 0 else "dve", cin[:, c, 3:3 + NT], banks[bk][:, 0:NT], PQ[bk], [cin])
                for c in range(12):
                    k.ts("dve", cv[:, c, :], cin[:, c, 0:NT], convb[:, c, 0:1], ALU.mult, [cin, convb], [cv])
                    for j in range(1, 4):
                        k.stt(cv[:, c, :], cin[:, c, j:j + NT], convb[:, c, j:j + 1], cv[:, c, :], ALU.mult, ALU.add, [cin, convb, cv], [cv])
                if st == nsteps - 1:
                    k.dma(job["conv_out"], cin[:, :, job["nvalid_last"]:job["nvalid_last"] + 3], [cin], [outT], key=cin, slow=True)
                k.cp("pool", cin[:, :, 0:3], cin[:, :, NT:NT + 3], [cin], [cin])
                k.act(cv[:], cv[:], AF.Silu, [cv], [cv])
                for c in range(8):
                    k.act(sq2[:], cv[:, c, :], AF.Square, [cv], [sq2])
                    k.mm(banks[0][:, 0:NT], ones_bf[:], sq2[:], [ones_bf, sq2], PQ[0])
                    k.act(rs2[:], banks[0][:, 0:NT], AF.Sqrt, PQ[0] + [epsT], [rs2], bias=epsT[:, 0:1])
                    k.recip(rs2[:], rs2[:], [rs2], [rs2])
                    if c < 4:
                        k.stt(cv[:, c, :], cv[:, c, :], 128.0 ** -0.5, rs2[:], ALU.mult, ALU.mult, [cv, rs2], [cv])
                    else:
                        k.tt("dve", cv[:, c, :], cv[:, c, :], rs2[:], ALU.mult, [cv, rs2], [cv])
                    k.cp("pool", cvb[:, c, :], cv[:, c, :], [cv], [cvb])
                k.act(bet[:], bbab[:, :, 0:4], AF.Sigmoid, [bbab], [bet])
                if job["gval"] is not None:
                    for tt in range(ntt):
                        k.ts("dve", bet[:, tt, :], bet[:, tt, :], gval[:, job["gval"]:job["gval"] + 1], ALU.mult, [bet, gval], [bet])
                k.ts("dve", nbet[:], bet[:], -1.0, ALU.mult, [bet], [nbet])
                for tt in range(ntt):
                    k.tt("dve", gg[:, tt, :], bbab[:, tt, 4:8], dtb[:], ALU.add, [bbab, dtb], [gg])
                k.act(gg[:], gg[:], AF.Exp, [gg], [gg])
                k.act(gg[:], gg[:], AF.Ln, [gg], [gg], bias=1.0)
                for tt in range(ntt):
                    k.tt("dve", gg[:, tt, :], gg[:, tt, :], cA[:], ALU.mult, [gg, cA], [gg])
                    if job["gval"] is not None:
                        k.ts("dve", gg[:, tt, :], gg[:, tt, :], gval[:, job["gval"]:job["gval"] + 1], ALU.mult, [gg, gval], [gg])
                k.ts("dve", ngg[:], gg[:], -1.0, ALU.mult, [gg], [ngg])
                if KDBG < 3:
                    continue
                for tt in range(ntt):
                    ts_ = slice(tt * 128, (tt + 1) * 128)
                    G = PQ[0]
                    k.mm(banks[0][:, 0:4], Ublk, gg[:, tt, :], [cst, gg], G)
                    k.mm(banks[0][:, 4:8], Lsblk, gg[:, tt, :], [cst, gg], G)
                    k.mm(banks[0][:, 8:12], half0, gg[:, tt, :], [cst, gg], G)
                    k.mm(banks[0][:, 12:16], half1, gg[:, tt, :], [cst, gg], G)
                    k.act(egc[:, tt, :], banks[0][:, 0:4], AF.Exp, G, [egc])
                    k.act(ekd[:, tt, :], banks[0][:, 4:8], AF.Exp, G, [ekd])
                    k.act(egl[:, tt, :], banks[0][:, 8:16], AF.Exp, G, [egl])
                    k.tt("dve", bege[:, tt, :], bet[:, tt, :], egc[:, tt, :], ALU.mult, [bet, egc], [bege])
                    BK = ((1, 2), (3, 4), (5, 6), (7, 0))
                    H4 = range(4)
                    for h in H4:
                        T_ = HT[h]
                        k.cp("pool", T_["NGb"][:], ngg[:, tt, h:h + 1].to_broadcast([128, 128]), [ngg], [T_["NGb"]])
                    for h in H4:
                        T_ = HT[h]
                        b0, b1 = BK[h]
                        k.mm(pq(b0, 0), Ublk, gg[:, tt, h:h + 1].to_broadcast([128, 128]), [cst, gg], [PB[b0]], start=True, stop=False)
                        k.mm(pq(b0, 0), T_["NGb"][:], Ublk, [cst, T_["NGb"]], [PB[b0]], start=False, stop=True)
                        k.mm(pq(b1, 1), cvb[:, 4 + h, ts_], cvb[:, 4 + h, ts_], [cvb], [PB[b1]])
                    for h in H4:
                        T_ = HT[h]
                        b0, b1 = BK[h]
                        k.stt(T_["E1m"][:], pq(b0, 0), 0.0, NEGML, ALU.min, ALU.add, [PB[b0], cst], [T_["E1m"]])
                        k.stt(T_["E2m"][:], pq(b0, 0), 0.0, POSMU, ALU.max, ALU.add, [PB[b0], cst], [T_["E2m"]])
                    for h in H4:
                        T_ = HT[h]
                        k.act(T_["E1m"][:], T_["E1m"][:], AF.Exp, [T_["E1m"]], [T_["E1m"]])
                        k.act(T_["E2m"][:], T_["E2m"][:], AF.Exp, [T_["E2m"]], [T_["E2m"]], scale=-1.0)
                    for h in H4:
                        T_ = HT[h]
                        b0, b1 = BK[h]
                        k.mm(pq(b0, 2), cvb[:, 4 + h, ts_], cvb[:, h, ts_], [cvb], [PB[b0]])
                    for h in H4:
                        T_ = HT[h]
                        b0, b1 = BK[h]
                        k.stt(T_["N"][:], pq(b1, 1), nbet[:, tt, h:h + 1], T_["E1m"][:], ALU.mult, ALU.mult, [PB[b1], nbet, T_["E1m"]], [T_["N"]])
                    for h in H4:
                        T_ = HT[h]
                        b0, b1 = BK[h]
                        k.tt("dve", T_["attnT"][:], pq(b0, 2), T_["E2m"][:], ALU.mult, [PB[b0], T_["E2m"]], [T_["attnT"]])
                    for h in H4:
                        T_ = HT[h]
                        b0, b1 = BK[h]
                        k.tr(pq(b1, 3), cv[:, 4 + h, ts_], ident, [cv, cst], [PB[b1]])
                        k.tr(pq(b0, 0), cv[:, 8 + h, ts_], ident, [cv, cst], [PB[b0]])
                    for h in H4:
                        T_ = HT[h]
                        b0, b1 = BK[h]
                        k.ts("dve", T_["kbg"][:], pq(b1, 3), bege[:, tt, h:h + 1], ALU.mult, [PB[b1], bege], [T_["kbg"]])
                        k.ts("dve", T_["kd"][:], pq(b1, 3), ekd[:, tt, h:h + 1], ALU.mult, [PB[b1], ekd], [T_["kd"]])
                    for h in H4:
                        T_ = HT[h]
                        b0, b1 = BK[h]
                        k.ts("dve", T_["vb"][:], pq(b0, 0), bet[:, tt, h:h + 1], ALU.mult, [PB[b0], bet], [T_["vb"]])
                    for h in H4:
                        T_ = HT[h]
                        b0, b1 = BK[h]
                        k.mm(pq(b1, 1), T_["N"][:], ident_bf[:], [T_["N"], ident_bf], [PB[b1]])
                    for h in H4:
                        T_ = HT[h]
                        b0, b1 = BK[h]
                        k.cp("act", T_["Nt"][:], pq(b1, 1), [PB[b1]], [T_["Nt"]])
                    for h in H4:
                        T_ = HT[h]
                        b0, b1 = BK[h]
                        k.tt("dve", T_["P"][:], pq(b1, 1), ident, ALU.add, [PB[b1], cst], [T_["P"]])
                    for h in H4:
                        T_ = HT[h]
                        k.cp("pool", T_["Pb"][:], T_["P"][:], [T_["P"]], [T_["Pb"]])
                    if KDBG < 4:
                        continue
                    for lv in range(1, 6):
                        for h in range(4):
                            T_ = HT[h]
                            b0, b1 = ((1, 2), (3, 4), (5, 6), (7, 0))[h]
                            Mp = T_["N"] if lv == 1 else T_[f"M{(lv - 1) % 2}"]
                            Mtp = T_["Nt"] if lv == 1 else T_[f"Mt{(lv - 1) % 2}"]
                            Mn = T_[f"M{lv % 2}"]
                            Mtn = T_[f"Mt{lv % 2}"]
                            k.mm(pq(b1, 2), Mtp[:], Mp[:], [Mtp, Mp], [PB[b1]])
                            if lv < 5:
                                k.mm(pq(b0, 1), Mp[:], Mtp[:], [Mtp, Mp], [PB[b0]])
                            k.cp("act", Mn[:], pq(b1, 2), [PB[b1]], [Mn])
                            if lv < 5:
                                k.cp("dve", Mtn[:], pq(b0, 1), [PB[b0]], [Mtn])
                        for h in range(4):
                            T_ = HT[h]
                            b0, b1 = ((1, 2), (3, 4), (5, 6), (7, 0))[h]
                            Mn = T_[f"M{lv % 2}"]
                            k.mm(pq(b1, 0), Mn[:], T_["Pb"][:], [Mn, T_["Pb"]], [PB[b1]])
                            k.tt("dve", T_["P"][:], T_["P"][:], pq(b1, 0), ALU.add, [T_["P"], PB[b1]], [T_["P"]])
                            if lv < 5:
                                k.cp("pool", T_["Pb"][:], T_["P"][:], [T_["P"]], [T_["Pb"]])
                    if KDBG < 5:
                        continue
                    BK = ((1, 2), (3, 4), (5, 6), (7, 0))
                    for h in range(4):
                        T_ = HT[h]
                        b0, b1 = BK[h]
                        k.mm(pq(b0, 1), T_["P"][:], T_["vb"][:], [T_["P"], T_["vb"]], [PQ[b0][1]])
                        k.cp("act", T_["u"][:], pq(b0, 1), [PQ[b0][1]], [T_["u"]])
                    for h in range(4):
                        T_ = HT[h]
                        b0, b1 = BK[h]
                        k.mm(pq(b1, 2), T_["kbg"][:], T_["P"][:], [T_["P"], T_["kbg"]], [PQ[b1][2]])
                        k.cp("dve", T_["wT"][:], pq(b1, 2), [PQ[b1][2]], [T_["wT"]])
                    for c in range(2):
                        r = slice(64 * c, 64 * c + 64)
                        for h in range(4):
                            T_ = HT[h]
                            b0, b1 = BK[h]
                            k.mm(banks[b0][r, 384:512], T_["wT"][:, r], Sst[h][:], [T_["wT"], Sst[h]], [PQ[b0][3]])
                            k.mm(banks[b0][r, 0:128], cv[:, h, tt * 128 + 64 * c:tt * 128 + 64 * c + 64], Sst[h][:], [cv, Sst[h]], [PQ[b0][0]])
                        for h in range(4):
                            T_ = HT[h]
                            b0, b1 = BK[h]
                            k.tt("dve", T_["vnew"][r, :], T_["u"][r, :], banks[b0][r, 384:512], ALU.subtract, [T_["u"], PQ[b0][3]], [T_["vnew"]])
                        for h in range(4):
                            T_ = HT[h]
                            b0, b1 = BK[h]
                            k.mm(banks[b1][r, 128:256], T_["attnT"][r, r], T_["vnew"][r, :], [T_["attnT"], T_["vnew"]], [PQ[b1][1]])
                            k.mm(pq(b1, 2), T_["kd"][r, :], T_["vnew"][r, :], [T_["kd"], T_["vnew"]], [PQ[b1][2]])
                        for h in range(4):
                            T_ = HT[h]
                            b0, b1 = BK[h]
                            k.stt(Sst[h][:], Sst[h][:], egl[:, tt, 4 * c + h:4 * c + h + 1], pq(b1, 2), ALU.mult, ALU.add, [Sst[h], egl, PQ[b1][2]], [Sst[h]])
                        for h in range(4):
                            T_ = HT[h]
                            b0, b1 = BK[h]
                            k.cp("act", T_["o1"][r, :], banks[b1][r, 128:256], [PQ[b1][1]], [T_["o1"]])
                        for h in range(4):
                            T_ = HT[h]
                            b0, b1 = BK[h]
                            k.stt(T_["o"][r, :], banks[b0][r, 0:128], egc[r, tt, h:h + 1], T_["o1"][r, :], ALU.mult, ALU.add, [PQ[b0][0], egc, T_["o1"]], [T_["o"]])
                    for h in range(4):
                        T_ = HT[h]
                        k.act(T_["tmp"][:], T_["o"][:], AF.Square, [T_["o"]], [T_["tmp"], T_["ms"]], accum=T_["ms"][:, 0:1])
                    for h in range(4):
                        T_ = HT[h]
                        k.act(T_["ms"][:], T_["ms"][:], AF.Sqrt, [T_["ms"], epsT], [T_["ms"]], scale=1.0 / 128, bias=epsT[:, 0:1])
                    for h in range(4):
                        T_ = HT[h]
                        k.recip(T_["ms"][:], T_["ms"][:], [T_["ms"]], [T_["ms"]])
                    for h in range(4):
                        T_ = HT[h]
                        k.stt(T_["tmp"][:], T_["o"][:], T_["ms"][:, 0:1], ngdn[:], ALU.mult, ALU.mult, [T_["o"], T_["ms"], ngdn], [T_["tmp"]])
                    for h in range(4):
                        T_ = HT[h]
                        k.tt("pool", T_["obn"][:], T_["tmp"][:], gbs[:, tt, h * 128:(h + 1) * 128], ALU.mult, [T_["tmp"], gbs], [T_["obn"]])
                    for h in range(4):
                        T_ = HT[h]
                        b0, b1 = BK[h]
                        k.tr(banks[b1][:, 384:512], T_["obn"][:], ident_bf[:], [T_["obn"], ident_bf], [PQ[b1][3]])
                    for h in range(4):
                        b0, b1 = BK[h]
                        k.cp("act", obT_t[:, h, ts_], banks[b1][:, 384:512], [PQ[b1][3]], [obT_t])
                if KDBG >= 5:
                    k.dma(job["obT_scr"][:, :, t0:t0 + NT], obT_t[:], [obT_t], [T_ob], key=obT_t)
            for h in range(4):
                k.dma(job["S_out"][h], Sst[h][:], [Sst[h]], [outT], key=Sst[h])
            S.barrier()

        with ExitStack() as es:
          if KDBG >= 0 and 'p' in KJOB:
            all_pass(es, dict(NT=NTP, L=LKP, x=I["xa"], S0=None, conv0=None, nvalid=128, nvalid_last=NTP,
                              k_out=O["k_all"], v_out=O["v_all"], ki_out=O["ki_all"], kT_scr=kT_p, v_scr=v_p, kiT_scr=kiT_p,
                              koff=0, obT_scr=obT_p, S_out=O["gdn_p"], conv_out=O["gconv_p"], gval=None))
        for sb_ in range(2 if (KDBG >= 0 and 's' in KJOB) else 0):
            with ExitStack() as es:
                all_pass(es, dict(NT=128, L=128, x=I["xs"][sb_], S0=I["state_gdn"][sb_], conv0=I["gconvT"][sb_], nvalid=16, nvalid_last=16,
                                  k_out=O["k_s"][sb_], v_out=O["v_s"][sb_], ki_out=O["ki_s"][sb_], kT_scr=kT_s[sb_], v_scr=v_s[sb_],
                                  kiT_scr=kiT_s[sb_], koff=PAST, obT_scr=obT_s[sb_], S_out=O["gdn_s"][sb_], conv_out=O["gconv_s"][sb_], gval=0))

        convf = sbt(es0, "convf", [128, 22, 3], F32)
        k.dma(convf[:], I["convf"], [], [convf])
        iota = sbt(es0, "iota", [128, 512], F32)
        k.dma(iota[:], I["iota"], [], [iota])
        lims = sbt(es0, "lims", [128, 24], F32)
        k.dma(lims[:], I["lims"], [], [lims])
        BT = sbt(es0, "BT", [128, 2, 8, 128], BF16)
        with ExitStack() as es:
            rb = sbt(es, "rb", [32, 8], F32)
            oh = sbt(es, "oh", [32, 384], F32)
            rb15 = sbt(es, "rb15", [8, 1], F32)
            tabs = sbt(es, "tabs", [8, 384], F32)
            BTf = sbt(es, "BTf", [128, 2, 8, 128], F32)
            k.dma(rb[:], I["relb"], [], [rb])
            k.dma(oh[:], I["ohrev"], [], [oh])
            k.dma(rb15[:], I["relb15"], [], [rb15])
            k.mm(banks[0][0:8, 0:384], rb[:], oh[:], [rb, oh], PQ[0])
            k.ts("dve", tabs[:], banks[0][0:8, 0:384], rb15[:, 0:1], ALU.subtract, PQ[0] + [rb15], [tabs])
            k.dma(tab_scr, tabs[:], [tabs], [T_tab], key=tabs)
            for kp in range(128):
                for dd in range(2):
                    base = 127 + 128 * dd - kp
                    k.dma(BTf[kp:kp + 1, dd, :, :], tab_scr[:, base:base + 128].unsqueeze(0), [T_tab], [BTf])
            k.cp("dve", BT[:], BTf[:], [BTf], [BT])
            ckf = sbt(es, "ckf", [128, 4, 512], F32)
            ckb = sbt(es, "ckb", [128, 4, 512], BF16)
            cvf = sbt(es, "cvf", [128, 512], F32)
            cvb = sbt(es, "cvb", [128, 512], BF16)
            cvb2 = sbt(es, "cvb2", [128, 8, 66], BF16)
            k.memset("pool", cvb2[:, :, 64:66], 1.0, [cvb2])
            for sb_ in range(2):
                ckv = I["cache_kT"][sb_].rearrange("(p r) t -> r p t", r=128)
                for pc in range(4):
                    k.dma(ckf[:], ckv[:, :, pc * 512:(pc + 1) * 512], [], [ckf])
                    k.cp("dve", ckb[:], ckf[:], [ckf], [ckb])
                    k.dma(kT_s[sb_][:, :, pc * 512:(pc + 1) * 512], ckb[:], [ckb], [T_kT], key=ckb)
                    k.dma(cvf[0:64, :], I["cache_kiT"][sb_][:, pc * 512:(pc + 1) * 512], [], [cvf])
                    k.cp("pool", cvb[0:64, :], cvf[0:64, :], [cvf], [cvb])
                    k.dma(kiT_s[sb_][:, pc * 512:(pc + 1) * 512], cvb[0:64, :], [cvb], [T_ki], key=cvb)
                for rb_ in range(16):
                    k.dma(cvf[:], I["cache_v"][sb_][rb_ * 128:(rb_ + 1) * 128, :], [], [cvf])
                    k.cp("pool", cvb2[:, :, 0:64], cvf[:].rearrange("p (h d) -> p h d", d=64), [cvf], [cvb2])
                    k.dma(v_s[sb_][rb_ * 128:(rb_ + 1) * 128, :], cvb2[:].rearrange("p h d -> p (h d)"), [cvb2], [T_v], key=cvb2)
            S.barrier()

        def pieces(n):
            out, c = [], 0
            while c < n:
                w = min(512, n - c)
                out.append((c, w))
                c += w
            return out

        def rms(src, dst, c0, n, sqb, rsb, bank=0):
            k.act(sqb[:, :, 0:n], src[:, :, c0:c0 + n], AF.Square, [src], [sqb])
            for kc in range(8):
                k.mm(banks[bank][:, 0:n], ones_bf[:], sqb[:, kc, 0:n], [ones_bf, sqb], PQ[bank], start=(kc == 0), stop=(kc == 7))
            k.act(rsb[:, 0:n], banks[bank][:, 0:n], AF.Sqrt, PQ[bank] + [epsT], [rsb], scale=1.0 / D, bias=epsT[:, 0:1])
            k.recip(rsb[:, 0:n], rsb[:, 0:n], [rsb], [rsb])
            if dst is not None:
                for kc in range(8):
                    k.tt("dve" if kc % 2 == 0 else "pool", dst[:, kc, c0:c0 + n], src[:, kc, c0:c0 + n], rsb[:, 0:n], ALU.mult, [src, rsb], [dst])

        NIT = 26

        def own_block(jb):
            NQ = jb["NQ"]
            NTOK = 128 * NQ
            with ExitStack() as esA:
                xo = sbt(esA, "xo", [128, 8, NTOK], F32)
                hT = sbt(esA, "hTo", [128, 8, NTOK], BF16)
                oaT = sbt(esA, "oaT", [128, 4, NTOK], BF16)
                sqb = sbt(esA, "sqb", [128, 8, 512], BF16)
                rsb = sbt(esA, "rsb", [128, 512], F32)
                k.dma(xo[:], jb["xsrc"], [], [xo])
                for (c0, n) in pieces(NTOK):
                    rms(xo, hT, c0, n, sqb, rsb)
                if KP2 < 1:
                    return
                with ExitStack() as es:
                    W2 = sbt(es, "W2", [128, 8, 1032], BF16)
                    k.dma(W2[:, :, 0:512], wscr_in[:, :, C_QA:C_QA + 512], [T_wscr_in], [W2])
                    k.dma(W2[:, :, 512:1024], wscr_in[:, :, C_QI:C_QI + 512], [T_wscr_in], [W2])
                    k.dma(W2[:, :, 1024:1032], wscr_in[:, :, C_WI:C_WI + 8], [T_wscr_in], [W2])
                    qaT = sbt(es, "qaT", [128, 4, NTOK], BF16)
                    qiT = sbt(es, "qiT", [128, 4, NTOK], BF16)
                    wiT = sbt(es, "wiT", [128, NQ, 8], F32)
                    n_ = 0
                    for (dstq, cb) in ((qaT, 0), (qiT, 512)):
                        for p in range(4):
                            for (c0, n) in pieces(NTOK):
                                bk = n_ % 2
                                n_ += 1
                                for kc in range(8):
                                    k.mm(banks[bk][:, 0:n], W2[:, kc, cb + p * 128:cb + (p + 1) * 128], hT[:, kc, c0:c0 + n], [W2, hT], PQ[bk], start=(kc == 0), stop=(kc == 7))
                                k.act(dstq[:, p, c0:c0 + n], banks[bk][:, 0:n], AF.Copy, PQ[bk], [dstq], scale=0.125)
                    for qb in range(NQ):
                        for kc in range(8):
                            k.mm(banks[2][:, 0:8], hT[:, kc, qb * 128:(qb + 1) * 128], W2[:, kc, 1024:1032], [W2, hT], PQ[2], start=(kc == 0), stop=(kc == 7))
                        k.ts("dve", wiT[:, qb, :], banks[2][:, 0:8], 8.0 ** -0.5, ALU.mult, PQ[2], [wiT])
                    if KP2 < 2:
                        return
                    LMAX = max(jb["L"])
                    kiT2 = sbt(es, "kiT2", [128, LMAX], BF16)
                    sc = sbt(es, "sc", [128, LMAX], F32)
                    Mb = sbt(es, "Mb", [128, LMAX], BF16)
                    MT = sbt(es, "MT", [128, LMAX // 128, 128], BF16)
                    tmpf = [sbt(es, f"tmpf{i}", [128, 512], F32) for i in range(2)]
                    pen = sbt(es, "pen", [128, 512], F32)
                    Kc = [sbt(es, f"Kc{i}", [128, 4, 512], BF16) for i in range(2)]
                    Vc = [sbt(es, f"Vc{i}", [128, 4, 8, 66], BF16) for i in range(2)]
                    PT = [sbt(es, f"PT{i}", [128, 4, 128], BF16) for i in range(4)]
                    oa = sbt(es, "oa", [128, 8, 64], BF16)
                    den = sbt(es, "den", [128, 8], F32)
                    sm = {nm: sbt(es, nm, [128, 1], F32) for nm in ("mx", "lo", "hi", "mid", "cnt", "ge", "d1", "d2", "off")}
                    for i in range(2):
                        k.memset("pool", Vc[i][:, :, :, 64:65], 1.0, [Vc[i]])
                    lgs = sbt(es, "lgs", [128, 512], F32)

                    def gen_S(qb):
                        L = jb["L"][qb]
                        nkb = L // 128
                        q0 = qb * 128
                        lcol = jb["limcol"] + qb
                        k.dma(kiT2[0:64, 0:L], jb["kiT"][:, 0:L], [T_ki], [kiT2])
                        k.dma(kiT2[64:128, 0:L], jb["kiT"][:, 0:L], [T_ki], [kiT2])
                        tiles = pieces(L)
                        n_ = 0
                        for (c0, w) in tiles:
                            for h in range(8):
                                p, r0 = h // 2, 64 * (h % 2)
                                bk = n_ % 2
                                tf = tmpf[n_ % 2]
                                n_ += 1
                                k.mm(banks[bk][:, 0:w], qiT[r0:r0 + 64, p, q0:q0 + 128], kiT2[r0:r0 + 64, c0:c0 + w], [qiT, kiT2], PQ[bk])
                                k.act(tf[:, 0:w], banks[bk][:, 0:w], AF.Relu, PQ[bk], [tf])
                                if h == 0:
                                    k.ts("dve", sc[:, c0:c0 + w], tf[:, 0:w], wiT[:, qb, 0:1], ALU.mult, [tf, wiT], [sc])
                                else:
                                    k.stt(sc[:, c0:c0 + w], tf[:, 0:w], wiT[:, qb, h:h + 1], sc[:, c0:c0 + w], ALU.mult, ALU.add, [tf, wiT, sc], [sc])
                            yield
                        k.S.op("dve", lambda e, o=sm["mx"][:], i=sc[:, 0:L]: e.tensor_reduce(out=o, in_=i, axis=AX.X, op=ALU.max, apply_absolute_value=True), reads=[sc], writes=[sm["mx"]])
                        k.ts("dve", sm["hi"][:], sm["mx"][:], 1.0, ALU.add, [sm["mx"]], [sm["hi"]])
                        k.ts("dve", sm["lo"][:], sm["mx"][:], -1.0, ALU.mult, [sm["mx"]], [sm["lo"]], s2=-1.0, op1=ALU.add)
                        (c0, w) = tiles[-1]
                        k.ts("dve", sm["off"][:], lims[:, lcol:lcol + 1], float(-c0), ALU.add, [lims], [sm["off"]])
                        k.ts("dve", pen[:, 0:w], iota[:, 0:w], sm["off"][:, 0:1], ALU.is_ge, [iota, sm["off"]], [pen], s2=NEG, op1=ALU.mult)
                        k.tt("dve", sc[:, c0:c0 + w], sc[:, c0:c0 + w], pen[:, 0:w], ALU.add, [sc, pen], [sc])
                        if jb["lolim"] is not None:
                            for ti in range(min(3, len(tiles))):
                                (c0, w) = tiles[ti]
                                k.ts("dve", sm["off"][:], lims[:, jb["lolim"]:jb["lolim"] + 1], float(-c0), ALU.add, [lims], [sm["off"]])
                                k.ts("dve", pen[:, 0:w], iota[:, 0:w], sm["off"][:, 0:1], ALU.is_lt, [iota, sm["off"]], [pen], s2=NEG, op1=ALU.mult)
                                k.tt("dve", sc[:, c0:c0 + w], sc[:, c0:c0 + w], pen[:, 0:w], ALU.add, [sc, pen], [sc])

                    def do_B(qb):
                        L = jb["L"][qb]
                        nkb = L // 128
                        q0 = qb * 128
                        lcol = jb["limcol"] + qb
                        k.tt("dve", sm["d1"][:], sm["hi"][:], sm["lo"][:], ALU.subtract, [sm["hi"], sm["lo"]], [sm["d1"]])
                        for it in range(NIT):
                            k.ts("dve", sm["d1"][:], sm["d1"][:], 0.5, ALU.mult, [sm["d1"]], [sm["d1"]])
                            k.tt("dve", sm["mid"][:], sm["lo"][:], sm["d1"][:], ALU.add, [sm["lo"], sm["d1"]], [sm["mid"]])
                            k.ts("dve", Mb[:, 0:L], sc[:, 0:L], sm["mid"][:, 0:1], ALU.is_gt, [sc, sm["mid"]], [Mb, sm["cnt"]], s2=0.0, op1=ALU.add, accum=sm["cnt"][:, 0:1])
                            k.stt(sm["ge"][:], sm["cnt"][:], 255.5, sm["d1"][:], ALU.is_ge, ALU.mult, [sm["cnt"], sm["d1"]], [sm["ge"]])
                            k.tt("dve", sm["lo"][:], sm["lo"][:], sm["ge"][:], ALU.add, [sm["lo"], sm["ge"]], [sm["lo"]])
                        k.ts("dve", Mb[:, 0:L], sc[:, 0:L], sm["lo"][:, 0:1], ALU.is_gt, [sc, sm["lo"]], [Mb])

                    def do_T(qb):
                        L = jb["L"][qb]
                        nkb = L // 128
                        q0 = qb * 128
                        lcol = jb["limcol"] + qb
                        for kb0 in range(0, nkb, 4):
                            nb = min(4, nkb - kb0)
                            bk = 2 + (kb0 // 4) % 2
                            for i in range(nb):
                                k.mm(banks[bk][:, i * 128:(i + 1) * 128], Mb[:, (kb0 + i) * 128:(kb0 + i + 1) * 128], ident_bf[:], [Mb, ident_bf], PQ[bk])
                            k.cp("act", MT[:, kb0:kb0 + nb, :].rearrange("p a b -> p (a b)"), banks[bk][:, 0:nb * 128], PQ[bk], [MT])

                    def gen_A(qb):
                        L = jb["L"][qb]
                        nkb = L // 128
                        q0 = qb * 128
                        nch = (L + 511) // 512
                        DEPTH = 2

                        def loads(ch):
                            wch = min(512, L - 512 * ch)
                            Kt, Vt = Kc[ch % 2], Vc[ch % 2]
                            k.dma(Kt[:, :, 0:wch], jb["kT"][:, :, 512 * ch:512 * ch + wch], [T_kT], [Kt])
                            k.dma(Vt[:, 0:wch // 128, :, :], jb["v"][512 * ch:512 * ch + wch, :].rearrange("(i p) (h d) -> p i h d", p=128, d=66), [T_v], [Vt])

                        units = []
                        for ch in range(nch):
                            wch = min(512, L - 512 * ch)
                            for i in range(wch // 128):
                                for g in range(2):
                                    units.append((ch, i, g))
                        last_of_chunk = {}
                        for idx, (ch, i, g) in enumerate(units):
                            last_of_chunk[ch] = idx

                        def logits(idx):
                            ch, i, g = units[idx]
                            bk = 2 + idx % 4
                            Kt = Kc[ch % 2]
                            r0 = 64 * g
                            for e4 in range(4):
                                k.mm(banks[bk][:, e4 * 128:(e4 + 1) * 128], Kt[r0:r0 + 64, e4, i * 128:(i + 1) * 128], qaT[r0:r0 + 64, e4, q0:q0 + 128], [Kt, qaT], PQ[bk], start=True, stop=True)

                        def softmax_pv(idx):
                            ch, i, g = units[idx]
                            kbg = 4 * ch + i
                            bk = 2 + idx % 4
                            pt = PT[idx % 4]
                            Vt = Vc[ch % 2]
                            diag = kbg >= nkb - 2
                            dd = 0 if kbg == nkb - 1 else 1
                            if diag:
                                k.tt("dve", lgs[:, :].rearrange("p (h q) -> p h q", q=128), banks[bk][:, :].rearrange("p (h q) -> p h q", q=128), BT[:, dd, g:8:2, :], ALU.add, PQ[bk] + [BT], [lgs])
                                k.act(pt[:].rearrange("p h q -> p (h q)"), lgs[:, :], AF.Exp, [lgs], [pt])
                            else:
                                k.act(pt[:].rearrange("p h q -> p (h q)"), banks[bk][:, :], AF.Exp, PQ[bk], [pt])
                            k.tt("dve", pt[:], pt[:], MT[:, kbg:kbg + 1, :].to_broadcast([128, 4, 128]), ALU.mult, [pt, MT], [pt])
                            for e4 in range(4):
                                h = 2 * e4 + g
                                k.mm(banks[6 + g][:, e4 * 65:(e4 + 1) * 65], pt[:, e4, :], Vt[:, i, h, 0:65], [pt, Vt], PQ[6 + g], start=(kbg == 0 and e4 == 0), stop=(kbg == nkb - 1 and e4 == 3))
                            if last_of_chunk[ch] == idx and ch + 2 < nch:
                                loads(ch + 2)

                        loads(0)
                        if nch > 1:
                            loads(1)
                        nu = len(units)
                        for idx in range(nu + DEPTH):
                            if idx < nu:
                                logits(idx)
                            if idx - DEPTH >= 0:
                                softmax_pv(idx - DEPTH)
                            if idx % 8 == 7:
                                yield

                    def do_N(qb):
                        L = jb["L"][qb]
                        nkb = L // 128
                        q0 = qb * 128
                        lcol = jb["limcol"] + qb
                        for g in range(2):
                            ov = banks[6 + g][:, 0:260].rearrange("p (h d) -> p h d", d=65)
                            k.ts("dve", den[:, 4 * g:4 * g + 4], ov[:, :, 64], 1e-30, ALU.add, PQ[6 + g], [den])
                            k.recip(den[:, 4 * g:4 * g + 4], den[:, 4 * g:4 * g + 4], [den], [den])
                            k.tt("dve", oa[:, g:8:2, :], ov[:, :, 0:64], den[:, 4 * g:4 * g + 4].unsqueeze(2).to_broadcast([128, 4, 64]), ALU.mult, PQ[6 + g] + [den], [oa])
                        oaf = oa[:].rearrange("p h d -> p (h d)")
                        for c in range(4):
                            k.mm(banks[2][:, c * 128:(c + 1) * 128], oaf[:, c * 128:(c + 1) * 128], ident_bf[:], [oa, ident_bf], PQ[2])
                        k.cp("act", oaT[:, :, q0:q0 + 128], banks[2][:, :].rearrange("p (c t) -> p c t", t=128), PQ[2], [oaT])

                    def drain(g):
                        for _ in g:
                            pass

                    def interleave(ga, gb):
                        alive = [ga, gb]
                        while alive:
                            for g_ in list(alive):
                                try:
                                    next(g_)
                                except StopIteration:
                                    alive.remove(g_)

                    drain(gen_S(0))
                    for qb in range(NQ):
                        do_B(qb)
                        do_T(qb)
                        if qb + 1 < NQ:
                            interleave(gen_A(qb), gen_S(qb + 1))
                        else:
                            drain(gen_A(qb))
                        do_N(qb)
                    S.barrier()
                if KP2 < 8:
                    return
                with ExitStack() as es:
                    Wpa = sbt(es, "Wpa", [128, 4, D], BF16)
                    Wpb = sbt(es, "Wpb", [128, 4, D], BF16)
                    Wg = sbt(es, "Wg", [128, 8, 2048], BF16)
                    Wo = sbt(es, "Wo", [128, 8, D], BF16)
                    Wpl = sbt(es, "Wpl", [128, 2, D], BF16)
                    k.dma(Wpa[:], ws_pa, [T_wscr_in], [Wpa])
                    k.dma(Wpb[:], ws_pb, [T_wscr_in], [Wpb])
                    k.dma(Wg[:], wscr_in[:, :, C_GA:C_GA + 2048], [T_wscr_in], [Wg])
                    k.dma(Wpl[:], ws_ple, [T_wscr_in], [Wpl])
                    obT = sbt(es, "obT", [128, 4, NTOK], BF16)
                    k.dma(obT[:], jb["obT"], [T_ob], [obT])
                    pf = sbt(es, "pf", [128, 2, NTOK], F32)
                    pb = sbt(es, "pb", [128, 2, NTOK], BF16)
                    k.dma(pf[:], jb["psrc"], [], [pf])
                    k.cp("pool", pb[:], pf[:], [pf], [pb])
                    mixT = sbt(es, "mixT", [128, 8, 512], BF16)
                    h2T = sbt(es, "h2T", [128, 8, 512], BF16)
                    actT = sbt(es, "actT", [128, 22, 512], BF16)
                    ughalo = sbt(es, "ughalo", [128, 22, 2], F32)
                    fco = sbt(es, "fco", [128, 22, 2], F32)
                    ugc = [sbt(es, f"ugc{i}", [128, 514], F32) for i in range(2)]
                    cva = [sbt(es, f"cva{i}", [128, 512], F32) for i in range(2)]
                    sga = sbt(es, "sga", [128, 512], F32)
                    sgb = sbt(es, "sgb", [128, 512], F32)
                    t1 = sbt(es, "t1", [128, 512], F32)
                    wug = [sbt(es, f"wug{i}", [128, 8, 128], BF16) for i in range(2)]
                    wuv = [sbt(es, f"wuv{i}", [128, 8, 128], BF16) for i in range(2)]
                    wdn = [sbt(es, "wdn0", [128, 22, 128], BF16)] * 2
                    ytok = sbt(es, "ytok", [128, D], F32)
                    if jb["fhalo"] is not None:
                        k.dma(ughalo[:], jb["fhalo"], [], [ughalo])
                    for (c0, n, halo) in jb["segs"]:
                        sg = slice(c0, c0 + n)
                        k.dma(Wo[:], ws_out, [T_wscr_in], [Wo])
                        for c in range(8):
                            cs = slice(c * 128, (c + 1) * 128)
                            bo = 4 * (c % 2)
                            for kc in range(4):
                                k.mm(banks[bo + 0][:, 0:n], Wpa[:, kc, cs], oaT[:, kc, sg], [Wpa, oaT], PQ[bo + 0], start=(kc == 0), stop=(kc == 3))
                            for kc in range(8):
                                k.mm(banks[bo + 1][:, 0:n], Wg[:, kc, cs], hT[:, kc, sg], [Wg, hT], PQ[bo + 1], start=(kc == 0), stop=(kc == 7))
                            k.act(sga[:, 0:n], banks[bo + 1][:, 0:n], AF.Sigmoid, PQ[bo + 1], [sga])
                            k.tt("dve", t1[:, 0:n], banks[bo + 0][:, 0:n], sga[:, 0:n], ALU.mult, PQ[bo + 0] + [sga], [t1])
                            for kc in range(4):
                                k.mm(banks[bo + 2][:, 0:n], Wpb[:, kc, cs], obT[:, kc, sg], [Wpb, obT], PQ[bo + 2], start=(kc == 0), stop=(kc == 3))
                            for kc in range(8):
                                k.mm(banks[bo + 3][:, 0:n], Wg[:, kc, 1024 + c * 128:1024 + (c + 1) * 128], hT[:, kc, sg], [Wg, hT], PQ[bo + 3], start=(kc == 0), stop=(kc == 7))
                            k.act(sgb[:, 0:n], banks[bo + 3][:, 0:n], AF.Sigmoid, PQ[bo + 3], [sgb])
                            k.tt("dve", sgb[:, 0:n], banks[bo + 2][:, 0:n], sgb[:, 0:n], ALU.mult, PQ[bo + 2] + [sgb], [sgb])
                            k.tt("pool", mixT[:, c, 0:n], t1[:, 0:n], sgb[:, 0:n], ALU.add, [t1, sgb], [mixT])
                        for c in range(8):
                            cs = slice(c * 128, (c + 1) * 128)
                            bk = 4 + c % 2
                            for kc in range(8):
                                k.mm(banks[bk][:, 0:n], Wo[:, kc, cs], mixT[:, kc, 0:n], [Wo, mixT], PQ[bk], start=(kc == 0), stop=(kc == 7))
                            k.tt("dve", xo[:, c, sg], xo[:, c, sg], banks[bk][:, 0:n], ALU.add, [xo] + PQ[bk], [xo])
                        k.act(sqb[:, :, 0:n], xo[:, :, sg], AF.Square, [xo], [sqb])
                        for kc in range(8):
                            k.mm(banks[0][:, 0:n], ones_bf[:], sqb[:, kc, 0:n], [ones_bf, sqb], PQ[0], start=(kc == 0), stop=(kc == 7))
                        k.act(rsb[:, 0:n], banks[0][:, 0:n], AF.Sqrt, PQ[0] + [epsT], [rsb], scale=1.0 / D, bias=epsT[:, 0:1])
                        k.recip(rsb[:, 0:n], rsb[:, 0:n], [rsb], [rsb])
                        for kc in range(8):
                            k.tt("dve" if kc % 2 == 0 else "pool", h2T[:, kc, 0:n], xo[:, kc, sg], rsb[:, 0:n], ALU.mult, [xo, rsb], [h2T])
                        for cc in range(22):
                            wg_, wv_ = wug[cc % 2], wuv[cc % 2]
                            k.dma(wg_[:], ws_up[:, :, cc * 128:(cc + 1) * 128], [T_wscr_in], [wg_])
                            bk = 1 + cc % 2
                            for kc in range(8):
                                k.mm(banks[bk][:, 0:n], wg_[:, kc, :], h2T[:, kc, 0:n], [wg_, h2T], PQ[bk], start=(kc == 0), stop=(kc == 7))
                            if halo:
                                k.cp("act", ughalo[:, cc, 0:n], banks[bk][:, 0:n], PQ[bk], [ughalo])
                                continue
                            k.dma(wv_[:], ws_up[:, :, DFF + cc * 128:DFF + (cc + 1) * 128], [T_wscr_in], [wv_])
                            ug, ca = ugc[cc % 2], cva[cc % 2]
                            k.cp("act", ug[:, 2:2 + n], banks[bk][:, 0:n], PQ[bk], [ug])
                            k.cp("pool", ug[:, 0:2], ughalo[:, cc, :], [ughalo], [ug])
                            k.ts("dve", ca[:, 0:n], ug[:, 0:n], convf[:, cc, 0:1], ALU.mult, [ug, convf], [ca])
                            k.stt(ca[:, 0:n], ug[:, 1:1 + n], convf[:, cc, 1:2], ca[:, 0:n], ALU.mult, ALU.add, [ug, convf, ca], [ca])
                            k.stt(ca[:, 0:n], ug[:, 2:2 + n], convf[:, cc, 2:3], ca[:, 0:n], ALU.mult, ALU.add, [ug, convf, ca], [ca])
                            k.act(ca[:, 0:n], ca[:, 0:n], AF.Gelu_apprx_tanh, [ca], [ca])
                            nv = jb["nvalid"]
                            k.cp("pool", fco[:, cc, :], ug[:, nv:nv + 2], [ug], [fco])
                            bk2 = 3 + cc % 2
                            for kc in range(8):
                                k.mm(banks[bk2][:, 0:n], wv_[:, kc, :], h2T[:, kc, 0:n], [wv_, h2T], PQ[bk2], start=(kc == 0), stop=(kc == 7))
                            k.tt("dve", actT[:, cc, 0:n], ca[:, 0:n], banks[bk2][:, 0:n], ALU.mult, [ca] + PQ[bk2], [actT])
                        if halo:
                            continue
                        for c in range(8):
                            wd_ = wdn[c % 2]
                            k.dma(wd_[:], ws_down[:, :, c * 128:(c + 1) * 128], [T_wscr_in], [wd_])
                            bk = 5 + c % 2
                            for cc in range(22):
                                k.mm(banks[bk][:, 0:n], wd_[:, cc, :], actT[:, cc, 0:n], [wd_, actT], PQ[bk], start=(cc == 0), stop=(cc == 21))
                            k.tt("dve", xo[:, c, sg], xo[:, c, sg], banks[bk][:, 0:n], ALU.add, [xo] + PQ[bk], [xo])
                        k.dma(Wo[:], ws_pg, [T_wscr_in], [Wo])
                        Wpg = Wo
                        k.act(sqb[:, :, 0:n], xo[:, :, sg], AF.Square, [xo], [sqb])
                        for kc in range(8):
                            k.mm(banks[0][:, 0:n], ones_bf[:], sqb[:, kc, 0:n], [ones_bf, sqb], PQ[0], start=(kc == 0), stop=(kc == 7))
                        k.act(rsb[:, 0:n], banks[0][:, 0:n], AF.Sqrt, PQ[0] + [epsT], [rsb], scale=1.0 / D, bias=epsT[:, 0:1])
                        k.recip(rsb[:, 0:n], rsb[:, 0:n], [rsb], [rsb])
                        for kc in range(8):
                            k.tt("dve" if kc % 2 == 0 else "pool", h2T[:, kc, 0:n], xo[:, kc, sg], rsb[:, 0:n], ALU.mult, [xo, rsb], [h2T])
                        for c in range(8):
                            cs = slice(c * 128, (c + 1) * 128)
                            for kc in range(8):
                                k.mm(banks[1][:, 0:n], Wpg[:, kc, cs], h2T[:, kc, 0:n], [Wpg, h2T], PQ[1], start=(kc == 0), stop=(kc == 7))
                            k.act(sga[:, 0:n], banks[1][:, 0:n], AF.Sigmoid, PQ[1], [sga])
                            for kc in range(2):
                                k.mm(banks[2][:, 0:n], Wpl[:, kc, cs], pb[:, kc, sg], [Wpl, pb], PQ[2], start=(kc == 0), stop=(kc == 1))
                            k.tt("dve", t1[:, 0:n], banks[2][:, 0:n], sga[:, 0:n], ALU.mult, PQ[2] + [sga], [t1])
                            k.tt("pool", xo[:, c, sg], xo[:, c, sg], t1[:, 0:n], ALU.add, [xo, t1], [xo])
                        k.act(sqb[:, :, 0:n], xo[:, :, sg], AF.Square, [xo], [sqb])
                        for kc in range(8):
                            k.mm(banks[0][:, 0:n], ones_bf[:], sqb[:, kc, 0:n], [ones_bf, sqb], PQ[0], start=(kc == 0), stop=(kc == 7))
                        k.act(rsb[:, 0:n], banks[0][:, 0:n], AF.Sqrt, PQ[0] + [epsT], [rsb], scale=1.0 / D, bias=epsT[:, 0:1])
                        k.recip(rsb[:, 0:n], rsb[:, 0:n], [rsb], [rsb])
                        for kc in range(8):
                            k.stt(xo[:, kc, sg], xo[:, kc, sg], gains[:, 24 + kc:25 + kc], rsb[:, 0:n], ALU.mult, ALU.mult, [xo, gains, rsb], [xo])
                        for tt in range(n // 128):
                            for cg in range(2):
                                bk = 3 + cg
                                for c4 in range(4):
                                    k.mm(banks[bk][:, c4 * 128:(c4 + 1) * 128], xo[:, 4 * cg + c4, c0 + tt * 128:c0 + (tt + 1) * 128], ident, [xo, cst], PQ[bk])
                                k.cp("act" if cg == 0 else "dve", ytok[:, cg * 512:(cg + 1) * 512], banks[bk][:, :], PQ[bk], [ytok])
                            nv = min(128, jb["nvalid"])
                            k.dma(jb["y_out"][tt * 128:tt * 128 + nv, :], ytok[0:nv, :], [ytok], [outT], key=ytok)
                        k.dma(jb["fconv_out"], fco[:], [fco], [outT], key=fco)
                    S.barrier()

        xav = I["xa"].rearrange("(kc p) t -> p kc t", p=128)
        pav = I["pT"].rearrange("(kc p) t -> p kc t", p=128)
        if "P" in KJOB2:
            for m in range(4):
                ps_ = 4 * m + 3
                t0 = 512 * ps_ - 128
                own_block(dict(NQ=5, xsrc=xav[:, :, t0:t0 + 640], psrc=pav[:, :, t0:t0 + 640], L=[128 * (4 * ps_ + r) for r in range(5)],
                               limcol=5 * m, lolim=22, kiT=kiT_p, kT=kT_p, v=v_p, obT=obT_p[:, :, t0:t0 + 640], fhalo=None,
                               segs=[(126, 2, True), (128, 512, False)], nvalid=512, y_out=O["y_own"][512 * m:512 * (m + 1), :], fconv_out=O["fconv_p"]))
        if "S" in KJOB2:
            for sb_ in range(2):
                own_block(dict(NQ=1, xsrc=I["xs"][sb_].rearrange("(kc p) t -> p kc t", p=128), psrc=I["psT"][sb_].rearrange("(kc p) t -> p kc t", p=128),
                               L=[LKS], limcol=20 + sb_, lolim=None, kiT=kiT_s[sb_], kT=kT_s[sb_], v=v_s[sb_], obT=obT_s[sb_], fhalo=I["fconvT"][sb_],
                               segs=[(0, 128, False)], nvalid=16, y_out=O["y_s"][sb_], fconv_out=O["fconv_s"][sb_]))

        print('nsem', S.nsem, {e: len(q) for e, q in S.q.items()})
        S.final_wait("sp", [outT])
        S.emit()
    return nc


def _consts():
    c = np.zeros((128, 7 * 128), np.float32)
    i = np.arange(128)
    same = (i[:, None] // 64) == (i[None, :] // 64)
    c[:, 0:128] = np.eye(128)
    c[:, 128:256] = ((i[:, None] <= i[None, :]) & same)
    c[:, 256:384] = ((i[:, None] > i[None, :]) & same)
    c[:, 384:512] = np.where((i[:, None] > i[None, :]) & same, 0.0, NEG)
    c[:, 512:640] = np.where((i[:, None] <= i[None, :]) & same, 0.0, -NEG)
    c[:, 640:768] = (i[:, None] < 64)
    c[:, 768:896] = (i[:, None] >= 64)
    return c


_NC = None


def kernel(**inp):
    global _NC
    f32 = np.float32
    xp = np.asarray(inp["x_prompt"], f32)
    xs = np.asarray(inp["x_sample"], f32)
    cst = _consts()
    gains = np.zeros((128, 32), f32)
    gains[:, 0:8] = np.asarray(inp["norm_mix"], f32)[0].reshape(8, 128).T
    gains[:, 8:16] = np.asarray(inp["norm_ffn"], f32)[0].reshape(8, 128).T
    gains[:, 16:24] = np.asarray(inp["norm_ple"], f32)[0].reshape(8, 128).T
    gains[:, 24:32] = np.asarray(inp["norm_final"], f32).reshape(8, 128).T
    convb = np.ascontiguousarray(np.asarray(inp["conv_b"], f32)[0].T.reshape(12, 128, 4).transpose(1, 0, 2))
    alog = np.ascontiguousarray(np.broadcast_to(np.asarray(inp["a_log"], f32)[0][None, :], (128, 4)))
    dtb = np.ascontiguousarray(np.broadcast_to(np.asarray(inp["dt_bias"], f32)[0][None, :], (128, 4)))
    ngdn = np.ascontiguousarray(np.broadcast_to(np.asarray(inp["norm_gdn"], f32)[0][None, :], (128, 128)))
    gval = np.zeros((128, 2), f32)
    gval[0:16, 0] = 1.0
    gval[:, 1] = 1.0
    w_in = np.ascontiguousarray(np.asarray(inp["w_in"], f32)[0])
    ck = np.asarray(inp["cache_k"], f32)[0]
    cvv = np.asarray(inp["cache_v"], f32)[0]
    cki = np.asarray(inp["cache_kidx"], f32)[0]
    sg = np.asarray(inp["state_gdn"], f32)[0]
    sgc = np.asarray(inp["state_gdn_conv"], f32)[0]
    def bucket_np(rel):
        half, max_exact = 16, 8
        ret = np.where(rel > 0, half, 0)
        n = np.abs(rel)
        nf = np.maximum(n, 1).astype(np.float32)
        large = max_exact + (np.log(nf / np.float32(max_exact)) / np.float32(np.log(128 / 8)) * np.float32(half - max_exact)).astype(np.int32)
        large = np.minimum(large, half - 1)
        return ret + np.where(n < max_exact, n, large)
    ohrev = np.zeros((32, 384), f32)
    sp_ = np.arange(383)
    ohrev[bucket_np(127 - sp_), sp_] = 1.0
    iota = np.ascontiguousarray(np.broadcast_to(np.arange(512, dtype=f32)[None, :], (128, 512)))
    relb = np.ascontiguousarray(np.asarray(inp["rel_bias"], f32))
    relb15 = np.ascontiguousarray(relb[15][:, None])
    convf = np.ascontiguousarray(np.asarray(inp["conv_ffn"], f32)[0].T.reshape(22, 128, 3).transpose(1, 0, 2))
    pp = np.asarray(inp["p_prompt"], f32)[0]
    psm = np.asarray(inp["p_sample"], f32)[0]
    sfc = np.asarray(inp["state_ffn_conv"], f32)[0]
    wts = dict(w_pa=np.ascontiguousarray(np.asarray(inp["w_proj_a"], f32)[0]), w_pb=np.ascontiguousarray(np.asarray(inp["w_proj_b"], f32)[0]),
               w_out=np.ascontiguousarray(np.asarray(inp["w_out"], f32)[0]), w_up=np.ascontiguousarray(np.asarray(inp["w_up"], f32)[0]),
               w_down=np.ascontiguousarray(np.asarray(inp["w_down"], f32)[0]), w_ple=np.ascontiguousarray(np.asarray(inp["w_ple"], f32)[0]),
               w_pg=np.ascontiguousarray(np.asarray(inp["w_ple_gate"], f32)[0]))
    in_maps = []
    for c in range(8):
        b, j = c // 4, c % 4
        pad = 512 * (3 - j)
        xa = np.zeros((D, LKP), f32)
        xa[:, pad:] = xp[b].T[:, :LKP - pad]
        sl = slice(2 * c, 2 * c + 2)
        xs_c = np.zeros((2, D, 128), f32)
        xs_c[:, :, 0:16] = xs[sl].transpose(0, 2, 1)
        m = dict(xa=xa, xs=xs_c, cst=cst, gains=gains, convb=convb, alog=alog, dtb=dtb, ngdn=ngdn, gval=gval, w_in=w_in,
                 cache_kT=np.ascontiguousarray(ck[sl].reshape(2, PAST, 512).transpose(0, 2, 1)),
                 cache_v=np.ascontiguousarray(cvv[sl].reshape(2, PAST, 512)),
                 cache_kiT=np.ascontiguousarray(cki[sl].transpose(0, 2, 1)),
                 state_gdn=np.ascontiguousarray(sg[sl]),
                 gconvT=np.ascontiguousarray(sgc[sl].transpose(0, 2, 1).reshape(2, 12, 128, 3).transpose(0, 2, 1, 3)))
        pT = np.zeros((256, LKP), f32)
        pT[:, pad:] = pp[b].T[:, :LKP - pad]
        psT = np.zeros((2, 256, 128), f32)
        psT[:, :, 0:16] = psm[sl].transpose(0, 2, 1)
        lims = np.zeros((128, 24), f32)
        ii = np.arange(128)
        for m_ in range(4):
            for qb in range(5):
                pt = 512 * (4 * m_ + 3) - 128 + 128 * qb + ii
                lims[:, 5 * m_ + qb] = (pt // 64 + 1) * 64
        lims[:, 20] = PAST + 16
        lims[:, 21] = PAST + 16
        lims[:, 22] = pad
        m.update(pT=pT, psT=psT, convf=convf, relb=relb, relb15=relb15, ohrev=ohrev, iota=iota, lims=lims,
                 fconvT=np.ascontiguousarray(sfc[sl].transpose(0, 2, 1).reshape(2, 22, 128, 2).transpose(0, 2, 1, 3)), **wts)
        in_maps.append(m)
    if _NC is None:
        _NC = build_program()
    res = run_bass_kernel_spmd(_NC, in_maps, core_ids=list(range(8)))
    R = res.results
    y_p = np.zeros((2, 8192, D), f32)
    for c in range(8):
        b, j = c // 4, c % 4
        for m_ in range(4):
            sg_ = 4 * m_ + j
            y_p[b, 512 * sg_:512 * (sg_ + 1)] = R[c]["y_own"][512 * m_:512 * (m_ + 1)]
    y_s = np.concatenate([R[c]["y_s"] for c in range(8)]).reshape(16, 16, D)
    k_p = np.stack([R[4 * b + 3]["k_all"] for b in range(2)]).reshape(1, 2, 8192, 8, 64)
    v_p = np.stack([R[4 * b + 3]["v_all"] for b in range(2)]).reshape(1, 2, 8192, 8, 64)
    ki_p = np.stack([R[4 * b + 3]["ki_all"] for b in range(2)]).reshape(1, 2, 8192, 64)
    gdn_p = np.stack([R[4 * b + 3]["gdn_p"] for b in range(2)]).reshape(1, 2, 4, 128, 128)
    gconv_p = np.stack([R[4 * b + 3]["gconv_p"].transpose(1, 0, 2).reshape(1536, 3).T for b in range(2)]).reshape(1, 2, 3, 1536)
    fconv_p = np.stack([R[4 * b + 3]["fconv_p"].transpose(1, 0, 2).reshape(DFF, 2).T for b in range(2)]).reshape(1, 2, 2, DFF)
    k_s = np.concatenate([R[c]["k_s"] for c in range(8)]).reshape(1, 16, 16, 8, 64)
    v_s = np.concatenate([R[c]["v_s"] for c in range(8)]).reshape(1, 16, 16, 8, 64)
    ki_s = np.concatenate([R[c]["ki_s"] for c in range(8)]).reshape(1, 16, 16, 64)
    gdn_s = np.concatenate([R[c]["gdn_s"] for c in range(8)]).reshape(1, 16, 4, 128, 128)
    gconv_s = np.concatenate([R[c]["gconv_s"] for c in range(8)])
    gconv_s = np.ascontiguousarray(gconv_s.transpose(0, 2, 1, 3).reshape(16, 1536, 3).transpose(0, 2, 1)).reshape(1, 16, 3, 1536)
    fconv_s = np.concatenate([R[c]["fconv_s"] for c in range(8)])
    fconv_s = np.ascontiguousarray(fconv_s.transpose(0, 2, 1, 3).reshape(16, DFF, 2).transpose(0, 2, 1)).reshape(1, 16, 2, DFF)
    return (y_p, y_s, k_p, v_p, ki_p, gdn_p, gconv_p, fconv_p, k_s, v_s, ki_s, gdn_s, gconv_s, fconv_s)
```

```python
import os
import numpy as np
from contextlib import ExitStack
import concourse.bass as bass
import concourse.mybir as mybir
from concourse.bass_utils import run_bass_kernel_spmd

F32 = mybir.dt.float32
BF16 = mybir.dt.bfloat16
AF = mybir.ActivationFunctionType
ALU = mybir.AluOpType
AX = mybir.AxisListType

EPOCH = 4096
KDBG = int(os.environ.get('KDBG', '9'))
KJOB = os.environ.get('KJOB', 'ps')
KSUB = int(os.environ.get('KSUB', '99'))
KJOB2 = os.environ.get('KJOB2', 'PS')
KP2 = int(os.environ.get('KP2', '99'))
KP3 = int(os.environ.get('KP3', '99'))
EPS = 1e-6
NEG = -1e30

D = 1024
LKP = 8192
NTP = 256
PAST = 2048
LKS = PAST + 128
DFF = 2816

C_QA, C_KA, C_VA, C_QI, C_KI, C_WI, C_QB, C_GB, C_BB, C_AB, C_GA, C_GBR = 0, 512, 1024, 1536, 2048, 2112, 2120, 3656, 4168, 4172, 4176, 5200
DIN = 6224


class Tl:
    __slots__ = ("t", "name", "lw", "rd", "excl")

    def __init__(self, t, name, excl=False):
        self.t = t
        self.name = name
        self.lw = None
        self.rd = []
        self.excl = excl

    def __getitem__(self, idx):
        return self.t[idx]


class Sched:
    ENGS = ("pe", "act", "dve", "pool", "sp")

    def __init__(self, nc, es):
        self.nc = nc
        self.es = es
        self.q = {e: [] for e in self.ENGS}
        self.cnt = {e: 0 for e in self.ENGS}
        self.sems = {e: [] for e in self.ENGS}
        self.seen = {e: {} for e in self.ENGS}
        self.dma_sems = {}
        self.keep = []
        self.nsem = 0

    def _newsem(self, name):
        self.nsem += 1
        return self.es.enter_context(self.nc.semaphore(name))

    def _eng_token(self, e):
        c = self.cnt[e]
        ep = c // EPOCH
        while len(self.sems[e]) <= ep:
            self.sems[e].append(self._newsem(f"s_{e}_{len(self.sems[e])}"))
        self.cnt[e] = c + 1
        return (("e", e, ep), self.sems[e][ep], (c % EPOCH) + 1)

    def _waits(self, e, reads, writes):
        toks = []
        for t in reads:
            if t.lw is not None:
                toks.append(t.lw)
            if t.excl:
                toks.extend(t.rd)
        for t in writes:
            if t.lw is not None:
                toks.append(t.lw)
            toks.extend(t.rd)
        best = {}
        for (key, sem, val) in toks:
            if key[0] == "e" and key[1] == "pe" and e == "pe":
                continue
            if best.get(key, (None, 0))[1] < val:
                best[key] = (sem, val)
        out = []
        seen = self.seen[e]
        for key, (sem, val) in best.items():
            if seen.get(key, 0) >= val:
                continue
            seen[key] = val
            out.append((sem, val))
        return out

    def op(self, e, fn, reads=(), writes=()):
        w = self._waits(e, reads, writes)
        tok = self._eng_token(e)
        self.q[e].append((w, fn, tok[1], 1))
        for t in writes:
            t.lw = tok
            t.rd = []
        for t in reads:
            if t.lw is not tok:
                t.rd.append(tok)
        return tok

    def dma(self, e, fn, reads=(), writes=(), key=None):
        w = self._waits(e, reads, writes)
        k = key if key is not None else (writes[0] if writes else reads[0])
        kid = id(k)
        ent = self.dma_sems.get(kid)
        if ent is None or ent[1] >= 32000:
            if ent is None:
                self.keep.append(k)
            if getattr(self, "pool", None):
                ent = self.pool.pop()
            else:
                self.semid = getattr(self, "semid", 0) + 1
                ent = [self._newsem(f"d_{self.nsem}"), 0, self.semid]
            self.dma_sems[kid] = ent
        ent[1] += 16
        tok = (("d", ent[2], 0), ent[0], ent[1])
        self.q[e].append((w, fn, ent[0], 16))
        for t in writes:
            t.lw = tok
            t.rd = []
        for t in reads:
            if t.lw is not tok:
                t.rd.append(tok)
        return tok

    def barrier(self):
        toks = []
        for e in self.ENGS:
            c = self.cnt[e]
            if c > 0:
                ep = (c - 1) // EPOCH
                toks.append((("e", e, ep), self.sems[e][ep], ((c - 1) % EPOCH) + 1))
        for kid, ent in self.dma_sems.items():
            if ent[1] > 0:
                toks.append((("d", ent[2], 0), ent[0], ent[1]))
        for e in self.ENGS:
            w = []
            seen = self.seen[e]
            for (key, sem, val) in toks:
                if key[0] == "e" and key[1] == e:
                    continue
                if seen.get(key, 0) >= val:
                    continue
                seen[key] = val
                w.append((sem, val))
            if w:
                self.q[e].append((w, None, None, 0))
        if not hasattr(self, "pool"):
            self.pool = []
        for kid, ent in self.dma_sems.items():
            if ent[1] < 30000:
                self.pool.append(ent)
        self.dma_sems = {}

    def final_wait(self, e, tiles):
        w = self._waits(e, tiles, tiles)
        self.q[e].append((w, None, None, 0))

    def emit(self):
        nc = self.nc
        with nc.Block() as block:
            def run(eng, name):
                for (w, fn, sem, inc) in self.q[name]:
                    for (s, v) in w:
                        eng.wait_ge(s, v)
                    if fn is not None:
                        fn(eng).then_inc(sem, inc)

            @block.tensor
            def _(eng):
                run(eng, "pe")

            @block.scalar
            def _(eng):
                run(eng, "act")

            @block.vector
            def _(eng):
                run(eng, "dve")

            @block.gpsimd
            def _(eng):
                run(eng, "pool")

            @block.sync
            def _(eng):
                run(eng, "sp")


class K:
    def __init__(self, nc, S):
        self.nc = nc
        self.S = S
        self.rr = 0

    def mm(self, out, lhsT, rhs, rd, wr, start=True, stop=True):
        self.S.op("pe", lambda e, o=out, l=lhsT, r=rhs, a=start, b=stop: e.matmul(o, lhsT=l, rhs=r, start=a, stop=b), reads=rd, writes=wr)

    def tr(self, out, in_, ident, rd, wr):
        self.S.op("pe", lambda e, o=out, i=in_, d=ident: e.matmul(o, lhsT=i, rhs=d, start=True, stop=True), reads=rd, writes=wr)

    def act(self, out, in_, func, rd, wr, scale=None, bias=None, accum=None):
        kw = {}
        if scale is not None:
            kw["scale"] = scale
        if bias is not None:
            kw["bias"] = bias
        if accum is not None:
            kw["accum_out"] = accum
        self.S.op("act", lambda e, o=out, i=in_, f=func, k=kw: e.activation(out=o, in_=i, func=f, **k), reads=rd, writes=wr)

    def ts(self, eng, out, in0, s1, op0, rd, wr, s2=None, op1=None, accum=None):
        kw = {}
        if op1 is not None:
            kw["op1"] = op1
        if accum is not None:
            kw["accum_out"] = accum
        self.S.op(eng, lambda e, o=out, i=in0, a=s1, b=s2, p=op0, k=kw: e.tensor_scalar(out=o, in0=i, scalar1=a, scalar2=b, op0=p, **k), reads=rd, writes=wr)

    def tt(self, eng, out, in0, in1, op, rd, wr):
        self.S.op(eng, lambda e, o=out, i=in0, j=in1, p=op: e.tensor_tensor(out=o, in0=i, in1=j, op=p), reads=rd, writes=wr)

    def stt(self, out, in0, sc, in1, op0, op1, rd, wr):
        self.S.op("dve", lambda e, o=out, i=in0, s=sc, j=in1, p=op0, q=op1: e.scalar_tensor_tensor(out=o, in0=i, scalar=s, in1=j, op0=p, op1=q), reads=rd, writes=wr)

    def cp(self, eng, out, in_, rd, wr):
        if eng == "act":
            self.S.op("act", lambda e, o=out, i=in_: e.copy(out=o, in_=i), reads=rd, writes=wr)
        else:
            self.S.op(eng, lambda e, o=out, i=in_: e.tensor_copy(out=o, in_=i), reads=rd, writes=wr)

    def memset(self, eng, ap, val, wr):
        self.S.op(eng, lambda e, a=ap, v=val: e.memset(a, v), writes=wr)

    def recip(self, out, in_, rd, wr):
        self.S.op("dve", lambda e, o=out, i=in_: e.reciprocal(out=o, in_=i), reads=rd, writes=wr)

    def dma(self, out, in_, rd, wr, q="sp", key=None, slow=False):
        if slow:
            self.S.dma(q, lambda e, o=out, i=in_: e.dma_start(out=o, in_=i, allow_slow_non_contiguous=True), reads=rd, writes=wr, key=key)
        else:
            self.S.dma(q, lambda e, o=out, i=in_: e.dma_start(out=o, in_=i), reads=rd, writes=wr, key=key)


def build_program():
    nc = bass.Bass("TRN2", target_bir_lowering=False)

    def din(name, shape, dt=F32):
        return nc.dram_tensor(name, list(shape), dt, kind="ExternalInput").ap()

    def dout(name, shape, dt=F32):
        return nc.dram_tensor(name, list(shape), dt, kind="ExternalOutput").ap()

    def dscr(name, shape, dt):
        return nc.dram_tensor(name, list(shape), dt, kind="Internal").ap()

    I = {}
    I["xa"] = din("xa", [D, LKP])
    I["xs"] = din("xs", [2, D, 128])
    I["cst"] = din("cst", [128, 7 * 128])
    I["gains"] = din("gains", [128, 32])
    I["convb"] = din("convb", [128, 12, 4])
    I["alog"] = din("alog", [128, 4])
    I["dtb"] = din("dtb", [128, 4])
    I["ngdn"] = din("ngdn", [128, 128])
    I["gval"] = din("gval", [128, 2])
    I["w_in"] = din("w_in", [D, DIN])
    I["cache_kT"] = din("cache_kT", [2, 512, PAST])
    I["cache_v"] = din("cache_v", [2, PAST, 512])
    I["cache_kiT"] = din("cache_kiT", [2, 64, PAST])
    I["state_gdn"] = din("state_gdn", [2, 4, 128, 128])
    I["gconvT"] = din("gconvT", [2, 128, 12, 3])
    I["pT"] = din("pT", [256, LKP])
    I["psT"] = din("psT", [2, 256, 128])
    I["w_pa"] = din("w_pa", [512, D])
    I["w_pb"] = din("w_pb", [512, D])
    I["w_out"] = din("w_out", [D, D])
    I["w_up"] = din("w_up", [D, 2 * DFF])
    I["w_down"] = din("w_down", [DFF, D])
    I["w_ple"] = din("w_ple", [256, D])
    I["w_pg"] = din("w_pg", [D, D])
    I["convf"] = din("convf", [128, 22, 3])
    I["relb"] = din("relb", [32, 8])
    I["relb15"] = din("relb15", [8, 1])
    I["ohrev"] = din("ohrev", [32, 384])
    I["iota"] = din("iota", [128, 512])
    I["lims"] = din("lims", [128, 24])
    I["fconvT"] = din("fconvT", [2, 128, 22, 2])
    O = {}
    O["y_own"] = dout("y_own", [2048, D])
    O["fconv_p"] = dout("fconv_p", [128, 22, 2])
    O["y_s"] = dout("y_s", [2, 16, D])
    O["fconv_s"] = dout("fconv_s", [2, 128, 22, 2])
    O["k_all"] = dout("k_all", [LKP, 512])
    O["v_all"] = dout("v_all", [LKP, 512])
    O["ki_all"] = dout("ki_all", [LKP, 64])
    O["gdn_p"] = dout("gdn_p", [4, 128, 128])
    O["gconv_p"] = dout("gconv_p", [128, 12, 3])
    O["k_s"] = dout("k_s", [2, 16, 512])
    O["v_s"] = dout("v_s", [2, 16, 512])
    O["ki_s"] = dout("ki_s", [2, 16, 64])
    O["gdn_s"] = dout("gdn_s", [2, 4, 128, 128])
    O["gconv_s"] = dout("gconv_s", [2, 128, 12, 3])
    wscr_in = dscr("wscr_in", [128, 8, DIN], BF16)
    ws_pa = dscr("ws_pa", [128, 4, D], BF16)
    ws_pb = dscr("ws_pb", [128, 4, D], BF16)
    ws_out = dscr("ws_out", [128, 8, D], BF16)
    ws_up = dscr("ws_up", [128, 8, 2 * DFF], BF16)
    ws_down = dscr("ws_down", [128, 22, D], BF16)
    ws_ple = dscr("ws_ple", [128, 2, D], BF16)
    ws_pg = dscr("ws_pg", [128, 8, D], BF16)
    tab_scr = dscr("tab_scr", [8, 384], F32)
    T_tab = Tl(None, "tab")
    kT_p = dscr("kT_p", [128, 4, LKP], BF16)
    v_p = dscr("v_p", [LKP, 528], BF16)
    kiT_p = dscr("kiT_p", [64, LKP], BF16)
    obT_p = dscr("obT_p", [128, 4, LKP], BF16)
    kT_s = dscr("kT_s", [2, 128, 4, LKS], BF16)
    v_s = dscr("v_s_scr", [2, LKS, 528], BF16)
    kiT_s = dscr("kiT_s", [2, 64, LKS], BF16)
    obT_s = dscr("obT_s", [2, 128, 4, 128], BF16)

    outT = Tl(None, "outputs")
    T_wscr_in = Tl(None, "wscr_in")
    T_kT = Tl(None, "kT")
    T_v = Tl(None, "v")
    T_ki = Tl(None, "kiT")
    T_ob = Tl(None, "obT")

    with ExitStack() as es0:
        S = Sched(nc, es0)
        k = K(nc, S)

        uid = [0]

        def sbt(es, name, shape, dt):
            uid[0] += 1
            nm = f"s{uid[0]}_{name}"
            return Tl(es.enter_context(nc.sbuf_tensor(nm, list(shape), dt)), nm)

        banks = [es0.enter_context(nc.psum_tensor(f"pb{i}", [128, 512], F32)) for i in range(8)]
        PB = [Tl(banks[b], f"pb{b}", excl=True) for b in range(8)]
        PQ = [[PB[b]] * 4 for b in range(8)]
        banks_bf = None

        def pq(b, q, rows=slice(0, 128), w=128):
            return banks[b][rows, q * 128:q * 128 + w]

        cst = sbt(es0, "cst", [128, 7 * 128], F32)
        k.dma(cst[:], I["cst"], [], [cst])
        ident = cst[:, 0:128]
        Ublk = cst[:, 128:256]
        Lsblk = cst[:, 256:384]
        NEGML = cst[:, 384:512]
        POSMU = cst[:, 512:640]
        half0 = cst[:, 640:768]
        half1 = cst[:, 768:896]
        gains = sbt(es0, "gains", [128, 32], F32)
        k.dma(gains[:], I["gains"], [], [gains])
        convb = sbt(es0, "convb", [128, 12, 4], F32)
        k.dma(convb[:], I["convb"], [], [convb])
        cA = sbt(es0, "cA", [128, 4], F32)
        k.dma(cA[:], I["alog"], [], [cA])
        dtb = sbt(es0, "dtb", [128, 4], F32)
        k.dma(dtb[:], I["dtb"], [], [dtb])
        ngdn = sbt(es0, "ngdn", [128, 128], F32)
        k.dma(ngdn[:], I["ngdn"], [], [ngdn])
        gval = sbt(es0, "gval", [128, 2], F32)
        k.dma(gval[:], I["gval"], [], [gval])
        epsT = sbt(es0, "epsT", [128, 1], F32)
        k.memset("pool", epsT[:], EPS, [epsT])
        ident_bf = sbt(es0, "ident_bf", [128, 128], BF16)
        k.cp("dve", ident_bf[:], ident, [cst], [ident_bf])
        ones_bf = sbt(es0, "ones_bf", [128, 128], BF16)
        k.memset("pool", ones_bf[:], 1.0, [ones_bf])
        k.act(cA[:], cA[:], AF.Exp, [cA], [cA])
        k.ts("dve", cA[:], cA[:], -1.0, ALU.mult, [cA], [cA])

        with ExitStack() as es:
            if KDBG < -1:
                raise_skip = True
            wf = [sbt(es, f"wf{i}", [128, 1556], F32) for i in range(2)]
            wb = [sbt(es, f"wb{i}", [128, 1556], BF16) for i in range(2)]
            cnt_ = [0]

            pcs = []

            def conv_w(src, dst, KC, C, gbase):
                v = src.rearrange("(kc p) c -> p kc c", p=128)
                npc = 1 if C <= 1556 else 4
                pw = C // npc
                for kc in range(KC):
                    for pc in range(npc):
                        pcs.append((v[:, kc, pc * pw:(pc + 1) * pw], dst[:, kc, pc * pw:(pc + 1) * pw], pw, None if gbase is None else gbase + kc))

            def conv_emit():
                def load(n):
                    k.dma(wf[n % 2][:, 0:pcs[n][2]], pcs[n][0], [], [wf[n % 2]], q="sp")
                if pcs:
                    load(0)
                for n, (src_, dst_, pw, gcol) in enumerate(pcs):
                    a, b_ = wf[n % 2], wb[n % 2]
                    if n + 1 < len(pcs):
                        load(n + 1)
                    eng = "dve" if n % 2 == 0 else "pool"
                    if gcol is None:
                        k.cp(eng, b_[:, 0:pw], a[:, 0:pw], [a], [b_])
                    else:
                        k.ts(eng, b_[:, 0:pw], a[:, 0:pw], gains[:, gcol:gcol + 1], ALU.mult, [a, gains], [b_])
                    k.dma(dst_, b_[:, 0:pw], [b_], [T_wscr_in], q="sp", key=b_)

            if KDBG >= -1:
                conv_w(I["w_in"], wscr_in, 8, DIN, 0)
                conv_w(I["w_pa"], ws_pa, 4, D, None)
                conv_w(I["w_pb"], ws_pb, 4, D, None)
                conv_w(I["w_out"], ws_out, 8, D, None)
                conv_w(I["w_up"], ws_up, 8, 2 * DFF, 8)
                conv_w(I["w_down"], ws_down, 22, D, None)
                conv_w(I["w_ple"], ws_ple, 2, D, None)
                conv_w(I["w_pg"], ws_pg, 8, D, 16)
                conv_emit()
            S.barrier()

        def all_pass(es, job):
            NT = job["NT"]
            ntt = NT // 128
            nsteps = min(job["L"] // NT, int(os.environ.get('KSTEPS', '999')))
            xsrc = job["x"]
            W1 = sbt(es, "W1", [128, 8, 3144], BF16)
            k.dma(W1[:, :, 0:1024], wscr_in[:, :, C_KA:C_KA + 1024], [T_wscr_in], [W1])
            k.dma(W1[:, :, 1024:1088], wscr_in[:, :, C_KI:C_KI + 64], [T_wscr_in], [W1])
            k.dma(W1[:, :, 1088:3144], wscr_in[:, :, C_QB:C_QB + 2056], [T_wscr_in], [W1])
            xa_t = [sbt(es, f"xa_t{i}", [128, 8, NT], F32) for i in range(2)]
            sq = sbt(es, "sq", [128, 8, NT], BF16)
            hT = sbt(es, "hT", [128, 8, NT], BF16)
            rs = sbt(es, "rs", [128, NT], F32)
            cin = sbt(es, "cin", [128, 12, NT + 3], F32)
            cv = sbt(es, "cv", [128, 12, NT], F32)
            cvb = sbt(es, "cvb", [128, 8, NT], BF16)
            sq2 = sbt(es, "sq2", [128, NT], BF16)
            rs2 = sbt(es, "rs2", [128, NT], F32)
            gbs = sbt(es, "gbs", [128, ntt, 512], F32)
            bbab = sbt(es, "bbab", [128, ntt, 8], F32)
            ktok = [sbt(es, f"ktok{i}", [128, 512], F32) for i in range(2)]
            vtok = [sbt(es, f"vtok{i}", [128, 512], F32) for i in range(2)]
            vbf = [sbt(es, f"vbf{i}", [128, 8, 66], BF16) for i in range(2)]
            for i in range(2):
                k.memset("pool", vbf[i][:, :, 64:66], 1.0, [vbf[i]])
            kitok = [sbt(es, f"kitok{i}", [128, 64], F32) for i in range(2)]
            kf_t = sbt(es, "kf_t", [128, 4, NT], BF16)
            kif_t = sbt(es, "kif_t", [64, NT], BF16)
            obT_t = sbt(es, "obT_t", [128, 4, NT], BF16)
            Sst = [sbt(es, f"Sst{h}", [128, 128], F32) for h in range(4)]
            bet = sbt(es, "bet", [128, ntt, 4], F32)
            nbet = sbt(es, "nbet", [128, ntt, 4], F32)
            gg = sbt(es, "gg", [128, ntt, 4], F32)
            ngg = sbt(es, "ngg", [128, ntt, 4], F32)
            egc = sbt(es, "egc", [128, ntt, 4], F32)
            ekd = sbt(es, "ekd", [128, ntt, 4], F32)
            egl = sbt(es, "egl", [128, ntt, 8], F32)
            bege = sbt(es, "bege", [128, ntt, 4], F32)
            HT = []
            for h in range(4):
                d = {}
                for nm in ("NGb", "E1m", "E2m", "P", "attnT", "kbg", "kd", "vb", "u", "wT", "vnew", "o1", "o", "tmp"):
                    d[nm] = sbt(es, f"{nm}{h}", [128, 128], F32)
                for nm in ("N", "Nt", "M0", "M1", "Mt0", "Mt1", "Pb"):
                    d[nm] = sbt(es, f"{nm}{h}", [128, 128], BF16)
                d["obn"] = sbt(es, f"obn{h}", [128, 128], BF16)
                d["ms"] = sbt(es, f"ms{h}", [128, 1], F32)
                HT.append(d)

            if job["S0"] is None:
                for h in range(4):
                    k.memset("pool", Sst[h][:], 0.0, [Sst[h]])
                k.memset("pool", cin[:, :, 0:3], 0.0, [cin])
            else:
                for h in range(4):
                    k.dma(Sst[h][:], job["S0"][h], [], [Sst[h]])
                k.dma(cin[:, :, 0:3], job["conv0"], [], [cin], slow=True)

            xv = xsrc.rearrange("(kc p) t -> p kc t", p=128)
            k.dma(xa_t[0][:], xv[:, :, 0:NT], [], [xa_t[0]])
            for st in range(nsteps):
                t0 = st * NT
                xt = xa_t[st % 2]
                if st + 1 < nsteps:
                    xn = xa_t[(st + 1) % 2]
                    k.dma(xn[:], xv[:, :, t0 + NT:t0 + 2 * NT], [], [xn])
                k.act(sq[:], xt[:], AF.Square, [xt], [sq])
                A = PQ[0]
                for kc in range(8):
                    k.mm(banks[0][:, 0:NT], ones_bf[:], sq[:, kc, :], [ones_bf, sq], A, start=(kc == 0), stop=(kc == 7))
                k.act(rs[:], banks[0][:, 0:NT], AF.Sqrt, A + [epsT], [rs], scale=1.0 / D, bias=epsT[:, 0:1])
                k.recip(rs[:], rs[:], [rs], [rs])
                for kc in range(8):
                    k.tt("dve" if kc % 2 == 0 else "pool", hT[:, kc, :], xt[:, kc, :], rs[:], ALU.mult, [xt, rs], [hT])
                if KDBG < 1:
                    continue
                for tt in range(ntt):
                    ts_ = slice(tt * 128, (tt + 1) * 128)
                    r0 = t0 + tt * 128
                    kt, vt, vb_, kit = ktok[tt % 2], vtok[tt % 2], vbf[tt % 2], kitok[tt % 2]
                    for (bk, c0, cw) in ((1, 0, 512), (2, 512, 512)):
                        for kc in range(8):
                            k.mm(banks[bk][:, 0:cw], hT[:, kc, ts_], W1[:, kc, c0:c0 + cw], [hT, W1], PQ[bk], start=(kc == 0), stop=(kc == 7))
                    k.cp("act", kt[:], banks[1][:, :], PQ[1], [kt])
                    k.cp("dve", vt[:], banks[2][:, :], PQ[2], [vt])
                    k.cp("pool", vb_[:, :, 0:64], vt[:].rearrange("p (h d) -> p h d", d=64), [vt], [vb_])
                    nv = job["nvalid"]
                    if nv >= 128:
                        k.dma(job["k_out"][r0:r0 + 128, :], kt[:], [kt], [outT], key=kt)
                        k.dma(job["v_out"][r0:r0 + 128, :], vt[:], [vt], [outT], key=vt)
                    else:
                        k.dma(job["k_out"][0:nv, :], kt[0:nv, :], [kt], [outT], key=kt)
                        k.dma(job["v_out"][0:nv, :], vt[0:nv, :], [vt], [outT], key=vt)
                    k.dma(job["v_scr"][job["koff"] + r0:job["koff"] + r0 + 128, :], vb_[:].rearrange("p h d -> p (h d)"), [vb_], [T_v], key=vb_)
                    for kc in range(8):
                        k.mm(banks[3][:, 0:64], hT[:, kc, ts_], W1[:, kc, 1024:1088], [hT, W1], PQ[3], start=(kc == 0), stop=(kc == 7))
                    k.cp("act", kit[:], banks[3][:, 0:64], PQ[3], [kit])
                    if nv >= 128:
                        k.dma(job["ki_out"][r0:r0 + 128, :], kit[:], [kit], [outT], key=kit)
                    else:
                        k.dma(job["ki_out"][0:nv, :], kit[0:nv, :], [kit], [outT], key=kit)
                    for kc in range(8):
                        k.mm(banks[4][:, :], hT[:, kc, ts_], W1[:, kc, 2624:3136], [hT, W1], PQ[4], start=(kc == 0), stop=(kc == 7))
                    k.act(gbs[:, tt, :], banks[4][:, :], AF.Silu, PQ[4], [gbs])
                    for kc in range(8):
                        k.mm(banks[3][:, 128:136], hT[:, kc, ts_], W1[:, kc, 3136:3144], [hT, W1], PQ[3], start=(kc == 0), stop=(kc == 7))
                    k.cp("dve", bbab[:, tt, :], banks[3][:, 128:136], PQ[3], [bbab])
                for p in range(4):
                    bk = 5 + (p % 2)
                    for kc in range(8):
                        k.mm(banks[bk][:, 0:NT], W1[:, kc, p * 128:(p + 1) * 128], hT[:, kc, :], [W1, hT], PQ[bk], start=(kc == 0), stop=(kc == 7))
                    k.cp("act" if p % 2 == 0 else "dve", kf_t[:, p, :], banks[bk][:, 0:NT], PQ[bk], [kf_t])
                k.dma(job["kT_scr"][:, :, job["koff"] + t0:job["koff"] + t0 + NT], kf_t[:], [kf_t], [T_kT], key=kf_t)
                for kc in range(8):
                    k.mm(banks[7][0:64, 0:NT], W1[:, kc, 1024:1088], hT[:, kc, :], [W1, hT], PQ[7], start=(kc == 0), stop=(kc == 7))
                k.cp("act", kif_t[:], banks[7][0:64, 0:NT], PQ[7], [kif_t])
                k.dma(job["kiT_scr"][:, job["koff"] + t0:job["koff"] + t0 + NT], kif_t[:], [kif_t], [T_ki], key=kif_t)
                if KDBG < 2:
                    continue
                for c in range(12):
                    bk = 5 + (c % 3)
                    for kc in range(8):
                        k.mm(banks[bk][:, 0:NT], W1[:, kc, 1088 + c * 128:1088 + (c + 1) * 128], hT[:, kc, :], [W1, hT], PQ[bk], start=(kc == 0), stop=(kc == 7))
                    k.cp("act" if c % 2 == 0 else "dve", cin[:, c, 3:3 + NT], banks[bk][:, 0:NT], PQ[bk], [cin])
                for c in range(12):
                    k.ts("dve", cv[:, c, :], cin[:, c, 0:NT], convb[:, c, 0:1], ALU.mult, [cin, convb], [cv])
                    for j in range(1, 4):
                        k.stt(cv[:, c, :], cin[:, c, j:j + NT], convb[:, c, j:j + 1], cv[:, c, :], ALU.mult, ALU.add, [cin, convb, cv], [cv])
                if st == nsteps - 1:
                    k.dma(job["conv_out"], cin[:, :, job["nvalid_last"]:job["nvalid_last"] + 3], [cin], [outT], key=cin, slow=True)
                k.cp("pool", cin[:, :, 0:3], cin[:, :, NT:NT + 3], [cin], [cin])
                k.act(cv[:], cv[:], AF.Silu, [cv], [cv])
                for c in range(8):
                    k.act(sq2[:], cv[:, c, :], AF.Square, [cv], [sq2])
                    k.mm(banks[0][:, 0:NT], ones_bf[:], sq2[:], [ones_bf, sq2], PQ[0])
                    k.act(rs2[:], banks[0][:, 0:NT], AF.Sqrt, PQ[0] + [epsT], [rs2], bias=epsT[:, 0:1])
                    k.recip(rs2[:], rs2[:], [rs2], [rs2])
                    if c < 4:
                        k.stt(cv[:, c, :], cv[:, c, :], 128.0 ** -0.5, rs2[:], ALU.mult, ALU.mult, [cv, rs2], [cv])
                    else:
                        k.tt("dve", cv[:, c, :], cv[:, c, :], rs2[:], ALU.mult, [cv, rs2], [cv])
                    k.cp("pool", cvb[:, c, :], cv[:, c, :], [cv], [cvb])
                k.act(bet[:], bbab[:, :, 0:4], AF.Sigmoid, [bbab], [bet])
                if job["gval"] is not None:
                    for tt in range(ntt):
                        k.ts("dve", bet[:, tt, :], bet[:, tt, :], gval[:, job["gval"]:job["gval"] + 1], ALU.mult, [bet, gval], [bet])
                k.ts("dve", nbet[:], bet[:], -1.0, ALU.mult, [bet], [nbet])
                for tt in range(ntt):
                    k.tt("dve", gg[:, tt, :], bbab[:, tt, 4:8], dtb[:], ALU.add, [bbab, dtb], [gg])
                k.act(gg[:], gg[:], AF.Exp, [gg], [gg])
                k.act(gg[:], gg[:], AF.Ln, [gg], [gg], bias=1.0)
                for tt in range(ntt):
                    k.tt("dve", gg[:, tt, :], gg[:, tt, :], cA[:], ALU.mult, [gg, cA], [gg])
                    if job["gval"] is not None:
                        k.ts("dve", gg[:, tt, :], gg[:, tt, :], gval[:, job["gval"]:job["gval"] + 1], ALU.mult, [gg, gval], [gg])
                k.ts("dve", ngg[:], gg[:], -1.0, ALU.mult, [gg], [ngg])
                if KDBG < 3:
                    continue
                for tt in range(ntt):
                    ts_ = slice(tt * 128, (tt + 1) * 128)
                    G = PQ[0]
                    k.mm(banks[0][:, 0:4], Ublk, gg[:, tt, :], [cst, gg], G)
                    k.mm(banks[0][:, 4:8], Lsblk, gg[:, tt, :], [cst, gg], G)
                    k.mm(banks[0][:, 8:12], half0, gg[:, tt, :], [cst, gg], G)
                    k.mm(banks[0][:, 12:16], half1, gg[:, tt, :], [cst, gg], G)
                    k.act(egc[:, tt, :], banks[0][:, 0:4], AF.Exp, G, [egc])
                    k.act(ekd[:, tt, :], banks[0][:, 4:8], AF.Exp, G, [ekd])
                    k.act(egl[:, tt, :], banks[0][:, 8:16], AF.Exp, G, [egl])
                    k.tt("dve", bege[:, tt, :], bet[:, tt, :], egc[:, tt, :], ALU.mult, [bet, egc], [bege])
                    BK = ((1, 2), (3, 4), (5, 6), (7, 0))
                    H4 = range(4)
                    for h in H4:
                        T_ = HT[h]
                        k.cp("pool", T_["NGb"][:], ngg[:, tt, h:h + 1].to_broadcast([128, 128]), [ngg], [T_["NGb"]])
                    for h in H4:
                        T_ = HT[h]
                        b0, b1 = BK[h]
                        k.mm(pq(b0, 0), Ublk, gg[:, tt, h:h + 1].to_broadcast([128, 128]), [cst, gg], [PB[b0]], start=True, stop=False)
                        k.mm(pq(b0, 0), T_["NGb"][:], Ublk, [cst, T_["NGb"]], [PB[b0]], start=False, stop=True)
                        k.mm(pq(b1, 1), cvb[:, 4 + h, ts_], cvb[:, 4 + h, ts_], [cvb], [PB[b1]])
                    for h in H4:
                        T_ = HT[h]
                        b0, b1 = BK[h]
                        k.stt(T_["E1m"][:], pq(b0, 0), 0.0, NEGML, ALU.min, ALU.add, [PB[b0], cst], [T_["E1m"]])
                        k.stt(T_["E2m"][:], pq(b0, 0), 0.0, POSMU, ALU.max, ALU.add, [PB[b0], cst], [T_["E2m"]])
                    for h in H4:
                        T_ = HT[h]
                        k.act(T_["E1m"][:], T_["E1m"][:], AF.Exp, [T_["E1m"]], [T_["E1m"]])
                        k.act(T_["E2m"][:], T_["E2m"][:], AF.Exp, [T_["E2m"]], [T_["E2m"]], scale=-1.0)
                    for h in H4:
                        T_ = HT[h]
                        b0, b1 = BK[h]
                        k.mm(pq(b0, 2), cvb[:, 4 + h, ts_], cvb[:, h, ts_], [cvb], [PB[b0]])
                    for h in H4:
                        T_ = HT[h]
                        b0, b1 = BK[h]
                        k.stt(T_["N"][:], pq(b1, 1), nbet[:, tt, h:h + 1], T_["E1m"][:], ALU.mult, ALU.mult, [PB[b1], nbet, T_["E1m"]], [T_["N"]])
                    for h in H4:
                        T_ = HT[h]
                        b0, b1 = BK[h]
                        k.tt("dve", T_["attnT"][:], pq(b0, 2), T_["E2m"][:], ALU.mult, [PB[b0], T_["E2m"]], [T_["attnT"]])
                    for h in H4:
                        T_ = HT[h]
                        b0, b1 = BK[h]
                        k.tr(pq(b1, 3), cv[:, 4 + h, ts_], ident, [cv, cst], [PB[b1]])
                        k.tr(pq(b0, 0), cv[:, 8 + h, ts_], ident, [cv, cst], [PB[b0]])
                    for h in H4:
                        T_ = HT[h]
                        b0, b1 = BK[h]
                        k.ts("dve", T_["kbg"][:], pq(b1, 3), bege[:, tt, h:h + 1], ALU.mult, [PB[b1], bege], [T_["kbg"]])
                        k.ts("dve", T_["kd"][:], pq(b1, 3), ekd[:, tt, h:h + 1], ALU.mult, [PB[b1], ekd], [T_["kd"]])
                    for h in H4:
                        T_ = HT[h]
                        b0, b1 = BK[h]
                        k.ts("dve", T_["vb"][:], pq(b0, 0), bet[:, tt, h:h + 1], ALU.mult, [PB[b0], bet], [T_["vb"]])
                    for h in H4:
                        T_ = HT[h]
                        b0, b1 = BK[h]
                        k.mm(pq(b1, 1), T_["N"][:], ident_bf[:], [T_["N"], ident_bf], [PB[b1]])
                    for h in H4:
                        T_ = HT[h]
                        b0, b1 = BK[h]
                        k.cp("act", T_["Nt"][:], pq(b1, 1), [PB[b1]], [T_["Nt"]])
                    for h in H4:
                        T_ = HT[h]
                        b0, b1 = BK[h]
                        k.tt("dve", T_["P"][:], pq(b1, 1), ident, ALU.add, [PB[b1], cst], [T_["P"]])
                    for h in H4:
                        T_ = HT[h]
                        k.cp("pool", T_["Pb"][:], T_["P"][:], [T_["P"]], [T_["Pb"]])
                    if KDBG < 4:
                        continue
                    for lv in range(1, 6):
                        for h in range(4):
                            T_ = HT[h]
                            b0, b1 = ((1, 2), (3, 4), (5, 6), (7, 0))[h]
                            Mp = T_["N"] if lv == 1 else T_[f"M{(lv - 1) % 2}"]
                            Mtp = T_["Nt"] if lv == 1 else T_[f"Mt{(lv - 1) % 2}"]
                            Mn = T_[f"M{lv % 2}"]
                            Mtn = T_[f"Mt{lv % 2}"]
                            k.mm(pq(b1, 2), Mtp[:], Mp[:], [Mtp, Mp], [PB[b1]])
                            if lv < 5:
                                k.mm(pq(b0, 1), Mp[:], Mtp[:], [Mtp, Mp], [PB[b0]])
                            k.cp("act", Mn[:], pq(b1, 2), [PB[b1]], [Mn])
                            if lv < 5:
                                k.cp("dve", Mtn[:], pq(b0, 1), [PB[b0]], [Mtn])
                        for h in range(4):
                            T_ = HT[h]
                            b0, b1 = ((1, 2), (3, 4), (5, 6), (7, 0))[h]
                            Mn = T_[f"M{lv % 2}"]
                            k.mm(pq(b1, 0), Mn[:], T_["Pb"][:], [Mn, T_["Pb"]], [PB[b1]])
                            k.tt("dve", T_["P"][:], T_["P"][:], pq(b1, 0), ALU.add, [T_["P"], PB[b1]], [T_["P"]])
                            if lv < 5:
                                k.cp("pool", T_["Pb"][:], T_["P"][:], [T_["P"]], [T_["Pb"]])
                    if KDBG < 5:
                        continue
                    BK = ((1, 2), (3, 4), (5, 6), (7, 0))
                    for h in range(4):
                        T_ = HT[h]
                        b0, b1 = BK[h]
                        k.mm(pq(b0, 1), T_["P"][:], T_["vb"][:], [T_["P"], T_["vb"]], [PQ[b0][1]])
                        k.cp("act", T_["u"][:], pq(b0, 1), [PQ[b0][1]], [T_["u"]])
                    for h in range(4):
                        T_ = HT[h]
                        b0, b1 = BK[h]
                        k.mm(pq(b1, 2), T_["kbg"][:], T_["P"][:], [T_["P"], T_["kbg"]], [PQ[b1][2]])
                        k.cp("dve", T_["wT"][:], pq(b1, 2), [PQ[b1][2]], [T_["wT"]])
                    for c in range(2):
                        r = slice(64 * c, 64 * c + 64)
                        for h in range(4):
                            T_ = HT[h]
                            b0, b1 = BK[h]
                            k.mm(banks[b0][r, 384:512], T_["wT"][:, r], Sst[h][:], [T_["wT"], Sst[h]], [PQ[b0][3]])
                            k.mm(banks[b0][r, 0:128], cv[:, h, tt * 128 + 64 * c:tt * 128 + 64 * c + 64], Sst[h][:], [cv, Sst[h]], [PQ[b0][0]])
                        for h in range(4):
                            T_ = HT[h]
                            b0, b1 = BK[h]
                            k.tt("dve", T_["vnew"][r, :], T_["u"][r, :], banks[b0][r, 384:512], ALU.subtract, [T_["u"], PQ[b0][3]], [T_["vnew"]])
                        for h in range(4):
                            T_ = HT[h]
                            b0, b1 = BK[h]
                            k.mm(banks[b1][r, 128:256], T_["attnT"][r, r], T_["vnew"][r, :], [T_["attnT"], T_["vnew"]], [PQ[b1][1]])
                            k.mm(pq(b1, 2), T_["kd"][r, :], T_["vnew"][r, :], [T_["kd"], T_["vnew"]], [PQ[b1][2]])
                        for h in range(4):
                            T_ = HT[h]
                            b0, b1 = BK[h]
                            k.stt(Sst[h][:], Sst[h][:], egl[:, tt, 4 * c + h:4 * c + h + 1], pq(b1, 2), ALU.mult, ALU.add, [Sst[h], egl, PQ[b1][2]], [Sst[h]])
                        for h in range(4):
                            T_ = HT[h]
                            b0, b1 = BK[h]
                            k.cp("act", T_["o1"][r, :], banks[b1][r, 128:256], [PQ[b1][1]], [T_["o1"]])
                        for h in range(4):
                            T_ = HT[h]
                            b0, b1 = BK[h]
                            k.stt(T_["o"][r, :], banks[b0][r, 0:128], egc[r, tt, h:h + 1], T_["o1"][r, :], ALU.mult, ALU.add, [PQ[b0][0], egc, T_["o1"]], [T_["o"]])
                    for h in range(4):
                        T_ = HT[h]
                        k.act(T_["tmp"][:], T_["o"][:], AF.Square, [T_["o"]], [T_["tmp"], T_["ms"]], accum=T_["ms"][:, 0:1])
                    for h in range(4):
                        T_ = HT[h]
                        k.act(T_["ms"][:], T_["ms"][:], AF.Sqrt, [T_["ms"], epsT], [T_["ms"]], scale=1.0 / 128, bias=epsT[:, 0:1])
                    for h in range(4):
                        T_ = HT[h]
                        k.recip(T_["ms"][:], T_["ms"][:], [T_["ms"]], [T_["ms"]])
                    for h in range(4):
                        T_ = HT[h]
                        k.stt(T_["tmp"][:], T_["o"][:], T_["ms"][:, 0:1], ngdn[:], ALU.mult, ALU.mult, [T_["o"], T_["ms"], ngdn], [T_["tmp"]])
                    for h in range(4):
                        T_ = HT[h]
                        k.tt("pool", T_["obn"][:], T_["tmp"][:], gbs[:, tt, h * 128:(h + 1) * 128], ALU.mult, [T_["tmp"], gbs], [T_["obn"]])
                    for h in range(4):
                        T_ = HT[h]
                        b0, b1 = BK[h]
                        k.tr(banks[b1][:, 384:512], T_["obn"][:], ident_bf[:], [T_["obn"], ident_bf], [PQ[b1][3]])
                    for h in range(4):
                        b0, b1 = BK[h]
                        k.cp("act", obT_t[:, h, ts_], banks[b1][:, 384:512], [PQ[b1][3]], [obT_t])
                if KDBG >= 5:
                    k.dma(job["obT_scr"][:, :, t0:t0 + NT], obT_t[:], [obT_t], [T_ob], key=obT_t)
            for h in range(4):
                k.dma(job["S_out"][h], Sst[h][:], [Sst[h]], [outT], key=Sst[h])
            S.barrier()

        with ExitStack() as es:
          if KDBG >= 0 and 'p' in KJOB:
            all_pass(es, dict(NT=NTP, L=LKP, x=I["xa"], S0=None, conv0=None, nvalid=128, nvalid_last=NTP,
                              k_out=O["k_all"], v_out=O["v_all"], ki_out=O["ki_all"], kT_scr=kT_p, v_scr=v_p, kiT_scr=kiT_p,
                              koff=0, obT_scr=obT_p, S_out=O["gdn_p"], conv_out=O["gconv_p"], gval=None))
        for sb_ in range(2 if (KDBG >= 0 and 's' in KJOB) else 0):
            with ExitStack() as es:
                all_pass(es, dict(NT=128, L=128, x=I["xs"][sb_], S0=I["state_gdn"][sb_], conv0=I["gconvT"][sb_], nvalid=16, nvalid_last=16,
                                  k_out=O["k_s"][sb_], v_out=O["v_s"][sb_], ki_out=O["ki_s"][sb_], kT_scr=kT_s[sb_], v_scr=v_s[sb_],
                                  kiT_scr=kiT_s[sb_], koff=PAST, obT_scr=obT_s[sb_], S_out=O["gdn_s"][sb_], conv_out=O["gconv_s"][sb_], gval=0))

        convf = sbt(es0, "convf", [128, 22, 3], F32)
        k.dma(convf[:], I["convf"], [], [convf])
        iota = sbt(es0, "iota", [128, 512], F32)
        k.dma(iota[:], I["iota"], [], [iota])
        lims = sbt(es0, "lims", [128, 24], F32)
        k.dma(lims[:], I["lims"], [], [lims])
        BT = sbt(es0, "BT", [128, 2, 8, 128], BF16)
        with ExitStack() as es:
            rb = sbt(es, "rb", [32, 8], F32)
            oh = sbt(es, "oh", [32, 384], F32)
            rb15 = sbt(es, "rb15", [8, 1], F32)
            tabs = sbt(es, "tabs", [8, 384], F32)
            BTf = sbt(es, "BTf", [128, 2, 8, 128], F32)
            k.dma(rb[:], I["relb"], [], [rb])
            k.dma(oh[:], I["ohrev"], [], [oh])
            k.dma(rb15[:], I["relb15"], [], [rb15])
            k.mm(banks[0][0:8, 0:384], rb[:], oh[:], [rb, oh], PQ[0])
            k.ts("dve", tabs[:], banks[0][0:8, 0:384], rb15[:, 0:1], ALU.subtract, PQ[0] + [rb15], [tabs])
            k.dma(tab_scr, tabs[:], [tabs], [T_tab], key=tabs)
            for kp in range(128):
                for dd in range(2):
                    base = 127 + 128 * dd - kp
                    k.dma(BTf[kp:kp + 1, dd, :, :], tab_scr[:, base:base + 128].unsqueeze(0), [T_tab], [BTf])
            k.cp("dve", BT[:], BTf[:], [BTf], [BT])
            ckf = sbt(es, "ckf", [128, 4, 512], F32)
            ckb = sbt(es, "ckb", [128, 4, 512], BF16)
            cvf = sbt(es, "cvf", [128, 512], F32)
            cvb = sbt(es, "cvb", [128, 512], BF16)
            cvb2 = sbt(es, "cvb2", [128, 8, 66], BF16)
            k.memset("pool", cvb2[:, :, 64:66], 1.0, [cvb2])
            for sb_ in range(2):
                ckv = I["cache_kT"][sb_].rearrange("(p r) t -> r p t", r=128)
                for pc in range(4):
                    k.dma(ckf[:], ckv[:, :, pc * 512:(pc + 1) * 512], [], [ckf])
                    k.cp("dve", ckb[:], ckf[:], [ckf], [ckb])
                    k.dma(kT_s[sb_][:, :, pc * 512:(pc + 1) * 512], ckb[:], [ckb], [T_kT], key=ckb)
                    k.dma(cvf[0:64, :], I["cache_kiT"][sb_][:, pc * 512:(pc + 1) * 512], [], [cvf])
                    k.cp("pool", cvb[0:64, :], cvf[0:64, :], [cvf], [cvb])
                    k.dma(kiT_s[sb_][:, pc * 512:(pc + 1) * 512], cvb[0:64, :], [cvb], [T_ki], key=cvb)
                for rb_ in range(16):
                    k.dma(cvf[:], I["cache_v"][sb_][rb_ * 128:(rb_ + 1) * 128, :], [], [cvf])
                    k.cp("pool", cvb2[:, :, 0:64], cvf[:].rearrange("p (h d) -> p h d", d=64), [cvf], [cvb2])
                    k.dma(v_s[sb_][rb_ * 128:(rb_ + 1) * 128, :], cvb2[:].rearrange("p h d -> p (h d)"), [cvb2], [T_v], key=cvb2)
            S.barrier()

        def pieces(n):
            out, c = [], 0
            while c < n:
                w = min(512, n - c)
                out.append((c, w))
                c += w
            return out

        def rms(src, dst, c0, n, sqb, rsb, bank=0):
            k.act(sqb[:, :, 0:n], src[:, :, c0:c0 + n], AF.Square, [src], [sqb])
            for kc in range(8):
                k.mm(banks[bank][:, 0:n], ones_bf[:], sqb[:, kc, 0:n], [ones_bf, sqb], PQ[bank], start=(kc == 0), stop=(kc == 7))
            k.act(rsb[:, 0:n], banks[bank][:, 0:n], AF.Sqrt, PQ[bank] + [epsT], [rsb], scale=1.0 / D, bias=epsT[:, 0:1])
            k.recip(rsb[:, 0:n], rsb[:, 0:n], [rsb], [rsb])
            if dst is not None:
                for kc in range(8):
                    k.tt("dve" if kc % 2 == 0 else "pool", dst[:, kc, c0:c0 + n], src[:, kc, c0:c0 + n], rsb[:, 0:n], ALU.mult, [src, rsb], [dst])

        NIT = 26

        def own_block(jb):
            NQ = jb["NQ"]
            NTOK = 128 * NQ
            with ExitStack() as esA:
                xo = sbt(esA, "xo", [128, 8, NTOK], F32)
                hT = sbt(esA, "hTo", [128, 8, NTOK], BF16)
                oaT = sbt(esA, "oaT", [128, 4, NTOK], BF16)
                sqb = sbt(esA, "sqb", [128, 8, 512], BF16)
                rsb = sbt(esA, "rsb", [128, 512], F32)
                k.dma(xo[:], jb["xsrc"], [], [xo])
                for (c0, n) in pieces(NTOK):
                    rms(xo, hT, c0, n, sqb, rsb)
                if KP2 < 1:
                    return
                with ExitStack() as es:
                    W2 = sbt(es, "W2", [128, 8, 1032], BF16)
                    k.dma(W2[:, :, 0:512], wscr_in[:, :, C_QA:C_QA + 512], [T_wscr_in], [W2])
                    k.dma(W2[:, :, 512:1024], wscr_in[:, :, C_QI:C_QI + 512], [T_wscr_in], [W2])
                    k.dma(W2[:, :, 1024:1032], wscr_in[:, :, C_WI:C_WI + 8], [T_wscr_in], [W2])
                    qaT = sbt(es, "qaT", [128, 4, NTOK], BF16)
                    qiT = sbt(es, "qiT", [128, 4, NTOK], BF16)
                    wiT = sbt(es, "wiT", [128, NQ, 8], F32)
                    n_ = 0
                    for (dstq, cb) in ((qaT, 0), (qiT, 512)):
                        for p in range(4):
                            for (c0, n) in pieces(NTOK):
                                bk = n_ % 2
                                n_ += 1
                                for kc in range(8):
                                    k.mm(banks[bk][:, 0:n], W2[:, kc, cb + p * 128:cb + (p + 1) * 128], hT[:, kc, c0:c0 + n], [W2, hT], PQ[bk], start=(kc == 0), stop=(kc == 7))
                                k.act(dstq[:, p, c0:c0 + n], banks[bk][:, 0:n], AF.Copy, PQ[bk], [dstq], scale=0.125)
                    for qb in range(NQ):
                        for kc in range(8):
                            k.mm(banks[2][:, 0:8], hT[:, kc, qb * 128:(qb + 1) * 128], W2[:, kc, 1024:1032], [W2, hT], PQ[2], start=(kc == 0), stop=(kc == 7))
                        k.ts("dve", wiT[:, qb, :], banks[2][:, 0:8], 8.0 ** -0.5, ALU.mult, PQ[2], [wiT])
                    if KP2 < 2:
                        return
                    LMAX = max(jb["L"])
                    kiT2 = sbt(es, "kiT2", [128, LMAX], BF16)
                    sc = sbt(es, "sc", [128, LMAX], F32)
                    Mb = sbt(es, "Mb", [128, LMAX], BF16)
                    MT = sbt(es, "MT", [128, LMAX // 128, 128], BF16)
                    tmpf = [sbt(es, f"tmpf{i}", [128, 512], F32) for i in range(2)]
                    pen = sbt(es, "pen", [128, 512], F32)
                    Kc = [sbt(es, f"Kc{i}", [128, 4, 512], BF16) for i in range(2)]
                    Vc = [sbt(es, f"Vc{i}", [128, 4, 8, 66], BF16) for i in range(2)]
                    PT = [sbt(es, f"PT{i}", [128, 4, 128], BF16) for i in range(4)]
                    oa = sbt(es, "oa", [128, 8, 64], BF16)
                    den = sbt(es, "den", [128, 8], F32)
                    sm = {nm: sbt(es, nm, [128, 1], F32) for nm in ("mx", "lo", "hi", "mid", "cnt", "ge", "d1", "d2", "off")}
                    for i in range(2):
                        k.memset("pool", Vc[i][:, :, :, 64:65], 1.0, [Vc[i]])
                    lgs = sbt(es, "lgs", [128, 512], F32)

                    def gen_S(qb):
                        L = jb["L"][qb]
                        nkb = L // 128
                        q0 = qb * 128
                        lcol = jb["limcol"] + qb
                        k.dma(kiT2[0:64, 0:L], jb["kiT"][:, 0:L], [T_ki], [kiT2])
                        k.dma(kiT2[64:128, 0:L], jb["kiT"][:, 0:L], [T_ki], [kiT2])
                        tiles = pieces(L)
                        n_ = 0
                        for (c0, w) in tiles:
                            for h in range(8):
                                p, r0 = h // 2, 64 * (h % 2)
                                bk = n_ % 2
                                tf = tmpf[n_ % 2]
                                n_ += 1
                                k.mm(banks[bk][:, 0:w], qiT[r0:r0 + 64, p, q0:q0 + 128], kiT2[r0:r0 + 64, c0:c0 + w], [qiT, kiT2], PQ[bk])
                                k.act(tf[:, 0:w], banks[bk][:, 0:w], AF.Relu, PQ[bk], [tf])
                                if h == 0:
                                    k.ts("dve", sc[:, c0:c0 + w], tf[:, 0:w], wiT[:, qb, 0:1], ALU.mult, [tf, wiT], [sc])
                                else:
                                    k.stt(sc[:, c0:c0 + w], tf[:, 0:w], wiT[:, qb, h:h + 1], sc[:, c0:c0 + w], ALU.mult, ALU.add, [tf, wiT, sc], [sc])
                            yield
                        k.S.op("dve", lambda e, o=sm["mx"][:], i=sc[:, 0:L]: e.tensor_reduce(out=o, in_=i, axis=AX.X, op=ALU.max, apply_absolute_value=True), reads=[sc], writes=[sm["mx"]])
                        k.ts("dve", sm["hi"][:], sm["mx"][:], 1.0, ALU.add, [sm["mx"]], [sm["hi"]])
                        k.ts("dve", sm["lo"][:], sm["mx"][:], -1.0, ALU.mult, [sm["mx"]], [sm["lo"]], s2=-1.0, op1=ALU.add)
                        (c0, w) = tiles[-1]
                        k.ts("dve", sm["off"][:], lims[:, lcol:lcol + 1], float(-c0), ALU.add, [lims], [sm["off"]])
                        k.ts("dve", pen[:, 0:w], iota[:, 0:w], sm["off"][:, 0:1], ALU.is_ge, [iota, sm["off"]], [pen], s2=NEG, op1=ALU.mult)
                        k.tt("dve", sc[:, c0:c0 + w], sc[:, c0:c0 + w], pen[:, 0:w], ALU.add, [sc, pen], [sc])
                        if jb["lolim"] is not None:
                            for ti in range(min(3, len(tiles))):
                                (c0, w) = tiles[ti]
                                k.ts("dve", sm["off"][:], lims[:, jb["lolim"]:jb["lolim"] + 1], float(-c0), ALU.add, [lims], [sm["off"]])
                                k.ts("dve", pen[:, 0:w], iota[:, 0:w], sm["off"][:, 0:1], ALU.is_lt, [iota, sm["off"]], [pen], s2=NEG, op1=ALU.mult)
                                k.tt("dve", sc[:, c0:c0 + w], sc[:, c0:c0 + w], pen[:, 0:w], ALU.add, [sc, pen], [sc])

                    def gen_B(qb):
                        L = jb["L"][qb]
                        nkb = L // 128
                        q0 = qb * 128
                        lcol = jb["limcol"] + qb
                        k.tt("dve", sm["d1"][:], sm["hi"][:], sm["lo"][:], ALU.subtract, [sm["hi"], sm["lo"]], [sm["d1"]])
                        for it in range(NIT):
                            k.ts("dve", sm["d1"][:], sm["d1"][:], 0.5, ALU.mult, [sm["d1"]], [sm["d1"]])
                            k.tt("dve", sm["mid"][:], sm["lo"][:], sm["d1"][:], ALU.add, [sm["lo"], sm["d1"]], [sm["mid"]])
                            k.ts("dve", Mb[:, 0:L], sc[:, 0:L], sm["mid"][:, 0:1], ALU.is_gt, [sc, sm["mid"]], [Mb, sm["cnt"]], s2=0.0, op1=ALU.add, accum=sm["cnt"][:, 0:1])
                            k.stt(sm["ge"][:], sm["cnt"][:], 255.5, sm["d1"][:], ALU.is_ge, ALU.mult, [sm["cnt"], sm["d1"]], [sm["ge"]])
                            k.tt("dve", sm["lo"][:], sm["lo"][:], sm["ge"][:], ALU.add, [sm["lo"], sm["ge"]], [sm["lo"]])
                            yield
                        k.ts("dve", Mb[:, 0:L], sc[:, 0:L], sm["lo"][:, 0:1], ALU.is_gt, [sc, sm["lo"]], [Mb])

                    def do_T(qb):
                        L = jb["L"][qb]
                        nkb = L // 128
                        q0 = qb * 128
                        lcol = jb["limcol"] + qb
                        for kb0 in range(0, nkb, 4):
                            nb = min(4, nkb - kb0)
                            bk = 2 + (kb0 // 4) % 2
                            for i in range(nb):
                                k.mm(banks[bk][:, i * 128:(i + 1) * 128], Mb[:, (kb0 + i) * 128:(kb0 + i + 1) * 128], ident_bf[:], [Mb, ident_bf], PQ[bk])
                            k.cp("act", MT[:, kb0:kb0 + nb, :].rearrange("p a b -> p (a b)"), banks[bk][:, 0:nb * 128], PQ[bk], [MT])

                    def gen_A(qb):
                        L = jb["L"][qb]
                        nkb = L // 128
                        q0 = qb * 128
                        nch = (L + 511) // 512
                        DEPTH = 2

                        def loads(ch):
                            wch = min(512, L - 512 * ch)
                            Kt, Vt = Kc[ch % 2], Vc[ch % 2]
                            k.dma(Kt[:, :, 0:wch], jb["kT"][:, :, 512 * ch:512 * ch + wch], [T_kT], [Kt])
                            k.dma(Vt[:, 0:wch // 128, :, :], jb["v"][512 * ch:512 * ch + wch, :].rearrange("(i p) (h d) -> p i h d", p=128, d=66), [T_v], [Vt])

                        units = []
                        for ch in range(nch):
                            wch = min(512, L - 512 * ch)
                            for i in range(wch // 128):
                                for g in range(2):
                                    units.append((ch, i, g))
                        last_of_chunk = {}
                        for idx, (ch, i, g) in enumerate(units):
                            last_of_chunk[ch] = idx

                        def logits(idx):
                            ch, i, g = units[idx]
                            bk = 2 + idx % 4
                            Kt = Kc[ch % 2]
                            r0 = 64 * g
                            for e4 in range(4):
                                k.mm(banks[bk][:, e4 * 128:(e4 + 1) * 128], Kt[r0:r0 + 64, e4, i * 128:(i + 1) * 128], qaT[r0:r0 + 64, e4, q0:q0 + 128], [Kt, qaT], PQ[bk], start=True, stop=True)

                        def softmax_pv(idx):
                            ch, i, g = units[idx]
                            kbg = 4 * ch + i
                            bk = 2 + idx % 4
                            pt = PT[idx % 4]
                            Vt = Vc[ch % 2]
                            diag = kbg >= nkb - 2
                            dd = 0 if kbg == nkb - 1 else 1
                            if diag:
                                k.tt("dve", lgs[:, :].rearrange("p (h q) -> p h q", q=128), banks[bk][:, :].rearrange("p (h q) -> p h q", q=128), BT[:, dd, g:8:2, :], ALU.add, PQ[bk] + [BT], [lgs])
                                k.act(pt[:].rearrange("p h q -> p (h q)"), lgs[:, :], AF.Exp, [lgs], [pt])
                            else:
                                k.act(pt[:].rearrange("p h q -> p (h q)"), banks[bk][:, :], AF.Exp, PQ[bk], [pt])
                            k.tt("dve", pt[:], pt[:], MT[:, kbg:kbg + 1, :].to_broadcast([128, 4, 128]), ALU.mult, [pt, MT], [pt])
                            for e4 in range(4):
                                h = 2 * e4 + g
                                k.mm(banks[6 + g][:, e4 * 65:(e4 + 1) * 65], pt[:, e4, :], Vt[:, i, h, 0:65], [pt, Vt], PQ[6 + g], start=(kbg == 0 and e4 == 0), stop=(kbg == nkb - 1 and e4 == 3))
                            if last_of_chunk[ch] == idx and ch + 2 < nch:
                                loads(ch + 2)

                        loads(0)
                        if nch > 1:
                            loads(1)
                        nu = len(units)
                        for idx in range(nu + DEPTH):
                            if idx < nu:
                                logits(idx)
                            if idx - DEPTH >= 0:
                                softmax_pv(idx - DEPTH)
                            if idx % 3 == 2:
                                yield

                    def do_N(qb):
                        L = jb["L"][qb]
                        nkb = L // 128
                        q0 = qb * 128
                        lcol = jb["limcol"] + qb
                        for g in range(2):
                            ov = banks[6 + g][:, 0:260].rearrange("p (h d) -> p h d", d=65)
                            k.ts("dve", den[:, 4 * g:4 * g + 4], ov[:, :, 64], 1e-30, ALU.add, PQ[6 + g], [den])
                            k.recip(den[:, 4 * g:4 * g + 4], den[:, 4 * g:4 * g + 4], [den], [den])
                            k.tt("dve", oa[:, g:8:2, :], ov[:, :, 0:64], den[:, 4 * g:4 * g + 4].unsqueeze(2).to_broadcast([128, 4, 64]), ALU.mult, PQ[6 + g] + [den], [oa])
                        oaf = oa[:].rearrange("p h d -> p (h d)")
                        for c in range(4):
                            k.mm(banks[2][:, c * 128:(c + 1) * 128], oaf[:, c * 128:(c + 1) * 128], ident_bf[:], [oa, ident_bf], PQ[2])
                        k.cp("act", oaT[:, :, q0:q0 + 128], banks[2][:, :].rearrange("p (c t) -> p c t", t=128), PQ[2], [oaT])

                    def drain(g):
                        for _ in g:
                            pass

                    def interleave(ga, gb):
                        alive = [ga, gb]
                        while alive:
                            for g_ in list(alive):
                                try:
                                    next(g_)
                                except StopIteration:
                                    alive.remove(g_)

                    def gen_SB(qb):
                        yield from gen_S(qb)
                        yield from gen_B(qb)

                    drain(gen_SB(0))
                    for qb in range(NQ):
                        do_T(qb)
                        if qb + 1 < NQ:
                            interleave(gen_A(qb), gen_SB(qb + 1))
                        else:
                            drain(gen_A(qb))
                        do_N(qb)
                    S.barrier()
                if KP2 < 8:
                    return
                with ExitStack() as es:
                    Wpa = sbt(es, "Wpa", [128, 4, D], BF16)
                    Wpb = sbt(es, "Wpb", [128, 4, D], BF16)
                    Wg = sbt(es, "Wg", [128, 8, 2048], BF16)
                    Wo = sbt(es, "Wo", [128, 8, D], BF16)
                    Wpl = sbt(es, "Wpl", [128, 2, D], BF16)
                    k.dma(Wpa[:], ws_pa, [T_wscr_in], [Wpa])
                    k.dma(Wpb[:], ws_pb, [T_wscr_in], [Wpb])
                    k.dma(Wg[:], wscr_in[:, :, C_GA:C_GA + 2048], [T_wscr_in], [Wg])
                    k.dma(Wpl[:], ws_ple, [T_wscr_in], [Wpl])
                    obT = sbt(es, "obT", [128, 4, NTOK], BF16)
                    k.dma(obT[:], jb["obT"], [T_ob], [obT])
                    pf = sbt(es, "pf", [128, 2, NTOK], F32)
                    pb = sbt(es, "pb", [128, 2, NTOK], BF16)
                    k.dma(pf[:], jb["psrc"], [], [pf])
                    k.cp("pool", pb[:], pf[:], [pf], [pb])
                    mixT = sbt(es, "mixT", [128, 8, 512], BF16)
                    h2T = sbt(es, "h2T", [128, 8, 512], BF16)
                    actT = sbt(es, "actT", [128, 22, 512], BF16)
                    ughalo = sbt(es, "ughalo", [128, 22, 2], F32)
                    fco = sbt(es, "fco", [128, 22, 2], F32)
                    ugc = [sbt(es, f"ugc{i}", [128, 514], F32) for i in range(2)]
                    cva = [sbt(es, f"cva{i}", [128, 512], F32) for i in range(2)]
                    sga = sbt(es, "sga", [128, 512], F32)
                    sgb = sbt(es, "sgb", [128, 512], F32)
                    t1 = sbt(es, "t1", [128, 512], F32)
                    wug = [sbt(es, f"wug{i}", [128, 8, 128], BF16) for i in range(2)]
                    wuv = [sbt(es, f"wuv{i}", [128, 8, 128], BF16) for i in range(2)]
                    wdn = [sbt(es, "wdn0", [128, 22, 128], BF16)] * 2
                    ytok = sbt(es, "ytok", [128, D], F32)
                    if jb["fhalo"] is not None:
                        k.dma(ughalo[:], jb["fhalo"], [], [ughalo])
                    for (c0, n, halo) in jb["segs"]:
                        sg = slice(c0, c0 + n)
                        k.dma(Wo[:], ws_out, [T_wscr_in], [Wo])
                        for c in range(8):
                            cs = slice(c * 128, (c + 1) * 128)
                            bo = 4 * (c % 2)
                            for kc in range(4):
                                k.mm(banks[bo + 0][:, 0:n], Wpa[:, kc, cs], oaT[:, kc, sg], [Wpa, oaT], PQ[bo + 0], start=(kc == 0), stop=(kc == 3))
                            for kc in range(8):
                                k.mm(banks[bo + 1][:, 0:n], Wg[:, kc, cs], hT[:, kc, sg], [Wg, hT], PQ[bo + 1], start=(kc == 0), stop=(kc == 7))
                            k.act(sga[:, 0:n], banks[bo + 1][:, 0:n], AF.Sigmoid, PQ[bo + 1], [sga])
                            k.tt("dve", t1[:, 0:n], banks[bo + 0][:, 0:n], sga[:, 0:n], ALU.mult, PQ[bo + 0] + [sga], [t1])
                            for kc in range(4):
                                k.mm(banks[bo + 2][:, 0:n], Wpb[:, kc, cs], obT[:, kc, sg], [Wpb, obT], PQ[bo + 2], start=(kc == 0), stop=(kc == 3))
                            for kc in range(8):
                                k.mm(banks[bo + 3][:, 0:n], Wg[:, kc, 1024 + c * 128:1024 + (c + 1) * 128], hT[:, kc, sg], [Wg, hT], PQ[bo + 3], start=(kc == 0), stop=(kc == 7))
                            k.act(sgb[:, 0:n], banks[bo + 3][:, 0:n], AF.Sigmoid, PQ[bo + 3], [sgb])
                            k.tt("dve", sgb[:, 0:n], banks[bo + 2][:, 0:n], sgb[:, 0:n], ALU.mult, PQ[bo + 2] + [sgb], [sgb])
                            k.tt("pool", mixT[:, c, 0:n], t1[:, 0:n], sgb[:, 0:n], ALU.add, [t1, sgb], [mixT])
                        for c in range(8):
                            cs = slice(c * 128, (c + 1) * 128)
                            bk = 4 + c % 2
                            for kc in range(8):
                                k.mm(banks[bk][:, 0:n], Wo[:, kc, cs], mixT[:, kc, 0:n], [Wo, mixT], PQ[bk], start=(kc == 0), stop=(kc == 7))
                            k.tt("dve", xo[:, c, sg], xo[:, c, sg], banks[bk][:, 0:n], ALU.add, [xo] + PQ[bk], [xo])
                        k.act(sqb[:, :, 0:n], xo[:, :, sg], AF.Square, [xo], [sqb])
                        for kc in range(8):
                            k.mm(banks[0][:, 0:n], ones_bf[:], sqb[:, kc, 0:n], [ones_bf, sqb], PQ[0], start=(kc == 0), stop=(kc == 7))
                        k.act(rsb[:, 0:n], banks[0][:, 0:n], AF.Sqrt, PQ[0] + [epsT], [rsb], scale=1.0 / D, bias=epsT[:, 0:1])
                        k.recip(rsb[:, 0:n], rsb[:, 0:n], [rsb], [rsb])
                        for kc in range(8):
                            k.tt("dve" if kc % 2 == 0 else "pool", h2T[:, kc, 0:n], xo[:, kc, sg], rsb[:, 0:n], ALU.mult, [xo, rsb], [h2T])
                        for cc in range(22):
                            wg_, wv_ = wug[cc % 2], wuv[cc % 2]
                            k.dma(wg_[:], ws_up[:, :, cc * 128:(cc + 1) * 128], [T_wscr_in], [wg_])
                            bk = 1 + cc % 2
                            for kc in range(8):
                                k.mm(banks[bk][:, 0:n], wg_[:, kc, :], h2T[:, kc, 0:n], [wg_, h2T], PQ[bk], start=(kc == 0), stop=(kc == 7))
                            if halo:
                                k.cp("act", ughalo[:, cc, 0:n], banks[bk][:, 0:n], PQ[bk], [ughalo])
                                continue
                            k.dma(wv_[:], ws_up[:, :, DFF + cc * 128:DFF + (cc + 1) * 128], [T_wscr_in], [wv_])
                            ug, ca = ugc[cc % 2], cva[cc % 2]
                            k.cp("act", ug[:, 2:2 + n], banks[bk][:, 0:n], PQ[bk], [ug])
                            k.cp("pool", ug[:, 0:2], ughalo[:, cc, :], [ughalo], [ug])
                            k.ts("dve", ca[:, 0:n], ug[:, 0:n], convf[:, cc, 0:1], ALU.mult, [ug, convf], [ca])
                            k.stt(ca[:, 0:n], ug[:, 1:1 + n], convf[:, cc, 1:2], ca[:, 0:n], ALU.mult, ALU.add, [ug, convf, ca], [ca])
                            k.stt(ca[:, 0:n], ug[:, 2:2 + n], convf[:, cc, 2:3], ca[:, 0:n], ALU.mult, ALU.add, [ug, convf, ca], [ca])
                            k.act(ca[:, 0:n], ca[:, 0:n], AF.Gelu_apprx_tanh, [ca], [ca])
                            nv = jb["nvalid"]
                            k.cp("pool", fco[:, cc, :], ug[:, nv:nv + 2], [ug], [fco])
                            bk2 = 3 + cc % 2
                            for kc in range(8):
                                k.mm(banks[bk2][:, 0:n], wv_[:, kc, :], h2T[:, kc, 0:n], [wv_, h2T], PQ[bk2], start=(kc == 0), stop=(kc == 7))
                            k.tt("dve", actT[:, cc, 0:n], ca[:, 0:n], banks[bk2][:, 0:n], ALU.mult, [ca] + PQ[bk2], [actT])
                        if halo:
                            continue
                        for c in range(8):
                            wd_ = wdn[c % 2]
                            k.dma(wd_[:], ws_down[:, :, c * 128:(c + 1) * 128], [T_wscr_in], [wd_])
                            bk = 5 + c % 2
                            for cc in range(22):
                                k.mm(banks[bk][:, 0:n], wd_[:, cc, :], actT[:, cc, 0:n], [wd_, actT], PQ[bk], start=(cc == 0), stop=(cc == 21))
                            k.tt("dve", xo[:, c, sg], xo[:, c, sg], banks[bk][:, 0:n], ALU.add, [xo] + PQ[bk], [xo])
                        k.dma(Wo[:], ws_pg, [T_wscr_in], [Wo])
                        Wpg = Wo
                        k.act(sqb[:, :, 0:n], xo[:, :, sg], AF.Square, [xo], [sqb])
                        for kc in range(8):
                            k.mm(banks[0][:, 0:n], ones_bf[:], sqb[:, kc, 0:n], [ones_bf, sqb], PQ[0], start=(kc == 0), stop=(kc == 7))
                        k.act(rsb[:, 0:n], banks[0][:, 0:n], AF.Sqrt, PQ[0] + [epsT], [rsb], scale=1.0 / D, bias=epsT[:, 0:1])
                        k.recip(rsb[:, 0:n], rsb[:, 0:n], [rsb], [rsb])
                        for kc in range(8):
                            k.tt("dve" if kc % 2 == 0 else "pool", h2T[:, kc, 0:n], xo[:, kc, sg], rsb[:, 0:n], ALU.mult, [xo, rsb], [h2T])
                        for c in range(8):
                            cs = slice(c * 128, (c + 1) * 128)
                            for kc in range(8):
                                k.mm(banks[1][:, 0:n], Wpg[:, kc, cs], h2T[:, kc, 0:n], [Wpg, h2T], PQ[1], start=(kc == 0), stop=(kc == 7))
                            k.act(sga[:, 0:n], banks[1][:, 0:n], AF.Sigmoid, PQ[1], [sga])
                            for kc in range(2):
                                k.mm(banks[2][:, 0:n], Wpl[:, kc, cs], pb[:, kc, sg], [Wpl, pb], PQ[2], start=(kc == 0), stop=(kc == 1))
                            k.tt("dve", t1[:, 0:n], banks[2][:, 0:n], sga[:, 0:n], ALU.mult, PQ[2] + [sga], [t1])
                            k.tt("pool", xo[:, c, sg], xo[:, c, sg], t1[:, 0:n], ALU.add, [xo, t1], [xo])
                        k.act(sqb[:, :, 0:n], xo[:, :, sg], AF.Square, [xo], [sqb])
                        for kc in range(8):
                            k.mm(banks[0][:, 0:n], ones_bf[:], sqb[:, kc, 0:n], [ones_bf, sqb], PQ[0], start=(kc == 0), stop=(kc == 7))
                        k.act(rsb[:, 0:n], banks[0][:, 0:n], AF.Sqrt, PQ[0] + [epsT], [rsb], scale=1.0 / D, bias=epsT[:, 0:1])
                        k.recip(rsb[:, 0:n], rsb[:, 0:n], [rsb], [rsb])
                        for kc in range(8):
                            k.stt(xo[:, kc, sg], xo[:, kc, sg], gains[:, 24 + kc:25 + kc], rsb[:, 0:n], ALU.mult, ALU.mult, [xo, gains, rsb], [xo])
                        for tt in range(n // 128):
                            for cg in range(2):
                                bk = 3 + cg
                                for c4 in range(4):
                                    k.mm(banks[bk][:, c4 * 128:(c4 + 1) * 128], xo[:, 4 * cg + c4, c0 + tt * 128:c0 + (tt + 1) * 128], ident, [xo, cst], PQ[bk])
                                k.cp("act" if cg == 0 else "dve", ytok[:, cg * 512:(cg + 1) * 512], banks[bk][:, :], PQ[bk], [ytok])
                            nv = min(128, jb["nvalid"])
                            k.dma(jb["y_out"][tt * 128:tt * 128 + nv, :], ytok[0:nv, :], [ytok], [outT], key=ytok)
                        k.dma(jb["fconv_out"], fco[:], [fco], [outT], key=fco)
                    S.barrier()

        xav = I["xa"].rearrange("(kc p) t -> p kc t", p=128)
        pav = I["pT"].rearrange("(kc p) t -> p kc t", p=128)
        if "P" in KJOB2:
            for m in range(4):
                ps_ = 4 * m + 3
                t0 = 512 * ps_ - 128
                own_block(dict(NQ=5, xsrc=xav[:, :, t0:t0 + 640], psrc=pav[:, :, t0:t0 + 640], L=[128 * (4 * ps_ + r) for r in range(5)],
                               limcol=5 * m, lolim=22, kiT=kiT_p, kT=kT_p, v=v_p, obT=obT_p[:, :, t0:t0 + 640], fhalo=None,
                               segs=[(126, 2, True), (128, 512, False)], nvalid=512, y_out=O["y_own"][512 * m:512 * (m + 1), :], fconv_out=O["fconv_p"]))
        if "S" in KJOB2:
            for sb_ in range(2):
                own_block(dict(NQ=1, xsrc=I["xs"][sb_].rearrange("(kc p) t -> p kc t", p=128), psrc=I["psT"][sb_].rearrange("(kc p) t -> p kc t", p=128),
                               L=[LKS], limcol=20 + sb_, lolim=None, kiT=kiT_s[sb_], kT=kT_s[sb_], v=v_s[sb_], obT=obT_s[sb_], fhalo=I["fconvT"][sb_],
                               segs=[(0, 128, False)], nvalid=16, y_out=O["y_s"][sb_], fconv_out=O["fconv_s"][sb_]))

        print('nsem', S.nsem, {e: len(q) for e, q in S.q.items()})
        S.final_wait("sp", [outT])
        S.emit()
    return nc


def _consts():
    c = np.zeros((128, 7 * 128), np.float32)
    i = np.arange(128)
    same = (i[:, None] // 64) == (i[None, :] // 64)
    c[:, 0:128] = np.eye(128)
    c[:, 128:256] = ((i[:, None] <= i[None, :]) & same)
    c[:, 256:384] = ((i[:, None] > i[None, :]) & same)
    c[:, 384:512] = np.where((i[:, None] > i[None, :]) & same, 0.0, NEG)
    c[:, 512:640] = np.where((i[:, None] <= i[None, :]) & same, 0.0, -NEG)
    c[:, 640:768] = (i[:, None] < 64)
    c[:, 768:896] = (i[:, None] >= 64)
    return c


_NC = None


def kernel(**inp):
    global _NC
    f32 = np.float32
    xp = np.asarray(inp["x_prompt"], f32)
    xs = np.asarray(inp["x_sample"], f32)
    cst = _consts()
    gains = np.zeros((128, 32), f32)
    gains[:, 0:8] = np.asarray(inp["norm_mix"], f32)[0].reshape(8, 128).T
    gains[:, 8:16] = np.asarray(inp["norm_ffn"], f32)[0].reshape(8, 128).T
    gains[:, 16:24] = np.asarray(inp["norm_ple"], f32)[0].reshape(8, 128).T
    gains[:, 24:32] = np.asarray(inp["norm_final"], f32).reshape(8, 128).T
    convb = np.ascontiguousarray(np.asarray(inp["conv_b"], f32)[0].T.reshape(12, 128, 4).transpose(1, 0, 2))
    alog = np.ascontiguousarray(np.broadcast_to(np.asarray(inp["a_log"], f32)[0][None, :], (128, 4)))
    dtb = np.ascontiguousarray(np.broadcast_to(np.asarray(inp["dt_bias"], f32)[0][None, :], (128, 4)))
    ngdn = np.ascontiguousarray(np.broadcast_to(np.asarray(inp["norm_gdn"], f32)[0][None, :], (128, 128)))
    gval = np.zeros((128, 2), f32)
    gval[0:16, 0] = 1.0
    gval[:, 1] = 1.0
    w_in = np.ascontiguousarray(np.asarray(inp["w_in"], f32)[0])
    ck = np.asarray(inp["cache_k"], f32)[0]
    cvv = np.asarray(inp["cache_v"], f32)[0]
    cki = np.asarray(inp["cache_kidx"], f32)[0]
    sg = np.asarray(inp["state_gdn"], f32)[0]
    sgc = np.asarray(inp["state_gdn_conv"], f32)[0]
    def bucket_np(rel):
        half, max_exact = 16, 8
        ret = np.where(rel > 0, half, 0)
        n = np.abs(rel)
        nf = np.maximum(n, 1).astype(np.float32)
        large = max_exact + (np.log(nf / np.float32(max_exact)) / np.float32(np.log(128 / 8)) * np.float32(half - max_exact)).astype(np.int32)
        large = np.minimum(large, half - 1)
        return ret + np.where(n < max_exact, n, large)
    ohrev = np.zeros((32, 384), f32)
    sp_ = np.arange(383)
    ohrev[bucket_np(127 - sp_), sp_] = 1.0
    iota = np.ascontiguousarray(np.broadcast_to(np.arange(512, dtype=f32)[None, :], (128, 512)))
    relb = np.ascontiguousarray(np.asarray(inp["rel_bias"], f32))
    relb15 = np.ascontiguousarray(relb[15][:, None])
    convf = np.ascontiguousarray(np.asarray(inp["conv_ffn"], f32)[0].T.reshape(22, 128, 3).transpose(1, 0, 2))
    pp = np.asarray(inp["p_prompt"], f32)[0]
    psm = np.asarray(inp["p_sample"], f32)[0]
    sfc = np.asarray(inp["state_ffn_conv"], f32)[0]
    wts = dict(w_pa=np.ascontiguousarray(np.asarray(inp["w_proj_a"], f32)[0]), w_pb=np.ascontiguousarray(np.asarray(inp["w_proj_b"], f32)[0]),
               w_out=np.ascontiguousarray(np.asarray(inp["w_out"], f32)[0]), w_up=np.ascontiguousarray(np.asarray(inp["w_up"], f32)[0]),
               w_down=np.ascontiguousarray(np.asarray(inp["w_down"], f32)[0]), w_ple=np.ascontiguousarray(np.asarray(inp["w_ple"], f32)[0]),
               w_pg=np.ascontiguousarray(np.asarray(inp["w_ple_gate"], f32)[0]))
    in_maps = []
    for c in range(8):
        b, j = c // 4, c % 4
        pad = 512 * (3 - j)
        xa = np.zeros((D, LKP), f32)
        xa[:, pad:] = xp[b].T[:, :LKP - pad]
        sl = slice(2 * c, 2 * c + 2)
        xs_c = np.zeros((2, D, 128), f32)
        xs_c[:, :, 0:16] = xs[sl].transpose(0, 2, 1)
        m = dict(xa=xa, xs=xs_c, cst=cst, gains=gains, convb=convb, alog=alog, dtb=dtb, ngdn=ngdn, gval=gval, w_in=w_in,
                 cache_kT=np.ascontiguousarray(ck[sl].reshape(2, PAST, 512).transpose(0, 2, 1)),
                 cache_v=np.ascontiguousarray(cvv[sl].reshape(2, PAST, 512)),
                 cache_kiT=np.ascontiguousarray(cki[sl].transpose(0, 2, 1)),
                 state_gdn=np.ascontiguousarray(sg[sl]),
                 gconvT=np.ascontiguousarray(sgc[sl].transpose(0, 2, 1).reshape(2, 12, 128, 3).transpose(0, 2, 1, 3)))
        pT = np.zeros((256, LKP), f32)
        pT[:, pad:] = pp[b].T[:, :LKP - pad]
        psT = np.zeros((2, 256, 128), f32)
        psT[:, :, 0:16] = psm[sl].transpose(0, 2, 1)
        lims = np.zeros((128, 24), f32)
        ii = np.arange(128)
        for m_ in range(4):
            for qb in range(5):
                pt = 512 * (4 * m_ + 3) - 128 + 128 * qb + ii
                lims[:, 5 * m_ + qb] = (pt // 64 + 1) * 64
        lims[:, 20] = PAST + 16
        lims[:, 21] = PAST + 16
        lims[:, 22] = pad
        m.update(pT=pT, psT=psT, convf=convf, relb=relb, relb15=relb15, ohrev=ohrev, iota=iota, lims=lims,
                 fconvT=np.ascontiguousarray(sfc[sl].transpose(0, 2, 1).reshape(2, 22, 128, 2).transpose(0, 2, 1, 3)), **wts)
        in_maps.append(m)
    if _NC is None:
        _NC = build_program()
    res = run_bass_kernel_spmd(_NC, in_maps, core_ids=list(range(8)))
    R = res.results
    y_p = np.zeros((2, 8192, D), f32)
    for c in range(8):
        b, j = c // 4, c % 4
        for m_ in range(4):
            sg_ = 4 * m_ + j
            y_p[b, 512 * sg_:512 * (sg_ + 1)] = R[c]["y_own"][512 * m_:512 * (m_ + 1)]
    y_s = np.concatenate([R[c]["y_s"] for c in range(8)]).reshape(16, 16, D)
    k_p = np.stack([R[4 * b + 3]["k_all"] for b in range(2)]).reshape(1, 2, 8192, 8, 64)
    v_p = np.stack([R[4 * b + 3]["v_all"] for b in range(2)]).reshape(1, 2, 8192, 8, 64)
    ki_p = np.stack([R[4 * b + 3]["ki_all"] for b in range(2)]).reshape(1, 2, 8192, 64)
    gdn_p = np.stack([R[4 * b + 3]["gdn_p"] for b in range(2)]).reshape(1, 2, 4, 128, 128)
    gconv_p = np.stack([R[4 * b + 3]["gconv_p"].transpose(1, 0, 2).reshape(1536, 3).T for b in range(2)]).reshape(1, 2, 3, 1536)
    fconv_p = np.stack([R[4 * b + 3]["fconv_p"].transpose(1, 0, 2).reshape(DFF, 2).T for b in range(2)]).reshape(1, 2, 2, DFF)
    k_s = np.concatenate([R[c]["k_s"] for c in range(8)]).reshape(1, 16, 16, 8, 64)
    v_s = np.concatenate([R[c]["v_s"] for c in range(8)]).reshape(1, 16, 16, 8, 64)
    ki_s = np.concatenate([R[c]["ki_s"] for c in range(8)]).reshape(1, 16, 16, 64)
    gdn_s = np.concatenate([R[c]["gdn_s"] for c in range(8)]).reshape(1, 16, 4, 128, 128)
    gconv_s = np.concatenate([R[c]["gconv_s"] for c in range(8)])
    gconv_s = np.ascontiguousarray(gconv_s.transpose(0, 2, 1, 3).reshape(16, 1536, 3).transpose(0, 2, 1)).reshape(1, 16, 3, 1536)
    fconv_s = np.concatenate([R[c]["fconv_s"] for c in range(8)])
    fconv_s = np.ascontiguousarray(fconv_s.transpose(0, 2, 1, 3).reshape(16, DFF, 2).transpose(0, 2, 1)).reshape(1, 16, 2, DFF)
    return (y_p, y_s, k_p, v_p, ki_p, gdn_p, gconv_p, fconv_p, k_s, v_s, ki_s, gdn_s, gconv_s, fconv_s)
```

```python
import os
import numpy as np
from contextlib import ExitStack
import concourse.bass as bass
import concourse.mybir as mybir
from concourse.bass_utils import run_bass_kernel_spmd

F32 = mybir.dt.float32
BF16 = mybir.dt.bfloat16
AF = mybir.ActivationFunctionType
ALU = mybir.AluOpType
AX = mybir.AxisListType

EPOCH = 4096
KDBG = int(os.environ.get('KDBG', '9'))
KJOB = os.environ.get('KJOB', 'ps')
KSUB = int(os.environ.get('KSUB', '99'))
KJOB2 = os.environ.get('KJOB2', 'PS')
KP2 = int(os.environ.get('KP2', '99'))
KP3 = int(os.environ.get('KP3', '99'))
EPS = 1e-6
NEG = -1e30

D = 1024
LKP = 8192
NTP = 256
PAST = 2048
LKS = PAST + 128
DFF = 2816

C_QA, C_KA, C_VA, C_QI, C_KI, C_WI, C_QB, C_GB, C_BB, C_AB, C_GA, C_GBR = 0, 512, 1024, 1536, 2048, 2112, 2120, 3656, 4168, 4172, 4176, 5200
DIN = 6224


class Tl:
    __slots__ = ("t", "name", "lw", "rd", "excl")

    def __init__(self, t, name, excl=False):
        self.t = t
        self.name = name
        self.lw = None
        self.rd = []
        self.excl = excl

    def __getitem__(self, idx):
        return self.t[idx]


class Sched:
    ENGS = ("pe", "act", "dve", "pool", "sp")

    def __init__(self, nc, es):
        self.nc = nc
        self.es = es
        self.q = {e: [] for e in self.ENGS}
        self.cnt = {e: 0 for e in self.ENGS}
        self.sems = {e: [] for e in self.ENGS}
        self.seen = {e: {} for e in self.ENGS}
        self.dma_sems = {}
        self.keep = []
        self.nsem = 0

    def _newsem(self, name):
        self.nsem += 1
        return self.es.enter_context(self.nc.semaphore(name))

    def _eng_token(self, e):
        c = self.cnt[e]
        ep = c // EPOCH
        while len(self.sems[e]) <= ep:
            self.sems[e].append(self._newsem(f"s_{e}_{len(self.sems[e])}"))
        self.cnt[e] = c + 1
        return (("e", e, ep), self.sems[e][ep], (c % EPOCH) + 1)

    def _waits(self, e, reads, writes):
        toks = []
        for t in reads:
            if t.lw is not None:
                toks.append(t.lw)
            if t.excl:
                toks.extend(t.rd)
        for t in writes:
            if t.lw is not None:
                toks.append(t.lw)
            toks.extend(t.rd)
        best = {}
        for (key, sem, val) in toks:
            if key[0] == "e" and key[1] == "pe" and e == "pe":
                continue
            if best.get(key, (None, 0))[1] < val:
                best[key] = (sem, val)
        out = []
        seen = self.seen[e]
        for key, (sem, val) in best.items():
            if seen.get(key, 0) >= val:
                continue
            seen[key] = val
            out.append((sem, val))
        return out

    def op(self, e, fn, reads=(), writes=()):
        w = self._waits(e, reads, writes)
        tok = self._eng_token(e)
        self.q[e].append((w, fn, tok[1], 1))
        for t in writes:
            t.lw = tok
            t.rd = []
        for t in reads:
            if t.lw is not tok:
                t.rd.append(tok)
        return tok

    def dma(self, e, fn, reads=(), writes=(), key=None):
        w = self._waits(e, reads, writes)
        k = key if key is not None else (writes[0] if writes else reads[0])
        kid = id(k)
        ent = self.dma_sems.get(kid)
        if ent is None or ent[1] >= 32000:
            if ent is None:
                self.keep.append(k)
            if getattr(self, "pool", None):
                ent = self.pool.pop()
            else:
                self.semid = getattr(self, "semid", 0) + 1
                ent = [self._newsem(f"d_{self.nsem}"), 0, self.semid]
            self.dma_sems[kid] = ent
        ent[1] += 16
        tok = (("d", ent[2], 0), ent[0], ent[1])
        self.q[e].append((w, fn, ent[0], 16))
        for t in writes:
            t.lw = tok
            t.rd = []
        for t in reads:
            if t.lw is not tok:
                t.rd.append(tok)
        return tok

    def barrier(self):
        toks = []
        for e in self.ENGS:
            c = self.cnt[e]
            if c > 0:
                ep = (c - 1) // EPOCH
                toks.append((("e", e, ep), self.sems[e][ep], ((c - 1) % EPOCH) + 1))
        for kid, ent in self.dma_sems.items():
            if ent[1] > 0:
                toks.append((("d", ent[2], 0), ent[0], ent[1]))
        for e in self.ENGS:
            w = []
            seen = self.seen[e]
            for (key, sem, val) in toks:
                if key[0] == "e" and key[1] == e:
                    continue
                if seen.get(key, 0) >= val:
                    continue
                seen[key] = val
                w.append((sem, val))
            if w:
                self.q[e].append((w, None, None, 0))
        if not hasattr(self, "pool"):
            self.pool = []
        for kid, ent in self.dma_sems.items():
            if ent[1] < 30000:
                self.pool.append(ent)
        self.dma_sems = {}

    def final_wait(self, e, tiles):
        w = self._waits(e, tiles, tiles)
        self.q[e].append((w, None, None, 0))

    def emit(self):
        nc = self.nc
        with nc.Block() as block:
            def run(eng, name):
                for (w, fn, sem, inc) in self.q[name]:
                    for (s, v) in w:
                        eng.wait_ge(s, v)
                    if fn is not None:
                        fn(eng).then_inc(sem, inc)

            @block.tensor
            def _(eng):
                run(eng, "pe")

            @block.scalar
            def _(eng):
                run(eng, "act")

            @block.vector
            def _(eng):
                run(eng, "dve")

            @block.gpsimd
            def _(eng):
                run(eng, "pool")

            @block.sync
            def _(eng):
                run(eng, "sp")


class K:
    def __init__(self, nc, S):
        self.nc = nc
        self.S = S
        self.rr = 0

    def mm(self, out, lhsT, rhs, rd, wr, start=True, stop=True):
        self.S.op("pe", lambda e, o=out, l=lhsT, r=rhs, a=start, b=stop: e.matmul(o, lhsT=l, rhs=r, start=a, stop=b), reads=rd, writes=wr)

    def tr(self, out, in_, ident, rd, wr):
        self.S.op("pe", lambda e, o=out, i=in_, d=ident: e.matmul(o, lhsT=i, rhs=d, start=True, stop=True), reads=rd, writes=wr)

    def act(self, out, in_, func, rd, wr, scale=None, bias=None, accum=None):
        kw = {}
        if scale is not None:
            kw["scale"] = scale
        if bias is not None:
            kw["bias"] = bias
        if accum is not None:
            kw["accum_out"] = accum
        self.S.op("act", lambda e, o=out, i=in_, f=func, k=kw: e.activation(out=o, in_=i, func=f, **k), reads=rd, writes=wr)

    def ts(self, eng, out, in0, s1, op0, rd, wr, s2=None, op1=None, accum=None):
        kw = {}
        if op1 is not None:
            kw["op1"] = op1
        if accum is not None:
            kw["accum_out"] = accum
        self.S.op(eng, lambda e, o=out, i=in0, a=s1, b=s2, p=op0, k=kw: e.tensor_scalar(out=o, in0=i, scalar1=a, scalar2=b, op0=p, **k), reads=rd, writes=wr)

    def tt(self, eng, out, in0, in1, op, rd, wr):
        self.S.op(eng, lambda e, o=out, i=in0, j=in1, p=op: e.tensor_tensor(out=o, in0=i, in1=j, op=p), reads=rd, writes=wr)

    def stt(self, out, in0, sc, in1, op0, op1, rd, wr):
        self.S.op("dve", lambda e, o=out, i=in0, s=sc, j=in1, p=op0, q=op1: e.scalar_tensor_tensor(out=o, in0=i, scalar=s, in1=j, op0=p, op1=q), reads=rd, writes=wr)

    def cp(self, eng, out, in_, rd, wr):
        if eng == "act":
            self.S.op("act", lambda e, o=out, i=in_: e.copy(out=o, in_=i), reads=rd, writes=wr)
        else:
            self.S.op(eng, lambda e, o=out, i=in_: e.tensor_copy(out=o, in_=i), reads=rd, writes=wr)

    def memset(self, eng, ap, val, wr):
        self.S.op(eng, lambda e, a=ap, v=val: e.memset(a, v), writes=wr)

    def recip(self, out, in_, rd, wr):
        self.S.op("dve", lambda e, o=out, i=in_: e.reciprocal(out=o, in_=i), reads=rd, writes=wr)

    def dma(self, out, in_, rd, wr, q="sp", key=None, slow=False):
        if slow:
            self.S.dma(q, lambda e, o=out, i=in_: e.dma_start(out=o, in_=i, allow_slow_non_contiguous=True), reads=rd, writes=wr, key=key)
        else:
            self.S.dma(q, lambda e, o=out, i=in_: e.dma_start(out=o, in_=i), reads=rd, writes=wr, key=key)


def build_program():
    nc = bass.Bass("TRN2", target_bir_lowering=False)

    def din(name, shape, dt=F32):
        return nc.dram_tensor(name, list(shape), dt, kind="ExternalInput").ap()

    def dout(name, shape, dt=F32):
        return nc.dram_tensor(name, list(shape), dt, kind="ExternalOutput").ap()

    def dscr(name, shape, dt):
        return nc.dram_tensor(name, list(shape), dt, kind="Internal").ap()

    I = {}
    I["xa"] = din("xa", [D, LKP])
    I["xs"] = din("xs", [2, D, 128])
    I["cst"] = din("cst", [128, 7 * 128])
    I["gains"] = din("gains", [128, 32])
    I["convb"] = din("convb", [128, 12, 4])
    I["alog"] = din("alog", [128, 4])
    I["dtb"] = din("dtb", [128, 4])
    I["ngdn"] = din("ngdn", [128, 128])
    I["gval"] = din("gval", [128, 2])
    I["w_in"] = din("w_in", [D, DIN])
    I["cache_kT"] = din("cache_kT", [2, 512, PAST])
    I["cache_v"] = din("cache_v", [2, PAST, 512])
    I["cache_kiT"] = din("cache_kiT", [2, 64, PAST])
    I["state_gdn"] = din("state_gdn", [2, 4, 128, 128])
    I["gconvT"] = din("gconvT", [2, 128, 12, 3])
    I["pT"] = din("pT", [256, LKP])
    I["psT"] = din("psT", [2, 256, 128])
    I["w_pa"] = din("w_pa", [512, D])
    I["w_pb"] = din("w_pb", [512, D])
    I["w_out"] = din("w_out", [D, D])
    I["w_up"] = din("w_up", [D, 2 * DFF])
    I["w_down"] = din("w_down", [DFF, D])
    I["w_ple"] = din("w_ple", [256, D])
    I["w_pg"] = din("w_pg", [D, D])
    I["convf"] = din("convf", [128, 22, 3])
    I["relb"] = din("relb", [32, 8])
    I["relb15"] = din("relb15", [8, 1])
    I["ohrev"] = din("ohrev", [32, 384])
    I["iota"] = din("iota", [128, 512])
    I["lims"] = din("lims", [128, 24])
    I["fconvT"] = din("fconvT", [2, 128, 22, 2])
    O = {}
    O["y_own"] = dout("y_own", [2048, D])
    O["fconv_p"] = dout("fconv_p", [128, 22, 2])
    O["y_s"] = dout("y_s", [2, 16, D])
    O["fconv_s"] = dout("fconv_s", [2, 128, 22, 2])
    O["k_all"] = dout("k_all", [LKP, 512])
    O["v_all"] = dout("v_all", [LKP, 512])
    O["ki_all"] = dout("ki_all", [LKP, 64])
    O["gdn_p"] = dout("gdn_p", [4, 128, 128])
    O["gconv_p"] = dout("gconv_p", [128, 12, 3])
    O["k_s"] = dout("k_s", [2, 16, 512])
    O["v_s"] = dout("v_s", [2, 16, 512])
    O["ki_s"] = dout("ki_s", [2, 16, 64])
    O["gdn_s"] = dout("gdn_s", [2, 4, 128, 128])
    O["gconv_s"] = dout("gconv_s", [2, 128, 12, 3])
    wscr_in = dscr("wscr_in", [128, 8, DIN], BF16)
    ws_pa = dscr("ws_pa", [128, 4, D], BF16)
    ws_pb = dscr("ws_pb", [128, 4, D], BF16)
    ws_out = dscr("ws_out", [128, 8, D], BF16)
    ws_up = dscr("ws_up", [128, 8, 2 * DFF], BF16)
    ws_down = dscr("ws_down", [128, 22, D], BF16)
    ws_ple = dscr("ws_ple", [128, 2, D], BF16)
    ws_pg = dscr("ws_pg", [128, 8, D], BF16)
    tab_scr = dscr("tab_scr", [8, 384], F32)
    T_tab = Tl(None, "tab")
    kT_p = dscr("kT_p", [128, 4, LKP], BF16)
    v_p = dscr("v_p", [LKP, 528], BF16)
    kiT_p = dscr("kiT_p", [64, LKP], BF16)
    obT_p = dscr("obT_p", [128, 4, LKP], BF16)
    kT_s = dscr("kT_s", [2, 128, 4, LKS], BF16)
    v_s = dscr("v_s_scr", [2, LKS, 528], BF16)
    kiT_s = dscr("kiT_s", [2, 64, LKS], BF16)
    obT_s = dscr("obT_s", [2, 128, 4, 128], BF16)

    outT = Tl(None, "outputs")
    T_wscr_in = Tl(None, "wscr_in")
    T_kT = Tl(None, "kT")
    T_v = Tl(None, "v")
    T_ki = Tl(None, "kiT")
    T_ob = Tl(None, "obT")

    with ExitStack() as es0:
        S = Sched(nc, es0)
        k = K(nc, S)

        uid = [0]

        def sbt(es, name, shape, dt):
            uid[0] += 1
            nm = f"s{uid[0]}_{name}"
            return Tl(es.enter_context(nc.sbuf_tensor(nm, list(shape), dt)), nm)

        banks = [es0.enter_context(nc.psum_tensor(f"pb{i}", [128, 512], F32)) for i in range(8)]
        PB = [Tl(banks[b], f"pb{b}", excl=True) for b in range(8)]
        PQ = [[PB[b]] * 4 for b in range(8)]
        banks_bf = None

        def pq(b, q, rows=slice(0, 128), w=128):
            return banks[b][rows, q * 128:q * 128 + w]

        cst = sbt(es0, "cst", [128, 7 * 128], F32)
        k.dma(cst[:], I["cst"], [], [cst])
        ident = cst[:, 0:128]
        Ublk = cst[:, 128:256]
        Lsblk = cst[:, 256:384]
        NEGML = cst[:, 384:512]
        POSMU = cst[:, 512:640]
        half0 = cst[:, 640:768]
        half1 = cst[:, 768:896]
        gains = sbt(es0, "gains", [128, 32], F32)
        k.dma(gains[:], I["gains"], [], [gains])
        convb = sbt(es0, "convb", [128, 12, 4], F32)
        k.dma(convb[:], I["convb"], [], [convb])
        cA = sbt(es0, "cA", [128, 4], F32)
        k.dma(cA[:], I["alog"], [], [cA])
        dtb = sbt(es0, "dtb", [128, 4], F32)
        k.dma(dtb[:], I["dtb"], [], [dtb])
        ngdn = sbt(es0, "ngdn", [128, 128], F32)
        k.dma(ngdn[:], I["ngdn"], [], [ngdn])
        gval = sbt(es0, "gval", [128, 2], F32)
        k.dma(gval[:], I["gval"], [], [gval])
        epsT = sbt(es0, "epsT", [128, 1], F32)
        k.memset("pool", epsT[:], EPS, [epsT])
        ident_bf = sbt(es0, "ident_bf", [128, 128], BF16)
        k.cp("dve", ident_bf[:], ident, [cst], [ident_bf])
        ones_bf = sbt(es0, "ones_bf", [128, 128], BF16)
        k.memset("pool", ones_bf[:], 1.0, [ones_bf])
        k.act(cA[:], cA[:], AF.Exp, [cA], [cA])
        k.ts("dve", cA[:], cA[:], -1.0, ALU.mult, [cA], [cA])

        with ExitStack() as es:
            if KDBG < -1:
                raise_skip = True
            wf = [sbt(es, f"wf{i}", [128, 1556], F32) for i in range(2)]
            wb = [sbt(es, f"wb{i}", [128, 1556], BF16) for i in range(2)]
            cnt_ = [0]

            pcs = []

            def conv_w(src, dst, KC, C, gbase):
                v = src.rearrange("(kc p) c -> p kc c", p=128)
                npc = 1 if C <= 1556 else 4
                pw = C // npc
                for kc in range(KC):
                    for pc in range(npc):
                        pcs.append((v[:, kc, pc * pw:(pc + 1) * pw], dst[:, kc, pc * pw:(pc + 1) * pw], pw, None if gbase is None else gbase + kc))

            def conv_emit():
                def load(n):
                    k.dma(wf[n % 2][:, 0:pcs[n][2]], pcs[n][0], [], [wf[n % 2]], q="sp")
                if pcs:
                    load(0)
                for n, (src_, dst_, pw, gcol) in enumerate(pcs):
                    a, b_ = wf[n % 2], wb[n % 2]
                    if n + 1 < len(pcs):
                        load(n + 1)
                    eng = "dve" if n % 2 == 0 else "pool"
                    if gcol is None:
                        k.cp(eng, b_[:, 0:pw], a[:, 0:pw], [a], [b_])
                    else:
                        k.ts(eng, b_[:, 0:pw], a[:, 0:pw], gains[:, gcol:gcol + 1], ALU.mult, [a, gains], [b_])
                    k.dma(dst_, b_[:, 0:pw], [b_], [T_wscr_in], q="sp", key=b_)

            if KDBG >= -1:
                conv_w(I["w_in"], wscr_in, 8, DIN, 0)
                conv_w(I["w_pa"], ws_pa, 4, D, None)
                conv_w(I["w_pb"], ws_pb, 4, D, None)
                conv_w(I["w_out"], ws_out, 8, D, None)
                conv_w(I["w_up"], ws_up, 8, 2 * DFF, 8)
                conv_w(I["w_down"], ws_down, 22, D, None)
                conv_w(I["w_ple"], ws_ple, 2, D, None)
                conv_w(I["w_pg"], ws_pg, 8, D, 16)
                conv_emit()
            S.barrier()

        def all_pass(es, job):
            NT = job["NT"]
            ntt = NT // 128
            nsteps = min(job["L"] // NT, int(os.environ.get('KSTEPS', '999')))
            xsrc = job["x"]
            W1 = sbt(es, "W1", [128, 8, 3144], BF16)
            k.dma(W1[:, :, 0:1024], wscr_in[:, :, C_KA:C_KA + 1024], [T_wscr_in], [W1])
            k.dma(W1[:, :, 1024:1088], wscr_in[:, :, C_KI:C_KI + 64], [T_wscr_in], [W1])
            k.dma(W1[:, :, 1088:3144], wscr_in[:, :, C_QB:C_QB + 2056], [T_wscr_in], [W1])
            xa_t = [sbt(es, f"xa_t{i}", [128, 8, NT], F32) for i in range(2)]
            sq = sbt(es, "sq", [128, 8, NT], BF16)
            hT = sbt(es, "hT", [128, 8, NT], BF16)
            rs = sbt(es, "rs", [128, NT], F32)
            cin = sbt(es, "cin", [128, 12, NT + 3], F32)
            cv = sbt(es, "cv", [128, 12, NT], F32)
            cvb = sbt(es, "cvb", [128, 8, NT], BF16)
            sq2 = sbt(es, "sq2", [128, NT], BF16)
            rs2 = sbt(es, "rs2", [128, NT], F32)
            gbs = sbt(es, "gbs", [128, ntt, 512], F32)
            bbab = sbt(es, "bbab", [128, ntt, 8], F32)
            ktok = [sbt(es, f"ktok{i}", [128, 512], F32) for i in range(2)]
            vtok = [sbt(es, f"vtok{i}", [128, 512], F32) for i in range(2)]
            vbf = [sbt(es, f"vbf{i}", [128, 8, 66], BF16) for i in range(2)]
            for i in range(2):
                k.memset("pool", vbf[i][:, :, 64:66], 1.0, [vbf[i]])
            kitok = [sbt(es, f"kitok{i}", [128, 64], F32) for i in range(2)]
            kf_t = sbt(es, "kf_t", [128, 4, NT], BF16)
            kif_t = sbt(es, "kif_t", [64, NT], BF16)
            obT_t = sbt(es, "obT_t", [128, 4, NT], BF16)
            Sst = [sbt(es, f"Sst{h}", [128, 128], F32) for h in range(4)]
            bet = sbt(es, "bet", [128, ntt, 4], F32)
            nbet = sbt(es, "nbet", [128, ntt, 4], F32)
            gg = sbt(es, "gg", [128, ntt, 4], F32)
            ngg = sbt(es, "ngg", [128, ntt, 4], F32)
            egc = sbt(es, "egc", [128, ntt, 4], F32)
            ekd = sbt(es, "ekd", [128, ntt, 4], F32)
            egl = sbt(es, "egl", [128, ntt, 8], F32)
            bege = sbt(es, "bege", [128, ntt, 4], F32)
            HT = []
            for h in range(4):
                d = {}
                for nm in ("NGb", "E1m", "E2m", "P", "attnT", "kbg", "kd", "vb", "u", "wT", "vnew", "o1", "o", "tmp"):
                    d[nm] = sbt(es, f"{nm}{h}", [128, 128], F32)
                for nm in ("N", "Nt", "M0", "M1", "Mt0", "Mt1", "Pb"):
                    d[nm] = sbt(es, f"{nm}{h}", [128, 128], BF16)
                d["obn"] = sbt(es, f"obn{h}", [128, 128], BF16)
                d["ms"] = sbt(es, f"ms{h}", [128, 1], F32)
                HT.append(d)

            if job["S0"] is None:
                for h in range(4):
                    k.memset("pool", Sst[h][:], 0.0, [Sst[h]])
                k.memset("pool", cin[:, :, 0:3], 0.0, [cin])
            else:
                for h in range(4):
                    k.dma(Sst[h][:], job["S0"][h], [], [Sst[h]])
                k.dma(cin[:, :, 0:3], job["conv0"], [], [cin], slow=True)

            xv = xsrc.rearrange("(kc p) t -> p kc t", p=128)
            k.dma(xa_t[0][:], xv[:, :, 0:NT], [], [xa_t[0]])
            for st in range(nsteps):
                t0 = st * NT
                xt = xa_t[st % 2]
                if st + 1 < nsteps:
                    xn = xa_t[(st + 1) % 2]
                    k.dma(xn[:], xv[:, :, t0 + NT:t0 + 2 * NT], [], [xn])
                k.act(sq[:], xt[:], AF.Square, [xt], [sq])
                A = PQ[0]
                for kc in range(8):
                    k.mm(banks[0][:, 0:NT], ones_bf[:], sq[:, kc, :], [ones_bf, sq], A, start=(kc == 0), stop=(kc == 7))
                k.act(rs[:], banks[0][:, 0:NT], AF.Sqrt, A + [epsT], [rs], scale=1.0 / D, bias=epsT[:, 0:1])
                k.recip(rs[:], rs[:], [rs], [rs])
                for kc in range(8):
                    k.tt("dve" if kc % 2 == 0 else "pool", hT[:, kc, :], xt[:, kc, :], rs[:], ALU.mult, [xt, rs], [hT])
                if KDBG < 1:
                    continue
                for tt in range(ntt):
                    ts_ = slice(tt * 128, (tt + 1) * 128)
                    r0 = t0 + tt * 128
                    kt, vt, vb_, kit = ktok[tt % 2], vtok[tt % 2], vbf[tt % 2], kitok[tt % 2]
                    for (bk, c0, cw) in ((1, 0, 512), (2, 512, 512)):
                        for kc in range(8):
                            k.mm(banks[bk][:, 0:cw], hT[:, kc, ts_], W1[:, kc, c0:c0 + cw], [hT, W1], PQ[bk], start=(kc == 0), stop=(kc == 7))
                    k.cp("act", kt[:], banks[1][:, :], PQ[1], [kt])
                    k.cp("dve", vt[:], banks[2][:, :], PQ[2], [vt])
                    k.cp("pool", vb_[:, :, 0:64], vt[:].rearrange("p (h d) -> p h d", d=64), [vt], [vb_])
                    nv = job["nvalid"]
                    if nv >= 128:
                        k.dma(job["k_out"][r0:r0 + 128, :], kt[:], [kt], [outT], key=kt)
                        k.dma(job["v_out"][r0:r0 + 128, :], vt[:], [vt], [outT], key=vt)
                    else:
                        k.dma(job["k_out"][0:nv, :], kt[0:nv, :], [kt], [outT], key=kt)
                        k.dma(job["v_out"][0:nv, :], vt[0:nv, :], [vt], [outT], key=vt)
                    k.dma(job["v_scr"][job["koff"] + r0:job["koff"] + r0 + 128, :], vb_[:].rearrange("p h d -> p (h d)"), [vb_], [T_v], key=vb_)
                    for kc in range(8):
                        k.mm(banks[3][:, 0:64], hT[:, kc, ts_], W1[:, kc, 1024:1088], [hT, W1], PQ[3], start=(kc == 0), stop=(kc == 7))
                    k.cp("act", kit[:], banks[3][:, 0:64], PQ[3], [kit])
                    if nv >= 128:
                        k.dma(job["ki_out"][r0:r0 + 128, :], kit[:], [kit], [outT], key=kit)
                    else:
                        k.dma(job["ki_out"][0:nv, :], kit[0:nv, :], [kit], [outT], key=kit)
                    for kc in range(8):
                        k.mm(banks[4][:, :], hT[:, kc, ts_], W1[:, kc, 2624:3136], [hT, W1], PQ[4], start=(kc == 0), stop=(kc == 7))
                    k.act(gbs[:, tt, :], banks[4][:, :], AF.Silu, PQ[4], [gbs])
                    for kc in range(8):
                        k.mm(banks[3][:, 128:136], hT[:, kc, ts_], W1[:, kc, 3136:3144], [hT, W1], PQ[3], start=(kc == 0), stop=(kc == 7))
                    k.cp("dve", bbab[:, tt, :], banks[3][:, 128:136], PQ[3], [bbab])
                for p in range(4):
                    bk = 5 + (p % 2)
                    for kc in range(8):
                        k.mm(banks[bk][:, 0:NT], W1[:, kc, p * 128:(p + 1) * 128], hT[:, kc, :], [W1, hT], PQ[bk], start=(kc == 0), stop=(kc == 7))
                    k.cp("act" if p % 2 == 0 else "dve", kf_t[:, p, :], banks[bk][:, 0:NT], PQ[bk], [kf_t])
                k.dma(job["kT_scr"][:, :, job["koff"] + t0:job["koff"] + t0 + NT], kf_t[:], [kf_t], [T_kT], key=kf_t)
                for kc in range(8):
                    k.mm(banks[7][0:64, 0:NT], W1[:, kc, 1024:1088], hT[:, kc, :], [W1, hT], PQ[7], start=(kc == 0), stop=(kc == 7))
                k.cp("act", kif_t[:], banks[7][0:64, 0:NT], PQ[7], [kif_t])
                k.dma(job["kiT_scr"][:, job["koff"] + t0:job["koff"] + t0 + NT], kif_t[:], [kif_t], [T_ki], key=kif_t)
                if KDBG < 2:
                    continue
                for c in range(12):
                    bk = 5 + (c % 3)
                    for kc in range(8):
                        k.mm(banks[bk][:, 0:NT], W1[:, kc, 1088 + c * 128:1088 + (c + 1) * 128], hT[:, kc, :], [W1, hT], PQ[bk], start=(kc == 0), stop=(kc == 7))
                    k.cp("act" if c % 2 == 0 else "dve", cin[:, c, 3:3 + NT], banks[bk][:, 0:NT], PQ[bk], [cin])
                for c in range(12):
                    k.ts("dve", cv[:, c, :], cin[:, c, 0:NT], convb[:, c, 0:1], ALU.mult, [cin, convb], [cv])
                    for j in range(1, 4):
                        k.stt(cv[:, c, :], cin[:, c, j:j + NT], convb[:, c, j:j + 1], cv[:, c, :], ALU.mult, ALU.add, [cin, convb, cv], [cv])
                if st == nsteps - 1:
                    k.dma(job["conv_out"], cin[:, :, job["nvalid_last"]:job["nvalid_last"] + 3], [cin], [outT], key=cin, slow=True)
                k.cp("pool", cin[:, :, 0:3], cin[:, :, NT:NT + 3], [cin], [cin])
                k.act(cv[:], cv[:], AF.Silu, [cv], [cv])
                for c in range(8):
                    k.act(sq2[:], cv[:, c, :], AF.Square, [cv], [sq2])
                    k.mm(banks[0][:, 0:NT], ones_bf[:], sq2[:], [ones_bf, sq2], PQ[0])
                    k.act(rs2[:], banks[0][:, 0:NT], AF.Sqrt, PQ[0] + [epsT], [rs2], bias=epsT[:, 0:1])
                    k.recip(rs2[:], rs2[:], [rs2], [rs2])
                    if c < 4:
                        k.stt(cv[:, c, :], cv[:, c, :], 128.0 ** -0.5, rs2[:], ALU.mult, ALU.mult, [cv, rs2], [cv])
                    else:
                        k.tt("dve", cv[:, c, :], cv[:, c, :], rs2[:], ALU.mult, [cv, rs2], [cv])
                    k.cp("pool", cvb[:, c, :], cv[:, c, :], [cv], [cvb])
                k.act(bet[:], bbab[:, :, 0:4], AF.Sigmoid, [bbab], [bet])
                if job["gval"] is not None:
                    for tt in range(ntt):
                        k.ts("dve", bet[:, tt, :], bet[:, tt, :], gval[:, job["gval"]:job["gval"] + 1], ALU.mult, [bet, gval], [bet])
                k.ts("dve", nbet[:], bet[:], -1.0, ALU.mult, [bet], [nbet])
                for tt in range(ntt):
                    k.tt("dve", gg[:, tt, :], bbab[:, tt, 4:8], dtb[:], ALU.add, [bbab, dtb], [gg])
                k.act(gg[:], gg[:], AF.Exp, [gg], [gg])
                k.act(gg[:], gg[:], AF.Ln, [gg], [gg], bias=1.0)
                for tt in range(ntt):
                    k.tt("dve", gg[:, tt, :], gg[:, tt, :], cA[:], ALU.mult, [gg, cA], [gg])
                    if job["gval"] is not None:
                        k.ts("dve", gg[:, tt, :], gg[:, tt, :], gval[:, job["gval"]:job["gval"] + 1], ALU.mult, [gg, gval], [gg])
                k.ts("dve", ngg[:], gg[:], -1.0, ALU.mult, [gg], [ngg])
                if KDBG < 3:
                    continue
                for tt in range(ntt):
                    ts_ = slice(tt * 128, (tt + 1) * 128)
                    G = PQ[0]
                    k.mm(banks[0][:, 0:4], Ublk, gg[:, tt, :], [cst, gg], G)
                    k.mm(banks[0][:, 4:8], Lsblk, gg[:, tt, :], [cst, gg], G)
                    k.mm(banks[0][:, 8:12], half0, gg[:, tt, :], [cst, gg], G)
                    k.mm(banks[0][:, 12:16], half1, gg[:, tt, :], [cst, gg], G)
                    k.act(egc[:, tt, :], banks[0][:, 0:4], AF.Exp, G, [egc])
                    k.act(ekd[:, tt, :], banks[0][:, 4:8], AF.Exp, G, [ekd])
                    k.act(egl[:, tt, :], banks[0][:, 8:16], AF.Exp, G, [egl])
                    k.tt("dve", bege[:, tt, :], bet[:, tt, :], egc[:, tt, :], ALU.mult, [bet, egc], [bege])
                    BK = ((1, 2), (3, 4), (5, 6), (7, 0))
                    H4 = range(4)
                    for h in H4:
                        T_ = HT[h]
                        k.cp("pool", T_["NGb"][:], ngg[:, tt, h:h + 1].to_broadcast([128, 128]), [ngg], [T_["NGb"]])
                    for h in H4:
                        T_ = HT[h]
                        b0, b1 = BK[h]
                        k.mm(pq(b0, 0), Ublk, gg[:, tt, h:h + 1].to_broadcast([128, 128]), [cst, gg], [PB[b0]], start=True, stop=False)
                        k.mm(pq(b0, 0), T_["NGb"][:], Ublk, [cst, T_["NGb"]], [PB[b0]], start=False, stop=True)
                        k.mm(pq(b1, 1), cvb[:, 4 + h, ts_], cvb[:, 4 + h, ts_], [cvb], [PB[b1]])
                    for h in H4:
                        T_ = HT[h]
                        b0, b1 = BK[h]
                        k.stt(T_["E1m"][:], pq(b0, 0), 0.0, NEGML, ALU.min, ALU.add, [PB[b0], cst], [T_["E1m"]])
                        k.stt(T_["E2m"][:], pq(b0, 0), 0.0, POSMU, ALU.max, ALU.add, [PB[b0], cst], [T_["E2m"]])
                    for h in H4:
                        T_ = HT[h]
                        k.act(T_["E1m"][:], T_["E1m"][:], AF.Exp, [T_["E1m"]], [T_["E1m"]])
                        k.act(T_["E2m"][:], T_["E2m"][:], AF.Exp, [T_["E2m"]], [T_["E2m"]], scale=-1.0)
                    for h in H4:
                        T_ = HT[h]
                        b0, b1 = BK[h]
                        k.mm(pq(b0, 2), cvb[:, 4 + h, ts_], cvb[:, h, ts_], [cvb], [PB[b0]])
                    for h in H4:
                        T_ = HT[h]
                        b0, b1 = BK[h]
                        k.stt(T_["N"][:], pq(b1, 1), nbet[:, tt, h:h + 1], T_["E1m"][:], ALU.mult, ALU.mult, [PB[b1], nbet, T_["E1m"]], [T_["N"]])
                    for h in H4:
                        T_ = HT[h]
                        b0, b1 = BK[h]
                        k.tt("dve", T_["attnT"][:], pq(b0, 2), T_["E2m"][:], ALU.mult, [PB[b0], T_["E2m"]], [T_["attnT"]])
                    for h in H4:
                        T_ = HT[h]
                        b0, b1 = BK[h]
                        k.tr(pq(b1, 3), cv[:, 4 + h, ts_], ident, [cv, cst], [PB[b1]])
                        k.tr(pq(b0, 0), cv[:, 8 + h, ts_], ident, [cv, cst], [PB[b0]])
                    for h in H4:
                        T_ = HT[h]
                        b0, b1 = BK[h]
                        k.ts("dve", T_["kbg"][:], pq(b1, 3), bege[:, tt, h:h + 1], ALU.mult, [PB[b1], bege], [T_["kbg"]])
                        k.ts("dve", T_["kd"][:], pq(b1, 3), ekd[:, tt, h:h + 1], ALU.mult, [PB[b1], ekd], [T_["kd"]])
                    for h in H4:
                        T_ = HT[h]
                        b0, b1 = BK[h]
                        k.ts("dve", T_["vb"][:], pq(b0, 0), bet[:, tt, h:h + 1], ALU.mult, [PB[b0], bet], [T_["vb"]])
                    for h in H4:
                        T_ = HT[h]
                        b0, b1 = BK[h]
                        k.mm(pq(b1, 1), T_["N"][:], ident_bf[:], [T_["N"], ident_bf], [PB[b1]])
                    for h in H4:
                        T_ = HT[h]
                        b0, b1 = BK[h]
                        k.cp("act", T_["Nt"][:], pq(b1, 1), [PB[b1]], [T_["Nt"]])
                    for h in H4:
                        T_ = HT[h]
                        b0, b1 = BK[h]
                        k.tt("dve", T_["P"][:], pq(b1, 1), ident, ALU.add, [PB[b1], cst], [T_["P"]])
                    for h in H4:
                        T_ = HT[h]
                        k.cp("pool", T_["Pb"][:], T_["P"][:], [T_["P"]], [T_["Pb"]])
                    if KDBG < 4:
                        continue
                    for lv in range(1, 6):
                        for h in range(4):
                            T_ = HT[h]
                            b0, b1 = ((1, 2), (3, 4), (5, 6), (7, 0))[h]
                            Mp = T_["N"] if lv == 1 else T_[f"M{(lv - 1) % 2}"]
                            Mtp = T_["Nt"] if lv == 1 else T_[f"Mt{(lv - 1) % 2}"]
                            Mn = T_[f"M{lv % 2}"]
                            Mtn = T_[f"Mt{lv % 2}"]
                            k.mm(pq(b1, 2), Mtp[:], Mp[:], [Mtp, Mp], [PB[b1]])
                            if lv < 5:
                                k.mm(pq(b0, 1), Mp[:], Mtp[:], [Mtp, Mp], [PB[b0]])
                            k.cp("act", Mn[:], pq(b1, 2), [PB[b1]], [Mn])
                            if lv < 5:
                                k.cp("dve", Mtn[:], pq(b0, 1), [PB[b0]], [Mtn])
                        for h in range(4):
                            T_ = HT[h]
                            b0, b1 = ((1, 2), (3, 4), (5, 6), (7, 0))[h]
                            Mn = T_[f"M{lv % 2}"]
                            k.mm(pq(b1, 0), Mn[:], T_["Pb"][:], [Mn, T_["Pb"]], [PB[b1]])
                            k.tt("dve", T_["P"][:], T_["P"][:], pq(b1, 0), ALU.add, [T_["P"], PB[b1]], [T_["P"]])
                            if lv < 5:
                                k.cp("pool", T_["Pb"][:], T_["P"][:], [T_["P"]], [T_["Pb"]])
                    if KDBG < 5:
                        continue
                    BK = ((1, 2), (3, 4), (5, 6), (7, 0))
                    for h in range(4):
                        T_ = HT[h]
                        b0, b1 = BK[h]
                        k.mm(pq(b0, 1), T_["P"][:], T_["vb"][:], [T_["P"], T_["vb"]], [PQ[b0][1]])
                        k.cp("act", T_["u"][:], pq(b0, 1), [PQ[b0][1]], [T_["u"]])
                    for h in range(4):
                        T_ = HT[h]
                        b0, b1 = BK[h]
                        k.mm(pq(b1, 2), T_["kbg"][:], T_["P"][:], [T_["P"], T_["kbg"]], [PQ[b1][2]])
                        k.cp("dve", T_["wT"][:], pq(b1, 2), [PQ[b1][2]], [T_["wT"]])
                    for c in range(2):
                        r = slice(64 * c, 64 * c + 64)
                        for h in range(4):
                            T_ = HT[h]
                            b0, b1 = BK[h]
                            k.mm(banks[b0][r, 384:512], T_["wT"][:, r], Sst[h][:], [T_["wT"], Sst[h]], [PQ[b0][3]])
                            k.mm(banks[b0][r, 0:128], cv[:, h, tt * 128 + 64 * c:tt * 128 + 64 * c + 64], Sst[h][:], [cv, Sst[h]], [PQ[b0][0]])
                        for h in range(4):
                            T_ = HT[h]
                            b0, b1 = BK[h]
                            k.tt("dve", T_["vnew"][r, :], T_["u"][r, :], banks[b0][r, 384:512], ALU.subtract, [T_["u"], PQ[b0][3]], [T_["vnew"]])
                        for h in range(4):
                            T_ = HT[h]
                            b0, b1 = BK[h]
                            k.mm(banks[b1][r, 128:256], T_["attnT"][r, r], T_["vnew"][r, :], [T_["attnT"], T_["vnew"]], [PQ[b1][1]])
                            k.mm(pq(b1, 2), T_["kd"][r, :], T_["vnew"][r, :], [T_["kd"], T_["vnew"]], [PQ[b1][2]])
                        for h in range(4):
                            T_ = HT[h]
                            b0, b1 = BK[h]
                            k.stt(Sst[h][:], Sst[h][:], egl[:, tt, 4 * c + h:4 * c + h + 1], pq(b1, 2), ALU.mult, ALU.add, [Sst[h], egl, PQ[b1][2]], [Sst[h]])
                        for h in range(4):
                            T_ = HT[h]
                            b0, b1 = BK[h]
                            k.cp("act", T_["o1"][r, :], banks[b1][r, 128:256], [PQ[b1][1]], [T_["o1"]])
                        for h in range(4):
                            T_ = HT[h]
                            b0, b1 = BK[h]
                            k.stt(T_["o"][r, :], banks[b0][r, 0:128], egc[r, tt, h:h + 1], T_["o1"][r, :], ALU.mult, ALU.add, [PQ[b0][0], egc, T_["o1"]], [T_["o"]])
                    for h in range(4):
                        T_ = HT[h]
                        k.act(T_["tmp"][:], T_["o"][:], AF.Square, [T_["o"]], [T_["tmp"], T_["ms"]], accum=T_["ms"][:, 0:1])
                    for h in range(4):
                        T_ = HT[h]
                        k.act(T_["ms"][:], T_["ms"][:], AF.Sqrt, [T_["ms"], epsT], [T_["ms"]], scale=1.0 / 128, bias=epsT[:, 0:1])
                    for h in range(4):
                        T_ = HT[h]
                        k.recip(T_["ms"][:], T_["ms"][:], [T_["ms"]], [T_["ms"]])
                    for h in range(4):
                        T_ = HT[h]
                        k.stt(T_["tmp"][:], T_["o"][:], T_["ms"][:, 0:1], ngdn[:], ALU.mult, ALU.mult, [T_["o"], T_["ms"], ngdn], [T_["tmp"]])
                    for h in range(4):
                        T_ = HT[h]
                        k.tt("pool", T_["obn"][:], T_["tmp"][:], gbs[:, tt, h * 128:(h + 1) * 128], ALU.mult, [T_["tmp"], gbs], [T_["obn"]])
                    for h in range(4):
                        T_ = HT[h]
                        b0, b1 = BK[h]
                        k.tr(banks[b1][:, 384:512], T_["obn"][:], ident_bf[:], [T_["obn"], ident_bf], [PQ[b1][3]])
                    for h in range(4):
                        b0, b1 = BK[h]
                        k.cp("act", obT_t[:, h, ts_], banks[b1][:, 384:512], [PQ[b1][3]], [obT_t])
                if KDBG >= 5:
                    k.dma(job["obT_scr"][:, :, t0:t0 + NT], obT_t[:], [obT_t], [T_ob], key=obT_t)
            for h in range(4):
                k.dma(job["S_out"][h], Sst[h][:], [Sst[h]], [outT], key=Sst[h])
            S.barrier()

        with ExitStack() as es:
          if KDBG >= 0 and 'p' in KJOB:
            all_pass(es, dict(NT=NTP, L=LKP, x=I["xa"], S0=None, conv0=None, nvalid=128, nvalid_last=NTP,
                              k_out=O["k_all"], v_out=O["v_all"], ki_out=O["ki_all"], kT_scr=kT_p, v_scr=v_p, kiT_scr=kiT_p,
                              koff=0, obT_scr=obT_p, S_out=O["gdn_p"], conv_out=O["gconv_p"], gval=None))
        for sb_ in range(2 if (KDBG >= 0 and 's' in KJOB) else 0):
            with ExitStack() as es:
                all_pass(es, dict(NT=128, L=128, x=I["xs"][sb_], S0=I["state_gdn"][sb_], conv0=I["gconvT"][sb_], nvalid=16, nvalid_last=16,
                                  k_out=O["k_s"][sb_], v_out=O["v_s"][sb_], ki_out=O["ki_s"][sb_], kT_scr=kT_s[sb_], v_scr=v_s[sb_],
                                  kiT_scr=kiT_s[sb_], koff=PAST, obT_scr=obT_s[sb_], S_out=O["gdn_s"][sb_], conv_out=O["gconv_s"][sb_], gval=0))

        convf = sbt(es0, "convf", [128, 22, 3], F32)
        k.dma(convf[:], I["convf"], [], [convf])
        iota = sbt(es0, "iota", [128, 512], F32)
        k.dma(iota[:], I["iota"], [], [iota])
        lims = sbt(es0, "lims", [128, 24], F32)
        k.dma(lims[:], I["lims"], [], [lims])
        BT = sbt(es0, "BT", [128, 2, 8, 128], BF16)
        with ExitStack() as es:
            rb = sbt(es, "rb", [32, 8], F32)
            oh = sbt(es, "oh", [32, 384], F32)
            rb15 = sbt(es, "rb15", [8, 1], F32)
            tabs = sbt(es, "tabs", [8, 384], F32)
            BTf = sbt(es, "BTf", [128, 2, 8, 128], F32)
            k.dma(rb[:], I["relb"], [], [rb])
            k.dma(oh[:], I["ohrev"], [], [oh])
            k.dma(rb15[:], I["relb15"], [], [rb15])
            k.mm(banks[0][0:8, 0:384], rb[:], oh[:], [rb, oh], PQ[0])
            k.ts("dve", tabs[:], banks[0][0:8, 0:384], rb15[:, 0:1], ALU.subtract, PQ[0] + [rb15], [tabs])
            k.dma(tab_scr, tabs[:], [tabs], [T_tab], key=tabs)
            for kp in range(128):
                for dd in range(2):
                    base = 127 + 128 * dd - kp
                    k.dma(BTf[kp:kp + 1, dd, :, :], tab_scr[:, base:base + 128].unsqueeze(0), [T_tab], [BTf])
            k.cp("dve", BT[:], BTf[:], [BTf], [BT])
            ckf = sbt(es, "ckf", [128, 4, 512], F32)
            ckb = sbt(es, "ckb", [128, 4, 512], BF16)
            cvf = sbt(es, "cvf", [128, 512], F32)
            cvb = sbt(es, "cvb", [128, 512], BF16)
            cvb2 = sbt(es, "cvb2", [128, 8, 66], BF16)
            k.memset("pool", cvb2[:, :, 64:66], 1.0, [cvb2])
            for sb_ in range(2):
                ckv = I["cache_kT"][sb_].rearrange("(p r) t -> r p t", r=128)
                for pc in range(4):
                    k.dma(ckf[:], ckv[:, :, pc * 512:(pc + 1) * 512], [], [ckf])
                    k.cp("dve", ckb[:], ckf[:], [ckf], [ckb])
                    k.dma(kT_s[sb_][:, :, pc * 512:(pc + 1) * 512], ckb[:], [ckb], [T_kT], key=ckb)
                    k.dma(cvf[0:64, :], I["cache_kiT"][sb_][:, pc * 512:(pc + 1) * 512], [], [cvf])
                    k.cp("pool", cvb[0:64, :], cvf[0:64, :], [cvf], [cvb])
                    k.dma(kiT_s[sb_][:, pc * 512:(pc + 1) * 512], cvb[0:64, :], [cvb], [T_ki], key=cvb)
                for rb_ in range(16):
                    k.dma(cvf[:], I["cache_v"][sb_][rb_ * 128:(rb_ + 1) * 128, :], [], [cvf])
                    k.cp("pool", cvb2[:, :, 0:64], cvf[:].rearrange("p (h d) -> p h d", d=64), [cvf], [cvb2])
                    k.dma(v_s[sb_][rb_ * 128:(rb_ + 1) * 128, :], cvb2[:].rearrange("p h d -> p (h d)"), [cvb2], [T_v], key=cvb2)
            S.barrier()

        def pieces(n):
            out, c = [], 0
            while c < n:
                w = min(512, n - c)
                out.append((c, w))
                c += w
            return out

        def rms(src, dst, c0, n, sqb, rsb, bank=0):
            k.act(sqb[:, :, 0:n], src[:, :, c0:c0 + n], AF.Square, [src], [sqb])
            for kc in range(8):
                k.mm(banks[bank][:, 0:n], ones_bf[:], sqb[:, kc, 0:n], [ones_bf, sqb], PQ[bank], start=(kc == 0), stop=(kc == 7))
            k.act(rsb[:, 0:n], banks[bank][:, 0:n], AF.Sqrt, PQ[bank] + [epsT], [rsb], scale=1.0 / D, bias=epsT[:, 0:1])
            k.recip(rsb[:, 0:n], rsb[:, 0:n], [rsb], [rsb])
            if dst is not None:
                for kc in range(8):
                    k.tt("dve" if kc % 2 == 0 else "pool", dst[:, kc, c0:c0 + n], src[:, kc, c0:c0 + n], rsb[:, 0:n], ALU.mult, [src, rsb], [dst])

        NIT = 26

        def own_block(jb):
            NQ = jb["NQ"]
            NTOK = 128 * NQ
            with ExitStack() as esA:
                xo = sbt(esA, "xo", [128, 8, NTOK], F32)
                hT = sbt(esA, "hTo", [128, 8, NTOK], BF16)
                oaT = sbt(esA, "oaT", [128, 4, NTOK], BF16)
                sqb = sbt(esA, "sqb", [128, 8, 512], BF16)
                rsb = sbt(esA, "rsb", [128, 512], F32)
                k.dma(xo[:], jb["xsrc"], [], [xo])
                for (c0, n) in pieces(NTOK):
                    rms(xo, hT, c0, n, sqb, rsb)
                if KP2 < 1:
                    return
                with ExitStack() as es:
                    W2 = sbt(es, "W2", [128, 8, 1032], BF16)
                    k.dma(W2[:, :, 0:512], wscr_in[:, :, C_QA:C_QA + 512], [T_wscr_in], [W2])
                    k.dma(W2[:, :, 512:1024], wscr_in[:, :, C_QI:C_QI + 512], [T_wscr_in], [W2])
                    k.dma(W2[:, :, 1024:1032], wscr_in[:, :, C_WI:C_WI + 8], [T_wscr_in], [W2])
                    qaT = sbt(es, "qaT", [128, 4, NTOK], BF16)
                    qiT = sbt(es, "qiT", [128, 4, NTOK], BF16)
                    wiT = sbt(es, "wiT", [128, NQ, 8], F32)
                    n_ = 0
                    for (dstq, cb) in ((qaT, 0), (qiT, 512)):
                        for p in range(4):
                            for (c0, n) in pieces(NTOK):
                                bk = n_ % 2
                                n_ += 1
                                for kc in range(8):
                                    k.mm(banks[bk][:, 0:n], W2[:, kc, cb + p * 128:cb + (p + 1) * 128], hT[:, kc, c0:c0 + n], [W2, hT], PQ[bk], start=(kc == 0), stop=(kc == 7))
                                k.act(dstq[:, p, c0:c0 + n], banks[bk][:, 0:n], AF.Copy, PQ[bk], [dstq], scale=0.125)
                    for qb in range(NQ):
                        for kc in range(8):
                            k.mm(banks[2][:, 0:8], hT[:, kc, qb * 128:(qb + 1) * 128], W2[:, kc, 1024:1032], [W2, hT], PQ[2], start=(kc == 0), stop=(kc == 7))
                        k.ts("dve", wiT[:, qb, :], banks[2][:, 0:8], 8.0 ** -0.5, ALU.mult, PQ[2], [wiT])
                    if KP2 < 2:
                        return
                    LMAX = max(jb["L"])
                    kiT2 = sbt(es, "kiT2", [128, LMAX], BF16)
                    sc = sbt(es, "sc", [128, LMAX], F32)
                    Mb = sbt(es, "Mb", [128, LMAX], BF16)
                    MT = sbt(es, "MT", [128, LMAX // 128, 128], BF16)
                    tmpf = [sbt(es, f"tmpf{i}", [128, 512], F32) for i in range(2)]
                    pen = sbt(es, "pen", [128, 512], F32)
                    Kc = [sbt(es, f"Kc{i}", [128, 4, 512], BF16) for i in range(2)]
                    Vc = [sbt(es, f"Vc{i}", [128, 4, 8, 66], BF16) for i in range(2)]
                    PT = [sbt(es, f"PT{i}", [128, 4, 128], BF16) for i in range(4)]
                    oa = sbt(es, "oa", [128, 8, 64], BF16)
                    den = sbt(es, "den", [128, 8], F32)
                    sm = {nm: sbt(es, nm, [128, 1], F32) for nm in ("mx", "lo", "hi", "mid", "cnt", "ge", "d1", "d2", "off")}
                    for i in range(2):
                        k.memset("pool", Vc[i][:, :, :, 64:65], 1.0, [Vc[i]])
                    lgs = sbt(es, "lgs", [128, 512], F32)

                    def gen_S(qb):
                        L = jb["L"][qb]
                        nkb = L // 128
                        q0 = qb * 128
                        lcol = jb["limcol"] + qb
                        k.dma(kiT2[0:64, 0:L], jb["kiT"][:, 0:L], [T_ki], [kiT2])
                        k.dma(kiT2[64:128, 0:L], jb["kiT"][:, 0:L], [T_ki], [kiT2])
                        tiles = pieces(L)
                        n_ = 0
                        for (c0, w) in tiles:
                            for h in range(8):
                                p, r0 = h // 2, 64 * (h % 2)
                                bk = n_ % 2
                                tf = tmpf[n_ % 2]
                                n_ += 1
                                k.mm(banks[bk][:, 0:w], qiT[r0:r0 + 64, p, q0:q0 + 128], kiT2[r0:r0 + 64, c0:c0 + w], [qiT, kiT2], PQ[bk])
                                k.act(tf[:, 0:w], banks[bk][:, 0:w], AF.Relu, PQ[bk], [tf])
                                if h == 0:
                                    k.ts("dve", sc[:, c0:c0 + w], tf[:, 0:w], wiT[:, qb, 0:1], ALU.mult, [tf, wiT], [sc])
                                else:
                                    k.stt(sc[:, c0:c0 + w], tf[:, 0:w], wiT[:, qb, h:h + 1], sc[:, c0:c0 + w], ALU.mult, ALU.add, [tf, wiT, sc], [sc])
                            yield
                        k.S.op("dve", lambda e, o=sm["mx"][:], i=sc[:, 0:L]: e.tensor_reduce(out=o, in_=i, axis=AX.X, op=ALU.max, apply_absolute_value=True), reads=[sc], writes=[sm["mx"]])
                        k.ts("dve", sm["hi"][:], sm["mx"][:], 1.0, ALU.add, [sm["mx"]], [sm["hi"]])
                        k.ts("dve", sm["lo"][:], sm["mx"][:], -1.0, ALU.mult, [sm["mx"]], [sm["lo"]], s2=-1.0, op1=ALU.add)
                        (c0, w) = tiles[-1]
                        k.ts("dve", sm["off"][:], lims[:, lcol:lcol + 1], float(-c0), ALU.add, [lims], [sm["off"]])
                        k.ts("dve", pen[:, 0:w], iota[:, 0:w], sm["off"][:, 0:1], ALU.is_ge, [iota, sm["off"]], [pen], s2=NEG, op1=ALU.mult)
                        k.tt("dve", sc[:, c0:c0 + w], sc[:, c0:c0 + w], pen[:, 0:w], ALU.add, [sc, pen], [sc])
                        if jb["lolim"] is not None:
                            for ti in range(min(3, len(tiles))):
                                (c0, w) = tiles[ti]
                                k.ts("dve", sm["off"][:], lims[:, jb["lolim"]:jb["lolim"] + 1], float(-c0), ALU.add, [lims], [sm["off"]])
                                k.ts("dve", pen[:, 0:w], iota[:, 0:w], sm["off"][:, 0:1], ALU.is_lt, [iota, sm["off"]], [pen], s2=NEG, op1=ALU.mult)
                                k.tt("dve", sc[:, c0:c0 + w], sc[:, c0:c0 + w], pen[:, 0:w], ALU.add, [sc, pen], [sc])

                    def gen_B(qb):
                        L = jb["L"][qb]
                        nkb = L // 128
                        q0 = qb * 128
                        lcol = jb["limcol"] + qb
                        k.tt("dve", sm["d1"][:], sm["hi"][:], sm["lo"][:], ALU.subtract, [sm["hi"], sm["lo"]], [sm["d1"]])
                        for it in range(NIT):
                            k.ts("dve", sm["d1"][:], sm["d1"][:], 0.5, ALU.mult, [sm["d1"]], [sm["d1"]])
                            k.tt("dve", sm["mid"][:], sm["lo"][:], sm["d1"][:], ALU.add, [sm["lo"], sm["d1"]], [sm["mid"]])
                            k.ts("dve", Mb[:, 0:L], sc[:, 0:L], sm["mid"][:, 0:1], ALU.is_gt, [sc, sm["mid"]], [Mb, sm["cnt"]], s2=0.0, op1=ALU.add, accum=sm["cnt"][:, 0:1])
                            k.stt(sm["ge"][:], sm["cnt"][:], 255.5, sm["d1"][:], ALU.is_ge, ALU.mult, [sm["cnt"], sm["d1"]], [sm["ge"]])
                            k.tt("dve", sm["lo"][:], sm["lo"][:], sm["ge"][:], ALU.add, [sm["lo"], sm["ge"]], [sm["lo"]])
                            yield
                        k.ts("dve", Mb[:, 0:L], sc[:, 0:L], sm["lo"][:, 0:1], ALU.is_gt, [sc, sm["lo"]], [Mb])

                    def do_T(qb):
                        L = jb["L"][qb]
                        nkb = L // 128
                        q0 = qb * 128
                        lcol = jb["limcol"] + qb
                        for kb0 in range(0, nkb, 4):
                            nb = min(4, nkb - kb0)
                            bk = 2 + (kb0 // 4) % 2
                            for i in range(nb):
                                k.mm(banks[bk][:, i * 128:(i + 1) * 128], Mb[:, (kb0 + i) * 128:(kb0 + i + 1) * 128], ident_bf[:], [Mb, ident_bf], PQ[bk])
                            k.cp("act", MT[:, kb0:kb0 + nb, :].rearrange("p a b -> p (a b)"), banks[bk][:, 0:nb * 128], PQ[bk], [MT])

                    def gen_A(qb):
                        L = jb["L"][qb]
                        nkb = L // 128
                        q0 = qb * 128
                        nch = (L + 511) // 512
                        DEPTH = 3

                        def loads(ch):
                            wch = min(512, L - 512 * ch)
                            Kt, Vt = Kc[ch % 2], Vc[ch % 2]
                            k.dma(Kt[:, :, 0:wch], jb["kT"][:, :, 512 * ch:512 * ch + wch], [T_kT], [Kt])
                            k.dma(Vt[:, 0:wch // 128, :, :], jb["v"][512 * ch:512 * ch + wch, :].rearrange("(i p) (h d) -> p i h d", p=128, d=66), [T_v], [Vt])

                        units = []
                        for ch in range(nch):
                            wch = min(512, L - 512 * ch)
                            for i in range(wch // 128):
                                for g in range(2):
                                    units.append((ch, i, g))
                        last_of_chunk = {}
                        for idx, (ch, i, g) in enumerate(units):
                            last_of_chunk[ch] = idx

                        def logits(idx):
                            ch, i, g = units[idx]
                            bk = 2 + idx % 4
                            Kt = Kc[ch % 2]
                            r0 = 64 * g
                            for e4 in range(4):
                                k.mm(banks[bk][:, e4 * 128:(e4 + 1) * 128], Kt[r0:r0 + 64, e4, i * 128:(i + 1) * 128], qaT[r0:r0 + 64, e4, q0:q0 + 128], [Kt, qaT], PQ[bk], start=True, stop=True)

                        def softmax_pv(idx):
                            ch, i, g = units[idx]
                            kbg = 4 * ch + i
                            bk = 2 + idx % 4
                            pt = PT[idx % 4]
                            Vt = Vc[ch % 2]
                            diag = kbg >= nkb - 2
                            dd = 0 if kbg == nkb - 1 else 1
                            if diag:
                                k.tt("dve", lgs[:, :].rearrange("p (h q) -> p h q", q=128), banks[bk][:, :].rearrange("p (h q) -> p h q", q=128), BT[:, dd, g:8:2, :], ALU.add, PQ[bk] + [BT], [lgs])
                                k.act(pt[:].rearrange("p h q -> p (h q)"), lgs[:, :], AF.Exp, [lgs], [pt])
                            else:
                                k.act(pt[:].rearrange("p h q -> p (h q)"), banks[bk][:, :], AF.Exp, PQ[bk], [pt])
                            k.tt("dve", pt[:], pt[:], MT[:, kbg:kbg + 1, :].to_broadcast([128, 4, 128]), ALU.mult, [pt, MT], [pt])
                            for e4 in range(4):
                                h = 2 * e4 + g
                                k.mm(banks[6 + g][:, e4 * 65:(e4 + 1) * 65], pt[:, e4, :], Vt[:, i, h, 0:65], [pt, Vt], PQ[6 + g], start=(kbg == 0 and e4 == 0), stop=(kbg == nkb - 1 and e4 == 3))
                            if last_of_chunk[ch] == idx and ch + 2 < nch:
                                loads(ch + 2)

                        loads(0)
                        if nch > 1:
                            loads(1)
                        nu = len(units)
                        for idx in range(nu + DEPTH):
                            if idx < nu:
                                logits(idx)
                            if idx - DEPTH >= 0:
                                softmax_pv(idx - DEPTH)
                            if idx % 3 == 2:
                                yield

                    def do_N(qb):
                        L = jb["L"][qb]
                        nkb = L // 128
                        q0 = qb * 128
                        lcol = jb["limcol"] + qb
                        for g in range(2):
                            ov = banks[6 + g][:, 0:260].rearrange("p (h d) -> p h d", d=65)
                            k.ts("dve", den[:, 4 * g:4 * g + 4], ov[:, :, 64], 1e-30, ALU.add, PQ[6 + g], [den])
                            k.recip(den[:, 4 * g:4 * g + 4], den[:, 4 * g:4 * g + 4], [den], [den])
                            k.tt("dve", oa[:, g:8:2, :], ov[:, :, 0:64], den[:, 4 * g:4 * g + 4].unsqueeze(2).to_broadcast([128, 4, 64]), ALU.mult, PQ[6 + g] + [den], [oa])
                        oaf = oa[:].rearrange("p h d -> p (h d)")
                        for c in range(4):
                            k.mm(banks[2][:, c * 128:(c + 1) * 128], oaf[:, c * 128:(c + 1) * 128], ident_bf[:], [oa, ident_bf], PQ[2])
                        k.cp("act", oaT[:, :, q0:q0 + 128], banks[2][:, :].rearrange("p (c t) -> p c t", t=128), PQ[2], [oaT])

                    def drain(g):
                        for _ in g:
                            pass

                    def interleave(ga, gb):
                        alive = [ga, gb]
                        while alive:
                            for g_ in list(alive):
                                try:
                                    next(g_)
                                except StopIteration:
                                    alive.remove(g_)

                    def gen_SB(qb):
                        yield from gen_S(qb)
                        yield from gen_B(qb)

                    drain(gen_SB(0))
                    for qb in range(NQ):
                        do_T(qb)
                        if qb + 1 < NQ:
                            interleave(gen_A(qb), gen_SB(qb + 1))
                        else:
                            drain(gen_A(qb))
                        do_N(qb)
                    S.barrier()
                if KP2 < 8:
                    return
                with ExitStack() as es:
                    Wpa = sbt(es, "Wpa", [128, 4, D], BF16)
                    Wpb = sbt(es, "Wpb", [128, 4, D], BF16)
                    Wg = sbt(es, "Wg", [128, 8, 2048], BF16)
                    Wo = sbt(es, "Wo", [128, 8, D], BF16)
                    Wpl = sbt(es, "Wpl", [128, 2, D], BF16)
                    k.dma(Wpa[:], ws_pa, [T_wscr_in], [Wpa])
                    k.dma(Wpb[:], ws_pb, [T_wscr_in], [Wpb])
                    k.dma(Wg[:], wscr_in[:, :, C_GA:C_GA + 2048], [T_wscr_in], [Wg])
                    k.dma(Wpl[:], ws_ple, [T_wscr_in], [Wpl])
                    obT = sbt(es, "obT", [128, 4, NTOK], BF16)
                    k.dma(obT[:], jb["obT"], [T_ob], [obT])
                    pf = sbt(es, "pf", [128, 2, NTOK], F32)
                    pb = sbt(es, "pb", [128, 2, NTOK], BF16)
                    k.dma(pf[:], jb["psrc"], [], [pf])
                    k.cp("pool", pb[:], pf[:], [pf], [pb])
                    mixT = sbt(es, "mixT", [128, 8, 512], BF16)
                    h2T = sbt(es, "h2T", [128, 8, 512], BF16)
                    actT = sbt(es, "actT", [128, 22, 512], BF16)
                    ughalo = sbt(es, "ughalo", [128, 22, 2], F32)
                    fco = sbt(es, "fco", [128, 22, 2], F32)
                    ugc = [sbt(es, f"ugc{i}", [128, 514], F32) for i in range(2)]
                    cva = [sbt(es, f"cva{i}", [128, 512], F32) for i in range(2)]
                    sga = sbt(es, "sga", [128, 512], F32)
                    sgb = sbt(es, "sgb", [128, 512], F32)
                    t1 = sbt(es, "t1", [128, 512], F32)
                    wug = [sbt(es, f"wug{i}", [128, 8, 128], BF16) for i in range(2)]
                    wuv = [sbt(es, f"wuv{i}", [128, 8, 128], BF16) for i in range(2)]
                    wdn = [sbt(es, "wdn0", [128, 22, 128], BF16)] * 2
                    ytok = sbt(es, "ytok", [128, D], F32)
                    if jb["fhalo"] is not None:
                        k.dma(ughalo[:], jb["fhalo"], [], [ughalo])
                    for (c0, n, halo) in jb["segs"]:
                        sg = slice(c0, c0 + n)
                        k.dma(Wo[:], ws_out, [T_wscr_in], [Wo])
                        for c in range(8):
                            cs = slice(c * 128, (c + 1) * 128)
                            bo = 4 * (c % 2)
                            for kc in range(4):
                                k.mm(banks[bo + 0][:, 0:n], Wpa[:, kc, cs], oaT[:, kc, sg], [Wpa, oaT], PQ[bo + 0], start=(kc == 0), stop=(kc == 3))
                            for kc in range(8):
                                k.mm(banks[bo + 1][:, 0:n], Wg[:, kc, cs], hT[:, kc, sg], [Wg, hT], PQ[bo + 1], start=(kc == 0), stop=(kc == 7))
                            k.act(sga[:, 0:n], banks[bo + 1][:, 0:n], AF.Sigmoid, PQ[bo + 1], [sga])
                            k.tt("dve", t1[:, 0:n], banks[bo + 0][:, 0:n], sga[:, 0:n], ALU.mult, PQ[bo + 0] + [sga], [t1])
                            for kc in range(4):
                                k.mm(banks[bo + 2][:, 0:n], Wpb[:, kc, cs], obT[:, kc, sg], [Wpb, obT], PQ[bo + 2], start=(kc == 0), stop=(kc == 3))
                            for kc in range(8):
                                k.mm(banks[bo + 3][:, 0:n], Wg[:, kc, 1024 + c * 128:1024 + (c + 1) * 128], hT[:, kc, sg], [Wg, hT], PQ[bo + 3], start=(kc == 0), stop=(kc == 7))
                            k.act(sgb[:, 0:n], banks[bo + 3][:, 0:n], AF.Sigmoid, PQ[bo + 3], [sgb])
                            k.tt("dve", sgb[:, 0:n], banks[bo + 2][:, 0:n], sgb[:, 0:n], ALU.mult, PQ[bo + 2] + [sgb], [sgb])
                            k.tt("pool", mixT[:, c, 0:n], t1[:, 0:n], sgb[:, 0:n], ALU.add, [t1, sgb], [mixT])
                        for c in range(8):
                            cs = slice(c * 128, (c + 1) * 128)
                            bk = 4 + c % 2
                            for kc in range(8):
                                k.mm(banks[bk][:, 0:n], Wo[:, kc, cs], mixT[:, kc, 0:n], [Wo, mixT], PQ[bk], start=(kc == 0), stop=(kc == 7))
                            k.tt("dve", xo[:, c, sg], xo[:, c, sg], banks[bk][:, 0:n], ALU.add, [xo] + PQ[bk], [xo])
                        k.act(sqb[:, :, 0:n], xo[:, :, sg], AF.Square, [xo], [sqb])
                        for kc in range(8):
                            k.mm(banks[0][:, 0:n], ones_bf[:], sqb[:, kc, 0:n], [ones_bf, sqb], PQ[0], start=(kc == 0), stop=(kc == 7))
                        k.act(rsb[:, 0:n], banks[0][:, 0:n], AF.Sqrt, PQ[0] + [epsT], [rsb], scale=1.0 / D, bias=epsT[:, 0:1])
                        k.recip(rsb[:, 0:n], rsb[:, 0:n], [rsb], [rsb])
                        for kc in range(8):
                            k.tt("dve" if kc % 2 == 0 else "pool", h2T[:, kc, 0:n], xo[:, kc, sg], rsb[:, 0:n], ALU.mult, [xo, rsb], [h2T])
                        for cc in range(22):
                            wg_, wv_ = wug[cc % 2], wuv[cc % 2]
                            k.dma(wg_[:], ws_up[:, :, cc * 128:(cc + 1) * 128], [T_wscr_in], [wg_])
                            bk = 1 + cc % 2
                            for kc in range(8):
                                k.mm(banks[bk][:, 0:n], wg_[:, kc, :], h2T[:, kc, 0:n], [wg_, h2T], PQ[bk], start=(kc == 0), stop=(kc == 7))
                            if halo:
                                k.cp("act", ughalo[:, cc, 0:n], banks[bk][:, 0:n], PQ[bk], [ughalo])
                                continue
                            k.dma(wv_[:], ws_up[:, :, DFF + cc * 128:DFF + (cc + 1) * 128], [T_wscr_in], [wv_])
                            ug, ca = ugc[cc % 2], cva[cc % 2]
                            k.cp("act", ug[:, 2:2 + n], banks[bk][:, 0:n], PQ[bk], [ug])
                            k.cp("pool", ug[:, 0:2], ughalo[:, cc, :], [ughalo], [ug])
                            k.ts("dve", ca[:, 0:n], ug[:, 0:n], convf[:, cc, 0:1], ALU.mult, [ug, convf], [ca])
                            k.stt(ca[:, 0:n], ug[:, 1:1 + n], convf[:, cc, 1:2], ca[:, 0:n], ALU.mult, ALU.add, [ug, convf, ca], [ca])
                            k.stt(ca[:, 0:n], ug[:, 2:2 + n], convf[:, cc, 2:3], ca[:, 0:n], ALU.mult, ALU.add, [ug, convf, ca], [ca])
                            k.act(ca[:, 0:n], ca[:, 0:n], AF.Gelu_apprx_tanh, [ca], [ca])
                            nv = jb["nvalid"]
                            k.cp("pool", fco[:, cc, :], ug[:, nv:nv + 2], [ug], [fco])
                            bk2 = 3 + cc % 2
                            for kc in range(8):
                                k.mm(banks[bk2][:, 0:n], wv_[:, kc, :], h2T[:, kc, 0:n], [wv_, h2T], PQ[bk2], start=(kc == 0), stop=(kc == 7))
                            k.tt("dve", actT[:, cc, 0:n], ca[:, 0:n], banks[bk2][:, 0:n], ALU.mult, [ca] + PQ[bk2], [actT])
                        if halo:
                            continue
                        for c in range(8):
                            wd_ = wdn[c % 2]
                            k.dma(wd_[:], ws_down[:, :, c * 128:(c + 1) * 128], [T_wscr_in], [wd_])
                            bk = 5 + c % 2
                            for cc in range(22):
                                k.mm(banks[bk][:, 0:n], wd_[:, cc, :], actT[:, cc, 0:n], [wd_, actT], PQ[bk], start=(cc == 0), stop=(cc == 21))
                            k.tt("dve", xo[:, c, sg], xo[:, c, sg], banks[bk][:, 0:n], ALU.add, [xo] + PQ[bk], [xo])
                        k.dma(Wo[:], ws_pg, [T_wscr_in], [Wo])
                        Wpg = Wo
                        k.act(sqb[:, :, 0:n], xo[:, :, sg], AF.Square, [xo], [sqb])
                        for kc in range(8):
                            k.mm(banks[0][:, 0:n], ones_bf[:], sqb[:, kc, 0:n], [ones_bf, sqb], PQ[0], start=(kc == 0), stop=(kc == 7))
                        k.act(rsb[:, 0:n], banks[0][:, 0:n], AF.Sqrt, PQ[0] + [epsT], [rsb], scale=1.0 / D, bias=epsT[:, 0:1])
                        k.recip(rsb[:, 0:n], rsb[:, 0:n], [rsb], [rsb])
                        for kc in range(8):
                            k.tt("dve" if kc % 2 == 0 else "pool", h2T[:, kc, 0:n], xo[:, kc, sg], rsb[:, 0:n], ALU.mult, [xo, rsb], [h2T])
                        for c in range(8):
                            cs = slice(c * 128, (c + 1) * 128)
                            bo = 2 * (c % 2)
                            for kc in range(8):
                                k.mm(banks[1 + bo][:, 0:n], Wpg[:, kc, cs], h2T[:, kc, 0:n], [Wpg, h2T], PQ[1 + bo], start=(kc == 0), stop=(kc == 7))
                            k.act(sga[:, 0:n], banks[1 + bo][:, 0:n], AF.Sigmoid, PQ[1 + bo], [sga])
                            for kc in range(2):
                                k.mm(banks[2 + bo][:, 0:n], Wpl[:, kc, cs], pb[:, kc, sg], [Wpl, pb], PQ[2 + bo], start=(kc == 0), stop=(kc == 1))
                            k.tt("dve", t1[:, 0:n], banks[2 + bo][:, 0:n], sga[:, 0:n], ALU.mult, PQ[2 + bo] + [sga], [t1])
                            k.tt("pool", xo[:, c, sg], xo[:, c, sg], t1[:, 0:n], ALU.add, [xo, t1], [xo])
                        k.act(sqb[:, :, 0:n], xo[:, :, sg], AF.Square, [xo], [sqb])
                        for kc in range(8):
                            k.mm(banks[0][:, 0:n], ones_bf[:], sqb[:, kc, 0:n], [ones_bf, sqb], PQ[0], start=(kc == 0), stop=(kc == 7))
                        k.act(rsb[:, 0:n], banks[0][:, 0:n], AF.Sqrt, PQ[0] + [epsT], [rsb], scale=1.0 / D, bias=epsT[:, 0:1])
                        k.recip(rsb[:, 0:n], rsb[:, 0:n], [rsb], [rsb])
                        for kc in range(8):
                            k.stt(xo[:, kc, sg], xo[:, kc, sg], gains[:, 24 + kc:25 + kc], rsb[:, 0:n], ALU.mult, ALU.mult, [xo, gains, rsb], [xo])
                        for tt in range(n // 128):
                            for cg in range(2):
                                bk = 3 + cg
                                for c4 in range(4):
                                    k.mm(banks[bk][:, c4 * 128:(c4 + 1) * 128], xo[:, 4 * cg + c4, c0 + tt * 128:c0 + (tt + 1) * 128], ident, [xo, cst], PQ[bk])
                                k.cp("act" if cg == 0 else "dve", ytok[:, cg * 512:(cg + 1) * 512], banks[bk][:, :], PQ[bk], [ytok])
                            nv = min(128, jb["nvalid"])
                            k.dma(jb["y_out"][tt * 128:tt * 128 + nv, :], ytok[0:nv, :], [ytok], [outT], key=ytok)
                        k.dma(jb["fconv_out"], fco[:], [fco], [outT], key=fco)
                    S.barrier()

        xav = I["xa"].rearrange("(kc p) t -> p kc t", p=128)
        pav = I["pT"].rearrange("(kc p) t -> p kc t", p=128)
        if "P" in KJOB2:
            for m in range(4):
                ps_ = 4 * m + 3
                t0 = 512 * ps_ - 128
                own_block(dict(NQ=5, xsrc=xav[:, :, t0:t0 + 640], psrc=pav[:, :, t0:t0 + 640], L=[128 * (4 * ps_ + r) for r in range(5)],
                               limcol=5 * m, lolim=22, kiT=kiT_p, kT=kT_p, v=v_p, obT=obT_p[:, :, t0:t0 + 640], fhalo=None,
                               segs=[(126, 2, True), (128, 512, False)], nvalid=512, y_out=O["y_own"][512 * m:512 * (m + 1), :], fconv_out=O["fconv_p"]))
        if "S" in KJOB2:
            for sb_ in range(2):
                own_block(dict(NQ=1, xsrc=I["xs"][sb_].rearrange("(kc p) t -> p kc t", p=128), psrc=I["psT"][sb_].rearrange("(kc p) t -> p kc t", p=128),
                               L=[LKS], limcol=20 + sb_, lolim=None, kiT=kiT_s[sb_], kT=kT_s[sb_], v=v_s[sb_], obT=obT_s[sb_], fhalo=I["fconvT"][sb_],
                               segs=[(0, 128, False)], nvalid=16, y_out=O["y_s"][sb_], fconv_out=O["fconv_s"][sb_]))

        print('nsem', S.nsem, {e: len(q) for e, q in S.q.items()})
        S.final_wait("sp", [outT])
        S.emit()
    return nc


def _consts():
    c = np.zeros((128, 7 * 128), np.float32)
    i = np.arange(128)
    same = (i[:, None] // 64) == (i[None, :] // 64)
    c[:, 0:128] = np.eye(128)
    c[:, 128:256] = ((i[:, None] <= i[None, :]) & same)
    c[:, 256:384] = ((i[:, None] > i[None, :]) & same)
    c[:, 384:512] = np.where((i[:, None] > i[None, :]) & same, 0.0, NEG)
    c[:, 512:640] = np.where((i[:, None] <= i[None, :]) & same, 0.0, -NEG)
    c[:, 640:768] = (i[:, None] < 64)
    c[:, 768:896] = (i[:, None] >= 64)
    return c


_NC = None


def kernel(**inp):
    global _NC
    f32 = np.float32
    xp = np.asarray(inp["x_prompt"], f32)
    xs = np.asarray(inp["x_sample"], f32)
    cst = _consts()
    gains = np.zeros((128, 32), f32)
    gains[:, 0:8] = np.asarray(inp["norm_mix"], f32)[0].reshape(8, 128).T
    gains[:, 8:16] = np.asarray(inp["norm_ffn"], f32)[0].reshape(8, 128).T
    gains[:, 16:24] = np.asarray(inp["norm_ple"], f32)[0].reshape(8, 128).T
    gains[:, 24:32] = np.asarray(inp["norm_final"], f32).reshape(8, 128).T
    convb = np.ascontiguousarray(np.asarray(inp["conv_b"], f32)[0].T.reshape(12, 128, 4).transpose(1, 0, 2))
    alog = np.ascontiguousarray(np.broadcast_to(np.asarray(inp["a_log"], f32)[0][None, :], (128, 4)))
    dtb = np.ascontiguousarray(np.broadcast_to(np.asarray(inp["dt_bias"], f32)[0][None, :], (128, 4)))
    ngdn = np.ascontiguousarray(np.broadcast_to(np.asarray(inp["norm_gdn"], f32)[0][None, :], (128, 128)))
    gval = np.zeros((128, 2), f32)
    gval[0:16, 0] = 1.0
    gval[:, 1] = 1.0
    w_in = np.ascontiguousarray(np.asarray(inp["w_in"], f32)[0])
    ck = np.asarray(inp["cache_k"], f32)[0]
    cvv = np.asarray(inp["cache_v"], f32)[0]
    cki = np.asarray(inp["cache_kidx"], f32)[0]
    sg = np.asarray(inp["state_gdn"], f32)[0]
    sgc = np.asarray(inp["state_gdn_conv"], f32)[0]
    def bucket_np(rel):
        half, max_exact = 16, 8
        ret = np.where(rel > 0, half, 0)
        n = np.abs(rel)
        nf = np.maximum(n, 1).astype(np.float32)
        large = max_exact + (np.log(nf / np.float32(max_exact)) / np.float32(np.log(128 / 8)) * np.float32(half - max_exact)).astype(np.int32)
        large = np.minimum(large, half - 1)
        return ret + np.where(n < max_exact, n, large)
    ohrev = np.zeros((32, 384), f32)
    sp_ = np.arange(383)
    ohrev[bucket_np(127 - sp_), sp_] = 1.0
    iota = np.ascontiguousarray(np.broadcast_to(np.arange(512, dtype=f32)[None, :], (128, 512)))
    relb = np.ascontiguousarray(np.asarray(inp["rel_bias"], f32))
    relb15 = np.ascontiguousarray(relb[15][:, None])
    convf = np.ascontiguousarray(np.asarray(inp["conv_ffn"], f32)[0].T.reshape(22, 128, 3).transpose(1, 0, 2))
    pp = np.asarray(inp["p_prompt"], f32)[0]
    psm = np.asarray(inp["p_sample"], f32)[0]
    sfc = np.asarray(inp["state_ffn_conv"], f32)[0]
    wts = dict(w_pa=np.ascontiguousarray(np.asarray(inp["w_proj_a"], f32)[0]), w_pb=np.ascontiguousarray(np.asarray(inp["w_proj_b"], f32)[0]),
               w_out=np.ascontiguousarray(np.asarray(inp["w_out"], f32)[0]), w_up=np.ascontiguousarray(np.asarray(inp["w_up"], f32)[0]),
               w_down=np.ascontiguousarray(np.asarray(inp["w_down"], f32)[0]), w_ple=np.ascontiguousarray(np.asarray(inp["w_ple"], f32)[0]),
               w_pg=np.ascontiguousarray(np.asarray(inp["w_ple_gate"], f32)[0]))
    in_maps = []
    for c in range(8):
        b, j = c // 4, c % 4
        pad = 512 * (3 - j)
        xa = np.zeros((D, LKP), f32)
        xa[:, pad:] = xp[b].T[:, :LKP - pad]
        sl = slice(2 * c, 2 * c + 2)
        xs_c = np.zeros((2, D, 128), f32)
        xs_c[:, :, 0:16] = xs[sl].transpose(0, 2, 1)
        m = dict(xa=xa, xs=xs_c, cst=cst, gains=gains, convb=convb, alog=alog, dtb=dtb, ngdn=ngdn, gval=gval, w_in=w_in,
                 cache_kT=np.ascontiguousarray(ck[sl].reshape(2, PAST, 512).transpose(0, 2, 1)),
                 cache_v=np.ascontiguousarray(cvv[sl].reshape(2, PAST, 512)),
                 cache_kiT=np.ascontiguousarray(cki[sl].transpose(0, 2, 1)),
                 state_gdn=np.ascontiguousarray(sg[sl]),
                 gconvT=np.ascontiguousarray(sgc[sl].transpose(0, 2, 1).reshape(2, 12, 128, 3).transpose(0, 2, 1, 3)))
        pT = np.zeros((256, LKP), f32)
        pT[:, pad:] = pp[b].T[:, :LKP - pad]
        psT = np.zeros((2, 256, 128), f32)
        psT[:, :, 0:16] = psm[sl].transpose(0, 2, 1)
        lims = np.zeros((128, 24), f32)
        ii = np.arange(128)
        for m_ in range(4):
            for qb in range(5):
                pt = 512 * (4 * m_ + 3) - 128 + 128 * qb + ii
                lims[:, 5 * m_ + qb] = (pt // 64 + 1) * 64
        lims[:, 20] = PAST + 16
        lims[:, 21] = PAST + 16
        lims[:, 22] = pad
        m.update(pT=pT, psT=psT, convf=convf, relb=relb, relb15=relb15, ohrev=ohrev, iota=iota, lims=lims,
                 fconvT=np.ascontiguousarray(sfc[sl].transpose(0, 2, 1).reshape(2, 22, 128, 2).transpose(0, 2, 1, 3)), **wts)
        in_maps.append(m)
    if _NC is None:
        _NC = build_program()
    res = run_bass_kernel_spmd(_NC, in_maps, core_ids=list(range(8)))
    R = res.results
    y_p = np.zeros((2, 8192, D), f32)
    for c in range(8):
        b, j = c // 4, c % 4
        for m_ in range(4):
            sg_ = 4 * m_ + j
            y_p[b, 512 * sg_:512 * (sg_ + 1)] = R[c]["y_own"][512 * m_:512 * (m_ + 1)]
    y_s = np.concatenate([R[c]["y_s"] for c in range(8)]).reshape(16, 16, D)
    k_p = np.stack([R[4 * b + 3]["k_all"] for b in range(2)]).reshape(1, 2, 8192, 8, 64)
    v_p = np.stack([R[4 * b + 3]["v_all"] for b in range(2)]).reshape(1, 2, 8192, 8, 64)
    ki_p = np.stack([R[4 * b + 3]["ki_all"] for b in range(2)]).reshape(1, 2, 8192, 64)
    gdn_p = np.stack([R[4 * b + 3]["gdn_p"] for b in range(2)]).reshape(1, 2, 4, 128, 128)
    gconv_p = np.stack([R[4 * b + 3]["gconv_p"].transpose(1, 0, 2).reshape(1536, 3).T for b in range(2)]).reshape(1, 2, 3, 1536)
    fconv_p = np.stack([R[4 * b + 3]["fconv_p"].transpose(1, 0, 2).reshape(DFF, 2).T for b in range(2)]).reshape(1, 2, 2, DFF)
    k_s = np.concatenate([R[c]["k_s"] for c in range(8)]).reshape(1, 16, 16, 8, 64)
    v_s = np.concatenate([R[c]["v_s"] for c in range(8)]).reshape(1, 16, 16, 8, 64)
    ki_s = np.concatenate([R[c]["ki_s"] for c in range(8)]).reshape(1, 16, 16, 64)
    gdn_s = np.concatenate([R[c]["gdn_s"] for c in range(8)]).reshape(1, 16, 4, 128, 128)
    gconv_s = np.concatenate([R[c]["gconv_s"] for c in range(8)])
    gconv_s = np.ascontiguousarray(gconv_s.transpose(0, 2, 1, 3).reshape(16, 1536, 3).transpose(0, 2, 1)).reshape(1, 16, 3, 1536)
    fconv_s = np.concatenate([R[c]["fconv_s"] for c in range(8)])
    fconv_s = np.ascontiguousarray(fconv_s.transpose(0, 2, 1, 3).reshape(16, DFF, 2).transpose(0, 2, 1)).reshape(1, 16, 2, DFF)
    return (y_p, y_s, k_p, v_p, ki_p, gdn_p, gconv_p, fconv_p, k_s, v_s, ki_s, gdn_s, gconv_s, fconv_s)
```

```python
import os
import numpy as np
from contextlib import ExitStack
import concourse.bass as bass
import concourse.mybir as mybir
from concourse.bass_utils import run_bass_kernel_spmd

F32 = mybir.dt.float32
BF16 = mybir.dt.bfloat16
AF = mybir.ActivationFunctionType
ALU = mybir.AluOpType
AX = mybir.AxisListType

EPOCH = 4096
KDBG = int(os.environ.get('KDBG', '9'))
KJOB = os.environ.get('KJOB', 'ps')
KSUB = int(os.environ.get('KSUB', '99'))
KJOB2 = os.environ.get('KJOB2', 'PS')
KP2 = int(os.environ.get('KP2', '99'))
KP3 = int(os.environ.get('KP3', '99'))
EPS = 1e-6
NEG = -1e30

D = 1024
LKP = 8192
NTP = 256
PAST = 2048
LKS = PAST + 128
DFF = 2816

C_QA, C_KA, C_VA, C_QI, C_KI, C_WI, C_QB, C_GB, C_BB, C_AB, C_GA, C_GBR = 0, 512, 1024, 1536, 2048, 2112, 2120, 3656, 4168, 4172, 4176, 5200
DIN = 6224


class Tl:
    __slots__ = ("t", "name", "lw", "rd", "excl")

    def __init__(self, t, name, excl=False):
        self.t = t
        self.name = name
        self.lw = None
        self.rd = []
        self.excl = excl

    def __getitem__(self, idx):
        return self.t[idx]


class Sched:
    ENGS = ("pe", "act", "dve", "pool", "sp")

    def __init__(self, nc, es):
        self.nc = nc
        self.es = es
        self.q = {e: [] for e in self.ENGS}
        self.cnt = {e: 0 for e in self.ENGS}
        self.sems = {e: [] for e in self.ENGS}
        self.seen = {e: {} for e in self.ENGS}
        self.dma_sems = {}
        self.keep = []
        self.nsem = 0

    def _newsem(self, name):
        self.nsem += 1
        return self.es.enter_context(self.nc.semaphore(name))

    def _eng_token(self, e):
        c = self.cnt[e]
        ep = c // EPOCH
        while len(self.sems[e]) <= ep:
            self.sems[e].append(self._newsem(f"s_{e}_{len(self.sems[e])}"))
        self.cnt[e] = c + 1
        return (("e", e, ep), self.sems[e][ep], (c % EPOCH) + 1)

    def _waits(self, e, reads, writes):
        toks = []
        for t in reads:
            if t.lw is not None:
                toks.append(t.lw)
            if t.excl:
                toks.extend(t.rd)
        for t in writes:
            if t.lw is not None:
                toks.append(t.lw)
            toks.extend(t.rd)
        best = {}
        for (key, sem, val) in toks:
            if key[0] == "e" and key[1] == "pe" and e == "pe":
                continue
            if best.get(key, (None, 0))[1] < val:
                best[key] = (sem, val)
        out = []
        seen = self.seen[e]
        for key, (sem, val) in best.items():
            if seen.get(key, 0) >= val:
                continue
            seen[key] = val
            out.append((sem, val))
        return out

    def op(self, e, fn, reads=(), writes=()):
        w = self._waits(e, reads, writes)
        tok = self._eng_token(e)
        self.q[e].append((w, fn, tok[1], 1))
        for t in writes:
            t.lw = tok
            t.rd = []
        for t in reads:
            if t.lw is not tok:
                t.rd.append(tok)
        return tok

    def dma(self, e, fn, reads=(), writes=(), key=None):
        w = self._waits(e, reads, writes)
        k = key if key is not None else (writes[0] if writes else reads[0])
        kid = id(k)
        ent = self.dma_sems.get(kid)
        if ent is None or ent[1] >= 32000:
            if ent is None:
                self.keep.append(k)
            if getattr(self, "pool", None):
                ent = self.pool.pop()
            else:
                self.semid = getattr(self, "semid", 0) + 1
                ent = [self._newsem(f"d_{self.nsem}"), 0, self.semid]
            self.dma_sems[kid] = ent
        ent[1] += 16
        tok = (("d", ent[2], 0), ent[0], ent[1])
        self.q[e].append((w, fn, ent[0], 16))
        for t in writes:
            t.lw = tok
            t.rd = []
        for t in reads:
            if t.lw is not tok:
                t.rd.append(tok)
        return tok

    def barrier(self):
        toks = []
        for e in self.ENGS:
            c = self.cnt[e]
            if c > 0:
                ep = (c - 1) // EPOCH
                toks.append((("e", e, ep), self.sems[e][ep], ((c - 1) % EPOCH) + 1))
        for kid, ent in self.dma_sems.items():
            if ent[1] > 0:
                toks.append((("d", ent[2], 0), ent[0], ent[1]))
        for e in self.ENGS:
            w = []
            seen = self.seen[e]
            for (key, sem, val) in toks:
                if key[0] == "e" and key[1] == e:
                    continue
                if seen.get(key, 0) >= val:
                    continue
                seen[key] = val
                w.append((sem, val))
            if w:
                self.q[e].append((w, None, None, 0))
        if not hasattr(self, "pool"):
            self.pool = []
        for kid, ent in self.dma_sems.items():
            if ent[1] < 30000:
                self.pool.append(ent)
        self.dma_sems = {}

    def final_wait(self, e, tiles):
        w = self._waits(e, tiles, tiles)
        self.q[e].append((w, None, None, 0))

    def emit(self):
        nc = self.nc
        with nc.Block() as block:
            def run(eng, name):
                for (w, fn, sem, inc) in self.q[name]:
                    for (s, v) in w:
                        eng.wait_ge(s, v)
                    if fn is not None:
                        fn(eng).then_inc(sem, inc)

            @block.tensor
            def _(eng):
                run(eng, "pe")

            @block.scalar
            def _(eng):
                run(eng, "act")

            @block.vector
            def _(eng):
                run(eng, "dve")

            @block.gpsimd
            def _(eng):
                run(eng, "pool")

            @block.sync
            def _(eng):
                run(eng, "sp")


class K:
    def __init__(self, nc, S):
        self.nc = nc
        self.S = S
        self.rr = 0

    def mm(self, out, lhsT, rhs, rd, wr, start=True, stop=True):
        self.S.op("pe", lambda e, o=out, l=lhsT, r=rhs, a=start, b=stop: e.matmul(o, lhsT=l, rhs=r, start=a, stop=b), reads=rd, writes=wr)

    def tr(self, out, in_, ident, rd, wr):
        self.S.op("pe", lambda e, o=out, i=in_, d=ident: e.matmul(o, lhsT=i, rhs=d, start=True, stop=True), reads=rd, writes=wr)

    def act(self, out, in_, func, rd, wr, scale=None, bias=None, accum=None):
        kw = {}
        if scale is not None:
            kw["scale"] = scale
        if bias is not None:
            kw["bias"] = bias
        if accum is not None:
            kw["accum_out"] = accum
        self.S.op("act", lambda e, o=out, i=in_, f=func, k=kw: e.activation(out=o, in_=i, func=f, **k), reads=rd, writes=wr)

    def ts(self, eng, out, in0, s1, op0, rd, wr, s2=None, op1=None, accum=None):
        kw = {}
        if op1 is not None:
            kw["op1"] = op1
        if accum is not None:
            kw["accum_out"] = accum
        self.S.op(eng, lambda e, o=out, i=in0, a=s1, b=s2, p=op0, k=kw: e.tensor_scalar(out=o, in0=i, scalar1=a, scalar2=b, op0=p, **k), reads=rd, writes=wr)

    def tt(self, eng, out, in0, in1, op, rd, wr):
        self.S.op(eng, lambda e, o=out, i=in0, j=in1, p=op: e.tensor_tensor(out=o, in0=i, in1=j, op=p), reads=rd, writes=wr)

    def stt(self, out, in0, sc, in1, op0, op1, rd, wr):
        self.S.op("dve", lambda e, o=out, i=in0, s=sc, j=in1, p=op0, q=op1: e.scalar_tensor_tensor(out=o, in0=i, scalar=s, in1=j, op0=p, op1=q), reads=rd, writes=wr)

    def cp(self, eng, out, in_, rd, wr):
        if eng == "act":
            self.S.op("act", lambda e, o=out, i=in_: e.copy(out=o, in_=i), reads=rd, writes=wr)
        else:
            self.S.op(eng, lambda e, o=out, i=in_: e.tensor_copy(out=o, in_=i), reads=rd, writes=wr)

    def memset(self, eng, ap, val, wr):
        self.S.op(eng, lambda e, a=ap, v=val: e.memset(a, v), writes=wr)

    def recip(self, out, in_, rd, wr):
        self.S.op("dve", lambda e, o=out, i=in_: e.reciprocal(out=o, in_=i), reads=rd, writes=wr)

    def dma(self, out, in_, rd, wr, q="sp", key=None, slow=False):
        if slow:
            self.S.dma(q, lambda e, o=out, i=in_: e.dma_start(out=o, in_=i, allow_slow_non_contiguous=True), reads=rd, writes=wr, key=key)
        else:
            self.S.dma(q, lambda e, o=out, i=in_: e.dma_start(out=o, in_=i), reads=rd, writes=wr, key=key)


def build_program():
    nc = bass.Bass("TRN2", target_bir_lowering=False)

    def din(name, shape, dt=F32):
        return nc.dram_tensor(name, list(shape), dt, kind="ExternalInput").ap()

    def dout(name, shape, dt=F32):
        return nc.dram_tensor(name, list(shape), dt, kind="ExternalOutput").ap()

    def dscr(name, shape, dt):
        return nc.dram_tensor(name, list(shape), dt, kind="Internal").ap()

    I = {}
    I["xa"] = din("xa", [D, LKP])
    I["xs"] = din("xs", [2, D, 128])
    I["cst"] = din("cst", [128, 7 * 128])
    I["gains"] = din("gains", [128, 32])
    I["convb"] = din("convb", [128, 12, 4])
    I["alog"] = din("alog", [128, 4])
    I["dtb"] = din("dtb", [128, 4])
    I["ngdn"] = din("ngdn", [128, 128])
    I["gval"] = din("gval", [128, 2])
    I["w_in"] = din("w_in", [D, DIN])
    I["cache_kT"] = din("cache_kT", [2, 512, PAST])
    I["cache_v"] = din("cache_v", [2, PAST, 512])
    I["cache_kiT"] = din("cache_kiT", [2, 64, PAST])
    I["state_gdn"] = din("state_gdn", [2, 4, 128, 128])
    I["gconvT"] = din("gconvT", [2, 128, 12, 3])
    I["pT"] = din("pT", [256, LKP])
    I["psT"] = din("psT", [2, 256, 128])
    I["w_pa"] = din("w_pa", [512, D])
    I["w_pb"] = din("w_pb", [512, D])
    I["w_out"] = din("w_out", [D, D])
    I["w_up"] = din("w_up", [D, 2 * DFF])
    I["w_down"] = din("w_down", [DFF, D])
    I["w_ple"] = din("w_ple", [256, D])
    I["w_pg"] = din("w_pg", [D, D])
    I["convf"] = din("convf", [128, 22, 3])
    I["relb"] = din("relb", [32, 8])
    I["relb15"] = din("relb15", [8, 1])
    I["ohrev"] = din("ohrev", [32, 384])
    I["iota"] = din("iota", [128, 512])
    I["lims"] = din("lims", [128, 24])
    I["fconvT"] = din("fconvT", [2, 128, 22, 2])
    O = {}
    O["y_own"] = dout("y_own", [2048, D])
    O["fconv_p"] = dout("fconv_p", [128, 22, 2])
    O["y_s"] = dout("y_s", [2, 16, D])
    O["fconv_s"] = dout("fconv_s", [2, 128, 22, 2])
    O["k_all"] = dout("k_all", [LKP, 512])
    O["v_all"] = dout("v_all", [LKP, 512])
    O["ki_all"] = dout("ki_all", [LKP, 64])
    O["gdn_p"] = dout("gdn_p", [4, 128, 128])
    O["gconv_p"] = dout("gconv_p", [128, 12, 3])
    O["k_s"] = dout("k_s", [2, 16, 512])
    O["v_s"] = dout("v_s", [2, 16, 512])
    O["ki_s"] = dout("ki_s", [2, 16, 64])
    O["gdn_s"] = dout("gdn_s", [2, 4, 128, 128])
    O["gconv_s"] = dout("gconv_s", [2, 128, 12, 3])
    wscr_in = dscr("wscr_in", [128, 8, DIN], BF16)
    ws_pa = dscr("ws_pa", [128, 4, D], BF16)
    ws_pb = dscr("ws_pb", [128, 4, D], BF16)
    ws_out = dscr("ws_out", [128, 8, D], BF16)
    ws_up = dscr("ws_up", [128, 8, 2 * DFF], BF16)
    ws_down = dscr("ws_down", [128, 22, D], BF16)
    ws_ple = dscr("ws_ple", [128, 2, D], BF16)
    ws_pg = dscr("ws_pg", [128, 8, D], BF16)
    tab_scr = dscr("tab_scr", [8, 384], F32)
    T_tab = Tl(None, "tab")
    kT_p = dscr("kT_p", [128, 4, LKP], BF16)
    v_p = dscr("v_p", [LKP, 528], BF16)
    kiT_p = dscr("kiT_p", [64, LKP], BF16)
    obT_p = dscr("obT_p", [128, 4, LKP], BF16)
    kT_s = dscr("kT_s", [2, 128, 4, LKS], BF16)
    v_s = dscr("v_s_scr", [2, LKS, 528], BF16)
    kiT_s = dscr("kiT_s", [2, 64, LKS], BF16)
    obT_s = dscr("obT_s", [2, 128, 4, 128], BF16)

    outT = Tl(None, "outputs")
    T_wscr_in = Tl(None, "wscr_in")
    T_kT = Tl(None, "kT")
    T_v = Tl(None, "v")
    T_ki = Tl(None, "kiT")
    T_ob = Tl(None, "obT")

    with ExitStack() as es0:
        S = Sched(nc, es0)
        k = K(nc, S)

        uid = [0]

        def sbt(es, name, shape, dt):
            uid[0] += 1
            nm = f"s{uid[0]}_{name}"
            return Tl(es.enter_context(nc.sbuf_tensor(nm, list(shape), dt)), nm)

        banks = [es0.enter_context(nc.psum_tensor(f"pb{i}", [128, 512], F32)) for i in range(8)]
        PB = [Tl(banks[b], f"pb{b}", excl=True) for b in range(8)]
        PQ = [[PB[b]] * 4 for b in range(8)]
        banks_bf = None

        def pq(b, q, rows=slice(0, 128), w=128):
            return banks[b][rows, q * 128:q * 128 + w]

        cst = sbt(es0, "cst", [128, 7 * 128], F32)
        k.dma(cst[:], I["cst"], [], [cst])
        ident = cst[:, 0:128]
        Ublk = cst[:, 128:256]
        Lsblk = cst[:, 256:384]
        NEGML = cst[:, 384:512]
        POSMU = cst[:, 512:640]
        half0 = cst[:, 640:768]
        half1 = cst[:, 768:896]
        gains = sbt(es0, "gains", [128, 32], F32)
        k.dma(gains[:], I["gains"], [], [gains])
        convb = sbt(es0, "convb", [128, 12, 4], F32)
        k.dma(convb[:], I["convb"], [], [convb])
        cA = sbt(es0, "cA", [128, 4], F32)
        k.dma(cA[:], I["alog"], [], [cA])
        dtb = sbt(es0, "dtb", [128, 4], F32)
        k.dma(dtb[:], I["dtb"], [], [dtb])
        ngdn = sbt(es0, "ngdn", [128, 128], F32)
        k.dma(ngdn[:], I["ngdn"], [], [ngdn])
        gval = sbt(es0, "gval", [128, 2], F32)
        k.dma(gval[:], I["gval"], [], [gval])
        epsT = sbt(es0, "epsT", [128, 1], F32)
        k.memset("pool", epsT[:], EPS, [epsT])
        ident_bf = sbt(es0, "ident_bf", [128, 128], BF16)
        k.cp("dve", ident_bf[:], ident, [cst], [ident_bf])
        ones_bf = sbt(es0, "ones_bf", [128, 128], BF16)
        k.memset("pool", ones_bf[:], 1.0, [ones_bf])
        k.act(cA[:], cA[:], AF.Exp, [cA], [cA])
        k.ts("dve", cA[:], cA[:], -1.0, ALU.mult, [cA], [cA])

        with ExitStack() as es:
            if KDBG < -1:
                raise_skip = True
            wf = [sbt(es, f"wf{i}", [128, 1556], F32) for i in range(2)]
            wb = [sbt(es, f"wb{i}", [128, 1556], BF16) for i in range(2)]
            cnt_ = [0]

            pcs = []

            def conv_w(src, dst, KC, C, gbase):
                v = src.rearrange("(kc p) c -> p kc c", p=128)
                npc = 1 if C <= 1556 else 4
                pw = C // npc
                for kc in range(KC):
                    for pc in range(npc):
                        pcs.append((v[:, kc, pc * pw:(pc + 1) * pw], dst[:, kc, pc * pw:(pc + 1) * pw], pw, None if gbase is None else gbase + kc))

            def conv_emit():
                def load(n):
                    k.dma(wf[n % 2][:, 0:pcs[n][2]], pcs[n][0], [], [wf[n % 2]], q="sp")
                if pcs:
                    load(0)
                for n, (src_, dst_, pw, gcol) in enumerate(pcs):
                    a, b_ = wf[n % 2], wb[n % 2]
                    if n + 1 < len(pcs):
                        load(n + 1)
                    eng = "dve" if n % 2 == 0 else "pool"
                    if gcol is None:
                        k.cp(eng, b_[:, 0:pw], a[:, 0:pw], [a], [b_])
                    else:
                        k.ts(eng, b_[:, 0:pw], a[:, 0:pw], gains[:, gcol:gcol + 1], ALU.mult, [a, gains], [b_])
                    k.dma(dst_, b_[:, 0:pw], [b_], [T_wscr_in], q="sp", key=b_)

            if KDBG >= -1:
                conv_w(I["w_in"], wscr_in, 8, DIN, 0)
                conv_w(I["w_pa"], ws_pa, 4, D, None)
                conv_w(I["w_pb"], ws_pb, 4, D, None)
                conv_w(I["w_out"], ws_out, 8, D, None)
                conv_w(I["w_up"], ws_up, 8, 2 * DFF, 8)
                conv_w(I["w_down"], ws_down, 22, D, None)
                conv_w(I["w_ple"], ws_ple, 2, D, None)
                conv_w(I["w_pg"], ws_pg, 8, D, 16)
                conv_emit()
            S.barrier()

        def all_pass(es, job):
            NT = job["NT"]
            ntt = NT // 128
            nsteps = min(job["L"] // NT, int(os.environ.get('KSTEPS', '999')))
            xsrc = job["x"]
            W1 = sbt(es, "W1", [128, 8, 3144], BF16)
            k.dma(W1[:, :, 0:1024], wscr_in[:, :, C_KA:C_KA + 1024], [T_wscr_in], [W1])
            k.dma(W1[:, :, 1024:1088], wscr_in[:, :, C_KI:C_KI + 64], [T_wscr_in], [W1])
            k.dma(W1[:, :, 1088:3144], wscr_in[:, :, C_QB:C_QB + 2056], [T_wscr_in], [W1])
            xa_t = [sbt(es, f"xa_t{i}", [128, 8, NT], F32) for i in range(2)]
            sq = sbt(es, "sq", [128, 8, NT], BF16)
            hT = sbt(es, "hT", [128, 8, NT], BF16)
            rs = sbt(es, "rs", [128, NT], F32)
            cin = sbt(es, "cin", [128, 12, NT + 3], F32)
            cv = sbt(es, "cv", [128, 12, NT], F32)
            cvb = sbt(es, "cvb", [128, 8, NT], BF16)
            sq2 = sbt(es, "sq2", [128, NT], BF16)
            rs2 = sbt(es, "rs2", [128, NT], F32)
            gbs = sbt(es, "gbs", [128, ntt, 512], F32)
            bbab = sbt(es, "bbab", [128, ntt, 8], F32)
            ktok = [sbt(es, f"ktok{i}", [128, 512], F32) for i in range(2)]
            vtok = [sbt(es, f"vtok{i}", [128, 512], F32) for i in range(2)]
            vbf = [sbt(es, f"vbf{i}", [128, 8, 66], BF16) for i in range(2)]
            for i in range(2):
                k.memset("pool", vbf[i][:, :, 64:66], 1.0, [vbf[i]])
            kitok = [sbt(es, f"kitok{i}", [128, 64], F32) for i in range(2)]
            kf_t = sbt(es, "kf_t", [128, 4, NT], BF16)
            kif_t = sbt(es, "kif_t", [64, NT], BF16)
            obT_t = sbt(es, "obT_t", [128, 4, NT], BF16)
            Sst = [sbt(es, f"Sst{h}", [128, 128], F32) for h in range(4)]
            bet = sbt(es, "bet", [128, ntt, 4], F32)
            nbet = sbt(es, "nbet", [128, ntt, 4], F32)
            gg = sbt(es, "gg", [128, ntt, 4], F32)
            ngg = sbt(es, "ngg", [128, ntt, 4], F32)
            egc = sbt(es, "egc", [128, ntt, 4], F32)
            ekd = sbt(es, "ekd", [128, ntt, 4], F32)
            egl = sbt(es, "egl", [128, ntt, 8], F32)
            bege = sbt(es, "bege", [128, ntt, 4], F32)
            HT = []
            for h in range(4):
                d = {}
                for nm in ("NGb", "E1m", "E2m", "P", "attnT", "kbg", "kd", "vb", "u", "wT", "vnew", "o1", "o", "tmp"):
                    d[nm] = sbt(es, f"{nm}{h}", [128, 128], F32)
                for nm in ("N", "Nt", "M0", "M1", "Mt0", "Mt1", "Pb"):
                    d[nm] = sbt(es, f"{nm}{h}", [128, 128], BF16)
                d["obn"] = sbt(es, f"obn{h}", [128, 128], BF16)
                d["ms"] = sbt(es, f"ms{h}", [128, 1], F32)
                HT.append(d)

            if job["S0"] is None:
                for h in range(4):
                    k.memset("pool", Sst[h][:], 0.0, [Sst[h]])
                k.memset("pool", cin[:, :, 0:3], 0.0, [cin])
            else:
                for h in range(4):
                    k.dma(Sst[h][:], job["S0"][h], [], [Sst[h]])
                k.dma(cin[:, :, 0:3], job["conv0"], [], [cin], slow=True)

            xv = xsrc.rearrange("(kc p) t -> p kc t", p=128)
            k.dma(xa_t[0][:], xv[:, :, 0:NT], [], [xa_t[0]])
            for st in range(nsteps):
                t0 = st * NT
                xt = xa_t[st % 2]
                if st + 1 < nsteps:
                    xn = xa_t[(st + 1) % 2]
                    k.dma(xn[:], xv[:, :, t0 + NT:t0 + 2 * NT], [], [xn])
                k.act(sq[:], xt[:], AF.Square, [xt], [sq])
                A = PQ[0]
                for kc in range(8):
                    k.mm(banks[0][:, 0:NT], ones_bf[:], sq[:, kc, :], [ones_bf, sq], A, start=(kc == 0), stop=(kc == 7))
                k.act(rs[:], banks[0][:, 0:NT], AF.Sqrt, A + [epsT], [rs], scale=1.0 / D, bias=epsT[:, 0:1])
                k.recip(rs[:], rs[:], [rs], [rs])
                for kc in range(8):
                    k.tt("dve" if kc % 2 == 0 else "pool", hT[:, kc, :], xt[:, kc, :], rs[:], ALU.mult, [xt, rs], [hT])
                if KDBG < 1:
                    continue
                for tt in range(ntt):
                    ts_ = slice(tt * 128, (tt + 1) * 128)
                    r0 = t0 + tt * 128
                    kt, vt, vb_, kit = ktok[tt % 2], vtok[tt % 2], vbf[tt % 2], kitok[tt % 2]
                    for (bk, c0, cw) in ((1, 0, 512), (2, 512, 512)):
                        for kc in range(8):
                            k.mm(banks[bk][:, 0:cw], hT[:, kc, ts_], W1[:, kc, c0:c0 + cw], [hT, W1], PQ[bk], start=(kc == 0), stop=(kc == 7))
                    k.cp("act", kt[:], banks[1][:, :], PQ[1], [kt])
                    k.cp("dve", vt[:], banks[2][:, :], PQ[2], [vt])
                    k.cp("pool", vb_[:, :, 0:64], vt[:].rearrange("p (h d) -> p h d", d=64), [vt], [vb_])
                    nv = job["nvalid"]
                    if nv >= 128:
                        k.dma(job["k_out"][r0:r0 + 128, :], kt[:], [kt], [outT], key=kt)
                        k.dma(job["v_out"][r0:r0 + 128, :], vt[:], [vt], [outT], key=vt)
                    else:
                        k.dma(job["k_out"][0:nv, :], kt[0:nv, :], [kt], [outT], key=kt)
                        k.dma(job["v_out"][0:nv, :], vt[0:nv, :], [vt], [outT], key=vt)
                    k.dma(job["v_scr"][job["koff"] + r0:job["koff"] + r0 + 128, :], vb_[:].rearrange("p h d -> p (h d)"), [vb_], [T_v], key=vb_)
                    for kc in range(8):
                        k.mm(banks[3][:, 0:64], hT[:, kc, ts_], W1[:, kc, 1024:1088], [hT, W1], PQ[3], start=(kc == 0), stop=(kc == 7))
                    k.cp("act", kit[:], banks[3][:, 0:64], PQ[3], [kit])
                    if nv >= 128:
                        k.dma(job["ki_out"][r0:r0 + 128, :], kit[:], [kit], [outT], key=kit)
                    else:
                        k.dma(job["ki_out"][0:nv, :], kit[0:nv, :], [kit], [outT], key=kit)
                    for kc in range(8):
                        k.mm(banks[4][:, :], hT[:, kc, ts_], W1[:, kc, 2624:3136], [hT, W1], PQ[4], start=(kc == 0), stop=(kc == 7))
                    k.act(gbs[:, tt, :], banks[4][:, :], AF.Silu, PQ[4], [gbs])
                    for kc in range(8):
                        k.mm(banks[3][:, 128:136], hT[:, kc, ts_], W1[:, kc, 3136:3144], [hT, W1], PQ[3], start=(kc == 0), stop=(kc == 7))
                    k.cp("dve", bbab[:, tt, :], banks[3][:, 128:136], PQ[3], [bbab])
                for p in range(4):
                    bk = 5 + (p % 2)
                    for kc in range(8):
                        k.mm(banks[bk][:, 0:NT], W1[:, kc, p * 128:(p + 1) * 128], hT[:, kc, :], [W1, hT], PQ[bk], start=(kc == 0), stop=(kc == 7))
                    k.cp("act" if p % 2 == 0 else "dve", kf_t[:, p, :], banks[bk][:, 0:NT], PQ[bk], [kf_t])
                k.dma(job["kT_scr"][:, :, job["koff"] + t0:job["koff"] + t0 + NT], kf_t[:], [kf_t], [T_kT], key=kf_t)
                for kc in range(8):
                    k.mm(banks[7][0:64, 0:NT], W1[:, kc, 1024:1088], hT[:, kc, :], [W1, hT], PQ[7], start=(kc == 0), stop=(kc == 7))
                k.cp("act", kif_t[:], banks[7][0:64, 0:NT], PQ[7], [kif_t])
                k.dma(job["kiT_scr"][:, job["koff"] + t0:job["koff"] + t0 + NT], kif_t[:], [kif_t], [T_ki], key=kif_t)
                if KDBG < 2:
                    continue
                for c in range(12):
                    bk = 5 + (c % 3)
                    for kc in range(8):
                        k.mm(banks[bk][:, 0:NT], W1[:, kc, 1088 + c * 128:1088 + (c + 1) * 128], hT[:, kc, :], [W1, hT], PQ[bk], start=(kc == 0), stop=(kc == 7))
                    k.cp("act" if c % 2 == 0 else "dve", cin[:, c, 3:3 + NT], banks[bk][:, 0:NT], PQ[bk], [cin])
                for c in range(12):
                    k.ts("dve", cv[:, c, :], cin[:, c, 0:NT], convb[:, c, 0:1], ALU.mult, [cin, convb], [cv])
                    for j in range(1, 4):
                        k.stt(cv[:, c, :], cin[:, c, j:j + NT], convb[:, c, j:j + 1], cv[:, c, :], ALU.mult, ALU.add, [cin, convb, cv], [cv])
                if st == nsteps - 1:
                    k.dma(job["conv_out"], cin[:, :, job["nvalid_last"]:job["nvalid_last"] + 3], [cin], [outT], key=cin, slow=True)
                k.cp("pool", cin[:, :, 0:3], cin[:, :, NT:NT + 3], [cin], [cin])
                k.act(cv[:], cv[:], AF.Silu, [cv], [cv])
                for c in range(8):
                    k.act(sq2[:], cv[:, c, :], AF.Square, [cv], [sq2])
                    k.mm(banks[0][:, 0:NT], ones_bf[:], sq2[:], [ones_bf, sq2], PQ[0])
                    k.act(rs2[:], banks[0][:, 0:NT], AF.Sqrt, PQ[0] + [epsT], [rs2], bias=epsT[:, 0:1])
                    k.recip(rs2[:], rs2[:], [rs2], [rs2])
                    if c < 4:
                        k.stt(cv[:, c, :], cv[:, c, :], 128.0 ** -0.5, rs2[:], ALU.mult, ALU.mult, [cv, rs2], [cv])
                    else:
                        k.tt("dve", cv[:, c, :], cv[:, c, :], rs2[:], ALU.mult, [cv, rs2], [cv])
                    k.cp("pool", cvb[:, c, :], cv[:, c, :], [cv], [cvb])
                k.act(bet[:], bbab[:, :, 0:4], AF.Sigmoid, [bbab], [bet])
                if job["gval"] is not None:
                    for tt in range(ntt):
                        k.ts("dve", bet[:, tt, :], bet[:, tt, :], gval[:, job["gval"]:job["gval"] + 1], ALU.mult, [bet, gval], [bet])
                k.ts("dve", nbet[:], bet[:], -1.0, ALU.mult, [bet], [nbet])
                for tt in range(ntt):
                    k.tt("dve", gg[:, tt, :], bbab[:, tt, 4:8], dtb[:], ALU.add, [bbab, dtb], [gg])
                k.act(gg[:], gg[:], AF.Exp, [gg], [gg])
                k.act(gg[:], gg[:], AF.Ln, [gg], [gg], bias=1.0)
                for tt in range(ntt):
                    k.tt("dve", gg[:, tt, :], gg[:, tt, :], cA[:], ALU.mult, [gg, cA], [gg])
                    if job["gval"] is not None:
                        k.ts("dve", gg[:, tt, :], gg[:, tt, :], gval[:, job["gval"]:job["gval"] + 1], ALU.mult, [gg, gval], [gg])
                k.ts("dve", ngg[:], gg[:], -1.0, ALU.mult, [gg], [ngg])
                if KDBG < 3:
                    continue
                for tt in range(ntt):
                    ts_ = slice(tt * 128, (tt + 1) * 128)
                    G = PQ[0]
                    k.mm(banks[0][:, 0:4], Ublk, gg[:, tt, :], [cst, gg], G)
                    k.mm(banks[0][:, 4:8], Lsblk, gg[:, tt, :], [cst, gg], G)
                    k.mm(banks[0][:, 8:12], half0, gg[:, tt, :], [cst, gg], G)
                    k.mm(banks[0][:, 12:16], half1, gg[:, tt, :], [cst, gg], G)
                    k.act(egc[:, tt, :], banks[0][:, 0:4], AF.Exp, G, [egc])
                    k.act(ekd[:, tt, :], banks[0][:, 4:8], AF.Exp, G, [ekd])
                    k.act(egl[:, tt, :], banks[0][:, 8:16], AF.Exp, G, [egl])
                    k.tt("dve", bege[:, tt, :], bet[:, tt, :], egc[:, tt, :], ALU.mult, [bet, egc], [bege])
                    BK = ((1, 2), (3, 4), (5, 6), (7, 0))
                    H4 = range(4)
                    for h in H4:
                        T_ = HT[h]
                        k.cp("pool", T_["NGb"][:], ngg[:, tt, h:h + 1].to_broadcast([128, 128]), [ngg], [T_["NGb"]])
                    for h in H4:
                        T_ = HT[h]
                        b0, b1 = BK[h]
                        k.mm(pq(b0, 0), Ublk, gg[:, tt, h:h + 1].to_broadcast([128, 128]), [cst, gg], [PB[b0]], start=True, stop=False)
                        k.mm(pq(b0, 0), T_["NGb"][:], Ublk, [cst, T_["NGb"]], [PB[b0]], start=False, stop=True)
                        k.mm(pq(b1, 1), cvb[:, 4 + h, ts_], cvb[:, 4 + h, ts_], [cvb], [PB[b1]])
                    for h in H4:
                        T_ = HT[h]
                        b0, b1 = BK[h]
                        k.stt(T_["E1m"][:], pq(b0, 0), 0.0, NEGML, ALU.min, ALU.add, [PB[b0], cst], [T_["E1m"]])
                        k.stt(T_["E2m"][:], pq(b0, 0), 0.0, POSMU, ALU.max, ALU.add, [PB[b0], cst], [T_["E2m"]])
                    for h in H4:
                        T_ = HT[h]
                        k.act(T_["E1m"][:], T_["E1m"][:], AF.Exp, [T_["E1m"]], [T_["E1m"]])
                        k.act(T_["E2m"][:], T_["E2m"][:], AF.Exp, [T_["E2m"]], [T_["E2m"]], scale=-1.0)
                    for h in H4:
                        T_ = HT[h]
                        b0, b1 = BK[h]
                        k.mm(pq(b0, 2), cvb[:, 4 + h, ts_], cvb[:, h, ts_], [cvb], [PB[b0]])
                    for h in H4:
                        T_ = HT[h]
                        b0, b1 = BK[h]
                        k.stt(T_["N"][:], pq(b1, 1), nbet[:, tt, h:h + 1], T_["E1m"][:], ALU.mult, ALU.mult, [PB[b1], nbet, T_["E1m"]], [T_["N"]])
                    for h in H4:
                        T_ = HT[h]
                        b0, b1 = BK[h]
                        k.tt("dve", T_["attnT"][:], pq(b0, 2), T_["E2m"][:], ALU.mult, [PB[b0], T_["E2m"]], [T_["attnT"]])
                    for h in H4:
                        T_ = HT[h]
                        b0, b1 = BK[h]
                        k.tr(pq(b1, 3), cv[:, 4 + h, ts_], ident, [cv, cst], [PB[b1]])
                        k.tr(pq(b0, 0), cv[:, 8 + h, ts_], ident, [cv, cst], [PB[b0]])
                    for h in H4:
                        T_ = HT[h]
                        b0, b1 = BK[h]
                        k.ts("dve", T_["kbg"][:], pq(b1, 3), bege[:, tt, h:h + 1], ALU.mult, [PB[b1], bege], [T_["kbg"]])
                        k.ts("dve", T_["kd"][:], pq(b1, 3), ekd[:, tt, h:h + 1], ALU.mult, [PB[b1], ekd], [T_["kd"]])
                    for h in H4:
                        T_ = HT[h]
                        b0, b1 = BK[h]
                        k.ts("dve", T_["vb"][:], pq(b0, 0), bet[:, tt, h:h + 1], ALU.mult, [PB[b0], bet], [T_["vb"]])
                    for h in H4:
                        T_ = HT[h]
                        b0, b1 = BK[h]
                        k.mm(pq(b1, 1), T_["N"][:], ident_bf[:], [T_["N"], ident_bf], [PB[b1]])
                    for h in H4:
                        T_ = HT[h]
                        b0, b1 = BK[h]
                        k.cp("act", T_["Nt"][:], pq(b1, 1), [PB[b1]], [T_["Nt"]])
                    for h in H4:
                        T_ = HT[h]
                        b0, b1 = BK[h]
                        k.tt("dve", T_["P"][:], pq(b1, 1), ident, ALU.add, [PB[b1], cst], [T_["P"]])
                    for h in H4:
                        T_ = HT[h]
                        k.cp("pool", T_["Pb"][:], T_["P"][:], [T_["P"]], [T_["Pb"]])
                    if KDBG < 4:
                        continue
                    for lv in range(1, 6):
                        for h in range(4):
                            T_ = HT[h]
                            b0, b1 = ((1, 2), (3, 4), (5, 6), (7, 0))[h]
                            Mp = T_["N"] if lv == 1 else T_[f"M{(lv - 1) % 2}"]
                            Mtp = T_["Nt"] if lv == 1 else T_[f"Mt{(lv - 1) % 2}"]
                            Mn = T_[f"M{lv % 2}"]
                            Mtn = T_[f"Mt{lv % 2}"]
                            k.mm(pq(b1, 2), Mtp[:], Mp[:], [Mtp, Mp], [PB[b1]])
                            if lv < 5:
                                k.mm(pq(b0, 1), Mp[:], Mtp[:], [Mtp, Mp], [PB[b0]])
                            k.cp("act", Mn[:], pq(b1, 2), [PB[b1]], [Mn])
                            if lv < 5:
                                k.cp("dve", Mtn[:], pq(b0, 1), [PB[b0]], [Mtn])
                        for h in range(4):
                            T_ = HT[h]
                            b0, b1 = ((1, 2), (3, 4), (5, 6), (7, 0))[h]
                            Mn = T_[f"M{lv % 2}"]
                            k.mm(pq(b1, 0), Mn[:], T_["Pb"][:], [Mn, T_["Pb"]], [PB[b1]])
                            k.tt("dve", T_["P"][:], T_["P"][:], pq(b1, 0), ALU.add, [T_["P"], PB[b1]], [T_["P"]])
                            if lv < 5:
                                k.cp("pool", T_["Pb"][:], T_["P"][:], [T_["P"]], [T_["Pb"]])
                    if KDBG < 5:
                        continue
                    BK = ((1, 2), (3, 4), (5, 6), (7, 0))
                    for h in range(4):
                        T_ = HT[h]
                        b0, b1 = BK[h]
                        k.mm(pq(b0, 1), T_["P"][:], T_["vb"][:], [T_["P"], T_["vb"]], [PQ[b0][1]])
                        k.cp("act", T_["u"][:], pq(b0, 1), [PQ[b0][1]], [T_["u"]])
                    for h in range(4):
                        T_ = HT[h]
                        b0, b1 = BK[h]
                        k.mm(pq(b1, 2), T_["kbg"][:], T_["P"][:], [T_["P"], T_["kbg"]], [PQ[b1][2]])
                        k.cp("dve", T_["wT"][:], pq(b1, 2), [PQ[b1][2]], [T_["wT"]])
                    for c in range(2):
                        r = slice(64 * c, 64 * c + 64)
                        for h in range(4):
                            T_ = HT[h]
                            b0, b1 = BK[h]
                            k.mm(banks[b0][r, 384:512], T_["wT"][:, r], Sst[h][:], [T_["wT"], Sst[h]], [PQ[b0][3]])
                            k.mm(banks[b0][r, 0:128], cv[:, h, tt * 128 + 64 * c:tt * 128 + 64 * c + 64], Sst[h][:], [cv, Sst[h]], [PQ[b0][0]])
                        for h in range(4):
                            T_ = HT[h]
                            b0, b1 = BK[h]
                            k.tt("dve", T_["vnew"][r, :], T_["u"][r, :], banks[b0][r, 384:512], ALU.subtract, [T_["u"], PQ[b0][3]], [T_["vnew"]])
                        for h in range(4):
                            T_ = HT[h]
                            b0, b1 = BK[h]
                            k.mm(banks[b1][r, 128:256], T_["attnT"][r, r], T_["vnew"][r, :], [T_["attnT"], T_["vnew"]], [PQ[b1][1]])
                            k.mm(pq(b1, 2), T_["kd"][r, :], T_["vnew"][r, :], [T_["kd"], T_["vnew"]], [PQ[b1][2]])
                        for h in range(4):
                            T_ = HT[h]
                            b0, b1 = BK[h]
                            k.stt(Sst[h][:], Sst[h][:], egl[:, tt, 4 * c + h:4 * c + h + 1], pq(b1, 2), ALU.mult, ALU.add, [Sst[h], egl, PQ[b1][2]], [Sst[h]])
                        for h in range(4):
                            T_ = HT[h]
                            b0, b1 = BK[h]
                            k.cp("act", T_["o1"][r, :], banks[b1][r, 128:256], [PQ[b1][1]], [T_["o1"]])
                        for h in range(4):
                            T_ = HT[h]
                            b0, b1 = BK[h]
                            k.stt(T_["o"][r, :], banks[b0][r, 0:128], egc[r, tt, h:h + 1], T_["o1"][r, :], ALU.mult, ALU.add, [PQ[b0][0], egc, T_["o1"]], [T_["o"]])
                    for h in range(4):
                        T_ = HT[h]
                        k.act(T_["tmp"][:], T_["o"][:], AF.Square, [T_["o"]], [T_["tmp"], T_["ms"]], accum=T_["ms"][:, 0:1])
                    for h in range(4):
                        T_ = HT[h]
                        k.act(T_["ms"][:], T_["ms"][:], AF.Sqrt, [T_["ms"], epsT], [T_["ms"]], scale=1.0 / 128, bias=epsT[:, 0:1])
                    for h in range(4):
                        T_ = HT[h]
                        k.recip(T_["ms"][:], T_["ms"][:], [T_["ms"]], [T_["ms"]])
                    for h in range(4):
                        T_ = HT[h]
                        k.stt(T_["tmp"][:], T_["o"][:], T_["ms"][:, 0:1], ngdn[:], ALU.mult, ALU.mult, [T_["o"], T_["ms"], ngdn], [T_["tmp"]])
                    for h in range(4):
                        T_ = HT[h]
                        k.tt("pool", T_["obn"][:], T_["tmp"][:], gbs[:, tt, h * 128:(h + 1) * 128], ALU.mult, [T_["tmp"], gbs], [T_["obn"]])
                    for h in range(4):
                        T_ = HT[h]
                        b0, b1 = BK[h]
                        k.tr(banks[b1][:, 384:512], T_["obn"][:], ident_bf[:], [T_["obn"], ident_bf], [PQ[b1][3]])
                    for h in range(4):
                        b0, b1 = BK[h]
                        k.cp("act", obT_t[:, h, ts_], banks[b1][:, 384:512], [PQ[b1][3]], [obT_t])
                if KDBG >= 5:
                    k.dma(job["obT_scr"][:, :, t0:t0 + NT], obT_t[:], [obT_t], [T_ob], key=obT_t)
            for h in range(4):
                k.dma(job["S_out"][h], Sst[h][:], [Sst[h]], [outT], key=Sst[h])
            S.barrier()

        with ExitStack() as es:
          if KDBG >= 0 and 'p' in KJOB:
            all_pass(es, dict(NT=NTP, L=LKP, x=I["xa"], S0=None, conv0=None, nvalid=128, nvalid_last=NTP,
                              k_out=O["k_all"], v_out=O["v_all"], ki_out=O["ki_all"], kT_scr=kT_p, v_scr=v_p, kiT_scr=kiT_p,
                              koff=0, obT_scr=obT_p, S_out=O["gdn_p"], conv_out=O["gconv_p"], gval=None))
        for sb_ in range(2 if (KDBG >= 0 and 's' in KJOB) else 0):
            with ExitStack() as es:
                all_pass(es, dict(NT=128, L=128, x=I["xs"][sb_], S0=I["state_gdn"][sb_], conv0=I["gconvT"][sb_], nvalid=16, nvalid_last=16,
                                  k_out=O["k_s"][sb_], v_out=O["v_s"][sb_], ki_out=O["ki_s"][sb_], kT_scr=kT_s[sb_], v_scr=v_s[sb_],
                                  kiT_scr=kiT_s[sb_], koff=PAST, obT_scr=obT_s[sb_], S_out=O["gdn_s"][sb_], conv_out=O["gconv_s"][sb_], gval=0))

        convf = sbt(es0, "convf", [128, 22, 3], F32)
        k.dma(convf[:], I["convf"], [], [convf])
        iota = sbt(es0, "iota", [128, 512], F32)
        k.dma(iota[:], I["iota"], [], [iota])
        lims = sbt(es0, "lims", [128, 24], F32)
        k.dma(lims[:], I["lims"], [], [lims])
        BT = sbt(es0, "BT", [128, 2, 8, 128], BF16)
        with ExitStack() as es:
            rb = sbt(es, "rb", [32, 8], F32)
            oh = sbt(es, "oh", [32, 384], F32)
            rb15 = sbt(es, "rb15", [8, 1], F32)
            tabs = sbt(es, "tabs", [8, 384], F32)
            BTf = sbt(es, "BTf", [128, 2, 8, 128], F32)
            k.dma(rb[:], I["relb"], [], [rb])
            k.dma(oh[:], I["ohrev"], [], [oh])
            k.dma(rb15[:], I["relb15"], [], [rb15])
            k.mm(banks[0][0:8, 0:384], rb[:], oh[:], [rb, oh], PQ[0])
            k.ts("dve", tabs[:], banks[0][0:8, 0:384], rb15[:, 0:1], ALU.subtract, PQ[0] + [rb15], [tabs])
            k.dma(tab_scr, tabs[:], [tabs], [T_tab], key=tabs)
            for kp in range(128):
                for dd in range(2):
                    base = 127 + 128 * dd - kp
                    k.dma(BTf[kp:kp + 1, dd, :, :], tab_scr[:, base:base + 128].unsqueeze(0), [T_tab], [BTf])
            k.cp("dve", BT[:], BTf[:], [BTf], [BT])
            ckf = sbt(es, "ckf", [128, 4, 512], F32)
            ckb = sbt(es, "ckb", [128, 4, 512], BF16)
            cvf = sbt(es, "cvf", [128, 512], F32)
            cvb = sbt(es, "cvb", [128, 512], BF16)
            cvb2 = sbt(es, "cvb2", [128, 8, 66], BF16)
            k.memset("pool", cvb2[:, :, 64:66], 1.0, [cvb2])
            for sb_ in range(2):
                ckv = I["cache_kT"][sb_].rearrange("(p r) t -> r p t", r=128)
                for pc in range(4):
                    k.dma(ckf[:], ckv[:, :, pc * 512:(pc + 1) * 512], [], [ckf])
                    k.cp("dve", ckb[:], ckf[:], [ckf], [ckb])
                    k.dma(kT_s[sb_][:, :, pc * 512:(pc + 1) * 512], ckb[:], [ckb], [T_kT], key=ckb)
                    k.dma(cvf[0:64, :], I["cache_kiT"][sb_][:, pc * 512:(pc + 1) * 512], [], [cvf])
                    k.cp("pool", cvb[0:64, :], cvf[0:64, :], [cvf], [cvb])
                    k.dma(kiT_s[sb_][:, pc * 512:(pc + 1) * 512], cvb[0:64, :], [cvb], [T_ki], key=cvb)
                for rb_ in range(16):
                    k.dma(cvf[:], I["cache_v"][sb_][rb_ * 128:(rb_ + 1) * 128, :], [], [cvf])
                    k.cp("pool", cvb2[:, :, 0:64], cvf[:].rearrange("p (h d) -> p h d", d=64), [cvf], [cvb2])
                    k.dma(v_s[sb_][rb_ * 128:(rb_ + 1) * 128, :], cvb2[:].rearrange("p h d -> p (h d)"), [cvb2], [T_v], key=cvb2)
            S.barrier()

        def pieces(n):
            out, c = [], 0
            while c < n:
                w = min(512, n - c)
                out.append((c, w))
                c += w
            return out

        def rms(src, dst, c0, n, sqb, rsb, bank=0):
            k.act(sqb[:, :, 0:n], src[:, :, c0:c0 + n], AF.Square, [src], [sqb])
            for kc in range(8):
                k.mm(banks[bank][:, 0:n], ones_bf[:], sqb[:, kc, 0:n], [ones_bf, sqb], PQ[bank], start=(kc == 0), stop=(kc == 7))
            k.act(rsb[:, 0:n], banks[bank][:, 0:n], AF.Sqrt, PQ[bank] + [epsT], [rsb], scale=1.0 / D, bias=epsT[:, 0:1])
            k.recip(rsb[:, 0:n], rsb[:, 0:n], [rsb], [rsb])
            if dst is not None:
                for kc in range(8):
                    k.tt("dve" if kc % 2 == 0 else "pool", dst[:, kc, c0:c0 + n], src[:, kc, c0:c0 + n], rsb[:, 0:n], ALU.mult, [src, rsb], [dst])

        NIT = 26

        def own_block(jb):
            NQ = jb["NQ"]
            NTOK = 128 * NQ
            with ExitStack() as esA:
                xo = sbt(esA, "xo", [128, 8, NTOK], F32)
                hT = sbt(esA, "hTo", [128, 8, NTOK], BF16)
                oaT = sbt(esA, "oaT", [128, 4, NTOK], BF16)
                sqb = sbt(esA, "sqb", [128, 8, 512], BF16)
                rsb = sbt(esA, "rsb", [128, 512], F32)
                k.dma(xo[:], jb["xsrc"], [], [xo])
                for (c0, n) in pieces(NTOK):
                    rms(xo, hT, c0, n, sqb, rsb)
                if KP2 < 1:
                    return
                with ExitStack() as es:
                    W2 = sbt(es, "W2", [128, 8, 1032], BF16)
                    k.dma(W2[:, :, 0:512], wscr_in[:, :, C_QA:C_QA + 512], [T_wscr_in], [W2])
                    k.dma(W2[:, :, 512:1024], wscr_in[:, :, C_QI:C_QI + 512], [T_wscr_in], [W2])
                    k.dma(W2[:, :, 1024:1032], wscr_in[:, :, C_WI:C_WI + 8], [T_wscr_in], [W2])
                    qaT = sbt(es, "qaT", [128, 4, NTOK], BF16)
                    qiT = sbt(es, "qiT", [128, 4, NTOK], BF16)
                    wiT = sbt(es, "wiT", [128, NQ, 8], F32)
                    n_ = 0
                    for (dstq, cb) in ((qaT, 0), (qiT, 512)):
                        for p in range(4):
                            for (c0, n) in pieces(NTOK):
                                bk = n_ % 2
                                n_ += 1
                                for kc in range(8):
                                    k.mm(banks[bk][:, 0:n], W2[:, kc, cb + p * 128:cb + (p + 1) * 128], hT[:, kc, c0:c0 + n], [W2, hT], PQ[bk], start=(kc == 0), stop=(kc == 7))
                                k.act(dstq[:, p, c0:c0 + n], banks[bk][:, 0:n], AF.Copy, PQ[bk], [dstq], scale=0.125)
                    for qb in range(NQ):
                        for kc in range(8):
                            k.mm(banks[2][:, 0:8], hT[:, kc, qb * 128:(qb + 1) * 128], W2[:, kc, 1024:1032], [W2, hT], PQ[2], start=(kc == 0), stop=(kc == 7))
                        k.ts("dve", wiT[:, qb, :], banks[2][:, 0:8], 8.0 ** -0.5, ALU.mult, PQ[2], [wiT])
                    if KP2 < 2:
                        return
                    LMAX = max(jb["L"])
                    kiT2 = sbt(es, "kiT2", [128, LMAX], BF16)
                    sc = sbt(es, "sc", [128, LMAX], F32)
                    Mb = sbt(es, "Mb", [128, LMAX], BF16)
                    MT = sbt(es, "MT", [128, LMAX // 128, 128], BF16)
                    tmpf = [sbt(es, f"tmpf{i}", [128, 512], F32) for i in range(2)]
                    pen = sbt(es, "pen", [128, 512], F32)
                    Kc = [sbt(es, f"Kc{i}", [128, 4, 512], BF16) for i in range(2)]
                    Vc = [sbt(es, f"Vc{i}", [128, 4, 8, 66], BF16) for i in range(2)]
                    PT = [sbt(es, f"PT{i}", [128, 4, 128], BF16) for i in range(4)]
                    oa = sbt(es, "oa", [128, 8, 64], BF16)
                    den = sbt(es, "den", [128, 8], F32)
                    sm = {nm: sbt(es, nm, [128, 1], F32) for nm in ("mx", "lo", "hi", "mid", "cnt", "ge", "d1", "d2", "off")}
                    for i in range(2):
                        k.memset("pool", Vc[i][:, :, :, 64:65], 1.0, [Vc[i]])
                    lgs = sbt(es, "lgs", [128, 512], F32)

                    def gen_S(qb):
                        L = jb["L"][qb]
                        nkb = L // 128
                        q0 = qb * 128
                        lcol = jb["limcol"] + qb
                        k.dma(kiT2[0:64, 0:L], jb["kiT"][:, 0:L], [T_ki], [kiT2])
                        k.dma(kiT2[64:128, 0:L], jb["kiT"][:, 0:L], [T_ki], [kiT2])
                        tiles = pieces(L)
                        n_ = 0
                        for (c0, w) in tiles:
                            for h in range(8):
                                p, r0 = h // 2, 64 * (h % 2)
                                bk = n_ % 2
                                tf = tmpf[n_ % 2]
                                n_ += 1
                                k.mm(banks[bk][:, 0:w], qiT[r0:r0 + 64, p, q0:q0 + 128], kiT2[r0:r0 + 64, c0:c0 + w], [qiT, kiT2], PQ[bk])
                                k.act(tf[:, 0:w], banks[bk][:, 0:w], AF.Relu, PQ[bk], [tf])
                                if h == 0:
                                    k.ts("dve", sc[:, c0:c0 + w], tf[:, 0:w], wiT[:, qb, 0:1], ALU.mult, [tf, wiT], [sc])
                                else:
                                    k.stt(sc[:, c0:c0 + w], tf[:, 0:w], wiT[:, qb, h:h + 1], sc[:, c0:c0 + w], ALU.mult, ALU.add, [tf, wiT, sc], [sc])
                            yield
                        k.S.op("dve", lambda e, o=sm["mx"][:], i=sc[:, 0:L]: e.tensor_reduce(out=o, in_=i, axis=AX.X, op=ALU.max, apply_absolute_value=True), reads=[sc], writes=[sm["mx"]])
                        k.ts("dve", sm["hi"][:], sm["mx"][:], 1.0, ALU.add, [sm["mx"]], [sm["hi"]])
                        k.ts("dve", sm["lo"][:], sm["mx"][:], -1.0, ALU.mult, [sm["mx"]], [sm["lo"]], s2=-1.0, op1=ALU.add)
                        (c0, w) = tiles[-1]
                        k.ts("dve", sm["off"][:], lims[:, lcol:lcol + 1], float(-c0), ALU.add, [lims], [sm["off"]])
                        k.ts("dve", pen[:, 0:w], iota[:, 0:w], sm["off"][:, 0:1], ALU.is_ge, [iota, sm["off"]], [pen], s2=NEG, op1=ALU.mult)
                        k.tt("dve", sc[:, c0:c0 + w], sc[:, c0:c0 + w], pen[:, 0:w], ALU.add, [sc, pen], [sc])
                        if jb["lolim"] is not None:
                            for ti in range(min(3, len(tiles))):
                                (c0, w) = tiles[ti]
                                k.ts("dve", sm["off"][:], lims[:, jb["lolim"]:jb["lolim"] + 1], float(-c0), ALU.add, [lims], [sm["off"]])
                                k.ts("dve", pen[:, 0:w], iota[:, 0:w], sm["off"][:, 0:1], ALU.is_lt, [iota, sm["off"]], [pen], s2=NEG, op1=ALU.mult)
                                k.tt("dve", sc[:, c0:c0 + w], sc[:, c0:c0 + w], pen[:, 0:w], ALU.add, [sc, pen], [sc])

                    def gen_B(qb):
                        L = jb["L"][qb]
                        nkb = L // 128
                        q0 = qb * 128
                        lcol = jb["limcol"] + qb
                        k.tt("dve", sm["d1"][:], sm["hi"][:], sm["lo"][:], ALU.subtract, [sm["hi"], sm["lo"]], [sm["d1"]])
                        for it in range(NIT):
                            k.ts("dve", sm["d1"][:], sm["d1"][:], 0.5, ALU.mult, [sm["d1"]], [sm["d1"]])
                            k.tt("dve", sm["mid"][:], sm["lo"][:], sm["d1"][:], ALU.add, [sm["lo"], sm["d1"]], [sm["mid"]])
                            k.ts("dve", Mb[:, 0:L], sc[:, 0:L], sm["mid"][:, 0:1], ALU.is_gt, [sc, sm["mid"]], [Mb, sm["cnt"]], s2=0.0, op1=ALU.add, accum=sm["cnt"][:, 0:1])
                            k.stt(sm["ge"][:], sm["cnt"][:], 255.5, sm["d1"][:], ALU.is_ge, ALU.mult, [sm["cnt"], sm["d1"]], [sm["ge"]])
                            k.tt("dve", sm["lo"][:], sm["lo"][:], sm["ge"][:], ALU.add, [sm["lo"], sm["ge"]], [sm["lo"]])
                            yield
                        k.ts("dve", Mb[:, 0:L], sc[:, 0:L], sm["lo"][:, 0:1], ALU.is_gt, [sc, sm["lo"]], [Mb])

                    def do_T(qb):
                        L = jb["L"][qb]
                        nkb = L // 128
                        q0 = qb * 128
                        lcol = jb["limcol"] + qb
                        for kb0 in range(0, nkb, 4):
                            nb = min(4, nkb - kb0)
                            bk = 2 + (kb0 // 4) % 2
                            for i in range(nb):
                                k.mm(banks[bk][:, i * 128:(i + 1) * 128], Mb[:, (kb0 + i) * 128:(kb0 + i + 1) * 128], ident_bf[:], [Mb, ident_bf], PQ[bk])
                            k.cp("act", MT[:, kb0:kb0 + nb, :].rearrange("p a b -> p (a b)"), banks[bk][:, 0:nb * 128], PQ[bk], [MT])

                    def gen_A(qb):
                        L = jb["L"][qb]
                        nkb = L // 128
                        q0 = qb * 128
                        nch = (L + 511) // 512
                        DEPTH = 3

                        def loads(ch):
                            wch = min(512, L - 512 * ch)
                            Kt, Vt = Kc[ch % 2], Vc[ch % 2]
                            k.dma(Kt[:, :, 0:wch], jb["kT"][:, :, 512 * ch:512 * ch + wch], [T_kT], [Kt])
                            k.dma(Vt[:, 0:wch // 128, :, :], jb["v"][512 * ch:512 * ch + wch, :].rearrange("(i p) (h d) -> p i h d", p=128, d=66), [T_v], [Vt])

                        units = []
                        for ch in range(nch):
                            wch = min(512, L - 512 * ch)
                            for i in range(wch // 128):
                                for g in range(2):
                                    units.append((ch, i, g))
                        last_of_chunk = {}
                        for idx, (ch, i, g) in enumerate(units):
                            last_of_chunk[ch] = idx

                        def logits(idx):
                            ch, i, g = units[idx]
                            bk = 2 + idx % 4
                            Kt = Kc[ch % 2]
                            r0 = 64 * g
                            for e4 in range(4):
                                k.mm(banks[bk][:, e4 * 128:(e4 + 1) * 128], Kt[r0:r0 + 64, e4, i * 128:(i + 1) * 128], qaT[r0:r0 + 64, e4, q0:q0 + 128], [Kt, qaT], PQ[bk], start=True, stop=True)

                        def softmax_pv(idx):
                            ch, i, g = units[idx]
                            kbg = 4 * ch + i
                            bk = 2 + idx % 4
                            pt = PT[idx % 4]
                            Vt = Vc[ch % 2]
                            diag = kbg >= nkb - 2
                            dd = 0 if kbg == nkb - 1 else 1
                            if diag:
                                k.tt("dve", lgs[:, :].rearrange("p (h q) -> p h q", q=128), banks[bk][:, :].rearrange("p (h q) -> p h q", q=128), BT[:, dd, g:8:2, :], ALU.add, PQ[bk] + [BT], [lgs])
                                k.act(pt[:].rearrange("p h q -> p (h q)"), lgs[:, :], AF.Exp, [lgs], [pt])
                            else:
                                k.act(pt[:].rearrange("p h q -> p (h q)"), banks[bk][:, :], AF.Exp, PQ[bk], [pt])
                            k.tt("pool" if idx % 2 == 1 else "dve", pt[:], pt[:], MT[:, kbg:kbg + 1, :].to_broadcast([128, 4, 128]), ALU.mult, [pt, MT], [pt])
                            for e4 in range(4):
                                h = 2 * e4 + g
                                k.mm(banks[6 + g][:, e4 * 65:(e4 + 1) * 65], pt[:, e4, :], Vt[:, i, h, 0:65], [pt, Vt], PQ[6 + g], start=(kbg == 0 and e4 == 0), stop=(kbg == nkb - 1 and e4 == 3))
                            if last_of_chunk[ch] == idx and ch + 2 < nch:
                                loads(ch + 2)

                        loads(0)
                        if nch > 1:
                            loads(1)
                        nu = len(units)
                        for idx in range(nu + DEPTH):
                            if idx < nu:
                                logits(idx)
                            if idx - DEPTH >= 0:
                                softmax_pv(idx - DEPTH)
                            if idx % 3 == 2:
                                yield

                    def do_N(qb):
                        L = jb["L"][qb]
                        nkb = L // 128
                        q0 = qb * 128
                        lcol = jb["limcol"] + qb
                        for g in range(2):
                            ov = banks[6 + g][:, 0:260].rearrange("p (h d) -> p h d", d=65)
                            k.ts("dve", den[:, 4 * g:4 * g + 4], ov[:, :, 64], 1e-30, ALU.add, PQ[6 + g], [den])
                            k.recip(den[:, 4 * g:4 * g + 4], den[:, 4 * g:4 * g + 4], [den], [den])
                            k.tt("dve", oa[:, g:8:2, :], ov[:, :, 0:64], den[:, 4 * g:4 * g + 4].unsqueeze(2).to_broadcast([128, 4, 64]), ALU.mult, PQ[6 + g] + [den], [oa])
                        oaf = oa[:].rearrange("p h d -> p (h d)")
                        for c in range(4):
                            k.mm(banks[2][:, c * 128:(c + 1) * 128], oaf[:, c * 128:(c + 1) * 128], ident_bf[:], [oa, ident_bf], PQ[2])
                        k.cp("act", oaT[:, :, q0:q0 + 128], banks[2][:, :].rearrange("p (c t) -> p c t", t=128), PQ[2], [oaT])

                    def drain(g):
                        for _ in g:
                            pass

                    def interleave(ga, gb):
                        alive = [ga, gb]
                        while alive:
                            for g_ in list(alive):
                                try:
                                    next(g_)
                                except StopIteration:
                                    alive.remove(g_)

                    def gen_SB(qb):
                        yield from gen_S(qb)
                        yield from gen_B(qb)

                    drain(gen_SB(0))
                    for qb in range(NQ):
                        do_T(qb)
                        if qb + 1 < NQ:
                            interleave(gen_A(qb), gen_SB(qb + 1))
                        else:
                            drain(gen_A(qb))
                        do_N(qb)
                    S.barrier()
                if KP2 < 8:
                    return
                with ExitStack() as es:
                    Wpa = sbt(es, "Wpa", [128, 4, D], BF16)
                    Wpb = sbt(es, "Wpb", [128, 4, D], BF16)
                    Wg = sbt(es, "Wg", [128, 8, 2048], BF16)
                    Wo = sbt(es, "Wo", [128, 8, D], BF16)
                    Wpl = sbt(es, "Wpl", [128, 2, D], BF16)
                    k.dma(Wpa[:], ws_pa, [T_wscr_in], [Wpa])
                    k.dma(Wpb[:], ws_pb, [T_wscr_in], [Wpb])
                    k.dma(Wg[:], wscr_in[:, :, C_GA:C_GA + 2048], [T_wscr_in], [Wg])
                    k.dma(Wpl[:], ws_ple, [T_wscr_in], [Wpl])
                    obT = sbt(es, "obT", [128, 4, NTOK], BF16)
                    k.dma(obT[:], jb["obT"], [T_ob], [obT])
                    pf = sbt(es, "pf", [128, 2, NTOK], F32)
                    pb = sbt(es, "pb", [128, 2, NTOK], BF16)
                    k.dma(pf[:], jb["psrc"], [], [pf])
                    k.cp("pool", pb[:], pf[:], [pf], [pb])
                    mixT = sbt(es, "mixT", [128, 8, 512], BF16)
                    h2T = sbt(es, "h2T", [128, 8, 512], BF16)
                    actT = sbt(es, "actT", [128, 22, 512], BF16)
                    ughalo = sbt(es, "ughalo", [128, 22, 2], F32)
                    fco = sbt(es, "fco", [128, 22, 2], F32)
                    ugc = [sbt(es, f"ugc{i}", [128, 514], F32) for i in range(2)]
                    cva = [sbt(es, f"cva{i}", [128, 512], F32) for i in range(2)]
                    sga = sbt(es, "sga", [128, 512], F32)
                    sgb = sbt(es, "sgb", [128, 512], F32)
                    t1 = sbt(es, "t1", [128, 512], F32)
                    wug = [sbt(es, f"wug{i}", [128, 8, 128], BF16) for i in range(2)]
                    wuv = [sbt(es, f"wuv{i}", [128, 8, 128], BF16) for i in range(2)]
                    wdn = [sbt(es, "wdn0", [128, 22, 128], BF16)] * 2
                    ytok = sbt(es, "ytok", [128, D], F32)
                    if jb["fhalo"] is not None:
                        k.dma(ughalo[:], jb["fhalo"], [], [ughalo])
                    for (c0, n, halo) in jb["segs"]:
                        sg = slice(c0, c0 + n)
                        k.dma(Wo[:], ws_out, [T_wscr_in], [Wo])
                        for c in range(8):
                            cs = slice(c * 128, (c + 1) * 128)
                            bo = 4 * (c % 2)
                            for kc in range(4):
                                k.mm(banks[bo + 0][:, 0:n], Wpa[:, kc, cs], oaT[:, kc, sg], [Wpa, oaT], PQ[bo + 0], start=(kc == 0), stop=(kc == 3))
                            for kc in range(8):
                                k.mm(banks[bo + 1][:, 0:n], Wg[:, kc, cs], hT[:, kc, sg], [Wg, hT], PQ[bo + 1], start=(kc == 0), stop=(kc == 7))
                            k.act(sga[:, 0:n], banks[bo + 1][:, 0:n], AF.Sigmoid, PQ[bo + 1], [sga])
                            k.tt("dve", t1[:, 0:n], banks[bo + 0][:, 0:n], sga[:, 0:n], ALU.mult, PQ[bo + 0] + [sga], [t1])
                            for kc in range(4):
                                k.mm(banks[bo + 2][:, 0:n], Wpb[:, kc, cs], obT[:, kc, sg], [Wpb, obT], PQ[bo + 2], start=(kc == 0), stop=(kc == 3))
                            for kc in range(8):
                                k.mm(banks[bo + 3][:, 0:n], Wg[:, kc, 1024 + c * 128:1024 + (c + 1) * 128], hT[:, kc, sg], [Wg, hT], PQ[bo + 3], start=(kc == 0), stop=(kc == 7))
                            k.act(sgb[:, 0:n], banks[bo + 3][:, 0:n], AF.Sigmoid, PQ[bo + 3], [sgb])
                            k.tt("dve", sgb[:, 0:n], banks[bo + 2][:, 0:n], sgb[:, 0:n], ALU.mult, PQ[bo + 2] + [sgb], [sgb])
                            k.tt("pool", mixT[:, c, 0:n], t1[:, 0:n], sgb[:, 0:n], ALU.add, [t1, sgb], [mixT])
                        for c in range(8):
                            cs = slice(c * 128, (c + 1) * 128)
                            bk = 4 + c % 2
                            for kc in range(8):
                                k.mm(banks[bk][:, 0:n], Wo[:, kc, cs], mixT[:, kc, 0:n], [Wo, mixT], PQ[bk], start=(kc == 0), stop=(kc == 7))
                            k.tt("dve", xo[:, c, sg], xo[:, c, sg], banks[bk][:, 0:n], ALU.add, [xo] + PQ[bk], [xo])
                        k.act(sqb[:, :, 0:n], xo[:, :, sg], AF.Square, [xo], [sqb])
                        for kc in range(8):
                            k.mm(banks[0][:, 0:n], ones_bf[:], sqb[:, kc, 0:n], [ones_bf, sqb], PQ[0], start=(kc == 0), stop=(kc == 7))
                        k.act(rsb[:, 0:n], banks[0][:, 0:n], AF.Sqrt, PQ[0] + [epsT], [rsb], scale=1.0 / D, bias=epsT[:, 0:1])
                        k.recip(rsb[:, 0:n], rsb[:, 0:n], [rsb], [rsb])
                        for kc in range(8):
                            k.tt("dve" if kc % 2 == 0 else "pool", h2T[:, kc, 0:n], xo[:, kc, sg], rsb[:, 0:n], ALU.mult, [xo, rsb], [h2T])
                        for cc in range(22):
                            wg_, wv_ = wug[cc % 2], wuv[cc % 2]
                            k.dma(wg_[:], ws_up[:, :, cc * 128:(cc + 1) * 128], [T_wscr_in], [wg_])
                            bk = 1 + cc % 2
                            for kc in range(8):
                                k.mm(banks[bk][:, 0:n], wg_[:, kc, :], h2T[:, kc, 0:n], [wg_, h2T], PQ[bk], start=(kc == 0), stop=(kc == 7))
                            if halo:
                                k.cp("act", ughalo[:, cc, 0:n], banks[bk][:, 0:n], PQ[bk], [ughalo])
                                continue
                            k.dma(wv_[:], ws_up[:, :, DFF + cc * 128:DFF + (cc + 1) * 128], [T_wscr_in], [wv_])
                            ug, ca = ugc[cc % 2], cva[cc % 2]
                            k.cp("act", ug[:, 2:2 + n], banks[bk][:, 0:n], PQ[bk], [ug])
                            k.cp("pool", ug[:, 0:2], ughalo[:, cc, :], [ughalo], [ug])
                            k.ts("dve", ca[:, 0:n], ug[:, 0:n], convf[:, cc, 0:1], ALU.mult, [ug, convf], [ca])
                            k.stt(ca[:, 0:n], ug[:, 1:1 + n], convf[:, cc, 1:2], ca[:, 0:n], ALU.mult, ALU.add, [ug, convf, ca], [ca])
                            k.stt(ca[:, 0:n], ug[:, 2:2 + n], convf[:, cc, 2:3], ca[:, 0:n], ALU.mult, ALU.add, [ug, convf, ca], [ca])
                            k.act(ca[:, 0:n], ca[:, 0:n], AF.Gelu_apprx_tanh, [ca], [ca])
                            nv = jb["nvalid"]
                            k.cp("pool", fco[:, cc, :], ug[:, nv:nv + 2], [ug], [fco])
                            bk2 = 3 + cc % 2
                            for kc in range(8):
                                k.mm(banks[bk2][:, 0:n], wv_[:, kc, :], h2T[:, kc, 0:n], [wv_, h2T], PQ[bk2], start=(kc == 0), stop=(kc == 7))
                            k.tt("dve", actT[:, cc, 0:n], ca[:, 0:n], banks[bk2][:, 0:n], ALU.mult, [ca] + PQ[bk2], [actT])
                        if halo:
                            continue
                        for c in range(8):
                            wd_ = wdn[c % 2]
                            k.dma(wd_[:], ws_down[:, :, c * 128:(c + 1) * 128], [T_wscr_in], [wd_])
                            bk = 5 + c % 2
                            for cc in range(22):
                                k.mm(banks[bk][:, 0:n], wd_[:, cc, :], actT[:, cc, 0:n], [wd_, actT], PQ[bk], start=(cc == 0), stop=(cc == 21))
                            k.tt("dve", xo[:, c, sg], xo[:, c, sg], banks[bk][:, 0:n], ALU.add, [xo] + PQ[bk], [xo])
                        k.dma(Wo[:], ws_pg, [T_wscr_in], [Wo])
                        Wpg = Wo
                        k.act(sqb[:, :, 0:n], xo[:, :, sg], AF.Square, [xo], [sqb])
                        for kc in range(8):
                            k.mm(banks[0][:, 0:n], ones_bf[:], sqb[:, kc, 0:n], [ones_bf, sqb], PQ[0], start=(kc == 0), stop=(kc == 7))
                        k.act(rsb[:, 0:n], banks[0][:, 0:n], AF.Sqrt, PQ[0] + [epsT], [rsb], scale=1.0 / D, bias=epsT[:, 0:1])
                        k.recip(rsb[:, 0:n], rsb[:, 0:n], [rsb], [rsb])
                        for kc in range(8):
                            k.tt("dve" if kc % 2 == 0 else "pool", h2T[:, kc, 0:n], xo[:, kc, sg], rsb[:, 0:n], ALU.mult, [xo, rsb], [h2T])
                        for c in range(8):
                            cs = slice(c * 128, (c + 1) * 128)
                            bo = 2 * (c % 2)
                            for kc in range(8):
                                k.mm(banks[1 + bo][:, 0:n], Wpg[:, kc, cs], h2T[:, kc, 0:n], [Wpg, h2T], PQ[1 + bo], start=(kc == 0), stop=(kc == 7))
                            k.act(sga[:, 0:n], banks[1 + bo][:, 0:n], AF.Sigmoid, PQ[1 + bo], [sga])
                            for kc in range(2):
                                k.mm(banks[2 + bo][:, 0:n], Wpl[:, kc, cs], pb[:, kc, sg], [Wpl, pb], PQ[2 + bo], start=(kc == 0), stop=(kc == 1))
                            k.tt("dve", t1[:, 0:n], banks[2 + bo][:, 0:n], sga[:, 0:n], ALU.mult, PQ[2 + bo] + [sga], [t1])
                            k.tt("pool", xo[:, c, sg], xo[:, c, sg], t1[:, 0:n], ALU.add, [xo, t1], [xo])
                        k.act(sqb[:, :, 0:n], xo[:, :, sg], AF.Square, [xo], [sqb])
                        for kc in range(8):
                            k.mm(banks[0][:, 0:n], ones_bf[:], sqb[:, kc, 0:n], [ones_bf, sqb], PQ[0], start=(kc == 0), stop=(kc == 7))
                        k.act(rsb[:, 0:n], banks[0][:, 0:n], AF.Sqrt, PQ[0] + [epsT], [rsb], scale=1.0 / D, bias=epsT[:, 0:1])
                        k.recip(rsb[:, 0:n], rsb[:, 0:n], [rsb], [rsb])
                        for kc in range(8):
                            k.stt(xo[:, kc, sg], xo[:, kc, sg], gains[:, 24 + kc:25 + kc], rsb[:, 0:n], ALU.mult, ALU.mult, [xo, gains, rsb], [xo])
                        for tt in range(n // 128):
                            for cg in range(2):
                                bk = 3 + cg
                                for c4 in range(4):
                                    k.mm(banks[bk][:, c4 * 128:(c4 + 1) * 128], xo[:, 4 * cg + c4, c0 + tt * 128:c0 + (tt + 1) * 128], ident, [xo, cst], PQ[bk])
                                k.cp("act" if cg == 0 else "dve", ytok[:, cg * 512:(cg + 1) * 512], banks[bk][:, :], PQ[bk], [ytok])
                            nv = min(128, jb["nvalid"])
                            k.dma(jb["y_out"][tt * 128:tt * 128 + nv, :], ytok[0:nv, :], [ytok], [outT], key=ytok)
                        k.dma(jb["fconv_out"], fco[:], [fco], [outT], key=fco)
                    S.barrier()

        xav = I["xa"].rearrange("(kc p) t -> p kc t", p=128)
        pav = I["pT"].rearrange("(kc p) t -> p kc t", p=128)
        if "P" in KJOB2:
            for m in range(4):
                ps_ = 4 * m + 3
                t0 = 512 * ps_ - 128
                own_block(dict(NQ=5, xsrc=xav[:, :, t0:t0 + 640], psrc=pav[:, :, t0:t0 + 640], L=[128 * (4 * ps_ + r) for r in range(5)],
                               limcol=5 * m, lolim=22, kiT=kiT_p, kT=kT_p, v=v_p, obT=obT_p[:, :, t0:t0 + 640], fhalo=None,
                               segs=[(126, 2, True), (128, 512, False)], nvalid=512, y_out=O["y_own"][512 * m:512 * (m + 1), :], fconv_out=O["fconv_p"]))
        if "S" in KJOB2:
            for sb_ in range(2):
                own_block(dict(NQ=1, xsrc=I["xs"][sb_].rearrange("(kc p) t -> p kc t", p=128), psrc=I["psT"][sb_].rearrange("(kc p) t -> p kc t", p=128),
                               L=[LKS], limcol=20 + sb_, lolim=None, kiT=kiT_s[sb_], kT=kT_s[sb_], v=v_s[sb_], obT=obT_s[sb_], fhalo=I["fconvT"][sb_],
                               segs=[(0, 128, False)], nvalid=16, y_out=O["y_s"][sb_], fconv_out=O["fconv_s"][sb_]))

        print('nsem', S.nsem, {e: len(q) for e, q in S.q.items()})
        S.final_wait("sp", [outT])
        S.emit()
    return nc


def _consts():
    c = np.zeros((128, 7 * 128), np.float32)
    i = np.arange(128)
    same = (i[:, None] // 64) == (i[None, :] // 64)
    c[:, 0:128] = np.eye(128)
    c[:, 128:256] = ((i[:, None] <= i[None, :]) & same)
    c[:, 256:384] = ((i[:, None] > i[None, :]) & same)
    c[:, 384:512] = np.where((i[:, None] > i[None, :]) & same, 0.0, NEG)
    c[:, 512:640] = np.where((i[:, None] <= i[None, :]) & same, 0.0, -NEG)
    c[:, 640:768] = (i[:, None] < 64)
    c[:, 768:896] = (i[:, None] >= 64)
    return c


_NC = None


def kernel(**inp):
    global _NC
    f32 = np.float32
    xp = np.asarray(inp["x_prompt"], f32)
    xs = np.asarray(inp["x_sample"], f32)
    cst = _consts()
    gains = np.zeros((128, 32), f32)
    gains[:, 0:8] = np.asarray(inp["norm_mix"], f32)[0].reshape(8, 128).T
    gains[:, 8:16] = np.asarray(inp["norm_ffn"], f32)[0].reshape(8, 128).T
    gains[:, 16:24] = np.asarray(inp["norm_ple"], f32)[0].reshape(8, 128).T
    gains[:, 24:32] = np.asarray(inp["norm_final"], f32).reshape(8, 128).T
    convb = np.ascontiguousarray(np.asarray(inp["conv_b"], f32)[0].T.reshape(12, 128, 4).transpose(1, 0, 2))
    alog = np.ascontiguousarray(np.broadcast_to(np.asarray(inp["a_log"], f32)[0][None, :], (128, 4)))
    dtb = np.ascontiguousarray(np.broadcast_to(np.asarray(inp["dt_bias"], f32)[0][None, :], (128, 4)))
    ngdn = np.ascontiguousarray(np.broadcast_to(np.asarray(inp["norm_gdn"], f32)[0][None, :], (128, 128)))
    gval = np.zeros((128, 2), f32)
    gval[0:16, 0] = 1.0
    gval[:, 1] = 1.0
    w_in = np.ascontiguousarray(np.asarray(inp["w_in"], f32)[0])
    ck = np.asarray(inp["cache_k"], f32)[0]
    cvv = np.asarray(inp["cache_v"], f32)[0]
    cki = np.asarray(inp["cache_kidx"], f32)[0]
    sg = np.asarray(inp["state_gdn"], f32)[0]
    sgc = np.asarray(inp["state_gdn_conv"], f32)[0]
    def bucket_np(rel):
        half, max_exact = 16, 8
        ret = np.where(rel > 0, half, 0)
        n = np.abs(rel)
        nf = np.maximum(n, 1).astype(np.float32)
        large = max_exact + (np.log(nf / np.float32(max_exact)) / np.float32(np.log(128 / 8)) * np.float32(half - max_exact)).astype(np.int32)
        large = np.minimum(large, half - 1)
        return ret + np.where(n < max_exact, n, large)
    ohrev = np.zeros((32, 384), f32)
    sp_ = np.arange(383)
    ohrev[bucket_np(127 - sp_), sp_] = 1.0
    iota = np.ascontiguousarray(np.broadcast_to(np.arange(512, dtype=f32)[None, :], (128, 512)))
    relb = np.ascontiguousarray(np.asarray(inp["rel_bias"], f32))
    relb15 = np.ascontiguousarray(relb[15][:, None])
    convf = np.ascontiguousarray(np.asarray(inp["conv_ffn"], f32)[0].T.reshape(22, 128, 3).transpose(1, 0, 2))
    pp = np.asarray(inp["p_prompt"], f32)[0]
    psm = np.asarray(inp["p_sample"], f32)[0]
    sfc = np.asarray(inp["state_ffn_conv"], f32)[0]
    wts = dict(w_pa=np.ascontiguousarray(np.asarray(inp["w_proj_a"], f32)[0]), w_pb=np.ascontiguousarray(np.asarray(inp["w_proj_b"], f32)[0]),
               w_out=np.ascontiguousarray(np.asarray(inp["w_out"], f32)[0]), w_up=np.ascontiguousarray(np.asarray(inp["w_up"], f32)[0]),
               w_down=np.ascontiguousarray(np.asarray(inp["w_down"], f32)[0]), w_ple=np.ascontiguousarray(np.asarray(inp["w_ple"], f32)[0]),
               w_pg=np.ascontiguousarray(np.asarray(inp["w_ple_gate"], f32)[0]))
    in_maps = []
    for c in range(8):
        b, j = c // 4, c % 4
        pad = 512 * (3 - j)
        xa = np.zeros((D, LKP), f32)
        xa[:, pad:] = xp[b].T[:, :LKP - pad]
        sl = slice(2 * c, 2 * c + 2)
        xs_c = np.zeros((2, D, 128), f32)
        xs_c[:, :, 0:16] = xs[sl].transpose(0, 2, 1)
        m = dict(xa=xa, xs=xs_c, cst=cst, gains=gains, convb=convb, alog=alog, dtb=dtb, ngdn=ngdn, gval=gval, w_in=w_in,
                 cache_kT=np.ascontiguousarray(ck[sl].reshape(2, PAST, 512).transpose(0, 2, 1)),
                 cache_v=np.ascontiguousarray(cvv[sl].reshape(2, PAST, 512)),
                 cache_kiT=np.ascontiguousarray(cki[sl].transpose(0, 2, 1)),
                 state_gdn=np.ascontiguousarray(sg[sl]),
                 gconvT=np.ascontiguousarray(sgc[sl].transpose(0, 2, 1).reshape(2, 12, 128, 3).transpose(0, 2, 1, 3)))
        pT = np.zeros((256, LKP), f32)
        pT[:, pad:] = pp[b].T[:, :LKP - pad]
        psT = np.zeros((2, 256, 128), f32)
        psT[:, :, 0:16] = psm[sl].transpose(0, 2, 1)
        lims = np.zeros((128, 24), f32)
        ii = np.arange(128)
        for m_ in range(4):
            for qb in range(5):
                pt = 512 * (4 * m_ + 3) - 128 + 128 * qb + ii
                lims[:, 5 * m_ + qb] = (pt // 64 + 1) * 64
        lims[:, 20] = PAST + 16
        lims[:, 21] = PAST + 16
        lims[:, 22] = pad
        m.update(pT=pT, psT=psT, convf=convf, relb=relb, relb15=relb15, ohrev=ohrev, iota=iota, lims=lims,
                 fconvT=np.ascontiguousarray(sfc[sl].transpose(0, 2, 1).reshape(2, 22, 128, 2).transpose(0, 2, 1, 3)), **wts)
        in_maps.append(m)
    if _NC is None:
        _NC = build_program()
    res = run_bass_kernel_spmd(_NC, in_maps, core_ids=list(range(8)))
    R = res.results
    y_p = np.zeros((2, 8192, D), f32)
    for c in range(8):
        b, j = c // 4, c % 4
        for m_ in range(4):
            sg_ = 4 * m_ + j
            y_p[b, 512 * sg_:512 * (sg_ + 1)] = R[c]["y_own"][512 * m_:512 * (m_ + 1)]
    y_s = np.concatenate([R[c]["y_s"] for c in range(8)]).reshape(16, 16, D)
    k_p = np.stack([R[4 * b + 3]["k_all"] for b in range(2)]).reshape(1, 2, 8192, 8, 64)
    v_p = np.stack([R[4 * b + 3]["v_all"] for b in range(2)]).reshape(1, 2, 8192, 8, 64)
    ki_p = np.stack([R[4 * b + 3]["ki_all"] for b in range(2)]).reshape(1, 2, 8192, 64)
    gdn_p = np.stack([R[4 * b + 3]["gdn_p"] for b in range(2)]).reshape(1, 2, 4, 128, 128)
    gconv_p = np.stack([R[4 * b + 3]["gconv_p"].transpose(1, 0, 2).reshape(1536, 3).T for b in range(2)]).reshape(1, 2, 3, 1536)
    fconv_p = np.stack([R[4 * b + 3]["fconv_p"].transpose(1, 0, 2).reshape(DFF, 2).T for b in range(2)]).reshape(1, 2, 2, DFF)
    k_s = np.concatenate([R[c]["k_s"] for c in range(8)]).reshape(1, 16, 16, 8, 64)
    v_s = np.concatenate([R[c]["v_s"] for c in range(8)]).reshape(1, 16, 16, 8, 64)
    ki_s = np.concatenate([R[c]["ki_s"] for c in range(8)]).reshape(1, 16, 16, 64)
    gdn_s = np.concatenate([R[c]["gdn_s"] for c in range(8)]).reshape(1, 16, 4, 128, 128)
    gconv_s = np.concatenate([R[c]["gconv_s"] for c in range(8)])
    gconv_s = np.ascontiguousarray(gconv_s.transpose(0, 2, 1, 3).reshape(16, 1536, 3).transpose(0, 2, 1)).reshape(1, 16, 3, 1536)
    fconv_s = np.concatenate([R[c]["fconv_s"] for c in range(8)])
    fconv_s = np.ascontiguousarray(fconv_s.transpose(0, 2, 1, 3).reshape(16, DFF, 2).transpose(0, 2, 1)).reshape(1, 16, 2, DFF)
    return (y_p, y_s, k_p, v_p, ki_p, gdn_p, gconv_p, fconv_p, k_s, v_s, ki_s, gdn_s, gconv_s, fconv_s)
```
